# Optimizing a Trainium2 kernel written in Bass

```python
import math
import jax
import jax.numpy as jnp
from jax import lax
import numpy as np

D_MODEL = 1024
BATCH = 8
SEQ = 2048
DEPTH = 4

HEAD_DIM = 64
DIFF_W = (D_MODEL * 3 // 8)
DELTA_W = (D_MODEL * 3 // 8)
CONV_CH = D_MODEL - DIFF_W - DELTA_W
DIFF_HEADS = DIFF_W // HEAD_DIM
DIFF_DIM = HEAD_DIM // 2
DELTA_HEADS = DELTA_W // HEAD_DIM
CONV_WIDTH = 31
SHORT_CONV = 3
CHUNK = 64
Q_BLOCK = 128
D_FF = 4 * D_MODEL
IN_W = 2 * CONV_CH + 3 * DIFF_W + 4 * DELTA_W + 4 * DELTA_HEADS
NORM_EPS = 1e-6

kernel_name = "hybrid_conv_diffattn_gdn_encoder"


def rms_norm(x, w, eps=NORM_EPS):
    xf = x.astype(jnp.float32)
    y = xf * lax.rsqrt(jnp.mean(xf * xf, axis=-1, keepdims=True) + eps)
    return (y * w.astype(jnp.float32)).astype(x.dtype)


def layer_norm(x, w, b, eps=1e-5):
    xf = x.astype(jnp.float32)
    mu = jnp.mean(xf, axis=-1, keepdims=True)
    xc = xf - mu
    var = jnp.mean(xc * xc, axis=-1, keepdims=True)
    y = xc * lax.rsqrt(var + eps) * w.astype(jnp.float32) + b.astype(jnp.float32)
    return y.astype(x.dtype)


def l2norm(x, eps=1e-6):
    xf = x.astype(jnp.float32)
    return xf * lax.rsqrt(jnp.sum(xf * xf, axis=-1, keepdims=True) + eps)


def depthwise_conv(x, w):
    k, c = w.shape
    pad = (k - 1) // 2
    return lax.conv_general_dilated(
        x, w[:, None, :].astype(x.dtype), window_strides=(1,), padding=[(pad, pad)],
        dimension_numbers=("NWC", "WIO", "NWC"), feature_group_count=c)


def alibi_slopes(n):
    def pow2(m):
        start = 2.0 ** (-8.0 / m)
        return [start ** (i + 1) for i in range(m)]
    if math.log2(n).is_integer():
        s = pow2(n)
    else:
        c = 2 ** int(math.floor(math.log2(n)))
        s = pow2(c) + pow2(2 * c)[0::2][: n - c]
    return jnp.asarray(s, dtype=jnp.float32)


def conv_module(u, dw_w, dw_b, ln_w, ln_b):
    a, g = jnp.split(u, 2, axis=-1)
    h = a * jax.nn.sigmoid(g)
    h = depthwise_conv(h, dw_w) + dw_b.astype(h.dtype)
    h = layer_norm(h, ln_w, ln_b)
    return jax.nn.silu(h)


def diff_attention(q, k, v, lam_q1, lam_k1, lam_q2, lam_k2, subln_w, layer_idx):
    B, S = q.shape[0], q.shape[1]
    H = DIFF_HEADS
    f32 = jnp.float32
    lambda_init = 0.8 - 0.6 * math.exp(-0.3 * layer_idx)
    lam = (jnp.exp(jnp.sum(lam_q1.astype(f32) * lam_k1.astype(f32)))
           - jnp.exp(jnp.sum(lam_q2.astype(f32) * lam_k2.astype(f32))) + lambda_init)
    slopes = jnp.repeat(alibi_slopes(H), 2)
    n_blk = S // Q_BLOCK
    qf = q.astype(f32).reshape(B, n_blk, Q_BLOCK, 2 * H, DIFF_DIM) * (DIFF_DIM ** -0.5)
    qf = jnp.moveaxis(qf, 1, 0)
    kf = k.astype(f32).reshape(B, S, 2 * H, DIFF_DIM)
    vf = v.astype(f32).reshape(B, S, H, HEAD_DIM)
    kpos = jnp.arange(S)

    def block(args):
        q_blk, blk = args
        s = jnp.einsum("bqmd,bkmd->bmqk", q_blk, kf)
        qpos = blk * Q_BLOCK + jnp.arange(Q_BLOCK)
        dist = jnp.abs(qpos[:, None] - kpos[None, :]).astype(f32)
        p = jax.nn.softmax(s - slopes[:, None, None] * dist, axis=-1)
        p = p.reshape(B, H, 2, Q_BLOCK, S)
        a = p[:, :, 0] - lam * p[:, :, 1]
        return jnp.einsum("bhqk,bkhe->bqhe", a, vf)

    o = lax.map(block, (qf, jnp.arange(n_blk)))
    o = jnp.moveaxis(o, 0, 1).reshape(B, S, H, HEAD_DIM)
    o = rms_norm(o, subln_w, eps=1e-5) * (1.0 - lambda_init)
    return o.reshape(B, S, H * HEAD_DIM).astype(q.dtype)


def chunk_gated_delta(q, k, v, g, beta):
    B, S, H, Dk = q.shape
    Dv = v.shape[-1]
    N = S // CHUNK
    q = q * (Dk ** -0.5)

    def chunks(t):
        return t.reshape(B, N, CHUNK, H, -1).transpose(0, 3, 1, 2, 4)

    qc, kc, vc = chunks(q), chunks(k), chunks(v)
    gc = jnp.cumsum(g.reshape(B, N, CHUNK, H).transpose(0, 3, 1, 2), axis=-1)
    bc = beta.reshape(B, N, CHUNK, H).transpose(0, 3, 1, 2)
    idx = jnp.arange(CHUNK)
    lower = idx[:, None] >= idx[None, :]
    strict = idx[:, None] > idx[None, :]
    decay = jnp.exp(jnp.where(lower, gc[..., :, None] - gc[..., None, :], -jnp.inf))
    kb = kc * bc[..., None]
    A = jnp.where(strict, jnp.einsum("bhncd,bhnsd->bhncs", kb, kc) * decay, 0.0)
    rhs = jnp.concatenate([vc * bc[..., None], kb * jnp.exp(gc)[..., None]], axis=-1)
    sol = lax.linalg.triangular_solve(A, rhs, left_side=True, lower=True, unit_diagonal=True)
    u, w = sol[..., :Dv], sol[..., Dv:]
    Aqk = jnp.einsum("bhncd,bhnsd->bhncs", qc, kc) * decay
    qg = qc * jnp.exp(gc)[..., None]
    kdec = kc * jnp.exp(gc[..., -1:] - gc)[..., None]
    glast = jnp.exp(gc[..., -1])

    def step(state, inp):
        u_i, w_i, qg_i, aqk_i, kdec_i, gl_i = inp
        v_new = u_i - jnp.einsum("bhcd,bhde->bhce", w_i, state)
        o = jnp.einsum("bhcd,bhde->bhce", qg_i, state) + jnp.einsum("bhcs,bhse->bhce", aqk_i, v_new)
        state = state * gl_i[..., None, None] + jnp.einsum("bhcd,bhce->bhde", kdec_i, v_new)
        return state, o

    xs = tuple(jnp.moveaxis(t, 2, 0) for t in (u, w, qg, Aqk, kdec, glast))
    s0 = jnp.zeros((B, H, Dk, Dv), jnp.float32)
    _, o = lax.scan(step, s0, xs)
    return o.transpose(1, 0, 3, 2, 4).reshape(B, S, H, Dv)


def gated_deltanet(qkv, z, b_f, b_b, a_f, a_b, conv_w, A_log, dt_bias, norm_w):
    B, S, _ = qkv.shape
    H, D = DELTA_HEADS, HEAD_DIM
    f32 = jnp.float32
    out_dtype = qkv.dtype
    qkv = jax.nn.silu(depthwise_conv(qkv, conv_w))
    q, k, v = jnp.split(qkv, 3, axis=-1)
    q = l2norm(q.reshape(B, S, H, D))
    k = l2norm(k.reshape(B, S, H, D))
    v = v.reshape(B, S, H, D).astype(f32)

    def log_decay(a, a_log, dtb):
        return -jnp.exp(a_log.astype(f32)) * jax.nn.softplus(a.astype(f32) + dtb.astype(f32))

    g_f = log_decay(a_f, A_log[0], dt_bias[0])
    g_b = log_decay(a_b, A_log[1], dt_bias[1])
    beta_f = jax.nn.sigmoid(b_f.astype(f32))
    beta_b = jax.nn.sigmoid(b_b.astype(f32))
    o_f = chunk_gated_delta(q, k, v, g_f, beta_f)
    rev = lambda t: jnp.flip(t, axis=1)
    o_b = rev(chunk_gated_delta(rev(q), rev(k), rev(v), rev(g_b), rev(beta_b)))
    o = rms_norm(o_f + o_b, norm_w) * jax.nn.silu(z.astype(f32).reshape(B, S, H, D))
    return o.reshape(B, S, H * D).astype(out_dtype)


def setup_inputs(seed: int = 0) -> dict:
    key = jax.random.key(seed)
    ks = jax.random.split(key, 24)
    f32 = jnp.float32
    nrm = lambda k, shape, scale: jax.random.normal(k, shape, f32) * scale
    gain = lambda k, shape: 1.0 + 0.05 * jax.random.normal(k, shape, f32)
    dt = jnp.exp(jax.random.uniform(ks[20], (DEPTH, 2, DELTA_HEADS), f32, math.log(1e-3), math.log(1e-1)))
    return {
        "x": nrm(ks[0], (BATCH, SEQ, D_MODEL), 1.0),
        "w_in": nrm(ks[1], (DEPTH, D_MODEL, IN_W), D_MODEL ** -0.5),
        "w_out": nrm(ks[2], (DEPTH, D_MODEL, D_MODEL), D_MODEL ** -0.5),
        "pre_mix_w": gain(ks[3], (DEPTH, D_MODEL)),
        "post_mix_w": gain(ks[4], (DEPTH, D_MODEL)),
        "pre_mlp_w": gain(ks[5], (DEPTH, D_MODEL)),
        "post_mlp_w": gain(ks[6], (DEPTH, D_MODEL)),
        "w_ff1": nrm(ks[7], (DEPTH, D_MODEL, D_FF), D_MODEL ** -0.5),
        "w_ff2": nrm(ks[8], (DEPTH, D_FF, D_MODEL), D_FF ** -0.5),
        "conv_dw_w": nrm(ks[9], (DEPTH, CONV_WIDTH, CONV_CH), CONV_WIDTH ** -0.5),
        "conv_dw_b": nrm(ks[10], (DEPTH, CONV_CH), 0.02),
        "conv_ln_w": gain(ks[11], (DEPTH, CONV_CH)),
        "conv_ln_b": nrm(ks[12], (DEPTH, CONV_CH), 0.02),
        "diff_lambda_q1": nrm(ks[13], (DEPTH, DIFF_DIM), 0.1),
        "diff_lambda_k1": nrm(ks[14], (DEPTH, DIFF_DIM), 0.1),
        "diff_lambda_q2": nrm(ks[15], (DEPTH, DIFF_DIM), 0.1),
        "diff_lambda_k2": nrm(ks[16], (DEPTH, DIFF_DIM), 0.1),
        "diff_subln_w": gain(ks[17], (DEPTH, HEAD_DIM)),
        "delta_conv_w": nrm(ks[18], (DEPTH, SHORT_CONV, 3 * DELTA_W), SHORT_CONV ** -0.5),
        "delta_A_log": jnp.log(jax.random.uniform(ks[19], (DEPTH, 2, DELTA_HEADS), f32, 1.0, 16.0)),
        "delta_dt_bias": dt + jnp.log(-jnp.expm1(-dt)),
        "delta_norm_w": gain(ks[21], (DEPTH, HEAD_DIM)),
    }


def reference(x, w_in, w_out, pre_mix_w, post_mix_w, pre_mlp_w, post_mlp_w, w_ff1, w_ff2,
              conv_dw_w, conv_dw_b, conv_ln_w, conv_ln_b,
              diff_lambda_q1, diff_lambda_k1, diff_lambda_q2, diff_lambda_k2, diff_subln_w,
              delta_conv_w, delta_A_log, delta_dt_bias, delta_norm_w):
    sizes = [2 * CONV_CH, DIFF_W, DIFF_W, DIFF_W, 3 * DELTA_W, DELTA_W,
             DELTA_HEADS, DELTA_HEADS, DELTA_HEADS, DELTA_HEADS]
    offs = [sum(sizes[: i + 1]) for i in range(len(sizes) - 1)]
    for l in range(DEPTH):
        h = rms_norm(x, pre_mix_w[l])
        u = h @ w_in[l]
        (u_conv, dq, dk, dv, gqkv, gz, b_f, b_b, a_f, a_b) = jnp.split(u, offs, axis=-1)
        y_conv = conv_module(u_conv, conv_dw_w[l], conv_dw_b[l], conv_ln_w[l], conv_ln_b[l])
        y_diff = diff_attention(dq, dk, dv, diff_lambda_q1[l], diff_lambda_k1[l],
                                diff_lambda_q2[l], diff_lambda_k2[l], diff_subln_w[l], l)
        y_delta = gated_deltanet(gqkv, gz, b_f, b_b, a_f, a_b, delta_conv_w[l],
                                 delta_A_log[l], delta_dt_bias[l], delta_norm_w[l])
        y = jnp.concatenate([y_conv, y_diff, y_delta], axis=-1) @ w_out[l]
        x = x + rms_norm(y, post_mix_w[l])
        h = rms_norm(x, pre_mlp_w[l])
        y = jnp.square(jax.nn.relu(h @ w_ff1[l])) @ w_ff2[l]
        x = x + rms_norm(y, post_mlp_w[l])
    return x
```

```python
import math
import os
from contextlib import ExitStack

import numpy as np
import concourse.bass as bass
import concourse.mybir as mybir
from concourse.bass_utils import run_bass_kernel_spmd

F32 = mybir.dt.float32
BF16 = mybir.dt.bfloat16
AF = mybir.ActivationFunctionType
ALU = mybir.AluOpType

D = 1024
IN_W = 3224
DFF = 4096
EPS = 1e-6
N_CORES = 8
SLOPES = [0.25, 0.0625, 0.015625, 0.00390625, 0.5, 0.125]
ALIBI_SKIP = 80.0


class Prog:
    QUEUES = ("pe", "act", "dve", "pool", "sp")

    def __init__(self, nc, es):
        self.nc = nc
        self.ops = []
        self.last_w = {}
        self.readers = {}
        self.sems = {}
        self.es = es
        self.cnt = {}
        self.waited = {q: {} for q in self.QUEUES}
        self.start = 0

    def sem(self, sig):
        if sig not in self.sems:
            nm = "s%d" % len(self.sems)
            self.sems[sig] = self.es.enter_context(self.nc.semaphore(nm))
        return self.sems[sig]

    def op(self, q, fn, reads=(), writes=(), dma=None):
        i = len(self.ops)
        sig = ("dma", dma) if dma is not None else q
        deps = {}
        for k in reads:
            w = self.last_w.get(k)
            if w is not None:
                deps[w] = True
        for k in writes:
            w = self.last_w.get(k)
            if w is not None:
                deps.setdefault(w, False)
            for r in self.readers.get(k, {}).values():
                deps.setdefault(r, False)
        need = []
        for j, raw in deps.items():
            oj = self.ops[j]
            if j < self.start:
                continue
            if oj["sig"] == q and dma is None:
                if q == "pe":
                    continue
            need.append(j)
            oj["needed"] = True
        self.ops.append(dict(q=q, fn=fn, sig=sig, deps=need, needed=(dma is not None), val=None))
        for k in writes:
            self.last_w[k] = i
            self.readers[k] = {}
        for k in reads:
            self.readers.setdefault(k, {})[sig] = i
        return i

    def wait_all_dma(self, q="sp"):
        need = []
        seen = set()
        for j in range(len(self.ops) - 1, -1, -1):
            oj = self.ops[j]
            if isinstance(oj["sig"], tuple) and oj["sig"] not in seen:
                seen.add(oj["sig"])
                need.append(j)
        self.ops.append(dict(q=q, fn=None, sig=q, deps=need, needed=False, val=None))

    def flush(self):
        self.wait_all_dma()
        nc = self.nc
        ops = self.ops[self.start:]
        self.start = len(self.ops)
        for o in ops:
            if o["needed"]:
                inc = 16 if isinstance(o["sig"], tuple) else 1
                self.cnt[o["sig"]] = self.cnt.get(o["sig"], 0) + inc
                o["val"] = self.cnt[o["sig"]]
                self.sem(o["sig"])
        allops = self.ops
        prog = self

        def run(qname, eng):
            waited = prog.waited[qname]
            for o in ops:
                if o["q"] != qname:
                    continue
                wl = {}
                for j in o["deps"]:
                    oj = allops[j]
                    wl[oj["sig"]] = max(wl.get(oj["sig"], 0), oj["val"])
                for sg, v in wl.items():
                    if waited.get(sg, 0) >= v:
                        continue
                    eng.wait_ge(prog.sems[sg], v)
                    waited[sg] = v
                if o["fn"] is not None:
                    inst = o["fn"](eng)
                    if o["needed"]:
                        inc = 16 if isinstance(o["sig"], tuple) else 1
                        inst.then_inc(prog.sems[o["sig"]], inc)

        with nc.Block() as block:
            @block.tensor
            def _(e):
                run("pe", e)

            @block.scalar
            def _(e):
                run("act", e)

            @block.vector
            def _(e):
                run("dve", e)

            @block.gpsimd
            def _(e):
                run("pool", e)

            @block.sync
            def _(e):
                run("sp", e)


class Builder:
    def __init__(self, S, depth, debug=False, phases=None):
        self.S = S
        self.depth = depth
        self.NT = S // 128
        self.TB = min(512, S)
        self.NTB = S // self.TB
        self.debug = debug
        self.phases = phases
        self.bank_rr = 0

    def ACT(self, out, in_, func, r, w, **kw):
        self.P.op("act", lambda e: e.activation(out=out, in_=in_, func=func, **kw), r, w)

    def TT(self, q, out, in0, in1, op, r, w):
        self.P.op(q, lambda e: e.tensor_tensor(out=out, in0=in0, in1=in1, op=op), r, w)

    def TS(self, q, out, in0, s1, op0, r, w, s2=None, op1=None):
        if op1 is None:
            self.P.op(q, lambda e: e.tensor_scalar(out=out, in0=in0, scalar1=s1, scalar2=None, op0=op0), r, w)
        else:
            self.P.op(q, lambda e: e.tensor_scalar(out=out, in0=in0, scalar1=s1, scalar2=s2, op0=op0, op1=op1), r, w)

    def STT(self, out, in0, scalar, in1, op0, op1, r, w):
        self.P.op("dve", lambda e: e.scalar_tensor_tensor(out=out, in0=in0, scalar=scalar, in1=in1, op0=op0, op1=op1), r, w)

    def CP(self, q, out, in_, r, w, scale=None):
        if q == "act":
            if scale is None:
                self.P.op("act", lambda e: e.activation(out=out, in_=in_, func=AF.Copy), r, w)
            else:
                self.P.op("act", lambda e: e.activation(out=out, in_=in_, func=AF.Copy, scale=scale), r, w)
        else:
            self.P.op(q, lambda e: e.tensor_copy(out=out, in_=in_), r, w)

    def MM(self, out, lhsT, rhs, start, stop, r, w, **kw):
        self.P.op("pe", lambda e: e.matmul(out, lhsT=lhsT, rhs=rhs, start=start, stop=stop, **kw), r, w)

    def TR(self, out, in_, ident, r, w):
        self.P.op("pe", lambda e: e.transpose(out=out, in_=in_, identity=ident), r, w)

    def DMA(self, q, out, in_, r, w, stream):
        self.P.op(q, lambda e: e.dma_start(out=out, in_=in_), r, w, dma=stream)

    def MEMSET(self, q, ap, val, w):
        self.P.op(q, lambda e: e.memset(ap, val), (), w)

    def RECIP(self, out, in_, r, w):
        self.P.op("dve", lambda e: e.reciprocal(out=out, in_=in_), r, w)

    def REDUCE(self, out, in_, r, w):
        self.P.op("dve", lambda e: e.tensor_reduce(out=out, in_=in_, axis=mybir.AxisListType.X, op=ALU.add), r, w)

    def ASEL(self, out, in_, pattern, cmp, fill, base, cm, r, w):
        self.P.op("pool", lambda e: e.affine_select(out=out, in_=in_, pattern=pattern, compare_op=cmp, fill=fill,
                                                    base=base, channel_multiplier=cm), r, w)

    def nb(self):
        b = self.bank_rr
        self.bank_rr = (self.bank_rr + 1) % 8
        return b

    def sb(self, ph, name, shape, dt):
        self.uid = getattr(self, "uid", 0) + 1
        return ph.enter_context(self.nc.sbuf_tensor("%s_u%d" % (name, self.uid), shape, dt))

    def psk(self, b):
        return "ps%d" % b

    def psbf(self, b, a):
        return self.ps[b][:].bitcast(BF16).rearrange("p (a b) -> p a b", a=a)

    def build(self):
        S, NT = self.S, self.NT
        L = self.depth
        nc = bass.Bass("TRN2", target_bir_lowering=False)
        self.nc = nc

        def din(name, shape, dt=F32):
            return nc.dram_tensor(name, shape, dt, kind="ExternalInput").ap()

        def dscr(name, shape, dt=F32):
            kind = "ExternalOutput" if self.debug else "Internal"
            return nc.dram_tensor(name, shape, dt, kind=kind).ap()

        self.x_in = din("x", [S, D])
        self.w_in = din("w_in", [L, D, IN_W])
        self.w_out = din("w_out", [L, D, D])
        self.w_ff1 = din("w_ff1", [L, D, DFF])
        self.w_ff2 = din("w_ff2", [L, DFF, D])
        self.normw = din("normw", [L, 4, D])
        self.convw = din("convw", [L, 256, 31])
        self.convp = din("convp", [L, 128, 2, 3])
        self.lam = din("lam", [L, 2, 2, 32])
        self.subw = din("subw", [L, 64])
        self.dconvw = din("dconvw", [L, 1152, 3])
        self.dgate = din("dgate", [L, 2, 12])
        self.dnormw = din("dnormw", [L, 64])
        self.dband = din("dband", [128, 2 * S - 128])
        self.bdmask = din("bdmask", [128, 128])
        self.out = nc.dram_tensor("out", [S, D], F32, kind="ExternalOutput").ap()
        self.xres = dscr("xres", [S, D])
        self.hg = dscr("hg", [2, 128, S], BF16)
        self.qT = dscr("qT", [4, 96, S], BF16)
        self.kT = dscr("kT", [4, 96, S], BF16)
        self.vtok = dscr("vtok", [S, 384], BF16)
        self.gqkvT = dscr("gqkvT", [9, 128, S])
        self.zs = dscr("zs", [S, 384], BF16)
        self.gates = dscr("gates", [S, 24])
        self.ycatT = dscr("ycatT", [8, 128, S], BF16)
        self.ypart = dscr("ypart", [S, D])

        with ExitStack() as es:
            self.P = Prog(nc, es)
            self.ps = [es.enter_context(nc.psum_tensor("psb%d" % i, [128, 512], F32)) for i in range(8)]
            self.identf = self.sb(es, "identf", [128, 128], F32)
            self.identb = self.sb(es, "identb", [128, 128], BF16)
            self.MEMSET("pool", self.identf[:], 1.0, ["identf"])
            self.ASEL(self.identf[:], self.identf[:], [[-1, 128]], ALU.is_equal, 0.0, 0, 1, ["identf"], ["identf"])
            self.CP("pool", self.identb[:], self.identf[:], ["identf"], ["identb"])
            self.P.flush()
            for l in range(L):
                self.layer(l)
            self.P.flush()
        return nc

    def want(self, name):
        return self.phases is None or name in self.phases

    def layer(self, l):
        S = self.S
        last = (l == self.depth - 1)
        src = self.x_in if l == 0 else self.xres
        if self.want("B"):
            self.phase_in_proj(l, src)
        self.merge_conv = self.want("C") and self.want("E")
        if self.want("C") and not self.merge_conv:
            self.phase_conv(l)
        if self.want("D"):
            self.phase_attn(l)
        if self.want("E"):
            self.phase_delta(l)
        if self.want("F") and self.want("G"):
            self.phase_out_mlp(l, src, self.out if last else self.xres)
        else:
            if self.want("F"):
                self.phase_out_proj(l, src)
            if self.want("G"):
                self.phase_mlp(l, self.out if last else self.xres)

    def norm_T(self, ph, src, wrow, hT):
        S, NT = self.S, self.NT
        xt = [self.sb(ph, "n_xt%d" % i, [128, D], F32) for i in range(2)]
        hb = [self.sb(ph, "n_hb%d" % i, [128, D], BF16) for i in range(2)]
        junk = self.sb(ph, "n_junk", [128, D], BF16)
        ssq = self.sb(ph, "n_ssq", [128, NT], F32)
        rstd = self.sb(ph, "n_rstd", [128, NT], F32)
        wbc = self.sb(ph, "n_wbc", [128, D], F32)
        self.DMA("sp", wbc[:], wrow.broadcast_to([128, D]), [], ["n_wbc"], "n_wbc")
        for t in range(NT):
            b = t % 2
            self.DMA("sp", xt[b][:], src[t * 128:(t + 1) * 128, :], [("x", t)], ["n_xt%d" % b], "n_xt%d" % b)
            self.ACT(junk[:], xt[b][:], AF.Square, ["n_xt%d" % b], ["n_junk", "n_ssq"], accum_out=ssq[:, t:t + 1])
        self.TS("dve", rstd[:], ssq[:], 1.0 / D, ALU.mult, ["n_ssq"], ["n_rstd"], s2=EPS, op1=ALU.add)
        self.ACT(rstd[:], rstd[:], AF.Sqrt, ["n_rstd"], ["n_rstd"])
        self.RECIP(rstd[:], rstd[:], ["n_rstd"], ["n_rstd"])
        for t in range(NT):
            b = t % 2
            self.DMA("sp", xt[b][:], src[t * 128:(t + 1) * 128, :], [("x", t)], ["n_xt%d" % b], "n_xt%d" % b)
            self.STT(hb[b][:], xt[b][:], rstd[:, t:t + 1], wbc[:], ALU.mult, ALU.mult,
                     ["n_xt%d" % b, "n_rstd", "n_wbc"], ["n_hb%d" % b])
            bk = self.nb()
            pv = self.psbf(bk, 8)
            for kc in range(8):
                self.TR(pv[:, kc, :], hb[b][:, kc * 128:(kc + 1) * 128], self.identb[:], ["n_hb%d" % b, "identb"], [self.psk(bk)])
            self.CP("act", hT[:, :, t * 128:(t + 1) * 128], pv[:, :, :], [], [self.psk(bk), "hT"])

    def phase_in_proj(self, l, src):
        S, NT, TB, NTB = self.S, self.NT, self.TB, self.NTB
        with ExitStack() as ph:
            hT = self.sb(ph, "hT", [128, 8, S], BF16)
            self.norm_T(ph, src, self.normw[l, 0:1, :], hT)
            groups = [("wA", 0, 512), ("wB", 512, 768), ("wC", 1280, 384), ("wD", 1664, 1152), ("wE", 2816, 408)]
            W = {}
            for nm, c0, n in groups:
                W[nm] = self.sb(ph, nm, [128, 8, n], BF16)
                self.DMA("pool", W[nm][:], self.w_in[l, :, c0:c0 + n].rearrange("(kc p) n -> p kc n", p=128), [], [nm], nm)
            hgs = self.sb(ph, "hgs", [128, 2, S], BF16)
            qs = self.sb(ph, "qs", [128, 4, S], BF16)
            ks = self.sb(ph, "ks", [128, 4, S], BF16)
            vs = self.sb(ph, "vs", [128, NT, 384], BF16)
            zst = self.sb(ph, "zst", [128, NT, 384], BF16)
            gst = self.sb(ph, "gst", [128, NT, 24], F32)
            sg = [self.sb(ph, "sg%d" % i, [128, TB], F32) for i in range(2)]
            dst = [self.sb(ph, "dst%d" % i, [128, TB], F32) for i in range(3)]
            i = 0
            for cc in range(2):
                for tb in range(NTB):
                    ts_ = slice(tb * TB, (tb + 1) * TB)
                    ba, bg = self.nb(), self.nb()
                    for kc in range(8):
                        self.MM(self.ps[bg][:, :TB], W["wA"][:, kc, 256 + cc * 128:256 + (cc + 1) * 128], hT[:, kc, ts_], kc == 0, kc == 7, ["wA", "hT"], [self.psk(bg)])
                    for kc in range(8):
                        self.MM(self.ps[ba][:, :TB], W["wA"][:, kc, cc * 128:(cc + 1) * 128], hT[:, kc, ts_], kc == 0, kc == 7, ["wA", "hT"], [self.psk(ba)])
                    s = i % 2
                    i += 1
                    self.ACT(sg[s][:], self.ps[bg][:, :TB], AF.Sigmoid, [], [self.psk(bg), "sg%d" % s])
                    self.TT("dve", hgs[:, cc, ts_], self.ps[ba][:, :TB], sg[s][:], ALU.mult, ["sg%d" % s], [self.psk(ba), "hgs"])
            for cc in range(2):
                self.DMA("sp", self.hg[cc], hgs[:, cc, :], ["hgs"], ["hg"], "st_hg")
            for which, stg, dst_d, scale in (("q", qs, self.qT, 32.0 ** -0.5), ("k", ks, self.kT, None)):
                base = 0 if which == "q" else 384
                for c in range(4):
                    for tb in range(NTB):
                        ts_ = slice(tb * TB, (tb + 1) * TB)
                        bk = self.nb()
                        for kc in range(8):
                            self.MM(self.ps[bk][0:96, :TB], W["wB"][:, kc, base + 96 * c:base + 96 * (c + 1)], hT[:, kc, ts_], kc == 0, kc == 7, ["wB", "hT"], [self.psk(bk)])
                        self.CP("act", stg[0:96, c, ts_], self.ps[bk][0:96, :TB], [], [self.psk(bk), which + "s"], scale=scale)
                for c in range(4):
                    self.DMA("sp", dst_d[c], stg[0:96, c, :], [which + "s"], [which + "T"], "st_" + which)
            for t in range(NT):
                bk = self.nb()
                for kc in range(8):
                    self.MM(self.ps[bk][:, :384], hT[:, kc, t * 128:(t + 1) * 128], W["wC"][:, kc, :], kc == 0, kc == 7, ["wC", "hT"], [self.psk(bk)])
                self.CP("dve", vs[:, t, :], self.ps[bk][:, :384], [], [self.psk(bk), "vs"])
            self.DMA("sp", self.vtok.rearrange("(t p) n -> p t n", p=128), vs[:], ["vs"], ["vtok"], "st_v")
            i = 0
            for c in range(9):
                for tb in range(NTB):
                    ts_ = slice(tb * TB, (tb + 1) * TB)
                    bk = self.nb()
                    for kc in range(8):
                        self.MM(self.ps[bk][:, :TB], W["wD"][:, kc, c * 128:(c + 1) * 128], hT[:, kc, ts_], kc == 0, kc == 7, ["wD", "hT"], [self.psk(bk)])
                    s = i % 3
                    i += 1
                    self.CP("dve" if i % 2 else "act", dst[s][:], self.ps[bk][:, :TB], [], [self.psk(bk), "dst%d" % s])
                    self.DMA("sp", self.gqkvT[c, :, ts_], dst[s][:], ["dst%d" % s], ["gqkvT"], "st_d%d" % s)
            for t in range(NT):
                bk = self.nb()
                for kc in range(8):
                    self.MM(self.ps[bk][:, :408], hT[:, kc, t * 128:(t + 1) * 128], W["wE"][:, kc, :], kc == 0, kc == 7, ["wE", "hT"], [self.psk(bk)])
                self.ACT(zst[:, t, :], self.ps[bk][:, :384], AF.Silu, [], [self.psk(bk), "zst"])
                self.CP("dve", gst[:, t, :], self.ps[bk][:, 384:408], [], [self.psk(bk), "gst"])
            self.DMA("sp", self.zs.rearrange("(t p) n -> p t n", p=128), zst[:], ["zst"], ["zs"], "st_z")
            self.DMA("sp", self.gates.rearrange("(t p) n -> p t n", p=128), gst[:], ["gst"], ["gates"], "st_g")
            self.P.flush()

    def post_norm_residual(self, ph_tiles, banks, t, src, dst, wbc, extra=None):
        xt, ytmp, small, junk = ph_tiles
        b = t % 2
        self.DMA("sp", xt[b][:], src[t * 128:(t + 1) * 128, :], [("x", t)], ["r_xt%d" % b], "r_xt%d" % b)
        ysrc = []
        for hf in range(2):
            bk = banks[hf]
            if extra is not None:
                etile, ekey = extra
                self.TT("dve", ytmp[b][:, hf * 512:(hf + 1) * 512], self.ps[bk][:, :], etile[:, hf * 512:(hf + 1) * 512], ALU.add,
                        [ekey], [self.psk(bk), "r_yt%d" % b])
            else:
                self.CP("dve", ytmp[b][:, hf * 512:(hf + 1) * 512], self.ps[bk][:, :], [], [self.psk(bk), "r_yt%d" % b])
        sm = small[b]
        self.ACT(junk[:], ytmp[b][:], AF.Square, ["r_yt%d" % b], ["r_junk", "r_sm%d" % b], accum_out=sm[:, 0:1])
        self.TS("dve", sm[:, 1:2], sm[:, 0:1], 1.0 / D, ALU.mult, ["r_sm%d" % b], ["r_sm%d" % b], s2=EPS, op1=ALU.add)
        self.ACT(sm[:, 2:3], sm[:, 1:2], AF.Sqrt, ["r_sm%d" % b], ["r_sm%d" % b])
        self.RECIP(sm[:, 3:4], sm[:, 2:3], ["r_sm%d" % b], ["r_sm%d" % b])
        self.STT(ytmp[b][:], ytmp[b][:], sm[:, 3:4], wbc[:], ALU.mult, ALU.mult, ["r_yt%d" % b, "r_sm%d" % b, "r_wbc"], ["r_yt%d" % b])
        self.TT("pool", xt[b][:], xt[b][:], ytmp[b][:], ALU.add, ["r_xt%d" % b, "r_yt%d" % b], ["r_xt%d" % b])
        self.DMA("sp", dst[t * 128:(t + 1) * 128, :], xt[b][:], ["r_xt%d" % b], [("x", t)], "r_st%d" % b)

    def res_tiles(self, ph):
        xt = [self.sb(ph, "r_xt%d" % i, [128, D], F32) for i in range(2)]
        ytmp = [self.sb(ph, "r_yt%d" % i, [128, D], F32) for i in range(2)]
        small = [self.sb(ph, "r_sm%d" % i, [128, 4], F32) for i in range(2)]
        junk = self.sb(ph, "r_junk", [128, D], BF16)
        return xt, ytmp, small, junk

    def phase_out_proj(self, l, src):
        S, NT = self.S, self.NT
        with ExitStack() as ph:
            yT = self.sb(ph, "ycat", [128, 8, S], BF16)
            wo = self.sb(ph, "wo", [128, 8, D], BF16)
            wbc = self.sb(ph, "r_wbc", [128, D], F32)
            tiles = self.res_tiles(ph)
            for c in range(8):
                self.DMA("sp", yT[:, c, :], self.ycatT[c], ["ycatT"], ["ycat"], "ld_ycat")
            self.DMA("pool", wo[:], self.w_out[l].rearrange("(kc p) n -> p kc n", p=128), [], ["wo"], "ld_wo")
            self.DMA("sp", wbc[:], self.normw[l, 1:2, :].broadcast_to([128, D]), [], ["r_wbc"], "ld_rwbc")
            for t in range(NT):
                banks = [self.nb(), self.nb()]
                for hf in range(2):
                    for kc in range(8):
                        self.MM(self.ps[banks[hf]][:, :], yT[:, kc, t * 128:(t + 1) * 128], wo[:, kc, hf * 512:(hf + 1) * 512], kc == 0, kc == 7,
                                ["ycat", "wo"], [self.psk(banks[hf])])
                self.post_norm_residual(tiles, banks, t, src, self.xres, wbc)
            self.P.flush()

    def phase_out_mlp(self, l, src, dst):
        S, NT = self.S, self.NT
        with ExitStack() as ph:
            hT = self.sb(ph, "hT", [128, 8, S], BF16)
            w1 = self.sb(ph, "w1", [128, 8, 2048], BF16)
            w2 = self.sb(ph, "w2", [128, 16, D], BF16)
            self.DMA("pool", w1[:], self.w_ff1[l, :, 0:2048].rearrange("(kc p) n -> p kc n", p=128), [], ["w1"], "ld_w1")
            self.DMA("pool", w2[:], self.w_ff2[l, 0:2048, :].rearrange("(j p) n -> p j n", p=128), [], ["w2"], "ld_w2")
            with ExitStack() as ph2:
                yT = self.sb(ph2, "ycat", [128, 8, S], BF16)
                wo = self.sb(ph2, "wo", [128, 8, D], BF16)
                wbc = self.sb(ph2, "r_wbc", [128, D], F32)
                tiles = self.res_tiles(ph2)
                for c in range(8):
                    self.DMA("sp", yT[:, c, :], self.ycatT[c], ["ycatT"], ["ycat"], "ld_ycat")
                self.DMA("pool", wo[:], self.w_out[l].rearrange("(kc p) n -> p kc n", p=128), [], ["wo"], "ld_wo")
                self.DMA("sp", wbc[:], self.normw[l, 1:2, :].broadcast_to([128, D]), [], ["r_wbc"], "ld_rwbc")
                for t in range(NT):
                    banks = [self.nb(), self.nb()]
                    for hf in range(2):
                        for kc in range(8):
                            self.MM(self.ps[banks[hf]][:, :], yT[:, kc, t * 128:(t + 1) * 128], wo[:, kc, hf * 512:(hf + 1) * 512], kc == 0, kc == 7,
                                    ["ycat", "wo"], [self.psk(banks[hf])])
                    self.post_norm_residual(tiles, banks, t, src, self.xres, wbc)
                self.norm_T(ph2, self.xres, self.normw[l, 2:3, :], hT)
                self.P.flush()
            self.mlp_main(ph, l, dst, hT, w1, w2, True)

    def phase_mlp(self, l, dst):
        S = self.S
        with ExitStack() as ph:
            hT = self.sb(ph, "hT", [128, 8, S], BF16)
            with ExitStack() as ph2:
                self.norm_T(ph2, self.xres, self.normw[l, 2:3, :], hT)
                self.P.flush()
            w1 = self.sb(ph, "w1", [128, 8, 2048], BF16)
            w2 = self.sb(ph, "w2", [128, 16, D], BF16)
            self.mlp_main(ph, l, dst, hT, w1, w2, False)

    def mlp_main(self, ph, l, dst, hT, w1, w2, first_loaded):
        S, NT, TB, NTB = self.S, self.NT, self.TB, self.NTB
        NQ = TB // 128
        if True:
            h1 = [self.sb(ph, "h1_%d" % i, [128, 16, TB], BF16) for i in range(2)]
            rl = [self.sb(ph, "rl%d" % i, [128, TB], BF16) for i in range(2)]
            yp = [self.sb(ph, "yp%d" % i, [128, D], F32) for i in range(2)]
            wbc = self.sb(ph, "r_wbc", [128, D], F32)
            tiles = self.res_tiles(ph)
            self.DMA("sp", wbc[:], self.normw[l, 3:4, :].broadcast_to([128, D]), [], ["r_wbc"], "ld_rwbc")
            for half in range(2):
                f0 = half * 2048
                if not (first_loaded and half == 0):
                    self.DMA("pool", w1[:], self.w_ff1[l, :, f0:f0 + 2048].rearrange("(kc p) n -> p kc n", p=128), [], ["w1"], "ld_w1")
                    self.DMA("pool", w2[:], self.w_ff2[l, f0:f0 + 2048, :].rearrange("(j p) n -> p j n", p=128), [], ["w2"], "ld_w2")
                for tb in range(NTB):
                    ts_ = slice(tb * TB, (tb + 1) * TB)
                    hb = h1[tb % 2]
                    hk = "h1_%d" % (tb % 2)
                    for j in range(16):
                        bk = self.nb()
                        for kc in range(8):
                            self.MM(self.ps[bk][:, :TB], w1[:, kc, j * 128:(j + 1) * 128], hT[:, kc, ts_], kc == 0, kc == 7, ["w1", "hT"], [self.psk(bk)])
                        r_ = j % 2
                        self.ACT(rl[r_][:], self.ps[bk][:, :TB], AF.Relu, [], [self.psk(bk), "rl%d" % r_])
                        self.TT("pool" if j % 2 else "dve", hb[:, j, :], rl[r_][:], rl[r_][:], ALU.mult, ["rl%d" % r_], [hk])
                    for ti in range(NQ):
                        t = tb * NQ + ti
                        banks = [self.nb(), self.nb()]
                        for hf in range(2):
                            for j in range(16):
                                self.MM(self.ps[banks[hf]][:, :], hb[:, j, ti * 128:(ti + 1) * 128], w2[:, j, hf * 512:(hf + 1) * 512], j == 0, j == 15,
                                        [hk, "w2"], [self.psk(banks[hf])])
                        if half == 0:
                            b = t % 2
                            for hf in range(2):
                                self.CP("dve" if hf else "act", yp[b][:, hf * 512:(hf + 1) * 512], self.ps[banks[hf]][:, :], [], [self.psk(banks[hf]), "yp%d" % b])
                            self.DMA("sp", self.ypart[t * 128:(t + 1) * 128, :], yp[b][:], ["yp%d" % b], [("ypart", t)], "st_yp%d" % b)
                        else:
                            b = t % 2
                            self.DMA("sp", yp[b][:], self.ypart[t * 128:(t + 1) * 128, :], [("ypart", t)], ["yp%d" % b], "ld_yp%d" % b)
                            self.post_norm_residual(tiles, banks, t, self.xres, dst, wbc, extra=(yp[b], "yp%d" % b))
            self.P.flush()


def prep_inputs(inp, S):
    f = lambda a: np.ascontiguousarray(np.asarray(a, dtype=np.float32))
    L = inp["w_in"].shape[0]
    shared = {
        "w_in": f(inp["w_in"]), "w_out": f(inp["w_out"]), "w_ff1": f(inp["w_ff1"]), "w_ff2": f(inp["w_ff2"]),
        "normw": f(np.stack([inp["pre_mix_w"], inp["post_mix_w"], inp["pre_mlp_w"], inp["post_mlp_w"]], axis=1)),
        "convw": f(np.transpose(np.asarray(inp["conv_dw_w"]), (0, 2, 1))),
        "convp": f(np.stack([np.asarray(inp["conv_dw_b"]).reshape(L, 2, 128), np.asarray(inp["conv_ln_w"]).reshape(L, 2, 128),
                             np.asarray(inp["conv_ln_b"]).reshape(L, 2, 128)], axis=-1).transpose(0, 2, 1, 3)),
        "lam": f(np.stack([np.stack([inp["diff_lambda_q1"], inp["diff_lambda_q2"]], axis=1),
                           np.stack([inp["diff_lambda_k1"], inp["diff_lambda_k2"]], axis=1)], axis=1)),
        "subw": f(inp["diff_subln_w"]),
        "dconvw": f(np.transpose(np.asarray(inp["delta_conv_w"]), (0, 2, 1))),
        "dgate": f(np.stack([np.asarray(inp["delta_A_log"]).reshape(L, 12), np.asarray(inp["delta_dt_bias"]).reshape(L, 12)], axis=1)),
        "dnormw": f(inp["delta_norm_w"]),
    }
    W = 2 * S - 128
    shared["dband"] = f(np.abs(np.arange(W)[None, :] - (S - 128) - np.arange(128)[:, None]))
    shared["bdmask"] = f((np.arange(128)[:, None] // 32) == (np.arange(128)[None, :] // 32))
    return shared


_NC_CACHE = {}


def kernel(**inputs):
    x = np.asarray(inputs["x"], dtype=np.float32)
    B, S, _ = x.shape
    L = inputs["w_in"].shape[0]
    shared = prep_inputs(inputs, S)
    key = (S, L)
    if key not in _NC_CACHE:
        _NC_CACHE[key] = Builder(S, L).build()
    nc = _NC_CACHE[key]
    in_maps = []
    for b in range(B):
        m = dict(shared)
        m["x"] = np.ascontiguousarray(x[b])
        in_maps.append(m)
    res = run_bass_kernel_spmd(nc, in_maps, core_ids=list(range(B)))
    return np.stack([np.asarray(r["out"], dtype=np.float32) for r in res.results], axis=0)


def phase_conv(self, l):
    with ExitStack() as ph:
        for _ in self.conv_body(ph, l):
            pass
        self.P.flush()


def conv_body(self, ph, l):
    S, NT, TB, NTB = self.S, self.NT, self.TB, self.NTB
    if True:
        hgp = self.sb(ph, "hgp", [128, 2, S + 30], BF16)
        wcol = self.sb(ph, "wcol", [128, 2, 31], F32)
        cpar = self.sb(ph, "cpar", [128, 2, 3], F32)
        diag = self.sb(ph, "diag", [128, 62, 128], BF16)
        onesb = self.sb(ph, "onesb", [128, 128], BF16)
        yb = [self.sb(ph, "yb%d" % i, [128, TB], F32) for i in range(2)]
        yh = [self.sb(ph, "yh%d" % i, [128, TB], BF16) for i in range(2)]
        zq = [self.sb(ph, "zq%d" % i, [128, TB], BF16) for i in range(2)]
        mean = self.sb(ph, "mean", [128, TB], F32)
        var = self.sb(ph, "var", [128, TB], F32)
        zt = [self.sb(ph, "zt%d" % i, [128, TB], F32) for i in range(2)]
        yout = self.sb(ph, "yout", [128, 2, S], BF16)
        self.MEMSET("pool", hgp[:, :, 0:15], 0.0, ["hgp"])
        self.MEMSET("pool", hgp[:, :, S + 15:S + 30], 0.0, ["hgp"])
        self.MEMSET("pool", onesb[:], 1.0 / 256.0, ["onesb"])
        for cc in range(2):
            self.DMA("sp", hgp[:, cc, 15:15 + S], self.hg[cc], ["hg"], ["hgp"], "ld_hgp")
        self.DMA("sp", wcol[:], self.convw[l].rearrange("(cc p) j -> p cc j", p=128), [], ["wcol"], "ld_wcol")
        self.DMA("sp", cpar[:], self.convp[l], [], ["cpar"], "ld_cpar")
        for cc in range(2):
            for j in range(31):
                if j % 3 == 1:
                    self.ACT(diag[:, cc * 31 + j, :], self.identf[:], AF.Copy, ["wcol", "identf"], ["diag"], scale=wcol[:, cc, j:j + 1])
                else:
                    self.TS("dve" if j % 3 == 0 else "pool", diag[:, cc * 31 + j, :], self.identf[:], wcol[:, cc, j:j + 1], ALU.mult, ["wcol", "identf"], ["diag"])
        yield
        for tb in range(NTB):
            ts_ = slice(tb * TB, (tb + 1) * TB)
            for cc in range(2):
                bk = self.nb()
                for j in range(31):
                    self.MM(self.ps[bk][:, :TB], diag[:, cc * 31 + j, :], hgp[:, cc, tb * TB + j:tb * TB + j + TB], j == 0, j == 30,
                            ["diag", "hgp"], [self.psk(bk)])
                self.ACT(yb[cc][:], self.ps[bk][:, :TB], AF.Identity, ["cpar"], [self.psk(bk), "yb%d" % cc], bias=cpar[:, cc, 0:1])
                self.CP("dve", yh[cc][:], yb[cc][:], ["yb%d" % cc], ["yh%d" % cc])
            bm = self.nb()
            for cc in range(2):
                self.MM(self.ps[bm][:, :TB], onesb[:], yh[cc][:], cc == 0, cc == 1, ["onesb", "yh%d" % cc], [self.psk(bm)])
            self.CP("dve", mean[:], self.ps[bm][:, :TB], [], [self.psk(bm), "mean"])
            for cc in range(2):
                self.TT("pool" if cc else "dve", zt[cc][:], yb[cc][:], mean[:], ALU.subtract, ["yb%d" % cc, "mean"], ["zt%d" % cc])
                self.ACT(zq[cc][:], zt[cc][:], AF.Square, ["zt%d" % cc], ["zq%d" % cc])
            be = self.nb()
            for cc in range(2):
                self.MM(self.ps[be][:, :TB], onesb[:], zq[cc][:], cc == 0, cc == 1, ["onesb", "zq%d" % cc], [self.psk(be)])
            self.TS("dve", var[:], self.ps[be][:, :TB], 1e-5, ALU.add, [], [self.psk(be), "var"])
            self.ACT(var[:], var[:], AF.Sqrt, ["var"], ["var"])
            self.RECIP(var[:], var[:], ["var"], ["var"])
            for cc in range(2):
                self.TT("pool" if cc else "dve", zt[cc][:], zt[cc][:], var[:], ALU.mult, ["zt%d" % cc, "var"], ["zt%d" % cc])
                self.ACT(yout[:, cc, ts_], zt[cc][:], AF.Silu, ["zt%d" % cc, "cpar"], ["yout"], scale=cpar[:, cc, 1:2], bias=cpar[:, cc, 2:3])
            yield
        for cc in range(2):
            self.DMA("sp", self.ycatT[cc], yout[:, cc, :], ["yout"], ["ycatT"], "st_yconv")


Builder.phase_conv = phase_conv
Builder.conv_body = conv_body


def phase_attn(self, l):
    S, NT = self.S, self.NT
    QB = min(512, S)
    NQB = S // QB
    NQ = QB // 128
    linit = 0.8 - 0.6 * math.exp(-0.3 * l)
    with ExitStack() as ph:
        QT = self.sb(ph, "QT", [128, 4, S], BF16)
        KT = self.sb(ph, "KT", [128, 4, S], BF16)
        V = self.sb(ph, "V", [128, NT, 6, 128], BF16)
        dband = self.sb(ph, "dband", [128, 2 * S - 128], F32)
        lamv = self.sb(ph, "lamv", [128, 2, 2, 32], F32)
        lprod = self.sb(ph, "lprod", [128, 2, 32], F32)
        lsm = self.sb(ph, "lsm", [128, 4], F32)
        wsub = self.sb(ph, "wsub", [128, 64], F32)
        ydiff = self.sb(ph, "ydiff", [128, NT, 384], BF16)
        yT = self.sb(ph, "yT", [128, 3, S], BF16)
        sc = [self.sb(ph, "sc%d" % i, [128, QB], F32) for i in range(3)]
        ET = [self.sb(ph, "ET%d" % i, [128, QB], BF16) for i in range(4)]
        rs = self.sb(ph, "rs", [128, NQ, 1], F32)
        oTb = self.sb(ph, "oTb", [65, QB], BF16)
        o1 = self.sb(ph, "o1", [128, NQ, 64], F32)
        o2 = self.sb(ph, "o2", [128, NQ, 64], F32)
        osq = self.sb(ph, "osq", [128, NQ, 64], F32)
        oss = self.sb(ph, "oss", [128, NQ], F32)
        for c in range(4):
            self.DMA("sp", QT[0:96, c, :], self.qT[c], ["qT"], ["QT"], "ld_QT")
            self.DMA("sp", KT[0:96, c, :], self.kT[c], ["kT"], ["KT"], "ld_KT")
        QTm = None
        if os.environ.get("K_QM", "0") == "1":
            QTm = self.sb(ph, "QTm", [128, 12, S], BF16)
            self.MEMSET("pool", QTm[0:96, :, :], 0.0, ["QTm"])
            for mi_ in range(12):
                c_, r_ = mi_ // 3, 32 * (mi_ % 3)
                eng_ = ("act", "dve", "pool")[mi_ % 3]
                self.CP(eng_, QTm[r_:r_ + 32, mi_, :], QT[r_:r_ + 32, c_, :], ["QT", "QTm"], ["QTm"])
        self.MEMSET("pool", V[:, :, :, 65:128], 0.0, ["V"])
        self.MEMSET("pool", V[:, :, :, 64:65], 1.0, ["V"])
        for t in range(NT):
            self.DMA("sp", V[:, t, :, 0:64], self.vtok[t * 128:(t + 1) * 128, :].rearrange("p (h e) -> p h e", h=6), ["vtok"], ["V"], "ld_V")
        self.DMA("sp", dband[:], self.dband, [], ["dband"], "ld_dband")
        self.DMA("sp", lamv[:].rearrange("p a b c -> p (a b c)"), self.lam[l:l + 1].rearrange("o a b c -> o (a b c)").broadcast_to([128, 128]), [], ["lamv"], "ld_lam")
        self.DMA("sp", wsub[:], self.subw[l:l + 1, :].broadcast_to([128, 64]), [], ["wsub"], "ld_subw")
        self.TT("dve", lprod[:], lamv[:, 0, :, :], lamv[:, 1, :, :], ALU.mult, ["lamv"], ["lprod"])
        self.REDUCE(lsm[:, 0:2], lprod[:], ["lprod"], ["lsm"])
        self.ACT(lsm[:, 0:2], lsm[:, 0:2], AF.Exp, ["lsm"], ["lsm"])
        self.TT("dve", lsm[:, 2:3], lsm[:, 1:2], lsm[:, 0:1], ALU.subtract, ["lsm"], ["lsm"])
        self.TS("dve", lsm[:, 3:4], lsm[:, 2:3], -linit, ALU.add, ["lsm"], ["lsm"])
        self.TS("dve", wsub[:], wsub[:], 1.0 - linit, ALU.mult, ["wsub"], ["wsub"])
        SB = [0, 1, 2]
        AB = [3, 4]
        LA = 2
        blocks = []
        ia = 0
        for h in range(6):
            for qb in range(NQB):
                for j in range(2):
                    ab = AB[ia % 2]
                    ia += 1
                    kts = []
                    for kt in range(NT):
                        dmin = max(0, kt * 128 - (qb * QB + QB - 1), qb * QB - (kt * 128 + 127))
                        if ALIBI_SKIP is None or dmin * SLOPES[h] <= ALIBI_SKIP:
                            kts.append(kt)
                    for kt in kts:
                        blocks.append((h, qb, j, kt, ab, kt == kts[0], kt == kts[-1]))
        nblk = len(blocks)

        def front(it):
            h, qb, j, kt, ab, kfirst, klast = blocks[it]
            m = SLOPES[h]
            mi = 2 * h + j
            c = mi // 3
            r0 = 32 * (mi % 3)
            sbk = SB[it % 3]
            si = it % 3
            ei = it % 4
            if QTm is not None:
                self.MM(self.ps[sbk][:, :QB], KT[0:96, c, kt * 128:(kt + 1) * 128], QTm[0:96, mi, qb * QB:(qb + 1) * QB], True, True,
                        ["KT", "QTm"], [self.psk(sbk)])
            else:
                self.MM(self.ps[sbk][:, :QB], KT[r0:r0 + 32, c, kt * 128:(kt + 1) * 128], QT[r0:r0 + 32, c, qb * QB:(qb + 1) * QB], True, True,
                        ["KT", "QT"], [self.psk(sbk)])
            for _ in range(int(os.environ.get("K_FILL", "0"))):
                self.MM(self.ps[7][:, :QB], KT[:, 0, 0:128], QT[:, 0, 0:QB], True, True, ["KT", "QT"], ["ps7"])
            off = qb * QB - kt * 128 + S - 128
            self.STT(sc[si][:], dband[:, off:off + QB], -m, self.ps[sbk][:, :QB], ALU.mult, ALU.add, ["dband"], [self.psk(sbk), "sc%d" % si])
            self.ACT(ET[ei][:], sc[si][:], AF.Exp, ["sc%d" % si], ["ET%d" % ei])

        pending = []

        def epilogue(h, qb, j, ab):
            accv = self.ps[ab][:, 0:NQ * 65].rearrange("p (a b) -> p a b", b=65)
            self.RECIP(rs[:], accv[:, :, 64:65], [], [self.psk(ab), "rs"])
            if j == 0:
                self.TT("dve", o1[:], accv[:, :, 0:64], rs[:].broadcast_to([128, NQ, 64]), ALU.mult, ["rs"], [self.psk(ab), "o1"])
                return
            self.TT("dve", o2[:], accv[:, :, 0:64], rs[:].broadcast_to([128, NQ, 64]), ALU.mult, ["rs"], [self.psk(ab), "o2"])
            self.STT(o2[:], o2[:], lsm[:, 3:4], o1[:], ALU.mult, ALU.add, ["o2", "o1", "lsm"], ["o2"])
            self.TT("pool", osq[:], o2[:], o2[:], ALU.mult, ["o2"], ["osq"])
            yield
            yield
            self.REDUCE(oss[:], osq[:], ["osq"], ["oss"])
            self.TS("dve", oss[:], oss[:], 1.0 / 64.0, ALU.mult, ["oss"], ["oss"], s2=1e-5, op1=ALU.add)
            self.ACT(oss[:], oss[:], AF.Sqrt, ["oss"], ["oss"])
            yield
            yield
            self.RECIP(oss[:], oss[:], ["oss"], ["oss"])
            self.TT("dve", o2[:], o2[:], oss[:].unsqueeze(2).broadcast_to([128, NQ, 64]), ALU.mult, ["o2", "oss"], ["o2"])
            self.TT("pool", ydiff[:, qb * NQ:(qb + 1) * NQ, h * 64:(h + 1) * 64], o2[:], wsub[:].unsqueeze(1).broadcast_to([128, NQ, 64]), ALU.mult,
                    ["o2", "wsub"], ["ydiff"])

        def back(it):
            h, qb, j, kt, ab, kfirst, klast = blocks[it]
            ei = it % 4
            accv = self.ps[ab][:, 0:NQ * 65].rearrange("p (a b) -> p a b", b=65)
            for qi in range(NQ):
                self.MM(accv[:, qi, :], ET[ei][:, qi * 128:(qi + 1) * 128], V[:, kt, h, 0:65], (kfirst and qi == 0), klast,
                        ["ET%d" % ei, "V"], [self.psk(ab)], skip_group_check=True)
            if klast:
                while pending:
                    pump()
                pending.append(epilogue(h, qb, j, ab))

        def pump():
            for g_ in list(pending):
                try:
                    next(g_)
                except StopIteration:
                    pending.remove(g_)

        for it in range(nblk + LA):
            if it < nblk:
                front(it)
            if it >= LA:
                back(it - LA)
            pump()
        while pending:
            pump()
        for t in range(NT):
            bk = 5 + (t % 2)
            pv = self.psbf(bk, 8)
            for c3 in range(3):
                self.TR(pv[:, c3, :], ydiff[:, t, c3 * 128:(c3 + 1) * 128], self.identb[:], ["ydiff", "identb"], [self.psk(bk)])
            self.CP("act", yT[:, :, t * 128:(t + 1) * 128], pv[:, 0:3, :], [], [self.psk(bk), "yT"])
        for c3 in range(3):
            self.DMA("sp", self.ycatT[2 + c3], yT[:, c3, :], ["yT"], ["ycatT"], "st_ydiff")
        self.P.flush()


Builder.phase_attn = phase_attn


def phase_delta(self, l):
    S, NT = self.S, self.NT
    H = 6
    with ExitStack() as ph:
        tm = self.sb(ph, "tm", [128, NT, 1152], BF16)
        with ExitStack() as p1:
            wc = self.sb(p1, "wc", [128, 9, 3], F32)
            xp = [self.sb(p1, "xp%d" % i, [128, S + 2], F32) for i in range(2)]
            acc = [self.sb(p1, "acc%d" % i, [128, S], F32) for i in range(2)]
            sT = [self.sb(p1, "sT%d" % i, [128, S], BF16) for i in range(2)]
            self.DMA("sp", wc[:], self.dconvw[l].rearrange("(c p) j -> p c j", p=128), [], ["wc"], "ld_wc")
            for i in range(2):
                self.MEMSET("pool", xp[i][:, 0:1], 0.0, ["xp%d" % i])
                self.MEMSET("pool", xp[i][:, S + 1:S + 2], 0.0, ["xp%d" % i])
            cgen = self.conv_body(p1, l) if (self.want("C") and self.merge_conv) else iter(())
            for c in range(9):
                b = c % 2
                next(cgen, None)
                self.DMA("sp", xp[b][:, 1:S + 1], self.gqkvT[c], ["gqkvT"], ["xp%d" % b], "ld_xp%d" % b)
                self.TS("dve", acc[b][:], xp[b][:, 0:S], wc[:, c, 0:1], ALU.mult, ["xp%d" % b, "wc"], ["acc%d" % b])
                self.STT(acc[b][:], xp[b][:, 1:S + 1], wc[:, c, 1:2], acc[b][:], ALU.mult, ALU.add, ["xp%d" % b, "wc", "acc%d" % b], ["acc%d" % b])
                self.STT(acc[b][:], xp[b][:, 2:S + 2], wc[:, c, 2:3], acc[b][:], ALU.mult, ALU.add, ["xp%d" % b, "wc", "acc%d" % b], ["acc%d" % b])
                self.ACT(sT[b][:], acc[b][:], AF.Silu, ["acc%d" % b], ["sT%d" % b])
                for t0 in range(0, NT, 8):
                    n = min(8, NT - t0)
                    bk = 6 + ((c * 2 + t0 // 8) % 2)
                    pv = self.psbf(bk, 8)
                    for i in range(n):
                        t = t0 + i
                        self.TR(pv[:, i, :], sT[b][:, t * 128:(t + 1) * 128], self.identb[:], ["sT%d" % b, "identb"], [self.psk(bk)])
                    self.CP("act" if (t0 // 8) % 2 else "dve", tm[:, t0:t0 + n, c * 128:(c + 1) * 128], pv[:, 0:n, :], [], [self.psk(bk), "tm"])
            for _ in cgen:
                pass
            self.P.flush()
        onesb = self.sb(ph, "onesb1", [128, 128], BF16)
        cf = self.sb(ph, "cf", [128, 128], F32)
        Lmat = [self.sb(ph, "Lmat%d" % d, [128, 128], BF16) for d in range(2)]
        maskneg = [self.sb(ph, "maskneg%d" % d, [128, 128], F32) for d in range(2)]
        nstr = [self.sb(ph, "nstr%d" % d, [128, 128], F32) for d in range(2)]
        self.MEMSET("pool", onesb[:], 1.0, ["onesb1"])
        for d in range(2):
            pat, cm = ([[1, 128]], -1) if d == 0 else ([[-1, 128]], 1)
            self.MEMSET("pool", cf[:], 1.0, ["cf"])
            self.ASEL(cf[:], cf[:], pat, ALU.is_ge, 0.0, 0, cm, ["cf"], ["cf"])
            self.CP("pool", Lmat[d][:], cf[:], ["cf"], ["Lmat%d" % d])
            self.MEMSET("pool", maskneg[d][:], 0.0, ["maskneg%d" % d])
            self.ASEL(maskneg[d][:], maskneg[d][:], pat, ALU.is_ge, -1e30, 0, cm, ["maskneg%d" % d], ["maskneg%d" % d])
            self.MEMSET("pool", nstr[d][:], -1.0, ["nstr%d" % d])
            self.ASEL(nstr[d][:], nstr[d][:], pat, ALU.is_gt, 0.0, 0, cm, ["nstr%d" % d], ["nstr%d" % d])
        groups = [[(0, 0, 4, 0)], [(0, 4, 6, 0), (1, 0, 2, 2)], [(1, 2, 6, 0)]]
        bd = self.sb(ph, "bd", [128, 128], F32)
        self.DMA("sp", bd[:], self.bdmask, [], ["bd"], "ld_bd")
        nstrd = [self.sb(ph, "nstrd%d" % d, [128, 128], F32) for d in range(2)]
        nstro = [self.sb(ph, "nstro%d" % d, [128, 128], F32) for d in range(2)]
        for d in range(2):
            self.TT("pool", nstrd[d][:], nstr[d][:], bd[:], ALU.mult, ["nstr%d" % d, "bd"], ["nstrd%d" % d])
            self.TT("pool", nstro[d][:], nstr[d][:], nstrd[d][:], ALU.subtract, ["nstr%d" % d, "nstrd%d" % d], ["nstro%d" % d])
        nstrg_d, nstrg_o = [], []
        for gi, g in enumerate(groups):
            tld = self.sb(ph, "nstrgd%d" % gi, [128, 4, 128], BF16)
            tlo = self.sb(ph, "nstrgo%d" % gi, [128, 4, 128], BF16)
            for (d, h0, h1, s0) in g:
                n = h1 - h0
                self.CP("pool", tld[:, s0:s0 + n, :], nstrd[d][:].unsqueeze(1).broadcast_to([128, n, 128]), ["nstrd%d" % d], ["nstrgd%d" % gi])
                self.CP("pool", tlo[:, s0:s0 + n, :], nstro[d][:].unsqueeze(1).broadcast_to([128, n, 128]), ["nstro%d" % d], ["nstrgo%d" % gi])
            nstrg_d.append(tld)
            nstrg_o.append(tlo)
        qnT = self.sb(ph, "qnT", [128, 3, S], BF16)
        zst = self.sb(ph, "zst", [128, NT, 384], BF16)
        osum = self.sb(ph, "osum", [128, NT, 384], F32)
        sqs = self.sb(ph, "sqs", [128, 768], F32)
        ssq = self.sb(ph, "ssq", [128, NT, 12], F32)
        gin = self.sb(ph, "gin", [128, NT, 24], F32)
        dg = self.sb(ph, "dg", [128, 2, 12], F32)
        nw = self.sb(ph, "nw", [128, 64], F32)
        G = {}
        for nm in ("sbt", "gl", "gc", "egc", "gt", "egt", "edec", "r1"):
            G[nm] = self.sb(ph, "g_" + nm, [128, 2, NT, 6], F32)
        gls = [self.sb(ph, "gls%d" % i, [128, 2, NT, 6], BF16) for i in range(3)]
        gcs = [self.sb(ph, "gcs%d" % i, [128, 2, NT, 6], BF16) for i in range(3)]
        egtS = self.sb(ph, "egtS", [128, 2, NT, 3], F32)
        self.DMA("sp", zst[:], self.zs.rearrange("(t p) n -> p t n", p=128), ["zs"], ["zst"], "ld_zs")
        self.DMA("sp", gin[:], self.gates.rearrange("(t p) n -> p t n", p=128), ["gates"], ["gin"], "ld_gin")
        self.DMA("sp", dg[:].rearrange("p a b -> p (a b)"), self.dgate[l:l + 1].rearrange("o a b -> o (a b)").broadcast_to([128, 24]), [], ["dg"], "ld_dg")
        self.DMA("sp", nw[:], self.dnormw[l:l + 1, :].broadcast_to([128, 64]), [], ["nw"], "ld_nw")
        for t in range(NT):
            self.ACT(sqs[:], tm[:, t, 0:768], AF.Square, ["tm"], ["sqs"])
            self.REDUCE(ssq[:, t, :], sqs[:].rearrange("p (a b) -> p a b", b=64), ["sqs"], ["ssq"])
        self.TS("dve", ssq[:], ssq[:], 1e-6, ALU.add, ["ssq"], ["ssq"])
        self.ACT(ssq[:], ssq[:], AF.Sqrt, ["ssq"], ["ssq"])
        self.RECIP(ssq[:], ssq[:], ["ssq"], ["ssq"])
        self.TS("dve", ssq[:, :, 0:6], ssq[:, :, 0:6], 0.125, ALU.mult, ["ssq"], ["ssq"])
        for t in range(NT):
            self.TT("dve", tm[:, t, 0:384].rearrange("p (h e) -> p h e", h=6), tm[:, t, 0:384].rearrange("p (h e) -> p h e", h=6),
                    ssq[:, t, 0:6].unsqueeze(2).broadcast_to([128, 6, 64]), ALU.mult, ["tm", "ssq"], ["tm"])
            self.TT("pool", tm[:, t, 384:768].rearrange("p (h e) -> p h e", h=6), tm[:, t, 384:768].rearrange("p (h e) -> p h e", h=6),
                    ssq[:, t, 6:12].unsqueeze(2).broadcast_to([128, 6, 64]), ALU.mult, ["tm", "ssq"], ["tm"])
            bk = 6 + (t % 2)
            pv = self.psbf(bk, 8)
            for c3 in range(3):
                self.TR(pv[:, c3, :], tm[:, t, c3 * 128:(c3 + 1) * 128], self.identb[:], ["tm", "identb"], [self.psk(bk)])
            self.CP("act", qnT[:, :, t * 128:(t + 1) * 128], pv[:, 0:3, :], [], [self.psk(bk), "qnT"])
        self.ACT(dg[:, 0, :], dg[:, 0, :], AF.Exp, ["dg"], ["dg"])
        self.TS("dve", dg[:, 0, :], dg[:, 0, :], -1.0, ALU.mult, ["dg"], ["dg"])
        for d in range(2):
            bsl = gin[:, :, d * 6:(d + 1) * 6]
            asl = gin[:, :, 12 + d * 6:12 + (d + 1) * 6]
            self.ACT(G["sbt"][:, d, :, :], bsl, AF.Sigmoid, ["gin"], ["sbt"])
            self.ACT(G["sbt"][:, d, :, :], G["sbt"][:, d, :, :], AF.Sqrt, ["sbt"], ["sbt"])
            self.TT("dve", G["gl"][:, d, :, :], asl, dg[:, 1, d * 6:(d + 1) * 6].unsqueeze(1).broadcast_to([128, NT, 6]), ALU.add, ["gin", "dg"], ["gl"])
            self.ACT(G["gl"][:, d, :, :], G["gl"][:, d, :, :], AF.Exp, ["gl"], ["gl"])
            self.TS("dve", G["gl"][:, d, :, :], G["gl"][:, d, :, :], 1.0, ALU.add, ["gl"], ["gl"])
            self.ACT(G["gl"][:, d, :, :], G["gl"][:, d, :, :], AF.Ln, ["gl"], ["gl"])
            self.TT("dve", G["gl"][:, d, :, :], G["gl"][:, d, :, :], dg[:, 0, d * 6:(d + 1) * 6].unsqueeze(1).broadcast_to([128, NT, 6]), ALU.mult, ["gl", "dg"], ["gl"])

        def split3(src, dst, key_src, key_dst):
            r1 = G["r1"]
            self.CP("dve", dst[0][:], src[:], [key_src], [key_dst])
            self.TT("dve", r1[:], src[:], dst[0][:], ALU.subtract, [key_src, key_dst], ["r1"])
            self.CP("dve", dst[1][:], r1[:], ["r1"], [key_dst])
            self.TT("dve", r1[:], r1[:], dst[1][:], ALU.subtract, ["r1", key_dst], ["r1"])
            self.CP("dve", dst[2][:], r1[:], ["r1"], [key_dst])

        split3(G["gl"], gls, "gl", "gls")
        for d in range(2):
            bk = self.nb()
            for i in range(3):
                self.MM(self.ps[bk][:, 0:NT * 6], Lmat[d][:], gls[i][:, d, :, :].rearrange("p t h -> p (t h)"), i == 0, i == 2, ["Lmat%d" % d, "gls"], [self.psk(bk)])
            self.CP("dve", G["gc"][:, d, :, :].rearrange("p t h -> p (t h)"), self.ps[bk][:, 0:NT * 6], [], [self.psk(bk), "gc"])
            bk = self.nb()
            for i in range(3):
                self.MM(self.ps[bk][:, 0:NT * 6], onesb[:], gls[i][:, d, :, :].rearrange("p t h -> p (t h)"), i == 0, i == 2, ["onesb1", "gls"], [self.psk(bk)])
            self.CP("dve", G["gt"][:, d, :, :].rearrange("p t h -> p (t h)"), self.ps[bk][:, 0:NT * 6], [], [self.psk(bk), "gt"])
        self.ACT(G["egc"][:], G["gc"][:], AF.Exp, ["gc"], ["egc"])
        self.ACT(G["egt"][:], G["gt"][:], AF.Exp, ["gt"], ["egt"])
        self.TT("dve", G["edec"][:], G["gt"][:], G["gc"][:], ALU.subtract, ["gt", "gc"], ["edec"])
        self.ACT(G["edec"][:], G["edec"][:], AF.Exp, ["edec"], ["edec"])
        split3(G["gc"], gcs, "gc", "gcs")
        ev = G["egt"][:].rearrange("p d t (a two) -> p d t a two", two=2)
        self.CP("pool", egtS[0:64], ev[0:64, :, :, :, 0], ["egt"], ["egtS"])
        self.CP("pool", egtS[64:128], ev[64:128, :, :, :, 1], ["egt"], ["egtS"])
        p3 = ExitStack()

        def mk(name, shape, dt):
            return (self.sb(p3, name, shape, dt), name)

        PS = []
        for pb in range(2):
            d_ = {}
            for d in range(2):
                d_["r0", d] = mk("r0_%d%d" % (pb, d), [128, 6, 128], BF16)
                d_["kp", d] = mk("kp_%d%d" % (pb, d), [128, 6, 64], BF16)
                d_["kdec", d] = mk("kdec_%d%d" % (pb, d), [128, 6, 64], BF16)
                for hf in range(2):
                    d_["kpT", d, hf] = mk("kpT_%d%d%d" % (pb, d, hf), [128, 3, 128], BF16)
                    self.MEMSET("pool", d_["kpT", d, hf][0][:], 0.0, [d_["kpT", d, hf][1]])
                d_["AqkT", d] = mk("AqkT_%d%d" % (pb, d), [128, 6, 128], BF16)
                d_["ru", d] = mk("ru_%d%d" % (pb, d), [128, 6, 64], F32)
                d_["rwb", d] = mk("rwb_%d%d" % (pb, d), [128, 6, 64], BF16)
            PS.append(d_)
        RT = []
        for d in range(2):
            d_ = {}
            d_["wT"] = mk("wT%d" % d, [128, 3, 128], BF16)
            d_["Sst"] = mk("Sst%d" % d, [128, 3, 64], F32)
            d_["Sbf"] = [mk("Sbf%d_%d" % (d, hf), [128, 3, 64], BF16) for hf in range(2)]
            d_["tmpS"] = mk("tmpS%d" % d, [128, 3, 64], F32)
            d_["vpp"] = mk("vpp%d" % d, [128, 6, 64], BF16)
            d_["o1"] = mk("do1_%d" % d, [128, 6, 64], F32)
            self.MEMSET("pool", d_["Sst"][0][:], 0.0, [d_["Sst"][1]])
            for hf in range(2):
                self.MEMSET("pool", d_["Sbf"][hf][0][:], 0.0, [d_["Sbf"][hf][1]])
            RT.append(d_)
        SL = []
        for sl in range(3):
            d_ = {}
            d_["dgi"] = [mk("dgi%d_%d" % (sl, i), [128, 4, 128], BF16) for i in range(3)]
            d_["d0"] = mk("d0_%d" % sl, [128, 4, 128], F32)
            d_["tmpk"] = mk("tmpk%d" % sl, [128, 4, 128], F32)
            d_["PT"] = [mk("PT%d_%d" % (sl, i), [128, 4, 128], BF16) for i in range(5)]
            d_["Pm"] = [mk("Pm%d_%d" % (sl, i), [128, 4, 128], BF16) for i in range(2)]
            d_["Pd0"] = mk("Pd0_%d" % sl, [128, 4, 128], BF16)
            d_["PoT"] = mk("PoT%d" % sl, [128, 4, 128], BF16)
            d_["XTb"] = mk("XTb%d" % sl, [128, 4, 128], BF16)
            d_["banks"] = (2 * sl, 2 * sl + 1)
            SL.append(d_)
        nidentb = self.sb(p3, "nidentb", [128, 128], BF16)
        self.TS("pool", nidentb[:], self.identf[:], -1.0, ALU.mult, ["identf"], ["nidentb"])
        touched = set()
        B_T, B_REC = 6, 7

        def v4(bk):
            return self.ps[bk][:].rearrange("p (a b) -> p a b", b=128)

        def w6(bk):
            return self.ps[bk][:, 0:384].rearrange("p (h e) -> p h e", h=6)

        def tiles_of(step):
            return [step, NT - 1 - step]

        def prep(step):
            tt = tiles_of(step)
            P_ = PS[step % 2]
            for d in range(2):
                t = tt[d]
                kp, kpk = P_["kp", d]
                r0, r0k = P_["r0", d]
                kdec, kdk = P_["kdec", d]
                sb6 = G["sbt"][:, d, t, :].unsqueeze(2).broadcast_to([128, 6, 64])
                kn6 = tm[:, t, 384:768].rearrange("p (h e) -> p h e", h=6)
                v6 = tm[:, t, 768:1152].rearrange("p (h e) -> p h e", h=6)
                self.TT("pool", kp[:], kn6, sb6, ALU.mult, ["tm", "sbt"], [kpk])
                self.TT("pool", r0[:, :, 0:64], v6, sb6, ALU.mult, ["tm", "sbt"], [r0k])
                self.TT("pool", r0[:, :, 64:128], kp[:], G["egc"][:, d, t, :].unsqueeze(2).broadcast_to([128, 6, 64]), ALU.mult, [kpk, "egc"], [r0k])
                self.TT("pool", kdec[:], kp[:], G["edec"][:, d, t, :].unsqueeze(2).broadcast_to([128, 6, 64]), ALU.mult, [kpk, "edec"], [kdk])
                pv = self.psbf(B_T, 8)
                kpf = kp[:].rearrange("p h e -> p (h e)")
                for c3 in range(3):
                    self.TR(pv[:, c3, :], kpf[:, c3 * 128:(c3 + 1) * 128], self.identb[:], [kpk, "identb"], [self.psk(B_T)])
                self.CP("act", P_["kpT", d, 0][0][0:64], pv[0:64, 0:3, :], [], [self.psk(B_T), P_["kpT", d, 0][1]])
                self.CP("dve", P_["kpT", d, 1][0][64:128], pv[64:128, 0:3, :], [], [self.psk(B_T), P_["kpT", d, 1][1]])

        def ut_group(sl, step, gi):
            g = groups[gi]
            tt = tiles_of(step)
            P_ = PS[step % 2]
            T_ = SL[sl]
            BA, BB = T_["banks"]
            kA, kB = self.psk(BA), self.psk(BB)
            dgi = T_["dgi"]
            d0, d0k = T_["d0"]
            tmpk, tmpkk = T_["tmpk"]
            PT = T_["PT"]
            Pm = T_["Pm"]
            Pd0, Pd0k = T_["Pd0"]
            PoT, PoTk = T_["PoT"]
            XTb, XTbk = T_["XTb"]
            Xb, Xbk = Pm[0]
            Qb, Qbk = Pm[1]
            XT2b, XT2bk = PT[1]
            xjb, xjbk = PT[2]
            yjb, yjbk = PT[3]
            combos = []
            for (d, h0, h1, s0) in g:
                for h in range(h0, h1):
                    combos.append((d, h, s0 + h - h0))
            for i in range(3):
                for (d, h0, h1, s0) in g:
                    n = h1 - h0
                    self.TT("pool", dgi[i][0][:, s0:s0 + n, :], self.identb[:].unsqueeze(1).broadcast_to([128, n, 128]),
                            gcs[i][:, d, tt[d], h0:h1].unsqueeze(2).broadcast_to([128, n, 128]), ALU.mult, ["identb", "gcs"], [dgi[i][1]])
            first = True
            for (d, h, s) in combos:
                for i in range(3):
                    self.MM(v4(BA)[:, s, :], onesb[:], dgi[i][0][:, s, :], first, False, ["onesb1", dgi[i][1]], [kA], skip_group_check=True)
                    first = False
            for (d, h, s) in combos:
                p = h // 2
                kT_, kTk = P_["kpT", d, h % 2]
                self.MM(v4(BB)[:, s, :], kT_[:, p, :], kT_[:, p, :], True, True, [kTk], [kB])
            yield
            for (d, h, s) in combos:
                self.STT(d0[:, s, :], v4(BA)[:, s, :], G["gc"][:, d, tt[d], h:h + 1], maskneg[d][:], ALU.subtract, ALU.add,
                         ["gc", "maskneg%d" % d], [kA, d0k])
            self.ACT(d0[:], d0[:], AF.Exp, [d0k], [d0k])
            yield
            self.TT("dve", tmpk[:], v4(BB)[:], d0[:], ALU.mult, [d0k], [kB, tmpkk])
            for (d, h, s) in combos:
                p = h // 2
                t = tt[d]
                kT_, kTk = P_["kpT", d, h % 2]
                self.MM(v4(BB)[:, s, :], kT_[:, p, :], qnT[:, p, t * 128:(t + 1) * 128], True, True, [kTk, "qnT"], [kB])
            self.TT("pool", PT[0][0][:], tmpk[:], nstrg_d[gi][:], ALU.mult, [tmpkk, "nstrgd%d" % gi], [PT[0][1]])
            self.TT("pool", PoT[:], tmpk[:], nstrg_o[gi][:], ALU.mult, [tmpkk, "nstrgo%d" % gi], [PoTk])
            yield
            for (d, h0, h1, s0) in g:
                n = h1 - h0
                Aq, Aqk_ = P_["AqkT", d]
                self.TT("dve", Aq[:, h0:h1, :], v4(BB)[:, s0:s0 + n, :], d0[:, s0:s0 + n, :], ALU.mult, [d0k], [kB, Aqk_])
            yield
            pv = self.psbf(B_T, 8)
            for (d, h, s) in combos:
                self.TR(pv[:, s, :], PT[0][0][:, s, :], self.identb[:], [PT[0][1], "identb"], [self.psk(B_T)])
            self.CP("act", Pd0[:], pv[:, 0:4, :], [], [self.psk(B_T), Pd0k])
            yield
            first = True
            for (d, h, s) in combos:
                self.MM(v4(BA)[:, s, :], self.identb[:], self.identb[:], first, False, ["identb"], [kA], skip_group_check=True)
                first = False
                self.MM(v4(BA)[:, s, :], Pd0[:, s, :], self.identb[:], False, False, [Pd0k, "identb"], [kA], skip_group_check=True)
            for k in range(5):
                Pk, Pkk = (Pd0, Pd0k) if k == 0 else Pm[k % 2]
                if k > 0:
                    self.CP("act", XTb[:], v4(BA)[:], [], [kA, XTbk])
                    for (d, h, s) in combos:
                        self.MM(v4(BA)[:, s, :], Pk[:, s, :], XTb[:, s, :], False, k == 4, [Pkk, XTbk], [kA], skip_group_check=True)
                if k < 4:
                    nx = (k + 1) % 2
                    for (d, h, s) in combos:
                        self.MM(v4(BB)[:, s, :], PT[k][0][:, s, :], Pk[:, s, :], True, True, [PT[k][1], Pkk], [kB])
                    yield
                    self.CP("dve", Pm[nx][0][:], v4(BB)[:], [], [kB, Pm[nx][1]])
                    for (d, h, s) in combos:
                        self.MM(v4(BB)[:, s, :], Pk[:, s, :], PT[k][0][:, s, :], True, True, [PT[k][1], Pkk], [kB])
                    yield
                    self.CP("act", PT[k + 1][0][:], v4(BB)[:], [], [kB, PT[k + 1][1]])
                yield
            self.CP("act", XTb[:], v4(BA)[:], [], [kA, XTbk])
            yield
            pv = self.psbf(B_T, 8)
            for (d, h, s) in combos:
                self.TR(pv[:, s, :], XTb[:, s, :], self.identb[:], [XTbk, "identb"], [self.psk(B_T)])
            self.CP("dve", Xb[:], pv[:, 0:4, :], [], [self.psk(B_T), Xbk])
            first = True
            for (d, h, s) in combos:
                self.MM(v4(BB)[:, s, :], self.identb[:], self.identb[:], first, False, ["identb"], [kB], skip_group_check=True)
                first = False
                self.MM(v4(BB)[:, s, :], nidentb[:], XTb[:, s, :], False, False, ["nidentb", XTbk], [kB], skip_group_check=True)
                self.MM(v4(BB)[:, s, :], Pd0[:, s, :], XTb[:, s, :], False, True, [Pd0k, XTbk], [kB], skip_group_check=True)
            yield
            self.CP("act", Qb[:], v4(BB)[:], [], [kB, Qbk])
            yield
            first = True
            for (d, h, s) in combos:
                self.MM(v4(BB)[:, s, :], self.identb[:], XTb[:, s, :], first, False, ["identb", XTbk], [kB], skip_group_check=True)
                first = False
                self.MM(v4(BB)[:, s, :], Xb[:, s, :], Qb[:, s, :], False, True, [Xbk, Qbk], [kB], skip_group_check=True)
            yield
            self.CP("dve", XT2b[:], v4(BB)[:], [], [kB, XT2bk])
            yield
            for it in range(4):
                if it > 0:
                    first = True
                    for (d, h, s) in combos:
                        r0, r0k = P_["r0", d]
                        self.MM(v4(BB)[:, s, :], self.identb[:], r0[:, h, :], first, False, ["identb", r0k], [kB], skip_group_check=True)
                        first = False
                        self.MM(v4(BB)[:, s, :], PoT[:, s, :], xjb[:, s, :], False, True, [PoTk, xjbk], [kB], skip_group_check=True)
                    yield
                    self.CP("act", yjb[:], v4(BB)[:], [], [kB, yjbk])
                    yield
                for (d, h, s) in combos:
                    r0, r0k = P_["r0", d]
                    rhs = r0[:, h, :] if it == 0 else yjb[:, s, :]
                    rkeys = [r0k] if it == 0 else [yjbk]
                    self.MM(v4(BA)[:, s, :], XT2b[:, s, :], rhs, True, True, [XT2bk] + rkeys, [kA])
                yield
                if it < 3:
                    self.CP("dve", xjb[:], v4(BA)[:], [], [kA, xjbk])
                    yield
            for (d, h0, h1, s0) in g:
                n = h1 - h0
                ru, ruk = P_["ru", d]
                rwb, rwbk = P_["rwb", d]
                self.CP("dve", ru[:, h0:h1, :], v4(BA)[:, s0:s0 + n, 0:64], [], [kA, ruk])
                self.CP("act", rwb[:, h0:h1, :], v4(BA)[:, s0:s0 + n, 64:128], [], [kA, rwbk])

        def rec_dir(step, d):
            tt = tiles_of(step)
            t = tt[d]
            P_ = PS[step % 2]
            R_ = RT[d]
            wT, wTk = R_["wT"]
            Sst, Sstk = R_["Sst"]
            Sbf = R_["Sbf"]
            tmpS, tmpSk = R_["tmpS"]
            vpp, vppk = R_["vpp"]
            o1, o1k = R_["o1"]
            ru, ruk = P_["ru", d]
            rwb, rwbk = P_["rwb", d]
            Aq, Aqk_ = P_["AqkT", d]
            kdec, kdk = P_["kdec", d]
            kR = self.psk(B_REC)
            pv = self.psbf(B_T, 8)
            rwf = rwb[:].rearrange("p h e -> p (h e)")
            for c3 in range(3):
                self.TR(pv[:, c3, :], rwf[:, c3 * 128:(c3 + 1) * 128], self.identb[:], [rwbk, "identb"], [self.psk(B_T)])
            self.CP("act", wT[:], pv[:, 0:3, :], [], [self.psk(B_T), wTk])
            yield
            for h in range(6):
                p = h // 2
                self.MM(w6(B_REC)[:, h, :], wT[:, p, :], Sbf[h % 2][0][:, p, :], True, True, [wTk, Sbf[h % 2][1]], [kR])
            yield
            self.TT("dve", vpp[:], ru[:], w6(B_REC), ALU.subtract, [ruk], [kR, vppk])
            yield
            for h in range(6):
                p = h // 2
                self.MM(w6(B_REC)[:, h, :], qnT[:, p, t * 128:(t + 1) * 128], Sbf[h % 2][0][:, p, :], True, True, ["qnT", Sbf[h % 2][1]], [kR])
            yield
            self.TT("dve", o1[:], w6(B_REC), G["egc"][:, d, t, :].unsqueeze(2).broadcast_to([128, 6, 64]), ALU.mult, ["egc"], [kR, o1k])
            yield
            for h in range(6):
                self.MM(w6(B_REC)[:, h, :], Aq[:, h, :], vpp[:, h, :], True, True, [Aqk_, vppk], [kR])
            yield
            ot = osum[:, t, :].rearrange("p (h e) -> p h e", h=6)
            if t not in touched:
                touched.add(t)
                self.TT("dve", ot, o1[:], w6(B_REC), ALU.add, [o1k], [kR, ("osum", t)])
            else:
                self.TT("dve", o1[:], o1[:], w6(B_REC), ALU.add, [o1k], [kR, o1k])
                self.TT("pool", ot, ot, o1[:], ALU.add, [o1k, ("osum", t)], [("osum", t)])
            yield
            for h in range(6):
                p, base = h // 2, 64 * (h % 2)
                self.MM(self.ps[B_REC][base:base + 64, p * 64:(p + 1) * 64], kdec[:, h, :], vpp[:, h, :], True, True, [kdk, vppk], [kR])
            self.TT("pool", tmpS[:], Sst[:], egtS[:, d, t, :].unsqueeze(2).broadcast_to([128, 3, 64]), ALU.mult, [Sstk, "egtS"], [tmpSk])
            yield
            self.TT("dve", Sst[:], tmpS[:], self.ps[B_REC][:, 0:192].rearrange("p (a b) -> p a b", b=64), ALU.add, [tmpSk], [kR, Sstk])
            yield
            self.CP("act", Sbf[0][0][0:64], Sst[0:64], [Sstk], [Sbf[0][1]])
            self.CP("pool", Sbf[1][0][64:128], Sst[64:128], [Sstk], [Sbf[1][1]])

        ut_stream = [(st, gi) for st in range(NT) for gi in range(3)]
        ut_done = [0] * NT
        rec_done = [False] * NT
        rec_next = 0
        rec_active = 0
        active = []
        free_slots = [0, 1, 2]
        nxt = 0
        while True:
            while free_slots and nxt < len(ut_stream):
                st, gi = ut_stream[nxt]
                if gi == 0 and st >= 2 and not rec_done[st - 2]:
                    break
                nxt += 1
                if gi == 0:
                    prep(st)
                sl = free_slots.pop(0)
                active.append(["ut", ut_group(sl, st, gi), sl, st])
            if rec_active == 0 and rec_next < NT and ut_done[rec_next] == 3:
                def rec_step(st_):
                    yield from rec_dir(st_, 0)
                    yield
                    yield from rec_dir(st_, 1)
                active.append(["rec", rec_step(rec_next), None, rec_next])
                rec_active = 1
                rec_next += 1
            if not active:
                break
            for a_ in list(active):
                try:
                    next(a_[1])
                except StopIteration:
                    active.remove(a_)
                    if a_[0] == "ut":
                        free_slots.append(a_[2])
                        ut_done[a_[3]] += 1
                    else:
                        rec_active -= 1
                        rec_done[a_[3]] = True
        self.P.flush()
        p3.close()
        oss = self.sb(ph, "doss", [128, NT, 6], F32)
        y1 = self.sb(ph, "dy1", [128, 384], F32)
        nwz = self.sb(ph, "nwz", [128, 384], F32)
        ydel = [self.sb(ph, "ydel%d" % i, [128, 384], BF16) for i in range(2)]
        yT = self.sb(ph, "dyT", [128, 3, S], BF16)
        for t in range(NT):
            self.ACT(sqs[:, 0:384], osum[:, t, :], AF.Square, [("osum", t)], ["sqs"])
            self.REDUCE(oss[:, t, :], sqs[:, 0:384].rearrange("p (a b) -> p a b", b=64), ["sqs"], ["doss"])
        self.TS("dve", oss[:], oss[:], 1.0 / 64.0, ALU.mult, ["doss"], ["doss"], s2=1e-6, op1=ALU.add)
        self.ACT(oss[:], oss[:], AF.Sqrt, ["doss"], ["doss"])
        self.RECIP(oss[:], oss[:], ["doss"], ["doss"])
        for t in range(NT):
            b = t % 2
            self.TT("dve", y1[:].rearrange("p (h e) -> p h e", h=6), osum[:, t, :].rearrange("p (h e) -> p h e", h=6),
                    oss[:, t, :].unsqueeze(2).broadcast_to([128, 6, 64]), ALU.mult, [("osum", t), "doss"], ["dy1"])
            self.TT("pool", nwz[:].rearrange("p (h e) -> p h e", h=6), zst[:, t, :].rearrange("p (h e) -> p h e", h=6),
                    nw[:].unsqueeze(1).broadcast_to([128, 6, 64]), ALU.mult, ["zst", "nw"], ["nwz"])
            self.TT("dve", ydel[b][:], y1[:], nwz[:], ALU.mult, ["dy1", "nwz"], ["ydel%d" % b])
            bk = 4 + (t % 2)
            pv = self.psbf(bk, 8)
            for c3 in range(3):
                self.TR(pv[:, c3, :], ydel[b][:, c3 * 128:(c3 + 1) * 128], self.identb[:], ["ydel%d" % b, "identb"], [self.psk(bk)])
            self.CP("act", yT[:, :, t * 128:(t + 1) * 128], pv[:, 0:3, :], [], [self.psk(bk), "dyT"])
        for c3 in range(3):
            self.DMA("sp", self.ycatT[5 + c3], yT[:, c3, :], ["dyT"], ["ycatT"], "st_ydel")
        self.P.flush()


Builder.phase_delta = phase_delta
```

```python
import math
import os
from contextlib import ExitStack

import numpy as np
import concourse.bass as bass
import concourse.mybir as mybir
from concourse.bass_utils import run_bass_kernel_spmd

F32 = mybir.dt.float32
BF16 = mybir.dt.bfloat16
AF = mybir.ActivationFunctionType
ALU = mybir.AluOpType

D = 1024
IN_W = 3224
DFF = 4096
EPS = 1e-6
N_CORES = 8
SLOPES = [0.25, 0.0625, 0.015625, 0.00390625, 0.5, 0.125]
ALIBI_SKIP = 60.0


class Prog:
    QUEUES = ("pe", "act", "dve", "pool", "sp")

    def __init__(self, nc, es):
        self.nc = nc
        self.ops = []
        self.last_w = {}
        self.readers = {}
        self.sems = {}
        self.es = es
        self.cnt = {}
        self.waited = {q: {} for q in self.QUEUES}
        self.start = 0

    def sem(self, sig):
        if sig not in self.sems:
            nm = "s%d" % len(self.sems)
            self.sems[sig] = self.es.enter_context(self.nc.semaphore(nm))
        return self.sems[sig]

    def op(self, q, fn, reads=(), writes=(), dma=None):
        i = len(self.ops)
        sig = ("dma", dma) if dma is not None else q
        deps = {}
        for k in reads:
            w = self.last_w.get(k)
            if w is not None:
                deps[w] = True
        for k in writes:
            w = self.last_w.get(k)
            if w is not None:
                deps.setdefault(w, False)
            for r in self.readers.get(k, {}).values():
                deps.setdefault(r, False)
        need = []
        for j, raw in deps.items():
            oj = self.ops[j]
            if j < self.start:
                continue
            if oj["sig"] == q and dma is None:
                if q == "pe":
                    continue
            need.append(j)
            oj["needed"] = True
        self.ops.append(dict(q=q, fn=fn, sig=sig, deps=need, needed=(dma is not None), val=None))
        for k in writes:
            self.last_w[k] = i
            self.readers[k] = {}
        for k in reads:
            self.readers.setdefault(k, {})[sig] = i
        return i

    def wait_all_dma(self, q="sp"):
        need = []
        seen = set()
        for j in range(len(self.ops) - 1, -1, -1):
            oj = self.ops[j]
            if isinstance(oj["sig"], tuple) and oj["sig"] not in seen:
                seen.add(oj["sig"])
                need.append(j)
        self.ops.append(dict(q=q, fn=None, sig=q, deps=need, needed=False, val=None))

    def flush(self):
        self.wait_all_dma()
        nc = self.nc
        ops = self.ops[self.start:]
        self.start = len(self.ops)
        for o in ops:
            if o["needed"]:
                inc = 16 if isinstance(o["sig"], tuple) else 1
                self.cnt[o["sig"]] = self.cnt.get(o["sig"], 0) + inc
                o["val"] = self.cnt[o["sig"]]
                self.sem(o["sig"])
        allops = self.ops
        prog = self

        def run(qname, eng):
            waited = prog.waited[qname]
            for o in ops:
                if o["q"] != qname:
                    continue
                wl = {}
                for j in o["deps"]:
                    oj = allops[j]
                    wl[oj["sig"]] = max(wl.get(oj["sig"], 0), oj["val"])
                for sg, v in wl.items():
                    if waited.get(sg, 0) >= v:
                        continue
                    eng.wait_ge(prog.sems[sg], v)
                    waited[sg] = v
                if o["fn"] is not None:
                    inst = o["fn"](eng)
                    if o["needed"]:
                        inc = 16 if isinstance(o["sig"], tuple) else 1
                        inst.then_inc(prog.sems[o["sig"]], inc)

        with nc.Block() as block:
            @block.tensor
            def _(e):
                run("pe", e)

            @block.scalar
            def _(e):
                run("act", e)

            @block.vector
            def _(e):
                run("dve", e)

            @block.gpsimd
            def _(e):
                run("pool", e)

            @block.sync
            def _(e):
                run("sp", e)


class Builder:
    def __init__(self, S, depth, debug=False, phases=None):
        self.S = S
        self.depth = depth
        self.NT = S // 128
        self.TB = min(512, S)
        self.NTB = S // self.TB
        self.debug = debug
        self.phases = phases
        self.bank_rr = 0

    def ACT(self, out, in_, func, r, w, **kw):
        self.P.op("act", lambda e: e.activation(out=out, in_=in_, func=func, **kw), r, w)

    def TT(self, q, out, in0, in1, op, r, w):
        self.P.op(q, lambda e: e.tensor_tensor(out=out, in0=in0, in1=in1, op=op), r, w)

    def TS(self, q, out, in0, s1, op0, r, w, s2=None, op1=None):
        if op1 is None:
            self.P.op(q, lambda e: e.tensor_scalar(out=out, in0=in0, scalar1=s1, scalar2=None, op0=op0), r, w)
        else:
            self.P.op(q, lambda e: e.tensor_scalar(out=out, in0=in0, scalar1=s1, scalar2=s2, op0=op0, op1=op1), r, w)

    def STT(self, out, in0, scalar, in1, op0, op1, r, w):
        self.P.op("dve", lambda e: e.scalar_tensor_tensor(out=out, in0=in0, scalar=scalar, in1=in1, op0=op0, op1=op1), r, w)

    def CP(self, q, out, in_, r, w, scale=None):
        if q == "act":
            if scale is None:
                self.P.op("act", lambda e: e.activation(out=out, in_=in_, func=AF.Copy), r, w)
            else:
                self.P.op("act", lambda e: e.activation(out=out, in_=in_, func=AF.Copy, scale=scale), r, w)
        else:
            self.P.op(q, lambda e: e.tensor_copy(out=out, in_=in_), r, w)

    def MM(self, out, lhsT, rhs, start, stop, r, w, **kw):
        self.P.op("pe", lambda e: e.matmul(out, lhsT=lhsT, rhs=rhs, start=start, stop=stop, **kw), r, w)

    def TR(self, out, in_, ident, r, w):
        self.P.op("pe", lambda e: e.transpose(out=out, in_=in_, identity=ident), r, w)

    def DMA(self, q, out, in_, r, w, stream):
        self.P.op(q, lambda e: e.dma_start(out=out, in_=in_), r, w, dma=stream)

    def MEMSET(self, q, ap, val, w):
        self.P.op(q, lambda e: e.memset(ap, val), (), w)

    def RECIP(self, out, in_, r, w):
        self.P.op("dve", lambda e: e.reciprocal(out=out, in_=in_), r, w)

    def REDUCE(self, out, in_, r, w):
        self.P.op("dve", lambda e: e.tensor_reduce(out=out, in_=in_, axis=mybir.AxisListType.X, op=ALU.add), r, w)

    def ASEL(self, out, in_, pattern, cmp, fill, base, cm, r, w):
        self.P.op("pool", lambda e: e.affine_select(out=out, in_=in_, pattern=pattern, compare_op=cmp, fill=fill,
                                                    base=base, channel_multiplier=cm), r, w)

    def nb(self):
        b = self.bank_rr
        self.bank_rr = (self.bank_rr + 1) % 8
        return b

    def sb(self, ph, name, shape, dt):
        self.uid = getattr(self, "uid", 0) + 1
        return ph.enter_context(self.nc.sbuf_tensor("%s_u%d" % (name, self.uid), shape, dt))

    def psk(self, b):
        return "ps%d" % b

    def psbf(self, b, a):
        return self.ps[b][:].bitcast(BF16).rearrange("p (a b) -> p a b", a=a)

    def build(self):
        S, NT = self.S, self.NT
        L = self.depth
        nc = bass.Bass("TRN2", target_bir_lowering=False)
        self.nc = nc

        def din(name, shape, dt=F32):
            return nc.dram_tensor(name, shape, dt, kind="ExternalInput").ap()

        def dscr(name, shape, dt=F32):
            kind = "ExternalOutput" if self.debug else "Internal"
            return nc.dram_tensor(name, shape, dt, kind=kind).ap()

        self.x_in = din("x", [S, D])
        self.w_in = din("w_in", [L, D, IN_W])
        self.w_out = din("w_out", [L, D, D])
        self.w_ff1 = din("w_ff1", [L, D, DFF])
        self.w_ff2 = din("w_ff2", [L, DFF, D])
        self.normw = din("normw", [L, 4, D])
        self.convw = din("convw", [L, 256, 31])
        self.convp = din("convp", [L, 128, 2, 3])
        self.lam = din("lam", [L, 2, 2, 32])
        self.subw = din("subw", [L, 64])
        self.dconvw = din("dconvw", [L, 1152, 3])
        self.dgate = din("dgate", [L, 2, 12])
        self.dnormw = din("dnormw", [L, 64])
        self.dband = din("dband", [128, 2 * S - 128])
        self.bdmask = din("bdmask", [128, 128])
        self.out = nc.dram_tensor("out", [S, D], F32, kind="ExternalOutput").ap()
        self.xres = dscr("xres", [S, D])
        self.hg = dscr("hg", [2, 128, S], BF16)
        self.qT = dscr("qT", [4, 96, S], BF16)
        self.kT = dscr("kT", [4, 96, S], BF16)
        self.vtok = dscr("vtok", [S, 384], BF16)
        self.gqkvT = dscr("gqkvT", [9, 128, S])
        self.zs = dscr("zs", [S, 384], BF16)
        self.gates = dscr("gates", [S, 24])
        self.ycatT = dscr("ycatT", [8, 128, S], BF16)
        self.ypart = dscr("ypart", [S, D])

        with ExitStack() as es:
            self.P = Prog(nc, es)
            self.ps = [es.enter_context(nc.psum_tensor("psb%d" % i, [128, 512], F32)) for i in range(8)]
            self.identf = self.sb(es, "identf", [128, 128], F32)
            self.identb = self.sb(es, "identb", [128, 128], BF16)
            self.MEMSET("pool", self.identf[:], 1.0, ["identf"])
            self.ASEL(self.identf[:], self.identf[:], [[-1, 128]], ALU.is_equal, 0.0, 0, 1, ["identf"], ["identf"])
            self.CP("pool", self.identb[:], self.identf[:], ["identf"], ["identb"])
            self.P.flush()
            for l in range(L):
                self.layer(l)
            self.P.flush()
        return nc

    def want(self, name):
        return self.phases is None or name in self.phases

    def layer(self, l):
        S = self.S
        last = (l == self.depth - 1)
        src = self.x_in if l == 0 else self.xres
        if self.want("B"):
            self.phase_in_proj(l, src)
        self.merge_conv = self.want("C") and self.want("E")
        if self.want("C") and not self.merge_conv:
            self.phase_conv(l)
        if self.want("D"):
            self.phase_attn(l)
        if self.want("E"):
            self.phase_delta(l)
        if self.want("F") and self.want("G"):
            self.phase_out_mlp(l, src, self.out if last else self.xres)
        else:
            if self.want("F"):
                self.phase_out_proj(l, src)
            if self.want("G"):
                self.phase_mlp(l, self.out if last else self.xres)

    def norm_T(self, ph, src, wrow, hT):
        S, NT = self.S, self.NT
        xt = [self.sb(ph, "n_xt%d" % i, [128, D], F32) for i in range(2)]
        hb = [self.sb(ph, "n_hb%d" % i, [128, D], BF16) for i in range(2)]
        junk = self.sb(ph, "n_junk", [128, D], BF16)
        ssq = self.sb(ph, "n_ssq", [128, NT], F32)
        rstd = self.sb(ph, "n_rstd", [128, NT], F32)
        wbc = self.sb(ph, "n_wbc", [128, D], F32)
        self.DMA("sp", wbc[:], wrow.broadcast_to([128, D]), [], ["n_wbc"], "n_wbc")
        for t in range(NT):
            b = t % 2
            self.DMA("sp", xt[b][:], src[t * 128:(t + 1) * 128, :], [("x", t)], ["n_xt%d" % b], "n_xt%d" % b)
            self.ACT(junk[:], xt[b][:], AF.Square, ["n_xt%d" % b], ["n_junk", "n_ssq"], accum_out=ssq[:, t:t + 1])
        self.TS("dve", rstd[:], ssq[:], 1.0 / D, ALU.mult, ["n_ssq"], ["n_rstd"], s2=EPS, op1=ALU.add)
        self.ACT(rstd[:], rstd[:], AF.Sqrt, ["n_rstd"], ["n_rstd"])
        self.RECIP(rstd[:], rstd[:], ["n_rstd"], ["n_rstd"])
        for t in range(NT):
            b = t % 2
            self.DMA("sp", xt[b][:], src[t * 128:(t + 1) * 128, :], [("x", t)], ["n_xt%d" % b], "n_xt%d" % b)
            self.STT(hb[b][:], xt[b][:], rstd[:, t:t + 1], wbc[:], ALU.mult, ALU.mult,
                     ["n_xt%d" % b, "n_rstd", "n_wbc"], ["n_hb%d" % b])
            bk = self.nb()
            pv = self.psbf(bk, 8)
            for kc in range(8):
                self.TR(pv[:, kc, :], hb[b][:, kc * 128:(kc + 1) * 128], self.identb[:], ["n_hb%d" % b, "identb"], [self.psk(bk)])
            self.CP("act", hT[:, :, t * 128:(t + 1) * 128], pv[:, :, :], [], [self.psk(bk), "hT"])

    def phase_in_proj(self, l, src):
        S, NT, TB, NTB = self.S, self.NT, self.TB, self.NTB
        with ExitStack() as ph:
            hT = self.sb(ph, "hT", [128, 8, S], BF16)
            self.norm_T(ph, src, self.normw[l, 0:1, :], hT)
            groups = [("wA", 0, 512), ("wB", 512, 768), ("wC", 1280, 384), ("wD", 1664, 1152), ("wE", 2816, 408)]
            W = {}
            for nm, c0, n in groups:
                W[nm] = self.sb(ph, nm, [128, 8, n], BF16)
                self.DMA("pool", W[nm][:], self.w_in[l, :, c0:c0 + n].rearrange("(kc p) n -> p kc n", p=128), [], [nm], nm)
            hgs = self.sb(ph, "hgs", [128, 2, S], BF16)
            qs = self.sb(ph, "qs", [128, 4, S], BF16)
            ks = self.sb(ph, "ks", [128, 4, S], BF16)
            vs = self.sb(ph, "vs", [128, NT, 384], BF16)
            zst = self.sb(ph, "zst", [128, NT, 384], BF16)
            gst = self.sb(ph, "gst", [128, NT, 24], F32)
            sg = [self.sb(ph, "sg%d" % i, [128, TB], F32) for i in range(2)]
            dst = [self.sb(ph, "dst%d" % i, [128, TB], F32) for i in range(3)]
            i = 0
            for cc in range(2):
                for tb in range(NTB):
                    ts_ = slice(tb * TB, (tb + 1) * TB)
                    ba, bg = self.nb(), self.nb()
                    for kc in range(8):
                        self.MM(self.ps[bg][:, :TB], W["wA"][:, kc, 256 + cc * 128:256 + (cc + 1) * 128], hT[:, kc, ts_], kc == 0, kc == 7, ["wA", "hT"], [self.psk(bg)])
                    for kc in range(8):
                        self.MM(self.ps[ba][:, :TB], W["wA"][:, kc, cc * 128:(cc + 1) * 128], hT[:, kc, ts_], kc == 0, kc == 7, ["wA", "hT"], [self.psk(ba)])
                    s = i % 2
                    i += 1
                    self.ACT(sg[s][:], self.ps[bg][:, :TB], AF.Sigmoid, [], [self.psk(bg), "sg%d" % s])
                    self.TT("dve", hgs[:, cc, ts_], self.ps[ba][:, :TB], sg[s][:], ALU.mult, ["sg%d" % s], [self.psk(ba), "hgs"])
            for cc in range(2):
                self.DMA("sp", self.hg[cc], hgs[:, cc, :], ["hgs"], ["hg"], "st_hg")
            for which, stg, dst_d, scale in (("q", qs, self.qT, 32.0 ** -0.5), ("k", ks, self.kT, None)):
                base = 0 if which == "q" else 384
                for c in range(4):
                    for tb in range(NTB):
                        ts_ = slice(tb * TB, (tb + 1) * TB)
                        bk = self.nb()
                        for kc in range(8):
                            self.MM(self.ps[bk][0:96, :TB], W["wB"][:, kc, base + 96 * c:base + 96 * (c + 1)], hT[:, kc, ts_], kc == 0, kc == 7, ["wB", "hT"], [self.psk(bk)])
                        self.CP("act", stg[0:96, c, ts_], self.ps[bk][0:96, :TB], [], [self.psk(bk), which + "s"], scale=scale)
                for c in range(4):
                    self.DMA("sp", dst_d[c], stg[0:96, c, :], [which + "s"], [which + "T"], "st_" + which)
            for t in range(NT):
                bk = self.nb()
                for kc in range(8):
                    self.MM(self.ps[bk][:, :384], hT[:, kc, t * 128:(t + 1) * 128], W["wC"][:, kc, :], kc == 0, kc == 7, ["wC", "hT"], [self.psk(bk)])
                self.CP("dve", vs[:, t, :], self.ps[bk][:, :384], [], [self.psk(bk), "vs"])
            self.DMA("sp", self.vtok.rearrange("(t p) n -> p t n", p=128), vs[:], ["vs"], ["vtok"], "st_v")
            i = 0
            for c in range(9):
                for tb in range(NTB):
                    ts_ = slice(tb * TB, (tb + 1) * TB)
                    bk = self.nb()
                    for kc in range(8):
                        self.MM(self.ps[bk][:, :TB], W["wD"][:, kc, c * 128:(c + 1) * 128], hT[:, kc, ts_], kc == 0, kc == 7, ["wD", "hT"], [self.psk(bk)])
                    s = i % 3
                    i += 1
                    self.CP("dve" if i % 2 else "act", dst[s][:], self.ps[bk][:, :TB], [], [self.psk(bk), "dst%d" % s])
                    self.DMA("sp", self.gqkvT[c, :, ts_], dst[s][:], ["dst%d" % s], ["gqkvT"], "st_d%d" % s)
            for t in range(NT):
                bk = self.nb()
                for kc in range(8):
                    self.MM(self.ps[bk][:, :408], hT[:, kc, t * 128:(t + 1) * 128], W["wE"][:, kc, :], kc == 0, kc == 7, ["wE", "hT"], [self.psk(bk)])
                self.ACT(zst[:, t, :], self.ps[bk][:, :384], AF.Silu, [], [self.psk(bk), "zst"])
                self.CP("dve", gst[:, t, :], self.ps[bk][:, 384:408], [], [self.psk(bk), "gst"])
            self.DMA("sp", self.zs.rearrange("(t p) n -> p t n", p=128), zst[:], ["zst"], ["zs"], "st_z")
            self.DMA("sp", self.gates.rearrange("(t p) n -> p t n", p=128), gst[:], ["gst"], ["gates"], "st_g")
            self.P.flush()

    def post_norm_residual(self, ph_tiles, banks, t, src, dst, wbc, extra=None):
        xt, ytmp, small, junk = ph_tiles
        b = t % 2
        self.DMA("sp", xt[b][:], src[t * 128:(t + 1) * 128, :], [("x", t)], ["r_xt%d" % b], "r_xt%d" % b)
        ysrc = []
        for hf in range(2):
            bk = banks[hf]
            if extra is not None:
                etile, ekey = extra
                self.TT("dve", ytmp[b][:, hf * 512:(hf + 1) * 512], self.ps[bk][:, :], etile[:, hf * 512:(hf + 1) * 512], ALU.add,
                        [ekey], [self.psk(bk), "r_yt%d" % b])
            else:
                self.CP("dve", ytmp[b][:, hf * 512:(hf + 1) * 512], self.ps[bk][:, :], [], [self.psk(bk), "r_yt%d" % b])
        sm = small[b]
        self.ACT(junk[:], ytmp[b][:], AF.Square, ["r_yt%d" % b], ["r_junk", "r_sm%d" % b], accum_out=sm[:, 0:1])
        self.TS("dve", sm[:, 1:2], sm[:, 0:1], 1.0 / D, ALU.mult, ["r_sm%d" % b], ["r_sm%d" % b], s2=EPS, op1=ALU.add)
        self.ACT(sm[:, 2:3], sm[:, 1:2], AF.Sqrt, ["r_sm%d" % b], ["r_sm%d" % b])
        self.RECIP(sm[:, 3:4], sm[:, 2:3], ["r_sm%d" % b], ["r_sm%d" % b])
        self.STT(ytmp[b][:], ytmp[b][:], sm[:, 3:4], wbc[:], ALU.mult, ALU.mult, ["r_yt%d" % b, "r_sm%d" % b, "r_wbc"], ["r_yt%d" % b])
        self.TT("pool", xt[b][:], xt[b][:], ytmp[b][:], ALU.add, ["r_xt%d" % b, "r_yt%d" % b], ["r_xt%d" % b])
        self.DMA("sp", dst[t * 128:(t + 1) * 128, :], xt[b][:], ["r_xt%d" % b], [("x", t)], "r_st%d" % b)

    def res_tiles(self, ph):
        xt = [self.sb(ph, "r_xt%d" % i, [128, D], F32) for i in range(2)]
        ytmp = [self.sb(ph, "r_yt%d" % i, [128, D], F32) for i in range(2)]
        small = [self.sb(ph, "r_sm%d" % i, [128, 4], F32) for i in range(2)]
        junk = self.sb(ph, "r_junk", [128, D], BF16)
        return xt, ytmp, small, junk

    def phase_out_proj(self, l, src):
        S, NT = self.S, self.NT
        with ExitStack() as ph:
            yT = self.sb(ph, "ycat", [128, 8, S], BF16)
            wo = self.sb(ph, "wo", [128, 8, D], BF16)
            wbc = self.sb(ph, "r_wbc", [128, D], F32)
            tiles = self.res_tiles(ph)
            for c in range(8):
                self.DMA("sp", yT[:, c, :], self.ycatT[c], ["ycatT"], ["ycat"], "ld_ycat")
            self.DMA("pool", wo[:], self.w_out[l].rearrange("(kc p) n -> p kc n", p=128), [], ["wo"], "ld_wo")
            self.DMA("sp", wbc[:], self.normw[l, 1:2, :].broadcast_to([128, D]), [], ["r_wbc"], "ld_rwbc")
            for t in range(NT):
                banks = [self.nb(), self.nb()]
                for hf in range(2):
                    for kc in range(8):
                        self.MM(self.ps[banks[hf]][:, :], yT[:, kc, t * 128:(t + 1) * 128], wo[:, kc, hf * 512:(hf + 1) * 512], kc == 0, kc == 7,
                                ["ycat", "wo"], [self.psk(banks[hf])])
                self.post_norm_residual(tiles, banks, t, src, self.xres, wbc)
            self.P.flush()

    def phase_out_mlp(self, l, src, dst):
        S, NT = self.S, self.NT
        with ExitStack() as ph:
            hT = self.sb(ph, "hT", [128, 8, S], BF16)
            w1 = self.sb(ph, "w1", [128, 8, 2048], BF16)
            w2 = self.sb(ph, "w2", [128, 16, D], BF16)
            self.DMA("pool", w1[:], self.w_ff1[l, :, 0:2048].rearrange("(kc p) n -> p kc n", p=128), [], ["w1"], "ld_w1")
            self.DMA("pool", w2[:], self.w_ff2[l, 0:2048, :].rearrange("(j p) n -> p j n", p=128), [], ["w2"], "ld_w2")
            with ExitStack() as ph2:
                yT = self.sb(ph2, "ycat", [128, 8, S], BF16)
                wo = self.sb(ph2, "wo", [128, 8, D], BF16)
                wbc = self.sb(ph2, "r_wbc", [128, D], F32)
                tiles = self.res_tiles(ph2)
                for c in range(8):
                    self.DMA("sp", yT[:, c, :], self.ycatT[c], ["ycatT"], ["ycat"], "ld_ycat")
                self.DMA("pool", wo[:], self.w_out[l].rearrange("(kc p) n -> p kc n", p=128), [], ["wo"], "ld_wo")
                self.DMA("sp", wbc[:], self.normw[l, 1:2, :].broadcast_to([128, D]), [], ["r_wbc"], "ld_rwbc")
                for t in range(NT):
                    banks = [self.nb(), self.nb()]
                    for hf in range(2):
                        for kc in range(8):
                            self.MM(self.ps[banks[hf]][:, :], yT[:, kc, t * 128:(t + 1) * 128], wo[:, kc, hf * 512:(hf + 1) * 512], kc == 0, kc == 7,
                                    ["ycat", "wo"], [self.psk(banks[hf])])
                    self.post_norm_residual(tiles, banks, t, src, self.xres, wbc)
                self.norm_T(ph2, self.xres, self.normw[l, 2:3, :], hT)
                self.P.flush()
            self.mlp_main(ph, l, dst, hT, w1, w2, True)

    def phase_mlp(self, l, dst):
        S = self.S
        with ExitStack() as ph:
            hT = self.sb(ph, "hT", [128, 8, S], BF16)
            with ExitStack() as ph2:
                self.norm_T(ph2, self.xres, self.normw[l, 2:3, :], hT)
                self.P.flush()
            w1 = self.sb(ph, "w1", [128, 8, 2048], BF16)
            w2 = self.sb(ph, "w2", [128, 16, D], BF16)
            self.mlp_main(ph, l, dst, hT, w1, w2, False)

    def mlp_main(self, ph, l, dst, hT, w1, w2, first_loaded):
        S, NT, TB, NTB = self.S, self.NT, self.TB, self.NTB
        NQ = TB // 128
        if True:
            h1 = [self.sb(ph, "h1_%d" % i, [128, 16, TB], BF16) for i in range(2)]
            rl = [self.sb(ph, "rl%d" % i, [128, TB], BF16) for i in range(2)]
            yp = [self.sb(ph, "yp%d" % i, [128, D], F32) for i in range(2)]
            wbc = self.sb(ph, "r_wbc", [128, D], F32)
            tiles = self.res_tiles(ph)
            self.DMA("sp", wbc[:], self.normw[l, 3:4, :].broadcast_to([128, D]), [], ["r_wbc"], "ld_rwbc")
            for half in range(2):
                f0 = half * 2048
                if not (first_loaded and half == 0):
                    self.DMA("pool", w1[:], self.w_ff1[l, :, f0:f0 + 2048].rearrange("(kc p) n -> p kc n", p=128), [], ["w1"], "ld_w1")
                    self.DMA("pool", w2[:], self.w_ff2[l, f0:f0 + 2048, :].rearrange("(j p) n -> p j n", p=128), [], ["w2"], "ld_w2")
                for tb in range(NTB):
                    ts_ = slice(tb * TB, (tb + 1) * TB)
                    hb = h1[tb % 2]
                    hk = "h1_%d" % (tb % 2)
                    for j in range(16):
                        bk = self.nb()
                        for kc in range(8):
                            self.MM(self.ps[bk][:, :TB], w1[:, kc, j * 128:(j + 1) * 128], hT[:, kc, ts_], kc == 0, kc == 7, ["w1", "hT"], [self.psk(bk)])
                        r_ = j % 2
                        self.ACT(rl[r_][:], self.ps[bk][:, :TB], AF.Relu, [], [self.psk(bk), "rl%d" % r_])
                        self.TT("pool" if j % 2 else "dve", hb[:, j, :], rl[r_][:], rl[r_][:], ALU.mult, ["rl%d" % r_], [hk])
                    for ti in range(NQ):
                        t = tb * NQ + ti
                        banks = [self.nb(), self.nb()]
                        for hf in range(2):
                            for j in range(16):
                                self.MM(self.ps[banks[hf]][:, :], hb[:, j, ti * 128:(ti + 1) * 128], w2[:, j, hf * 512:(hf + 1) * 512], j == 0, j == 15,
                                        [hk, "w2"], [self.psk(banks[hf])])
                        if half == 0:
                            b = t % 2
                            for hf in range(2):
                                self.CP("dve" if hf else "act", yp[b][:, hf * 512:(hf + 1) * 512], self.ps[banks[hf]][:, :], [], [self.psk(banks[hf]), "yp%d" % b])
                            self.DMA("sp", self.ypart[t * 128:(t + 1) * 128, :], yp[b][:], ["yp%d" % b], [("ypart", t)], "st_yp%d" % b)
                        else:
                            b = t % 2
                            self.DMA("sp", yp[b][:], self.ypart[t * 128:(t + 1) * 128, :], [("ypart", t)], ["yp%d" % b], "ld_yp%d" % b)
                            self.post_norm_residual(tiles, banks, t, self.xres, dst, wbc, extra=(yp[b], "yp%d" % b))
            self.P.flush()


def prep_inputs(inp, S):
    f = lambda a: np.ascontiguousarray(np.asarray(a, dtype=np.float32))
    L = inp["w_in"].shape[0]
    shared = {
        "w_in": f(inp["w_in"]), "w_out": f(inp["w_out"]), "w_ff1": f(inp["w_ff1"]), "w_ff2": f(inp["w_ff2"]),
        "normw": f(np.stack([inp["pre_mix_w"], inp["post_mix_w"], inp["pre_mlp_w"], inp["post_mlp_w"]], axis=1)),
        "convw": f(np.transpose(np.asarray(inp["conv_dw_w"]), (0, 2, 1))),
        "convp": f(np.stack([np.asarray(inp["conv_dw_b"]).reshape(L, 2, 128), np.asarray(inp["conv_ln_w"]).reshape(L, 2, 128),
                             np.asarray(inp["conv_ln_b"]).reshape(L, 2, 128)], axis=-1).transpose(0, 2, 1, 3)),
        "lam": f(np.stack([np.stack([inp["diff_lambda_q1"], inp["diff_lambda_q2"]], axis=1),
                           np.stack([inp["diff_lambda_k1"], inp["diff_lambda_k2"]], axis=1)], axis=1)),
        "subw": f(inp["diff_subln_w"]),
        "dconvw": f(np.transpose(np.asarray(inp["delta_conv_w"]), (0, 2, 1))),
        "dgate": f(np.stack([np.asarray(inp["delta_A_log"]).reshape(L, 12), np.asarray(inp["delta_dt_bias"]).reshape(L, 12)], axis=1)),
        "dnormw": f(inp["delta_norm_w"]),
    }
    W = 2 * S - 128
    shared["dband"] = f(np.abs(np.arange(W)[None, :] - (S - 128) - np.arange(128)[:, None]))
    shared["bdmask"] = f((np.arange(128)[:, None] // 32) == (np.arange(128)[None, :] // 32))
    return shared


_NC_CACHE = {}


def kernel(**inputs):
    x = np.asarray(inputs["x"], dtype=np.float32)
    B, S, _ = x.shape
    L = inputs["w_in"].shape[0]
    shared = prep_inputs(inputs, S)
    key = (S, L)
    if key not in _NC_CACHE:
        _NC_CACHE[key] = Builder(S, L).build()
    nc = _NC_CACHE[key]
    in_maps = []
    for b in range(B):
        m = dict(shared)
        m["x"] = np.ascontiguousarray(x[b])
        in_maps.append(m)
    res = run_bass_kernel_spmd(nc, in_maps, core_ids=list(range(B)))
    return np.stack([np.asarray(r["out"], dtype=np.float32) for r in res.results], axis=0)


def phase_conv(self, l):
    with ExitStack() as ph:
        for _ in self.conv_body(ph, l):
            pass
        self.P.flush()


def conv_body(self, ph, l):
    S, NT, TB, NTB = self.S, self.NT, self.TB, self.NTB
    if True:
        hgp = self.sb(ph, "hgp", [128, 2, S + 30], BF16)
        wcol = self.sb(ph, "wcol", [128, 2, 31], F32)
        cpar = self.sb(ph, "cpar", [128, 2, 3], F32)
        diag = self.sb(ph, "diag", [128, 62, 128], BF16)
        onesb = self.sb(ph, "onesb", [128, 128], BF16)
        yb = [self.sb(ph, "yb%d" % i, [128, TB], F32) for i in range(2)]
        yh = [self.sb(ph, "yh%d" % i, [128, TB], BF16) for i in range(2)]
        zq = [self.sb(ph, "zq%d" % i, [128, TB], BF16) for i in range(2)]
        mean = self.sb(ph, "mean", [128, TB], F32)
        var = self.sb(ph, "var", [128, TB], F32)
        zt = [self.sb(ph, "zt%d" % i, [128, TB], F32) for i in range(2)]
        yout = self.sb(ph, "yout", [128, 2, S], BF16)
        self.MEMSET("pool", hgp[:, :, 0:15], 0.0, ["hgp"])
        self.MEMSET("pool", hgp[:, :, S + 15:S + 30], 0.0, ["hgp"])
        self.MEMSET("pool", onesb[:], 1.0 / 256.0, ["onesb"])
        for cc in range(2):
            self.DMA("sp", hgp[:, cc, 15:15 + S], self.hg[cc], ["hg"], ["hgp"], "ld_hgp")
        self.DMA("sp", wcol[:], self.convw[l].rearrange("(cc p) j -> p cc j", p=128), [], ["wcol"], "ld_wcol")
        self.DMA("sp", cpar[:], self.convp[l], [], ["cpar"], "ld_cpar")
        for cc in range(2):
            for j in range(31):
                if j % 3 == 1:
                    self.ACT(diag[:, cc * 31 + j, :], self.identf[:], AF.Copy, ["wcol", "identf"], ["diag"], scale=wcol[:, cc, j:j + 1])
                else:
                    self.TS("dve" if j % 3 == 0 else "pool", diag[:, cc * 31 + j, :], self.identf[:], wcol[:, cc, j:j + 1], ALU.mult, ["wcol", "identf"], ["diag"])
        yield
        for tb in range(NTB):
            ts_ = slice(tb * TB, (tb + 1) * TB)
            for cc in range(2):
                bk = self.nb()
                for j in range(31):
                    self.MM(self.ps[bk][:, :TB], diag[:, cc * 31 + j, :], hgp[:, cc, tb * TB + j:tb * TB + j + TB], j == 0, j == 30,
                            ["diag", "hgp"], [self.psk(bk)])
                self.ACT(yb[cc][:], self.ps[bk][:, :TB], AF.Identity, ["cpar"], [self.psk(bk), "yb%d" % cc], bias=cpar[:, cc, 0:1])
                self.CP("dve", yh[cc][:], yb[cc][:], ["yb%d" % cc], ["yh%d" % cc])
            bm = self.nb()
            for cc in range(2):
                self.MM(self.ps[bm][:, :TB], onesb[:], yh[cc][:], cc == 0, cc == 1, ["onesb", "yh%d" % cc], [self.psk(bm)])
            self.CP("dve", mean[:], self.ps[bm][:, :TB], [], [self.psk(bm), "mean"])
            for cc in range(2):
                self.TT("pool" if cc else "dve", zt[cc][:], yb[cc][:], mean[:], ALU.subtract, ["yb%d" % cc, "mean"], ["zt%d" % cc])
                self.ACT(zq[cc][:], zt[cc][:], AF.Square, ["zt%d" % cc], ["zq%d" % cc])
            be = self.nb()
            for cc in range(2):
                self.MM(self.ps[be][:, :TB], onesb[:], zq[cc][:], cc == 0, cc == 1, ["onesb", "zq%d" % cc], [self.psk(be)])
            self.TS("dve", var[:], self.ps[be][:, :TB], 1e-5, ALU.add, [], [self.psk(be), "var"])
            self.ACT(var[:], var[:], AF.Sqrt, ["var"], ["var"])
            self.RECIP(var[:], var[:], ["var"], ["var"])
            for cc in range(2):
                self.TT("pool" if cc else "dve", zt[cc][:], zt[cc][:], var[:], ALU.mult, ["zt%d" % cc, "var"], ["zt%d" % cc])
                self.ACT(yout[:, cc, ts_], zt[cc][:], AF.Silu, ["zt%d" % cc, "cpar"], ["yout"], scale=cpar[:, cc, 1:2], bias=cpar[:, cc, 2:3])
            yield
        for cc in range(2):
            self.DMA("sp", self.ycatT[cc], yout[:, cc, :], ["yout"], ["ycatT"], "st_yconv")


Builder.phase_conv = phase_conv
Builder.conv_body = conv_body


def phase_attn(self, l):
    S, NT = self.S, self.NT
    QB = min(512, S)
    NQB = S // QB
    NQ = QB // 128
    linit = 0.8 - 0.6 * math.exp(-0.3 * l)
    with ExitStack() as ph:
        QT = self.sb(ph, "QT", [128, 4, S], BF16)
        KT = self.sb(ph, "KT", [128, 4, S], BF16)
        V = self.sb(ph, "V", [128, NT, 6, 128], BF16)
        dband = self.sb(ph, "dband", [128, 2 * S - 128], F32)
        lamv = self.sb(ph, "lamv", [128, 2, 2, 32], F32)
        lprod = self.sb(ph, "lprod", [128, 2, 32], F32)
        lsm = self.sb(ph, "lsm", [128, 4], F32)
        wsub = self.sb(ph, "wsub", [128, 64], F32)
        ydiff = self.sb(ph, "ydiff", [128, NT, 384], BF16)
        yT = self.sb(ph, "yT", [128, 3, S], BF16)
        sc = [self.sb(ph, "sc%d" % i, [128, QB], F32) for i in range(3)]
        ET = [self.sb(ph, "ET%d" % i, [128, QB], BF16) for i in range(4)]
        rs = self.sb(ph, "rs", [128, NQ, 1], F32)
        oTb = self.sb(ph, "oTb", [65, QB], BF16)
        o1 = self.sb(ph, "o1", [128, NQ, 64], F32)
        o2 = self.sb(ph, "o2", [128, NQ, 64], F32)
        osq = self.sb(ph, "osq", [128, NQ, 64], F32)
        oss = self.sb(ph, "oss", [128, NQ], F32)
        for c in range(4):
            self.DMA("sp", QT[0:96, c, :], self.qT[c], ["qT"], ["QT"], "ld_QT")
            self.DMA("sp", KT[0:96, c, :], self.kT[c], ["kT"], ["KT"], "ld_KT")
        QTm = None
        if os.environ.get("K_QM", "0") == "1":
            QTm = self.sb(ph, "QTm", [128, 12, S], BF16)
            self.MEMSET("pool", QTm[0:96, :, :], 0.0, ["QTm"])
            for mi_ in range(12):
                c_, r_ = mi_ // 3, 32 * (mi_ % 3)
                eng_ = ("act", "dve", "pool")[mi_ % 3]
                self.CP(eng_, QTm[r_:r_ + 32, mi_, :], QT[r_:r_ + 32, c_, :], ["QT", "QTm"], ["QTm"])
        self.MEMSET("pool", V[:, :, :, 65:128], 0.0, ["V"])
        self.MEMSET("pool", V[:, :, :, 64:65], 1.0, ["V"])
        for t in range(NT):
            self.DMA("sp", V[:, t, :, 0:64], self.vtok[t * 128:(t + 1) * 128, :].rearrange("p (h e) -> p h e", h=6), ["vtok"], ["V"], "ld_V")
        self.DMA("sp", dband[:], self.dband, [], ["dband"], "ld_dband")
        self.DMA("sp", lamv[:].rearrange("p a b c -> p (a b c)"), self.lam[l:l + 1].rearrange("o a b c -> o (a b c)").broadcast_to([128, 128]), [], ["lamv"], "ld_lam")
        self.DMA("sp", wsub[:], self.subw[l:l + 1, :].broadcast_to([128, 64]), [], ["wsub"], "ld_subw")
        self.TT("dve", lprod[:], lamv[:, 0, :, :], lamv[:, 1, :, :], ALU.mult, ["lamv"], ["lprod"])
        self.REDUCE(lsm[:, 0:2], lprod[:], ["lprod"], ["lsm"])
        self.ACT(lsm[:, 0:2], lsm[:, 0:2], AF.Exp, ["lsm"], ["lsm"])
        self.TT("dve", lsm[:, 2:3], lsm[:, 1:2], lsm[:, 0:1], ALU.subtract, ["lsm"], ["lsm"])
        self.TS("dve", lsm[:, 3:4], lsm[:, 2:3], -linit, ALU.add, ["lsm"], ["lsm"])
        self.TS("dve", wsub[:], wsub[:], 1.0 - linit, ALU.mult, ["wsub"], ["wsub"])
        SB = [0, 1, 2]
        AB = [3, 4]
        LA = 2
        blocks = []
        ia = 0
        for h in range(6):
            for qb in range(NQB):
                for j in range(2):
                    ab = AB[ia % 2]
                    ia += 1
                    kts = []
                    for kt in range(NT):
                        dmin = max(0, kt * 128 - (qb * QB + QB - 1), qb * QB - (kt * 128 + 127))
                        if ALIBI_SKIP is None or dmin * SLOPES[h] <= ALIBI_SKIP:
                            kts.append(kt)
                    for kt in kts:
                        blocks.append((h, qb, j, kt, ab, kt == kts[0], kt == kts[-1]))
        nblk = len(blocks)

        def front(it):
            h, qb, j, kt, ab, kfirst, klast = blocks[it]
            m = SLOPES[h]
            mi = 2 * h + j
            c = mi // 3
            r0 = 32 * (mi % 3)
            sbk = SB[it % 3]
            si = it % 3
            ei = it % 4
            if QTm is not None:
                self.MM(self.ps[sbk][:, :QB], KT[0:96, c, kt * 128:(kt + 1) * 128], QTm[0:96, mi, qb * QB:(qb + 1) * QB], True, True,
                        ["KT", "QTm"], [self.psk(sbk)])
            else:
                self.MM(self.ps[sbk][:, :QB], KT[r0:r0 + 32, c, kt * 128:(kt + 1) * 128], QT[r0:r0 + 32, c, qb * QB:(qb + 1) * QB], True, True,
                        ["KT", "QT"], [self.psk(sbk)])
            for _ in range(int(os.environ.get("K_FILL", "0"))):
                self.MM(self.ps[7][:, :QB], KT[:, 0, 0:128], QT[:, 0, 0:QB], True, True, ["KT", "QT"], ["ps7"])
            off = qb * QB - kt * 128 + S - 128
            self.STT(sc[si][:], dband[:, off:off + QB], -m, self.ps[sbk][:, :QB], ALU.mult, ALU.add, ["dband"], [self.psk(sbk), "sc%d" % si])
            self.ACT(ET[ei][:], sc[si][:], AF.Exp, ["sc%d" % si], ["ET%d" % ei])

        pending = []

        def epilogue(h, qb, j, ab):
            accv = self.ps[ab][:, 0:NQ * 65].rearrange("p (a b) -> p a b", b=65)
            self.RECIP(rs[:], accv[:, :, 64:65], [], [self.psk(ab), "rs"])
            if j == 0:
                self.TT("dve", o1[:], accv[:, :, 0:64], rs[:].broadcast_to([128, NQ, 64]), ALU.mult, ["rs"], [self.psk(ab), "o1"])
                return
            self.TT("dve", o2[:], accv[:, :, 0:64], rs[:].broadcast_to([128, NQ, 64]), ALU.mult, ["rs"], [self.psk(ab), "o2"])
            self.STT(o2[:], o2[:], lsm[:, 3:4], o1[:], ALU.mult, ALU.add, ["o2", "o1", "lsm"], ["o2"])
            self.TT("pool", osq[:], o2[:], o2[:], ALU.mult, ["o2"], ["osq"])
            yield
            yield
            self.REDUCE(oss[:], osq[:], ["osq"], ["oss"])
            self.TS("dve", oss[:], oss[:], 1.0 / 64.0, ALU.mult, ["oss"], ["oss"], s2=1e-5, op1=ALU.add)
            self.ACT(oss[:], oss[:], AF.Sqrt, ["oss"], ["oss"])
            yield
            yield
            self.RECIP(oss[:], oss[:], ["oss"], ["oss"])
            self.TT("dve", o2[:], o2[:], oss[:].unsqueeze(2).broadcast_to([128, NQ, 64]), ALU.mult, ["o2", "oss"], ["o2"])
            self.TT("pool", ydiff[:, qb * NQ:(qb + 1) * NQ, h * 64:(h + 1) * 64], o2[:], wsub[:].unsqueeze(1).broadcast_to([128, NQ, 64]), ALU.mult,
                    ["o2", "wsub"], ["ydiff"])

        def back(it):
            h, qb, j, kt, ab, kfirst, klast = blocks[it]
            ei = it % 4
            accv = self.ps[ab][:, 0:NQ * 65].rearrange("p (a b) -> p a b", b=65)
            for qi in range(NQ):
                self.MM(accv[:, qi, :], ET[ei][:, qi * 128:(qi + 1) * 128], V[:, kt, h, 0:65], (kfirst and qi == 0), klast,
                        ["ET%d" % ei, "V"], [self.psk(ab)], skip_group_check=True)
            if klast:
                while pending:
                    pump()
                pending.append(epilogue(h, qb, j, ab))

        def pump():
            for g_ in list(pending):
                try:
                    next(g_)
                except StopIteration:
                    pending.remove(g_)

        for it in range(nblk + LA):
            if it < nblk:
                front(it)
            if it >= LA:
                back(it - LA)
            pump()
        while pending:
            pump()
        for t in range(NT):
            bk = 5 + (t % 2)
            pv = self.psbf(bk, 8)
            for c3 in range(3):
                self.TR(pv[:, c3, :], ydiff[:, t, c3 * 128:(c3 + 1) * 128], self.identb[:], ["ydiff", "identb"], [self.psk(bk)])
            self.CP("act", yT[:, :, t * 128:(t + 1) * 128], pv[:, 0:3, :], [], [self.psk(bk), "yT"])
        for c3 in range(3):
            self.DMA("sp", self.ycatT[2 + c3], yT[:, c3, :], ["yT"], ["ycatT"], "st_ydiff")
        self.P.flush()


Builder.phase_attn = phase_attn


def phase_delta(self, l):
    S, NT = self.S, self.NT
    H = 6
    with ExitStack() as ph:
        tm = self.sb(ph, "tm", [128, NT, 1152], BF16)
        with ExitStack() as p1:
            wc = self.sb(p1, "wc", [128, 9, 3], F32)
            xp = [self.sb(p1, "xp%d" % i, [128, S + 2], F32) for i in range(2)]
            acc = [self.sb(p1, "acc%d" % i, [128, S], F32) for i in range(2)]
            sT = [self.sb(p1, "sT%d" % i, [128, S], BF16) for i in range(2)]
            self.DMA("sp", wc[:], self.dconvw[l].rearrange("(c p) j -> p c j", p=128), [], ["wc"], "ld_wc")
            for i in range(2):
                self.MEMSET("pool", xp[i][:, 0:1], 0.0, ["xp%d" % i])
                self.MEMSET("pool", xp[i][:, S + 1:S + 2], 0.0, ["xp%d" % i])
            cgen = self.conv_body(p1, l) if (self.want("C") and self.merge_conv) else iter(())
            for c in range(9):
                b = c % 2
                next(cgen, None)
                self.DMA("sp", xp[b][:, 1:S + 1], self.gqkvT[c], ["gqkvT"], ["xp%d" % b], "ld_xp%d" % b)
                self.TS("dve", acc[b][:], xp[b][:, 0:S], wc[:, c, 0:1], ALU.mult, ["xp%d" % b, "wc"], ["acc%d" % b])
                self.STT(acc[b][:], xp[b][:, 1:S + 1], wc[:, c, 1:2], acc[b][:], ALU.mult, ALU.add, ["xp%d" % b, "wc", "acc%d" % b], ["acc%d" % b])
                self.STT(acc[b][:], xp[b][:, 2:S + 2], wc[:, c, 2:3], acc[b][:], ALU.mult, ALU.add, ["xp%d" % b, "wc", "acc%d" % b], ["acc%d" % b])
                self.ACT(sT[b][:], acc[b][:], AF.Silu, ["acc%d" % b], ["sT%d" % b])
                for t0 in range(0, NT, 8):
                    n = min(8, NT - t0)
                    bk = 6 + ((c * 2 + t0 // 8) % 2)
                    pv = self.psbf(bk, 8)
                    for i in range(n):
                        t = t0 + i
                        self.TR(pv[:, i, :], sT[b][:, t * 128:(t + 1) * 128], self.identb[:], ["sT%d" % b, "identb"], [self.psk(bk)])
                    self.CP("act" if (t0 // 8) % 2 else "dve", tm[:, t0:t0 + n, c * 128:(c + 1) * 128], pv[:, 0:n, :], [], [self.psk(bk), "tm"])
            for _ in cgen:
                pass
            self.P.flush()
        onesb = self.sb(ph, "onesb1", [128, 128], BF16)
        cf = self.sb(ph, "cf", [128, 128], F32)
        Lmat = [self.sb(ph, "Lmat%d" % d, [128, 128], BF16) for d in range(2)]
        maskneg = [self.sb(ph, "maskneg%d" % d, [128, 128], F32) for d in range(2)]
        nstr = [self.sb(ph, "nstr%d" % d, [128, 128], F32) for d in range(2)]
        self.MEMSET("pool", onesb[:], 1.0, ["onesb1"])
        for d in range(2):
            pat, cm = ([[1, 128]], -1) if d == 0 else ([[-1, 128]], 1)
            self.MEMSET("pool", cf[:], 1.0, ["cf"])
            self.ASEL(cf[:], cf[:], pat, ALU.is_ge, 0.0, 0, cm, ["cf"], ["cf"])
            self.CP("pool", Lmat[d][:], cf[:], ["cf"], ["Lmat%d" % d])
            self.MEMSET("pool", maskneg[d][:], 0.0, ["maskneg%d" % d])
            self.ASEL(maskneg[d][:], maskneg[d][:], pat, ALU.is_ge, -1e30, 0, cm, ["maskneg%d" % d], ["maskneg%d" % d])
            self.MEMSET("pool", nstr[d][:], -1.0, ["nstr%d" % d])
            self.ASEL(nstr[d][:], nstr[d][:], pat, ALU.is_gt, 0.0, 0, cm, ["nstr%d" % d], ["nstr%d" % d])
        groups = [[(0, 0, 4, 0)], [(0, 4, 6, 0), (1, 0, 2, 2)], [(1, 2, 6, 0)]]
        bd = self.sb(ph, "bd", [128, 128], F32)
        self.DMA("sp", bd[:], self.bdmask, [], ["bd"], "ld_bd")
        nstrd = [self.sb(ph, "nstrd%d" % d, [128, 128], F32) for d in range(2)]
        nstro = [self.sb(ph, "nstro%d" % d, [128, 128], F32) for d in range(2)]
        for d in range(2):
            self.TT("pool", nstrd[d][:], nstr[d][:], bd[:], ALU.mult, ["nstr%d" % d, "bd"], ["nstrd%d" % d])
            self.TT("pool", nstro[d][:], nstr[d][:], nstrd[d][:], ALU.subtract, ["nstr%d" % d, "nstrd%d" % d], ["nstro%d" % d])
        nstrg_d, nstrg_o = [], []
        for gi, g in enumerate(groups):
            tld = self.sb(ph, "nstrgd%d" % gi, [128, 4, 128], BF16)
            tlo = self.sb(ph, "nstrgo%d" % gi, [128, 4, 128], BF16)
            for (d, h0, h1, s0) in g:
                n = h1 - h0
                self.CP("pool", tld[:, s0:s0 + n, :], nstrd[d][:].unsqueeze(1).broadcast_to([128, n, 128]), ["nstrd%d" % d], ["nstrgd%d" % gi])
                self.CP("pool", tlo[:, s0:s0 + n, :], nstro[d][:].unsqueeze(1).broadcast_to([128, n, 128]), ["nstro%d" % d], ["nstrgo%d" % gi])
            nstrg_d.append(tld)
            nstrg_o.append(tlo)
        qnT = self.sb(ph, "qnT", [128, 3, S], BF16)
        zst = self.sb(ph, "zst", [128, NT, 384], BF16)
        osum = self.sb(ph, "osum", [128, NT, 384], F32)
        sqs = self.sb(ph, "sqs", [128, 768], F32)
        ssq = self.sb(ph, "ssq", [128, NT, 12], F32)
        gin = self.sb(ph, "gin", [128, NT, 24], F32)
        dg = self.sb(ph, "dg", [128, 2, 12], F32)
        nw = self.sb(ph, "nw", [128, 64], F32)
        G = {}
        for nm in ("sbt", "gl", "gc", "egc", "gt", "egt", "edec", "r1"):
            G[nm] = self.sb(ph, "g_" + nm, [128, 2, NT, 6], F32)
        gls = [self.sb(ph, "gls%d" % i, [128, 2, NT, 6], BF16) for i in range(3)]
        gcs = [self.sb(ph, "gcs%d" % i, [128, 2, NT, 6], BF16) for i in range(3)]
        egtS = self.sb(ph, "egtS", [128, 2, NT, 3], F32)
        self.DMA("sp", zst[:], self.zs.rearrange("(t p) n -> p t n", p=128), ["zs"], ["zst"], "ld_zs")
        self.DMA("sp", gin[:], self.gates.rearrange("(t p) n -> p t n", p=128), ["gates"], ["gin"], "ld_gin")
        self.DMA("sp", dg[:].rearrange("p a b -> p (a b)"), self.dgate[l:l + 1].rearrange("o a b -> o (a b)").broadcast_to([128, 24]), [], ["dg"], "ld_dg")
        self.DMA("sp", nw[:], self.dnormw[l:l + 1, :].broadcast_to([128, 64]), [], ["nw"], "ld_nw")
        for t in range(NT):
            self.ACT(sqs[:], tm[:, t, 0:768], AF.Square, ["tm"], ["sqs"])
            self.REDUCE(ssq[:, t, :], sqs[:].rearrange("p (a b) -> p a b", b=64), ["sqs"], ["ssq"])
        self.TS("dve", ssq[:], ssq[:], 1e-6, ALU.add, ["ssq"], ["ssq"])
        self.ACT(ssq[:], ssq[:], AF.Sqrt, ["ssq"], ["ssq"])
        self.RECIP(ssq[:], ssq[:], ["ssq"], ["ssq"])
        self.TS("dve", ssq[:, :, 0:6], ssq[:, :, 0:6], 0.125, ALU.mult, ["ssq"], ["ssq"])
        for t in range(NT):
            self.TT("dve", tm[:, t, 0:384].rearrange("p (h e) -> p h e", h=6), tm[:, t, 0:384].rearrange("p (h e) -> p h e", h=6),
                    ssq[:, t, 0:6].unsqueeze(2).broadcast_to([128, 6, 64]), ALU.mult, ["tm", "ssq"], ["tm"])
            self.TT("pool", tm[:, t, 384:768].rearrange("p (h e) -> p h e", h=6), tm[:, t, 384:768].rearrange("p (h e) -> p h e", h=6),
                    ssq[:, t, 6:12].unsqueeze(2).broadcast_to([128, 6, 64]), ALU.mult, ["tm", "ssq"], ["tm"])
            bk = 6 + (t % 2)
            pv = self.psbf(bk, 8)
            for c3 in range(3):
                self.TR(pv[:, c3, :], tm[:, t, c3 * 128:(c3 + 1) * 128], self.identb[:], ["tm", "identb"], [self.psk(bk)])
            self.CP("act", qnT[:, :, t * 128:(t + 1) * 128], pv[:, 0:3, :], [], [self.psk(bk), "qnT"])
        self.ACT(dg[:, 0, :], dg[:, 0, :], AF.Exp, ["dg"], ["dg"])
        self.TS("dve", dg[:, 0, :], dg[:, 0, :], -1.0, ALU.mult, ["dg"], ["dg"])
        for d in range(2):
            bsl = gin[:, :, d * 6:(d + 1) * 6]
            asl = gin[:, :, 12 + d * 6:12 + (d + 1) * 6]
            self.ACT(G["sbt"][:, d, :, :], bsl, AF.Sigmoid, ["gin"], ["sbt"])
            self.ACT(G["sbt"][:, d, :, :], G["sbt"][:, d, :, :], AF.Sqrt, ["sbt"], ["sbt"])
            self.TT("dve", G["gl"][:, d, :, :], asl, dg[:, 1, d * 6:(d + 1) * 6].unsqueeze(1).broadcast_to([128, NT, 6]), ALU.add, ["gin", "dg"], ["gl"])
            self.ACT(G["gl"][:, d, :, :], G["gl"][:, d, :, :], AF.Exp, ["gl"], ["gl"])
            self.TS("dve", G["gl"][:, d, :, :], G["gl"][:, d, :, :], 1.0, ALU.add, ["gl"], ["gl"])
            self.ACT(G["gl"][:, d, :, :], G["gl"][:, d, :, :], AF.Ln, ["gl"], ["gl"])
            self.TT("dve", G["gl"][:, d, :, :], G["gl"][:, d, :, :], dg[:, 0, d * 6:(d + 1) * 6].unsqueeze(1).broadcast_to([128, NT, 6]), ALU.mult, ["gl", "dg"], ["gl"])

        def split3(src, dst, key_src, key_dst):
            r1 = G["r1"]
            self.CP("dve", dst[0][:], src[:], [key_src], [key_dst])
            self.TT("dve", r1[:], src[:], dst[0][:], ALU.subtract, [key_src, key_dst], ["r1"])
            self.CP("dve", dst[1][:], r1[:], ["r1"], [key_dst])
            self.TT("dve", r1[:], r1[:], dst[1][:], ALU.subtract, ["r1", key_dst], ["r1"])
            self.CP("dve", dst[2][:], r1[:], ["r1"], [key_dst])

        split3(G["gl"], gls, "gl", "gls")
        for d in range(2):
            bk = self.nb()
            for i in range(3):
                self.MM(self.ps[bk][:, 0:NT * 6], Lmat[d][:], gls[i][:, d, :, :].rearrange("p t h -> p (t h)"), i == 0, i == 2, ["Lmat%d" % d, "gls"], [self.psk(bk)])
            self.CP("dve", G["gc"][:, d, :, :].rearrange("p t h -> p (t h)"), self.ps[bk][:, 0:NT * 6], [], [self.psk(bk), "gc"])
            bk = self.nb()
            for i in range(3):
                self.MM(self.ps[bk][:, 0:NT * 6], onesb[:], gls[i][:, d, :, :].rearrange("p t h -> p (t h)"), i == 0, i == 2, ["onesb1", "gls"], [self.psk(bk)])
            self.CP("dve", G["gt"][:, d, :, :].rearrange("p t h -> p (t h)"), self.ps[bk][:, 0:NT * 6], [], [self.psk(bk), "gt"])
        self.ACT(G["egc"][:], G["gc"][:], AF.Exp, ["gc"], ["egc"])
        self.ACT(G["egt"][:], G["gt"][:], AF.Exp, ["gt"], ["egt"])
        self.TT("dve", G["edec"][:], G["gt"][:], G["gc"][:], ALU.subtract, ["gt", "gc"], ["edec"])
        self.ACT(G["edec"][:], G["edec"][:], AF.Exp, ["edec"], ["edec"])
        split3(G["gc"], gcs, "gc", "gcs")
        ev = G["egt"][:].rearrange("p d t (a two) -> p d t a two", two=2)
        self.CP("pool", egtS[0:64], ev[0:64, :, :, :, 0], ["egt"], ["egtS"])
        self.CP("pool", egtS[64:128], ev[64:128, :, :, :, 1], ["egt"], ["egtS"])
        p3 = ExitStack()

        def mk(name, shape, dt):
            return (self.sb(p3, name, shape, dt), name)

        PS = []
        for pb in range(2):
            d_ = {}
            for d in range(2):
                d_["r0", d] = mk("r0_%d%d" % (pb, d), [128, 6, 128], BF16)
                d_["kp", d] = mk("kp_%d%d" % (pb, d), [128, 6, 64], BF16)
                d_["kdec", d] = mk("kdec_%d%d" % (pb, d), [128, 6, 64], BF16)
                for hf in range(2):
                    d_["kpT", d, hf] = mk("kpT_%d%d%d" % (pb, d, hf), [128, 3, 128], BF16)
                    self.MEMSET("pool", d_["kpT", d, hf][0][:], 0.0, [d_["kpT", d, hf][1]])
                d_["AqkT", d] = mk("AqkT_%d%d" % (pb, d), [128, 6, 128], BF16)
                d_["ru", d] = mk("ru_%d%d" % (pb, d), [128, 6, 64], F32)
                d_["rwb", d] = mk("rwb_%d%d" % (pb, d), [128, 6, 64], BF16)
            PS.append(d_)
        RT = []
        for d in range(2):
            d_ = {}
            d_["wT"] = mk("wT%d" % d, [128, 3, 128], BF16)
            d_["Sst"] = mk("Sst%d" % d, [128, 3, 64], F32)
            d_["Sbf"] = [mk("Sbf%d_%d" % (d, hf), [128, 3, 64], BF16) for hf in range(2)]
            d_["tmpS"] = mk("tmpS%d" % d, [128, 3, 64], F32)
            d_["vpp"] = mk("vpp%d" % d, [128, 6, 64], BF16)
            d_["o1"] = mk("do1_%d" % d, [128, 6, 64], F32)
            self.MEMSET("pool", d_["Sst"][0][:], 0.0, [d_["Sst"][1]])
            for hf in range(2):
                self.MEMSET("pool", d_["Sbf"][hf][0][:], 0.0, [d_["Sbf"][hf][1]])
            RT.append(d_)
        SL = []
        for sl in range(3):
            d_ = {}
            d_["dgi"] = [mk("dgi%d_%d" % (sl, i), [128, 4, 128], BF16) for i in range(3)]
            d_["d0"] = mk("d0_%d" % sl, [128, 4, 128], F32)
            d_["tmpk"] = mk("tmpk%d" % sl, [128, 4, 128], F32)
            d_["PT"] = [mk("PT%d_%d" % (sl, i), [128, 4, 128], BF16) for i in range(5)]
            d_["Pm"] = [mk("Pm%d_%d" % (sl, i), [128, 4, 128], BF16) for i in range(2)]
            d_["Pd0"] = mk("Pd0_%d" % sl, [128, 4, 128], BF16)
            d_["PoT"] = mk("PoT%d" % sl, [128, 4, 128], BF16)
            d_["XTb"] = mk("XTb%d" % sl, [128, 4, 128], BF16)
            d_["banks"] = (2 * sl, 2 * sl + 1)
            SL.append(d_)
        nidentb = self.sb(p3, "nidentb", [128, 128], BF16)
        self.TS("pool", nidentb[:], self.identf[:], -1.0, ALU.mult, ["identf"], ["nidentb"])
        touched = set()
        B_T, B_REC = 6, 7

        def v4(bk):
            return self.ps[bk][:].rearrange("p (a b) -> p a b", b=128)

        def w6(bk):
            return self.ps[bk][:, 0:384].rearrange("p (h e) -> p h e", h=6)

        def tiles_of(step):
            return [step, NT - 1 - step]

        def prep(step):
            tt = tiles_of(step)
            P_ = PS[step % 2]
            for d in range(2):
                t = tt[d]
                kp, kpk = P_["kp", d]
                r0, r0k = P_["r0", d]
                kdec, kdk = P_["kdec", d]
                sb6 = G["sbt"][:, d, t, :].unsqueeze(2).broadcast_to([128, 6, 64])
                kn6 = tm[:, t, 384:768].rearrange("p (h e) -> p h e", h=6)
                v6 = tm[:, t, 768:1152].rearrange("p (h e) -> p h e", h=6)
                self.TT("pool", kp[:], kn6, sb6, ALU.mult, ["tm", "sbt"], [kpk])
                self.TT("pool", r0[:, :, 0:64], v6, sb6, ALU.mult, ["tm", "sbt"], [r0k])
                self.TT("pool", r0[:, :, 64:128], kp[:], G["egc"][:, d, t, :].unsqueeze(2).broadcast_to([128, 6, 64]), ALU.mult, [kpk, "egc"], [r0k])
                self.TT("pool", kdec[:], kp[:], G["edec"][:, d, t, :].unsqueeze(2).broadcast_to([128, 6, 64]), ALU.mult, [kpk, "edec"], [kdk])
                pv = self.psbf(B_T, 8)
                kpf = kp[:].rearrange("p h e -> p (h e)")
                for c3 in range(3):
                    self.TR(pv[:, c3, :], kpf[:, c3 * 128:(c3 + 1) * 128], self.identb[:], [kpk, "identb"], [self.psk(B_T)])
                self.CP("act", P_["kpT", d, 0][0][0:64], pv[0:64, 0:3, :], [], [self.psk(B_T), P_["kpT", d, 0][1]])
                self.CP("dve", P_["kpT", d, 1][0][64:128], pv[64:128, 0:3, :], [], [self.psk(B_T), P_["kpT", d, 1][1]])

        def ut_group(sl, step, gi):
            g = groups[gi]
            tt = tiles_of(step)
            P_ = PS[step % 2]
            T_ = SL[sl]
            BA, BB = T_["banks"]
            kA, kB = self.psk(BA), self.psk(BB)
            dgi = T_["dgi"]
            d0, d0k = T_["d0"]
            tmpk, tmpkk = T_["tmpk"]
            PT = T_["PT"]
            Pm = T_["Pm"]
            Pd0, Pd0k = T_["Pd0"]
            PoT, PoTk = T_["PoT"]
            XTb, XTbk = T_["XTb"]
            Xb, Xbk = Pm[0]
            Qb, Qbk = Pm[1]
            XT2b, XT2bk = PT[1]
            xjb, xjbk = PT[2]
            yjb, yjbk = PT[3]
            combos = []
            for (d, h0, h1, s0) in g:
                for h in range(h0, h1):
                    combos.append((d, h, s0 + h - h0))
            for i in range(3):
                for (d, h0, h1, s0) in g:
                    n = h1 - h0
                    self.TT("pool", dgi[i][0][:, s0:s0 + n, :], self.identb[:].unsqueeze(1).broadcast_to([128, n, 128]),
                            gcs[i][:, d, tt[d], h0:h1].unsqueeze(2).broadcast_to([128, n, 128]), ALU.mult, ["identb", "gcs"], [dgi[i][1]])
            first = True
            for (d, h, s) in combos:
                for i in range(3):
                    self.MM(v4(BA)[:, s, :], onesb[:], dgi[i][0][:, s, :], first, False, ["onesb1", dgi[i][1]], [kA], skip_group_check=True)
                    first = False
            for (d, h, s) in combos:
                p = h // 2
                kT_, kTk = P_["kpT", d, h % 2]
                self.MM(v4(BB)[:, s, :], kT_[:, p, :], kT_[:, p, :], True, True, [kTk], [kB])
            yield
            for (d, h, s) in combos:
                self.STT(d0[:, s, :], v4(BA)[:, s, :], G["gc"][:, d, tt[d], h:h + 1], maskneg[d][:], ALU.subtract, ALU.add,
                         ["gc", "maskneg%d" % d], [kA, d0k])
            self.ACT(d0[:], d0[:], AF.Exp, [d0k], [d0k])
            yield
            self.TT("dve", tmpk[:], v4(BB)[:], d0[:], ALU.mult, [d0k], [kB, tmpkk])
            for (d, h, s) in combos:
                p = h // 2
                t = tt[d]
                kT_, kTk = P_["kpT", d, h % 2]
                self.MM(v4(BB)[:, s, :], kT_[:, p, :], qnT[:, p, t * 128:(t + 1) * 128], True, True, [kTk, "qnT"], [kB])
            self.TT("pool", PT[0][0][:], tmpk[:], nstrg_d[gi][:], ALU.mult, [tmpkk, "nstrgd%d" % gi], [PT[0][1]])
            self.TT("pool", PoT[:], tmpk[:], nstrg_o[gi][:], ALU.mult, [tmpkk, "nstrgo%d" % gi], [PoTk])
            yield
            for (d, h0, h1, s0) in g:
                n = h1 - h0
                Aq, Aqk_ = P_["AqkT", d]
                self.TT("dve", Aq[:, h0:h1, :], v4(BB)[:, s0:s0 + n, :], d0[:, s0:s0 + n, :], ALU.mult, [d0k], [kB, Aqk_])
            yield
            pv = self.psbf(B_T, 8)
            for (d, h, s) in combos:
                self.TR(pv[:, s, :], PT[0][0][:, s, :], self.identb[:], [PT[0][1], "identb"], [self.psk(B_T)])
            self.CP("act", Pd0[:], pv[:, 0:4, :], [], [self.psk(B_T), Pd0k])
            yield
            first = True
            for (d, h, s) in combos:
                self.MM(v4(BA)[:, s, :], self.identb[:], self.identb[:], first, False, ["identb"], [kA], skip_group_check=True)
                first = False
                self.MM(v4(BA)[:, s, :], Pd0[:, s, :], self.identb[:], False, False, [Pd0k, "identb"], [kA], skip_group_check=True)
            for k in range(5):
                Pk, Pkk = (Pd0, Pd0k) if k == 0 else Pm[k % 2]
                if k > 0:
                    self.CP("act", XTb[:], v4(BA)[:], [], [kA, XTbk])
                    for (d, h, s) in combos:
                        self.MM(v4(BA)[:, s, :], Pk[:, s, :], XTb[:, s, :], False, k == 4, [Pkk, XTbk], [kA], skip_group_check=True)
                if k < 4:
                    nx = (k + 1) % 2
                    for (d, h, s) in combos:
                        self.MM(v4(BB)[:, s, :], PT[k][0][:, s, :], Pk[:, s, :], True, True, [PT[k][1], Pkk], [kB])
                    yield
                    self.CP("dve", Pm[nx][0][:], v4(BB)[:], [], [kB, Pm[nx][1]])
                    for (d, h, s) in combos:
                        self.MM(v4(BB)[:, s, :], Pk[:, s, :], PT[k][0][:, s, :], True, True, [PT[k][1], Pkk], [kB])
                    yield
                    self.CP("act", PT[k + 1][0][:], v4(BB)[:], [], [kB, PT[k + 1][1]])
                yield
            self.CP("act", XTb[:], v4(BA)[:], [], [kA, XTbk])
            yield
            pv = self.psbf(B_T, 8)
            for (d, h, s) in combos:
                self.TR(pv[:, s, :], XTb[:, s, :], self.identb[:], [XTbk, "identb"], [self.psk(B_T)])
            self.CP("dve", Xb[:], pv[:, 0:4, :], [], [self.psk(B_T), Xbk])
            first = True
            for (d, h, s) in combos:
                self.MM(v4(BB)[:, s, :], self.identb[:], self.identb[:], first, False, ["identb"], [kB], skip_group_check=True)
                first = False
                self.MM(v4(BB)[:, s, :], nidentb[:], XTb[:, s, :], False, False, ["nidentb", XTbk], [kB], skip_group_check=True)
                self.MM(v4(BB)[:, s, :], Pd0[:, s, :], XTb[:, s, :], False, True, [Pd0k, XTbk], [kB], skip_group_check=True)
            yield
            self.CP("act", Qb[:], v4(BB)[:], [], [kB, Qbk])
            yield
            first = True
            for (d, h, s) in combos:
                self.MM(v4(BB)[:, s, :], self.identb[:], XTb[:, s, :], first, False, ["identb", XTbk], [kB], skip_group_check=True)
                first = False
                self.MM(v4(BB)[:, s, :], Xb[:, s, :], Qb[:, s, :], False, True, [Xbk, Qbk], [kB], skip_group_check=True)
            yield
            self.CP("dve", XT2b[:], v4(BB)[:], [], [kB, XT2bk])
            yield
            for it in range(4):
                if it > 0:
                    first = True
                    for (d, h, s) in combos:
                        r0, r0k = P_["r0", d]
                        self.MM(v4(BB)[:, s, :], self.identb[:], r0[:, h, :], first, False, ["identb", r0k], [kB], skip_group_check=True)
                        first = False
                        self.MM(v4(BB)[:, s, :], PoT[:, s, :], xjb[:, s, :], False, True, [PoTk, xjbk], [kB], skip_group_check=True)
                    yield
                    self.CP("act", yjb[:], v4(BB)[:], [], [kB, yjbk])
                    yield
                for (d, h, s) in combos:
                    r0, r0k = P_["r0", d]
                    rhs = r0[:, h, :] if it == 0 else yjb[:, s, :]
                    rkeys = [r0k] if it == 0 else [yjbk]
                    self.MM(v4(BA)[:, s, :], XT2b[:, s, :], rhs, True, True, [XT2bk] + rkeys, [kA])
                yield
                if it < 3:
                    self.CP("dve", xjb[:], v4(BA)[:], [], [kA, xjbk])
                    yield
            for (d, h0, h1, s0) in g:
                n = h1 - h0
                ru, ruk = P_["ru", d]
                rwb, rwbk = P_["rwb", d]
                self.CP("dve", ru[:, h0:h1, :], v4(BA)[:, s0:s0 + n, 0:64], [], [kA, ruk])
                self.CP("act", rwb[:, h0:h1, :], v4(BA)[:, s0:s0 + n, 64:128], [], [kA, rwbk])

        def rec_dir(step, d):
            tt = tiles_of(step)
            t = tt[d]
            P_ = PS[step % 2]
            R_ = RT[d]
            wT, wTk = R_["wT"]
            Sst, Sstk = R_["Sst"]
            Sbf = R_["Sbf"]
            tmpS, tmpSk = R_["tmpS"]
            vpp, vppk = R_["vpp"]
            o1, o1k = R_["o1"]
            ru, ruk = P_["ru", d]
            rwb, rwbk = P_["rwb", d]
            Aq, Aqk_ = P_["AqkT", d]
            kdec, kdk = P_["kdec", d]
            kR = self.psk(B_REC)
            pv = self.psbf(B_T, 8)
            rwf = rwb[:].rearrange("p h e -> p (h e)")
            for c3 in range(3):
                self.TR(pv[:, c3, :], rwf[:, c3 * 128:(c3 + 1) * 128], self.identb[:], [rwbk, "identb"], [self.psk(B_T)])
            self.CP("act", wT[:], pv[:, 0:3, :], [], [self.psk(B_T), wTk])
            yield
            for h in range(6):
                p = h // 2
                self.MM(w6(B_REC)[:, h, :], wT[:, p, :], Sbf[h % 2][0][:, p, :], True, True, [wTk, Sbf[h % 2][1]], [kR])
            yield
            self.TT("dve", vpp[:], ru[:], w6(B_REC), ALU.subtract, [ruk], [kR, vppk])
            yield
            for h in range(6):
                p = h // 2
                self.MM(w6(B_REC)[:, h, :], qnT[:, p, t * 128:(t + 1) * 128], Sbf[h % 2][0][:, p, :], True, True, ["qnT", Sbf[h % 2][1]], [kR])
            yield
            self.TT("dve", o1[:], w6(B_REC), G["egc"][:, d, t, :].unsqueeze(2).broadcast_to([128, 6, 64]), ALU.mult, ["egc"], [kR, o1k])
            yield
            for h in range(6):
                self.MM(w6(B_REC)[:, h, :], Aq[:, h, :], vpp[:, h, :], True, True, [Aqk_, vppk], [kR])
            yield
            ot = osum[:, t, :].rearrange("p (h e) -> p h e", h=6)
            if t not in touched:
                touched.add(t)
                self.TT("dve", ot, o1[:], w6(B_REC), ALU.add, [o1k], [kR, ("osum", t)])
            else:
                self.TT("dve", o1[:], o1[:], w6(B_REC), ALU.add, [o1k], [kR, o1k])
                self.TT("pool", ot, ot, o1[:], ALU.add, [o1k, ("osum", t)], [("osum", t)])
            yield
            for h in range(6):
                p, base = h // 2, 64 * (h % 2)
                self.MM(self.ps[B_REC][base:base + 64, p * 64:(p + 1) * 64], kdec[:, h, :], vpp[:, h, :], True, True, [kdk, vppk], [kR])
            self.TT("pool", tmpS[:], Sst[:], egtS[:, d, t, :].unsqueeze(2).broadcast_to([128, 3, 64]), ALU.mult, [Sstk, "egtS"], [tmpSk])
            yield
            self.TT("dve", Sst[:], tmpS[:], self.ps[B_REC][:, 0:192].rearrange("p (a b) -> p a b", b=64), ALU.add, [tmpSk], [kR, Sstk])
            yield
            self.CP("act", Sbf[0][0][0:64], Sst[0:64], [Sstk], [Sbf[0][1]])
            self.CP("pool", Sbf[1][0][64:128], Sst[64:128], [Sstk], [Sbf[1][1]])

        ut_stream = [(st, gi) for st in range(NT) for gi in range(3)]
        ut_done = [0] * NT
        rec_done = [False] * NT
        rec_next = 0
        rec_active = 0
        active = []
        free_slots = [0, 1, 2]
        nxt = 0
        while True:
            while free_slots and nxt < len(ut_stream):
                st, gi = ut_stream[nxt]
                if gi == 0 and st >= 2 and not rec_done[st - 2]:
                    break
                nxt += 1
                if gi == 0:
                    prep(st)
                sl = free_slots.pop(0)
                active.append(["ut", ut_group(sl, st, gi), sl, st])
            if rec_active == 0 and rec_next < NT and ut_done[rec_next] == 3:
                def rec_step(st_):
                    yield from rec_dir(st_, 0)
                    yield
                    yield from rec_dir(st_, 1)
                active.append(["rec", rec_step(rec_next), None, rec_next])
                rec_active = 1
                rec_next += 1
            if not active:
                break
            for a_ in list(active):
                try:
                    next(a_[1])
                except StopIteration:
                    active.remove(a_)
                    if a_[0] == "ut":
                        free_slots.append(a_[2])
                        ut_done[a_[3]] += 1
                    else:
                        rec_active -= 1
                        rec_done[a_[3]] = True
        self.P.flush()
        p3.close()
        oss = self.sb(ph, "doss", [128, NT, 6], F32)
        y1 = self.sb(ph, "dy1", [128, 384], F32)
        nwz = self.sb(ph, "nwz", [128, 384], F32)
        ydel = [self.sb(ph, "ydel%d" % i, [128, 384], BF16) for i in range(2)]
        yT = self.sb(ph, "dyT", [128, 3, S], BF16)
        for t in range(NT):
            self.ACT(sqs[:, 0:384], osum[:, t, :], AF.Square, [("osum", t)], ["sqs"])
            self.REDUCE(oss[:, t, :], sqs[:, 0:384].rearrange("p (a b) -> p a b", b=64), ["sqs"], ["doss"])
        self.TS("dve", oss[:], oss[:], 1.0 / 64.0, ALU.mult, ["doss"], ["doss"], s2=1e-6, op1=ALU.add)
        self.ACT(oss[:], oss[:], AF.Sqrt, ["doss"], ["doss"])
        self.RECIP(oss[:], oss[:], ["doss"], ["doss"])
        for t in range(NT):
            b = t % 2
            self.TT("dve", y1[:].rearrange("p (h e) -> p h e", h=6), osum[:, t, :].rearrange("p (h e) -> p h e", h=6),
                    oss[:, t, :].unsqueeze(2).broadcast_to([128, 6, 64]), ALU.mult, [("osum", t), "doss"], ["dy1"])
            self.TT("pool", nwz[:].rearrange("p (h e) -> p h e", h=6), zst[:, t, :].rearrange("p (h e) -> p h e", h=6),
                    nw[:].unsqueeze(1).broadcast_to([128, 6, 64]), ALU.mult, ["zst", "nw"], ["nwz"])
            self.TT("dve", ydel[b][:], y1[:], nwz[:], ALU.mult, ["dy1", "nwz"], ["ydel%d" % b])
            bk = 4 + (t % 2)
            pv = self.psbf(bk, 8)
            for c3 in range(3):
                self.TR(pv[:, c3, :], ydel[b][:, c3 * 128:(c3 + 1) * 128], self.identb[:], ["ydel%d" % b, "identb"], [self.psk(bk)])
            self.CP("act", yT[:, :, t * 128:(t + 1) * 128], pv[:, 0:3, :], [], [self.psk(bk), "dyT"])
        for c3 in range(3):
            self.DMA("sp", self.ycatT[5 + c3], yT[:, c3, :], ["dyT"], ["ycatT"], "st_ydel")
        self.P.flush()


Builder.phase_delta = phase_delta
```

```python
import math
import os
from contextlib import ExitStack

import numpy as np
import concourse.bass as bass
import concourse.mybir as mybir
from concourse.bass_utils import run_bass_kernel_spmd

F32 = mybir.dt.float32
BF16 = mybir.dt.bfloat16
AF = mybir.ActivationFunctionType
ALU = mybir.AluOpType

D = 1024
IN_W = 3224
DFF = 4096
EPS = 1e-6
N_CORES = 8
SLOPES = [0.25, 0.0625, 0.015625, 0.00390625, 0.5, 0.125]
ALIBI_SKIP = 80.0


class Prog:
    QUEUES = ("pe", "act", "dve", "pool", "sp")

    def __init__(self, nc, es):
        self.nc = nc
        self.ops = []
        self.last_w = {}
        self.readers = {}
        self.sems = {}
        self.es = es
        self.cnt = {}
        self.waited = {q: {} for q in self.QUEUES}
        self.start = 0

    def sem(self, sig):
        if sig not in self.sems:
            nm = "s%d" % len(self.sems)
            self.sems[sig] = self.es.enter_context(self.nc.semaphore(nm))
        return self.sems[sig]

    def op(self, q, fn, reads=(), writes=(), dma=None):
        i = len(self.ops)
        sig = ("dma", dma) if dma is not None else q
        deps = {}
        for k in reads:
            w = self.last_w.get(k)
            if w is not None:
                deps[w] = True
        for k in writes:
            w = self.last_w.get(k)
            if w is not None:
                deps.setdefault(w, False)
            for r in self.readers.get(k, {}).values():
                deps.setdefault(r, False)
        need = []
        for j, raw in deps.items():
            oj = self.ops[j]
            if j < self.start:
                continue
            if oj["sig"] == q and dma is None:
                if q == "pe":
                    continue
            need.append(j)
            oj["needed"] = True
        self.ops.append(dict(q=q, fn=fn, sig=sig, deps=need, needed=(dma is not None), val=None))
        for k in writes:
            self.last_w[k] = i
            self.readers[k] = {}
        for k in reads:
            self.readers.setdefault(k, {})[sig] = i
        return i

    def wait_all_dma(self, q="sp"):
        need = []
        seen = set()
        for j in range(len(self.ops) - 1, -1, -1):
            oj = self.ops[j]
            if isinstance(oj["sig"], tuple) and oj["sig"] not in seen:
                seen.add(oj["sig"])
                need.append(j)
        self.ops.append(dict(q=q, fn=None, sig=q, deps=need, needed=False, val=None))

    def flush(self):
        self.wait_all_dma()
        nc = self.nc
        ops = self.ops[self.start:]
        self.start = len(self.ops)
        for o in ops:
            if o["needed"]:
                inc = 16 if isinstance(o["sig"], tuple) else 1
                self.cnt[o["sig"]] = self.cnt.get(o["sig"], 0) + inc
                o["val"] = self.cnt[o["sig"]]
                self.sem(o["sig"])
        allops = self.ops
        prog = self

        def run(qname, eng):
            waited = prog.waited[qname]
            for o in ops:
                if o["q"] != qname:
                    continue
                wl = {}
                for j in o["deps"]:
                    oj = allops[j]
                    wl[oj["sig"]] = max(wl.get(oj["sig"], 0), oj["val"])
                for sg, v in wl.items():
                    if waited.get(sg, 0) >= v:
                        continue
                    eng.wait_ge(prog.sems[sg], v)
                    waited[sg] = v
                if o["fn"] is not None:
                    inst = o["fn"](eng)
                    if o["needed"]:
                        inc = 16 if isinstance(o["sig"], tuple) else 1
                        inst.then_inc(prog.sems[o["sig"]], inc)

        with nc.Block() as block:
            @block.tensor
            def _(e):
                run("pe", e)

            @block.scalar
            def _(e):
                run("act", e)

            @block.vector
            def _(e):
                run("dve", e)

            @block.gpsimd
            def _(e):
                run("pool", e)

            @block.sync
            def _(e):
                run("sp", e)


class Builder:
    def __init__(self, S, depth, debug=False, phases=None):
        self.S = S
        self.depth = depth
        self.NT = S // 128
        self.TB = min(512, S)
        self.NTB = S // self.TB
        self.debug = debug
        self.phases = phases
        self.bank_rr = 0

    def ACT(self, out, in_, func, r, w, **kw):
        self.P.op("act", lambda e: e.activation(out=out, in_=in_, func=func, **kw), r, w)

    def TT(self, q, out, in0, in1, op, r, w):
        self.P.op(q, lambda e: e.tensor_tensor(out=out, in0=in0, in1=in1, op=op), r, w)

    def TS(self, q, out, in0, s1, op0, r, w, s2=None, op1=None):
        if op1 is None:
            self.P.op(q, lambda e: e.tensor_scalar(out=out, in0=in0, scalar1=s1, scalar2=None, op0=op0), r, w)
        else:
            self.P.op(q, lambda e: e.tensor_scalar(out=out, in0=in0, scalar1=s1, scalar2=s2, op0=op0, op1=op1), r, w)

    def STT(self, out, in0, scalar, in1, op0, op1, r, w):
        self.P.op("dve", lambda e: e.scalar_tensor_tensor(out=out, in0=in0, scalar=scalar, in1=in1, op0=op0, op1=op1), r, w)

    def CP(self, q, out, in_, r, w, scale=None):
        if q == "act":
            if scale is None:
                self.P.op("act", lambda e: e.activation(out=out, in_=in_, func=AF.Copy), r, w)
            else:
                self.P.op("act", lambda e: e.activation(out=out, in_=in_, func=AF.Copy, scale=scale), r, w)
        else:
            self.P.op(q, lambda e: e.tensor_copy(out=out, in_=in_), r, w)

    def MM(self, out, lhsT, rhs, start, stop, r, w, **kw):
        self.P.op("pe", lambda e: e.matmul(out, lhsT=lhsT, rhs=rhs, start=start, stop=stop, **kw), r, w)

    def TR(self, out, in_, ident, r, w):
        self.P.op("pe", lambda e: e.transpose(out=out, in_=in_, identity=ident), r, w)

    def DMA(self, q, out, in_, r, w, stream):
        self.P.op(q, lambda e: e.dma_start(out=out, in_=in_), r, w, dma=stream)

    def MEMSET(self, q, ap, val, w):
        self.P.op(q, lambda e: e.memset(ap, val), (), w)

    def RECIP(self, out, in_, r, w):
        self.P.op("dve", lambda e: e.reciprocal(out=out, in_=in_), r, w)

    def REDUCE(self, out, in_, r, w):
        self.P.op("dve", lambda e: e.tensor_reduce(out=out, in_=in_, axis=mybir.AxisListType.X, op=ALU.add), r, w)

    def ASEL(self, out, in_, pattern, cmp, fill, base, cm, r, w):
        self.P.op("pool", lambda e: e.affine_select(out=out, in_=in_, pattern=pattern, compare_op=cmp, fill=fill,
                                                    base=base, channel_multiplier=cm), r, w)

    def nb(self):
        b = self.bank_rr
        self.bank_rr = (self.bank_rr + 1) % 8
        return b

    def sb(self, ph, name, shape, dt):
        self.uid = getattr(self, "uid", 0) + 1
        return ph.enter_context(self.nc.sbuf_tensor("%s_u%d" % (name, self.uid), shape, dt))

    def psk(self, b):
        return "ps%d" % b

    def psbf(self, b, a):
        return self.ps[b][:].bitcast(BF16).rearrange("p (a b) -> p a b", a=a)

    def build(self):
        S, NT = self.S, self.NT
        L = self.depth
        nc = bass.Bass("TRN2", target_bir_lowering=False)
        self.nc = nc

        def din(name, shape, dt=F32):
            return nc.dram_tensor(name, shape, dt, kind="ExternalInput").ap()

        def dscr(name, shape, dt=F32):
            kind = "ExternalOutput" if self.debug else "Internal"
            return nc.dram_tensor(name, shape, dt, kind=kind).ap()

        self.x_in = din("x", [S, D])
        self.w_in = din("w_in", [L, D, IN_W])
        self.w_out = din("w_out", [L, D, D])
        self.w_ff1 = din("w_ff1", [L, D, DFF])
        self.w_ff2 = din("w_ff2", [L, DFF, D])
        self.normw = din("normw", [L, 4, D])
        self.convw = din("convw", [L, 256, 31])
        self.convp = din("convp", [L, 128, 2, 3])
        self.lam = din("lam", [L, 2, 2, 32])
        self.subw = din("subw", [L, 64])
        self.dconvw = din("dconvw", [L, 1152, 3])
        self.dgate = din("dgate", [L, 2, 12])
        self.dnormw = din("dnormw", [L, 64])
        self.dband = din("dband", [128, 2 * S - 128])
        self.bdmask = din("bdmask", [128, 128])
        self.out = nc.dram_tensor("out", [S, D], F32, kind="ExternalOutput").ap()
        self.xres = dscr("xres", [S, D])
        self.hg = dscr("hg", [2, 128, S], BF16)
        self.qT = dscr("qT", [4, 96, S], BF16)
        self.kT = dscr("kT", [4, 96, S], BF16)
        self.vtok = dscr("vtok", [S, 384], BF16)
        self.gqkvT = dscr("gqkvT", [9, 128, S])
        self.zs = dscr("zs", [S, 384], BF16)
        self.gates = dscr("gates", [S, 24])
        self.ycatT = dscr("ycatT", [8, 128, S], BF16)
        self.ypart = dscr("ypart", [S, D])

        with ExitStack() as es:
            self.P = Prog(nc, es)
            self.ps = [es.enter_context(nc.psum_tensor("psb%d" % i, [128, 512], F32)) for i in range(8)]
            self.identf = self.sb(es, "identf", [128, 128], F32)
            self.identb = self.sb(es, "identb", [128, 128], BF16)
            self.MEMSET("pool", self.identf[:], 1.0, ["identf"])
            self.ASEL(self.identf[:], self.identf[:], [[-1, 128]], ALU.is_equal, 0.0, 0, 1, ["identf"], ["identf"])
            self.CP("pool", self.identb[:], self.identf[:], ["identf"], ["identb"])
            self.P.flush()
            for l in range(L):
                self.layer(l)
            self.P.flush()
        return nc

    def want(self, name):
        return self.phases is None or name in self.phases

    def layer(self, l):
        S = self.S
        last = (l == self.depth - 1)
        src = self.x_in if l == 0 else self.xres
        if self.want("B"):
            self.phase_in_proj(l, src)
        self.merge_conv = self.want("C") and self.want("E")
        if self.want("C") and not self.merge_conv:
            self.phase_conv(l)
        if self.want("D"):
            self.phase_attn(l)
        if self.want("E"):
            self.phase_delta(l)
        if self.want("F") and self.want("G"):
            self.phase_out_mlp(l, src, self.out if last else self.xres)
        else:
            if self.want("F"):
                self.phase_out_proj(l, src)
            if self.want("G"):
                self.phase_mlp(l, self.out if last else self.xres)

    def norm_T(self, ph, src, wrow, hT):
        S, NT = self.S, self.NT
        xt = [self.sb(ph, "n_xt%d" % i, [128, D], F32) for i in range(2)]
        hb = [self.sb(ph, "n_hb%d" % i, [128, D], BF16) for i in range(2)]
        junk = self.sb(ph, "n_junk", [128, D], BF16)
        ssq = self.sb(ph, "n_ssq", [128, NT], F32)
        rstd = self.sb(ph, "n_rstd", [128, NT], F32)
        wbc = self.sb(ph, "n_wbc", [128, D], F32)
        self.DMA("sp", wbc[:], wrow.broadcast_to([128, D]), [], ["n_wbc"], "n_wbc")
        for t in range(NT):
            b = t % 2
            self.DMA("sp", xt[b][:], src[t * 128:(t + 1) * 128, :], [("x", t)], ["n_xt%d" % b], "n_xt%d" % b)
            self.ACT(junk[:], xt[b][:], AF.Square, ["n_xt%d" % b], ["n_junk", "n_ssq"], accum_out=ssq[:, t:t + 1])
        self.TS("dve", rstd[:], ssq[:], 1.0 / D, ALU.mult, ["n_ssq"], ["n_rstd"], s2=EPS, op1=ALU.add)
        self.ACT(rstd[:], rstd[:], AF.Sqrt, ["n_rstd"], ["n_rstd"])
        self.RECIP(rstd[:], rstd[:], ["n_rstd"], ["n_rstd"])
        for t in range(NT):
            b = t % 2
            self.DMA("sp", xt[b][:], src[t * 128:(t + 1) * 128, :], [("x", t)], ["n_xt%d" % b], "n_xt%d" % b)
            self.STT(hb[b][:], xt[b][:], rstd[:, t:t + 1], wbc[:], ALU.mult, ALU.mult,
                     ["n_xt%d" % b, "n_rstd", "n_wbc"], ["n_hb%d" % b])
            bk = self.nb()
            pv = self.psbf(bk, 8)
            for kc in range(8):
                self.TR(pv[:, kc, :], hb[b][:, kc * 128:(kc + 1) * 128], self.identb[:], ["n_hb%d" % b, "identb"], [self.psk(bk)])
            self.CP("act", hT[:, :, t * 128:(t + 1) * 128], pv[:, :, :], [], [self.psk(bk), "hT"])

    def phase_in_proj(self, l, src):
        S, NT, TB, NTB = self.S, self.NT, self.TB, self.NTB
        with ExitStack() as ph:
            hT = self.sb(ph, "hT", [128, 8, S], BF16)
            self.norm_T(ph, src, self.normw[l, 0:1, :], hT)
            groups = [("wA", 0, 512), ("wB", 512, 768), ("wC", 1280, 384), ("wD", 1664, 1152), ("wE", 2816, 408)]
            W = {}
            for nm, c0, n in groups:
                W[nm] = self.sb(ph, nm, [128, 8, n], BF16)
                self.DMA("pool", W[nm][:], self.w_in[l, :, c0:c0 + n].rearrange("(kc p) n -> p kc n", p=128), [], [nm], nm)
            hgs = self.sb(ph, "hgs", [128, 2, S], BF16)
            qs = self.sb(ph, "qs", [128, 4, S], BF16)
            ks = self.sb(ph, "ks", [128, 4, S], BF16)
            vs = self.sb(ph, "vs", [128, NT, 384], BF16)
            zst = self.sb(ph, "zst", [128, NT, 384], BF16)
            gst = self.sb(ph, "gst", [128, NT, 24], F32)
            sg = [self.sb(ph, "sg%d" % i, [128, TB], F32) for i in range(2)]
            dst = [self.sb(ph, "dst%d" % i, [128, TB], F32) for i in range(3)]
            i = 0
            for cc in range(2):
                for tb in range(NTB):
                    ts_ = slice(tb * TB, (tb + 1) * TB)
                    ba, bg = self.nb(), self.nb()
                    for kc in range(8):
                        self.MM(self.ps[bg][:, :TB], W["wA"][:, kc, 256 + cc * 128:256 + (cc + 1) * 128], hT[:, kc, ts_], kc == 0, kc == 7, ["wA", "hT"], [self.psk(bg)])
                    for kc in range(8):
                        self.MM(self.ps[ba][:, :TB], W["wA"][:, kc, cc * 128:(cc + 1) * 128], hT[:, kc, ts_], kc == 0, kc == 7, ["wA", "hT"], [self.psk(ba)])
                    s = i % 2
                    i += 1
                    self.ACT(sg[s][:], self.ps[bg][:, :TB], AF.Sigmoid, [], [self.psk(bg), "sg%d" % s])
                    self.TT("dve", hgs[:, cc, ts_], self.ps[ba][:, :TB], sg[s][:], ALU.mult, ["sg%d" % s], [self.psk(ba), "hgs"])
            for cc in range(2):
                self.DMA("sp", self.hg[cc], hgs[:, cc, :], ["hgs"], ["hg"], "st_hg")
            for which, stg, dst_d, scale in (("q", qs, self.qT, 32.0 ** -0.5), ("k", ks, self.kT, None)):
                base = 0 if which == "q" else 384
                for c in range(4):
                    for tb in range(NTB):
                        ts_ = slice(tb * TB, (tb + 1) * TB)
                        bk = self.nb()
                        for kc in range(8):
                            self.MM(self.ps[bk][0:96, :TB], W["wB"][:, kc, base + 96 * c:base + 96 * (c + 1)], hT[:, kc, ts_], kc == 0, kc == 7, ["wB", "hT"], [self.psk(bk)])
                        self.CP("act", stg[0:96, c, ts_], self.ps[bk][0:96, :TB], [], [self.psk(bk), which + "s"], scale=scale)
                for c in range(4):
                    self.DMA("sp", dst_d[c], stg[0:96, c, :], [which + "s"], [which + "T"], "st_" + which)
            for t in range(NT):
                bk = self.nb()
                for kc in range(8):
                    self.MM(self.ps[bk][:, :384], hT[:, kc, t * 128:(t + 1) * 128], W["wC"][:, kc, :], kc == 0, kc == 7, ["wC", "hT"], [self.psk(bk)])
                self.CP("dve", vs[:, t, :], self.ps[bk][:, :384], [], [self.psk(bk), "vs"])
            self.DMA("sp", self.vtok.rearrange("(t p) n -> p t n", p=128), vs[:], ["vs"], ["vtok"], "st_v")
            i = 0
            for c in range(9):
                for tb in range(NTB):
                    ts_ = slice(tb * TB, (tb + 1) * TB)
                    bk = self.nb()
                    for kc in range(8):
                        self.MM(self.ps[bk][:, :TB], W["wD"][:, kc, c * 128:(c + 1) * 128], hT[:, kc, ts_], kc == 0, kc == 7, ["wD", "hT"], [self.psk(bk)])
                    s = i % 3
                    i += 1
                    self.CP("dve" if i % 2 else "act", dst[s][:], self.ps[bk][:, :TB], [], [self.psk(bk), "dst%d" % s])
                    self.DMA("sp", self.gqkvT[c, :, ts_], dst[s][:], ["dst%d" % s], ["gqkvT"], "st_d%d" % s)
            for t in range(NT):
                bk = self.nb()
                for kc in range(8):
                    self.MM(self.ps[bk][:, :408], hT[:, kc, t * 128:(t + 1) * 128], W["wE"][:, kc, :], kc == 0, kc == 7, ["wE", "hT"], [self.psk(bk)])
                self.ACT(zst[:, t, :], self.ps[bk][:, :384], AF.Silu, [], [self.psk(bk), "zst"])
                self.CP("dve", gst[:, t, :], self.ps[bk][:, 384:408], [], [self.psk(bk), "gst"])
            self.DMA("sp", self.zs.rearrange("(t p) n -> p t n", p=128), zst[:], ["zst"], ["zs"], "st_z")
            self.DMA("sp", self.gates.rearrange("(t p) n -> p t n", p=128), gst[:], ["gst"], ["gates"], "st_g")
            self.P.flush()

    def post_norm_residual(self, ph_tiles, banks, t, src, dst, wbc, extra=None):
        xt, ytmp, small, junk = ph_tiles
        b = t % 2
        self.DMA("sp", xt[b][:], src[t * 128:(t + 1) * 128, :], [("x", t)], ["r_xt%d" % b], "r_xt%d" % b)
        ysrc = []
        for hf in range(2):
            bk = banks[hf]
            if extra is not None:
                etile, ekey = extra
                self.TT("dve", ytmp[b][:, hf * 512:(hf + 1) * 512], self.ps[bk][:, :], etile[:, hf * 512:(hf + 1) * 512], ALU.add,
                        [ekey], [self.psk(bk), "r_yt%d" % b])
            else:
                self.CP("dve", ytmp[b][:, hf * 512:(hf + 1) * 512], self.ps[bk][:, :], [], [self.psk(bk), "r_yt%d" % b])
        sm = small[b]
        self.ACT(junk[:], ytmp[b][:], AF.Square, ["r_yt%d" % b], ["r_junk", "r_sm%d" % b], accum_out=sm[:, 0:1])
        self.TS("dve", sm[:, 1:2], sm[:, 0:1], 1.0 / D, ALU.mult, ["r_sm%d" % b], ["r_sm%d" % b], s2=EPS, op1=ALU.add)
        self.ACT(sm[:, 2:3], sm[:, 1:2], AF.Sqrt, ["r_sm%d" % b], ["r_sm%d" % b])
        self.RECIP(sm[:, 3:4], sm[:, 2:3], ["r_sm%d" % b], ["r_sm%d" % b])
        self.STT(ytmp[b][:], ytmp[b][:], sm[:, 3:4], wbc[:], ALU.mult, ALU.mult, ["r_yt%d" % b, "r_sm%d" % b, "r_wbc"], ["r_yt%d" % b])
        self.TT("pool", xt[b][:], xt[b][:], ytmp[b][:], ALU.add, ["r_xt%d" % b, "r_yt%d" % b], ["r_xt%d" % b])
        self.DMA("sp", dst[t * 128:(t + 1) * 128, :], xt[b][:], ["r_xt%d" % b], [("x", t)], "r_st%d" % b)

    def res_tiles(self, ph):
        xt = [self.sb(ph, "r_xt%d" % i, [128, D], F32) for i in range(2)]
        ytmp = [self.sb(ph, "r_yt%d" % i, [128, D], F32) for i in range(2)]
        small = [self.sb(ph, "r_sm%d" % i, [128, 4], F32) for i in range(2)]
        junk = self.sb(ph, "r_junk", [128, D], BF16)
        return xt, ytmp, small, junk

    def phase_out_proj(self, l, src):
        S, NT = self.S, self.NT
        with ExitStack() as ph:
            yT = self.sb(ph, "ycat", [128, 8, S], BF16)
            wo = self.sb(ph, "wo", [128, 8, D], BF16)
            wbc = self.sb(ph, "r_wbc", [128, D], F32)
            tiles = self.res_tiles(ph)
            for c in range(8):
                self.DMA("sp", yT[:, c, :], self.ycatT[c], ["ycatT"], ["ycat"], "ld_ycat")
            self.DMA("pool", wo[:], self.w_out[l].rearrange("(kc p) n -> p kc n", p=128), [], ["wo"], "ld_wo")
            self.DMA("sp", wbc[:], self.normw[l, 1:2, :].broadcast_to([128, D]), [], ["r_wbc"], "ld_rwbc")
            for t in range(NT):
                banks = [self.nb(), self.nb()]
                for hf in range(2):
                    for kc in range(8):
                        self.MM(self.ps[banks[hf]][:, :], yT[:, kc, t * 128:(t + 1) * 128], wo[:, kc, hf * 512:(hf + 1) * 512], kc == 0, kc == 7,
                                ["ycat", "wo"], [self.psk(banks[hf])])
                self.post_norm_residual(tiles, banks, t, src, self.xres, wbc)
            self.P.flush()

    def phase_out_mlp(self, l, src, dst):
        S, NT = self.S, self.NT
        with ExitStack() as ph:
            hT = self.sb(ph, "hT", [128, 8, S], BF16)
            w1 = self.sb(ph, "w1", [128, 8, 2048], BF16)
            w2 = self.sb(ph, "w2", [128, 16, D], BF16)
            self.DMA("pool", w1[:], self.w_ff1[l, :, 0:2048].rearrange("(kc p) n -> p kc n", p=128), [], ["w1"], "ld_w1")
            self.DMA("pool", w2[:], self.w_ff2[l, 0:2048, :].rearrange("(j p) n -> p j n", p=128), [], ["w2"], "ld_w2")
            with ExitStack() as ph2:
                yT = self.sb(ph2, "ycat", [128, 8, S], BF16)
                wo = self.sb(ph2, "wo", [128, 8, D], BF16)
                wbc = self.sb(ph2, "r_wbc", [128, D], F32)
                tiles = self.res_tiles(ph2)
                for c in range(8):
                    self.DMA("sp", yT[:, c, :], self.ycatT[c], ["ycatT"], ["ycat"], "ld_ycat")
                self.DMA("pool", wo[:], self.w_out[l].rearrange("(kc p) n -> p kc n", p=128), [], ["wo"], "ld_wo")
                self.DMA("sp", wbc[:], self.normw[l, 1:2, :].broadcast_to([128, D]), [], ["r_wbc"], "ld_rwbc")
                for t in range(NT):
                    banks = [self.nb(), self.nb()]
                    for hf in range(2):
                        for kc in range(8):
                            self.MM(self.ps[banks[hf]][:, :], yT[:, kc, t * 128:(t + 1) * 128], wo[:, kc, hf * 512:(hf + 1) * 512], kc == 0, kc == 7,
                                    ["ycat", "wo"], [self.psk(banks[hf])])
                    self.post_norm_residual(tiles, banks, t, src, self.xres, wbc)
                self.norm_T(ph2, self.xres, self.normw[l, 2:3, :], hT)
                self.P.flush()
            self.mlp_main(ph, l, dst, hT, w1, w2, True)

    def phase_mlp(self, l, dst):
        S = self.S
        with ExitStack() as ph:
            hT = self.sb(ph, "hT", [128, 8, S], BF16)
            with ExitStack() as ph2:
                self.norm_T(ph2, self.xres, self.normw[l, 2:3, :], hT)
                self.P.flush()
            w1 = self.sb(ph, "w1", [128, 8, 2048], BF16)
            w2 = self.sb(ph, "w2", [128, 16, D], BF16)
            self.mlp_main(ph, l, dst, hT, w1, w2, False)

    def mlp_main(self, ph, l, dst, hT, w1, w2, first_loaded):
        S, NT, TB, NTB = self.S, self.NT, self.TB, self.NTB
        NQ = TB // 128
        if True:
            h1 = [self.sb(ph, "h1_%d" % i, [128, 16, TB], BF16) for i in range(2)]
            rl = [self.sb(ph, "rl%d" % i, [128, TB], BF16) for i in range(2)]
            yp = [self.sb(ph, "yp%d" % i, [128, D], F32) for i in range(2)]
            wbc = self.sb(ph, "r_wbc", [128, D], F32)
            tiles = self.res_tiles(ph)
            self.DMA("sp", wbc[:], self.normw[l, 3:4, :].broadcast_to([128, D]), [], ["r_wbc"], "ld_rwbc")
            for half in range(2):
                f0 = half * 2048
                if not (first_loaded and half == 0):
                    self.DMA("pool", w1[:], self.w_ff1[l, :, f0:f0 + 2048].rearrange("(kc p) n -> p kc n", p=128), [], ["w1"], "ld_w1")
                    self.DMA("pool", w2[:], self.w_ff2[l, f0:f0 + 2048, :].rearrange("(j p) n -> p j n", p=128), [], ["w2"], "ld_w2")
                for tb in range(NTB):
                    ts_ = slice(tb * TB, (tb + 1) * TB)
                    hb = h1[tb % 2]
                    hk = "h1_%d" % (tb % 2)
                    for j in range(16):
                        bk = self.nb()
                        for kc in range(8):
                            self.MM(self.ps[bk][:, :TB], w1[:, kc, j * 128:(j + 1) * 128], hT[:, kc, ts_], kc == 0, kc == 7, ["w1", "hT"], [self.psk(bk)])
                        r_ = j % 2
                        self.ACT(rl[r_][:], self.ps[bk][:, :TB], AF.Relu, [], [self.psk(bk), "rl%d" % r_])
                        self.TT("pool" if j % 2 else "dve", hb[:, j, :], rl[r_][:], rl[r_][:], ALU.mult, ["rl%d" % r_], [hk])
                    for ti in range(NQ):
                        t = tb * NQ + ti
                        banks = [self.nb(), self.nb()]
                        for hf in range(2):
                            for j in range(16):
                                self.MM(self.ps[banks[hf]][:, :], hb[:, j, ti * 128:(ti + 1) * 128], w2[:, j, hf * 512:(hf + 1) * 512], j == 0, j == 15,
                                        [hk, "w2"], [self.psk(banks[hf])])
                        if half == 0:
                            b = t % 2
                            for hf in range(2):
                                self.CP("dve" if hf else "act", yp[b][:, hf * 512:(hf + 1) * 512], self.ps[banks[hf]][:, :], [], [self.psk(banks[hf]), "yp%d" % b])
                            self.DMA("sp", self.ypart[t * 128:(t + 1) * 128, :], yp[b][:], ["yp%d" % b], [("ypart", t)], "st_yp%d" % b)
                        else:
                            b = t % 2
                            self.DMA("sp", yp[b][:], self.ypart[t * 128:(t + 1) * 128, :], [("ypart", t)], ["yp%d" % b], "ld_yp%d" % b)
                            self.post_norm_residual(tiles, banks, t, self.xres, dst, wbc, extra=(yp[b], "yp%d" % b))
            self.P.flush()


def prep_inputs(inp, S):
    f = lambda a: np.ascontiguousarray(np.asarray(a, dtype=np.float32))
    L = inp["w_in"].shape[0]
    shared = {
        "w_in": f(inp["w_in"]), "w_out": f(inp["w_out"]), "w_ff1": f(inp["w_ff1"]), "w_ff2": f(inp["w_ff2"]),
        "normw": f(np.stack([inp["pre_mix_w"], inp["post_mix_w"], inp["pre_mlp_w"], inp["post_mlp_w"]], axis=1)),
        "convw": f(np.transpose(np.asarray(inp["conv_dw_w"]), (0, 2, 1))),
        "convp": f(np.stack([np.asarray(inp["conv_dw_b"]).reshape(L, 2, 128), np.asarray(inp["conv_ln_w"]).reshape(L, 2, 128),
                             np.asarray(inp["conv_ln_b"]).reshape(L, 2, 128)], axis=-1).transpose(0, 2, 1, 3)),
        "lam": f(np.stack([np.stack([inp["diff_lambda_q1"], inp["diff_lambda_q2"]], axis=1),
                           np.stack([inp["diff_lambda_k1"], inp["diff_lambda_k2"]], axis=1)], axis=1)),
        "subw": f(inp["diff_subln_w"]),
        "dconvw": f(np.transpose(np.asarray(inp["delta_conv_w"]), (0, 2, 1))),
        "dgate": f(np.stack([np.asarray(inp["delta_A_log"]).reshape(L, 12), np.asarray(inp["delta_dt_bias"]).reshape(L, 12)], axis=1)),
        "dnormw": f(inp["delta_norm_w"]),
    }
    W = 2 * S - 128
    shared["dband"] = f(np.abs(np.arange(W)[None, :] - (S - 128) - np.arange(128)[:, None]))
    shared["bdmask"] = f((np.arange(128)[:, None] // 32) == (np.arange(128)[None, :] // 32))
    return shared


_NC_CACHE = {}


def kernel(**inputs):
    x = np.asarray(inputs["x"], dtype=np.float32)
    B, S, _ = x.shape
    L = inputs["w_in"].shape[0]
    shared = prep_inputs(inputs, S)
    key = (S, L)
    if key not in _NC_CACHE:
        _NC_CACHE[key] = Builder(S, L).build()
    nc = _NC_CACHE[key]
    in_maps = []
    for b in range(B):
        m = dict(shared)
        m["x"] = np.ascontiguousarray(x[b])
        in_maps.append(m)
    res = run_bass_kernel_spmd(nc, in_maps, core_ids=list(range(B)))
    return np.stack([np.asarray(r["out"], dtype=np.float32) for r in res.results], axis=0)


def phase_conv(self, l):
    with ExitStack() as ph:
        for _ in self.conv_body(ph, l):
            pass
        self.P.flush()


def conv_body(self, ph, l):
    S, NT, TB, NTB = self.S, self.NT, self.TB, self.NTB
    if True:
        hgp = self.sb(ph, "hgp", [128, 2, S + 30], BF16)
        wcol = self.sb(ph, "wcol", [128, 2, 31], F32)
        cpar = self.sb(ph, "cpar", [128, 2, 3], F32)
        diag = self.sb(ph, "diag", [128, 62, 128], BF16)
        onesb = self.sb(ph, "onesb", [128, 128], BF16)
        yb = [self.sb(ph, "yb%d" % i, [128, TB], F32) for i in range(2)]
        yh = [self.sb(ph, "yh%d" % i, [128, TB], BF16) for i in range(2)]
        zq = [self.sb(ph, "zq%d" % i, [128, TB], BF16) for i in range(2)]
        mean = self.sb(ph, "mean", [128, TB], F32)
        var = self.sb(ph, "var", [128, TB], F32)
        zt = [self.sb(ph, "zt%d" % i, [128, TB], F32) for i in range(2)]
        yout = self.sb(ph, "yout", [128, 2, S], BF16)
        self.MEMSET("pool", hgp[:, :, 0:15], 0.0, ["hgp"])
        self.MEMSET("pool", hgp[:, :, S + 15:S + 30], 0.0, ["hgp"])
        self.MEMSET("pool", onesb[:], 1.0 / 256.0, ["onesb"])
        for cc in range(2):
            self.DMA("sp", hgp[:, cc, 15:15 + S], self.hg[cc], ["hg"], ["hgp"], "ld_hgp")
        self.DMA("sp", wcol[:], self.convw[l].rearrange("(cc p) j -> p cc j", p=128), [], ["wcol"], "ld_wcol")
        self.DMA("sp", cpar[:], self.convp[l], [], ["cpar"], "ld_cpar")
        for cc in range(2):
            for j in range(31):
                if j % 3 == 1:
                    self.ACT(diag[:, cc * 31 + j, :], self.identf[:], AF.Copy, ["wcol", "identf"], ["diag"], scale=wcol[:, cc, j:j + 1])
                else:
                    self.TS("dve" if j % 3 == 0 else "pool", diag[:, cc * 31 + j, :], self.identf[:], wcol[:, cc, j:j + 1], ALU.mult, ["wcol", "identf"], ["diag"])
        yield
        for tb in range(NTB):
            ts_ = slice(tb * TB, (tb + 1) * TB)
            for cc in range(2):
                bk = self.nb()
                for j in range(31):
                    self.MM(self.ps[bk][:, :TB], diag[:, cc * 31 + j, :], hgp[:, cc, tb * TB + j:tb * TB + j + TB], j == 0, j == 30,
                            ["diag", "hgp"], [self.psk(bk)])
                self.ACT(yb[cc][:], self.ps[bk][:, :TB], AF.Identity, ["cpar"], [self.psk(bk), "yb%d" % cc], bias=cpar[:, cc, 0:1])
                self.CP("dve", yh[cc][:], yb[cc][:], ["yb%d" % cc], ["yh%d" % cc])
            bm = self.nb()
            for cc in range(2):
                self.MM(self.ps[bm][:, :TB], onesb[:], yh[cc][:], cc == 0, cc == 1, ["onesb", "yh%d" % cc], [self.psk(bm)])
            self.CP("dve", mean[:], self.ps[bm][:, :TB], [], [self.psk(bm), "mean"])
            for cc in range(2):
                self.TT("pool" if cc else "dve", zt[cc][:], yb[cc][:], mean[:], ALU.subtract, ["yb%d" % cc, "mean"], ["zt%d" % cc])
                self.ACT(zq[cc][:], zt[cc][:], AF.Square, ["zt%d" % cc], ["zq%d" % cc])
            be = self.nb()
            for cc in range(2):
                self.MM(self.ps[be][:, :TB], onesb[:], zq[cc][:], cc == 0, cc == 1, ["onesb", "zq%d" % cc], [self.psk(be)])
            self.TS("dve", var[:], self.ps[be][:, :TB], 1e-5, ALU.add, [], [self.psk(be), "var"])
            self.ACT(var[:], var[:], AF.Sqrt, ["var"], ["var"])
            self.RECIP(var[:], var[:], ["var"], ["var"])
            for cc in range(2):
                self.TT("pool" if cc else "dve", zt[cc][:], zt[cc][:], var[:], ALU.mult, ["zt%d" % cc, "var"], ["zt%d" % cc])
                self.ACT(yout[:, cc, ts_], zt[cc][:], AF.Silu, ["zt%d" % cc, "cpar"], ["yout"], scale=cpar[:, cc, 1:2], bias=cpar[:, cc, 2:3])
            yield
        for cc in range(2):
            self.DMA("sp", self.ycatT[cc], yout[:, cc, :], ["yout"], ["ycatT"], "st_yconv")


Builder.phase_conv = phase_conv
Builder.conv_body = conv_body


def phase_attn(self, l):
    S, NT = self.S, self.NT
    QB = min(512, S)
    NQB = S // QB
    NQ = QB // 128
    linit = 0.8 - 0.6 * math.exp(-0.3 * l)
    with ExitStack() as ph:
        QT = self.sb(ph, "QT", [128, 4, S], BF16)
        KT = self.sb(ph, "KT", [128, 4, S], BF16)
        V = self.sb(ph, "V", [128, NT, 6, 128], BF16)
        dband = self.sb(ph, "dband", [128, 2 * S - 128], F32)
        lamv = self.sb(ph, "lamv", [128, 2, 2, 32], F32)
        lprod = self.sb(ph, "lprod", [128, 2, 32], F32)
        lsm = self.sb(ph, "lsm", [128, 4], F32)
        wsub = self.sb(ph, "wsub", [128, 64], F32)
        ydiff = self.sb(ph, "ydiff", [128, NT, 384], BF16)
        yT = self.sb(ph, "yT", [128, 3, S], BF16)
        sc = [self.sb(ph, "sc%d" % i, [128, QB], F32) for i in range(4)]
        ET = [self.sb(ph, "ET%d" % i, [128, QB], BF16) for i in range(5)]
        rs = self.sb(ph, "rs", [128, NQ, 1], F32)
        oTb = self.sb(ph, "oTb", [65, QB], BF16)
        o1 = self.sb(ph, "o1", [128, NQ, 64], F32)
        o2 = self.sb(ph, "o2", [128, NQ, 64], F32)
        osq = self.sb(ph, "osq", [128, NQ, 64], F32)
        oss = self.sb(ph, "oss", [128, NQ], F32)
        for c in range(4):
            self.DMA("sp", QT[0:96, c, :], self.qT[c], ["qT"], ["QT"], "ld_QT")
            self.DMA("sp", KT[0:96, c, :], self.kT[c], ["kT"], ["KT"], "ld_KT")
        QTm = None
        if os.environ.get("K_QM", "0") == "1":
            QTm = self.sb(ph, "QTm", [128, 12, S], BF16)
            self.MEMSET("pool", QTm[0:96, :, :], 0.0, ["QTm"])
            for mi_ in range(12):
                c_, r_ = mi_ // 3, 32 * (mi_ % 3)
                eng_ = ("act", "dve", "pool")[mi_ % 3]
                self.CP(eng_, QTm[r_:r_ + 32, mi_, :], QT[r_:r_ + 32, c_, :], ["QT", "QTm"], ["QTm"])
        self.MEMSET("pool", V[:, :, :, 65:128], 0.0, ["V"])
        self.MEMSET("pool", V[:, :, :, 64:65], 1.0, ["V"])
        for t in range(NT):
            self.DMA("sp", V[:, t, :, 0:64], self.vtok[t * 128:(t + 1) * 128, :].rearrange("p (h e) -> p h e", h=6), ["vtok"], ["V"], "ld_V")
        self.DMA("sp", dband[:], self.dband, [], ["dband"], "ld_dband")
        self.DMA("sp", lamv[:].rearrange("p a b c -> p (a b c)"), self.lam[l:l + 1].rearrange("o a b c -> o (a b c)").broadcast_to([128, 128]), [], ["lamv"], "ld_lam")
        self.DMA("sp", wsub[:], self.subw[l:l + 1, :].broadcast_to([128, 64]), [], ["wsub"], "ld_subw")
        self.TT("dve", lprod[:], lamv[:, 0, :, :], lamv[:, 1, :, :], ALU.mult, ["lamv"], ["lprod"])
        self.REDUCE(lsm[:, 0:2], lprod[:], ["lprod"], ["lsm"])
        self.ACT(lsm[:, 0:2], lsm[:, 0:2], AF.Exp, ["lsm"], ["lsm"])
        self.TT("dve", lsm[:, 2:3], lsm[:, 1:2], lsm[:, 0:1], ALU.subtract, ["lsm"], ["lsm"])
        self.TS("dve", lsm[:, 3:4], lsm[:, 2:3], -linit, ALU.add, ["lsm"], ["lsm"])
        self.TS("dve", wsub[:], wsub[:], 1.0 - linit, ALU.mult, ["wsub"], ["wsub"])
        SB = [0, 1, 2, 5]
        AB = [3, 4]
        LA = 3
        blocks = []
        ia = 0
        for h in range(6):
            for qb in range(NQB):
                for j in range(2):
                    ab = AB[ia % 2]
                    ia += 1
                    kts = []
                    for kt in range(NT):
                        dmin = max(0, kt * 128 - (qb * QB + QB - 1), qb * QB - (kt * 128 + 127))
                        if ALIBI_SKIP is None or dmin * SLOPES[h] <= ALIBI_SKIP:
                            kts.append(kt)
                    for kt in kts:
                        blocks.append((h, qb, j, kt, ab, kt == kts[0], kt == kts[-1]))
        nblk = len(blocks)

        def front(it):
            h, qb, j, kt, ab, kfirst, klast = blocks[it]
            m = SLOPES[h]
            mi = 2 * h + j
            c = mi // 3
            r0 = 32 * (mi % 3)
            sbk = SB[it % 4]
            si = it % 4
            ei = it % 5
            if QTm is not None:
                self.MM(self.ps[sbk][:, :QB], KT[0:96, c, kt * 128:(kt + 1) * 128], QTm[0:96, mi, qb * QB:(qb + 1) * QB], True, True,
                        ["KT", "QTm"], [self.psk(sbk)])
            else:
                self.MM(self.ps[sbk][:, :QB], KT[r0:r0 + 32, c, kt * 128:(kt + 1) * 128], QT[r0:r0 + 32, c, qb * QB:(qb + 1) * QB], True, True,
                        ["KT", "QT"], [self.psk(sbk)])
            for _ in range(int(os.environ.get("K_FILL", "0"))):
                self.MM(self.ps[7][:, :QB], KT[:, 0, 0:128], QT[:, 0, 0:QB], True, True, ["KT", "QT"], ["ps7"])
            off = qb * QB - kt * 128 + S - 128
            self.STT(sc[si][:], dband[:, off:off + QB], -m, self.ps[sbk][:, :QB], ALU.mult, ALU.add, ["dband"], [self.psk(sbk), "sc%d" % si])
            self.ACT(ET[ei][:], sc[si][:], AF.Exp, ["sc%d" % si], ["ET%d" % ei])

        pending = []

        def epilogue(h, qb, j, ab):
            accv = self.ps[ab][:, 0:NQ * 65].rearrange("p (a b) -> p a b", b=65)
            self.RECIP(rs[:], accv[:, :, 64:65], [], [self.psk(ab), "rs"])
            if j == 0:
                self.TT("dve", o1[:], accv[:, :, 0:64], rs[:].broadcast_to([128, NQ, 64]), ALU.mult, ["rs"], [self.psk(ab), "o1"])
                return
            self.TT("dve", o2[:], accv[:, :, 0:64], rs[:].broadcast_to([128, NQ, 64]), ALU.mult, ["rs"], [self.psk(ab), "o2"])
            self.STT(o2[:], o2[:], lsm[:, 3:4], o1[:], ALU.mult, ALU.add, ["o2", "o1", "lsm"], ["o2"])
            self.TT("pool", osq[:], o2[:], o2[:], ALU.mult, ["o2"], ["osq"])
            yield
            yield
            self.REDUCE(oss[:], osq[:], ["osq"], ["oss"])
            self.TS("dve", oss[:], oss[:], 1.0 / 64.0, ALU.mult, ["oss"], ["oss"], s2=1e-5, op1=ALU.add)
            self.ACT(oss[:], oss[:], AF.Sqrt, ["oss"], ["oss"])
            yield
            yield
            self.RECIP(oss[:], oss[:], ["oss"], ["oss"])
            self.TT("dve", o2[:], o2[:], oss[:].unsqueeze(2).broadcast_to([128, NQ, 64]), ALU.mult, ["o2", "oss"], ["o2"])
            self.TT("pool", ydiff[:, qb * NQ:(qb + 1) * NQ, h * 64:(h + 1) * 64], o2[:], wsub[:].unsqueeze(1).broadcast_to([128, NQ, 64]), ALU.mult,
                    ["o2", "wsub"], ["ydiff"])

        def back(it):
            h, qb, j, kt, ab, kfirst, klast = blocks[it]
            ei = it % 5
            accv = self.ps[ab][:, 0:NQ * 65].rearrange("p (a b) -> p a b", b=65)
            for qi in range(NQ):
                self.MM(accv[:, qi, :], ET[ei][:, qi * 128:(qi + 1) * 128], V[:, kt, h, 0:65], (kfirst and qi == 0), klast,
                        ["ET%d" % ei, "V"], [self.psk(ab)], skip_group_check=True)
            if klast:
                while pending:
                    pump()
                pending.append(epilogue(h, qb, j, ab))

        def pump():
            for g_ in list(pending):
                try:
                    next(g_)
                except StopIteration:
                    pending.remove(g_)

        for it in range(nblk + LA):
            if it < nblk:
                front(it)
            if it >= LA:
                back(it - LA)
            pump()
        while pending:
            pump()
        for t in range(NT):
            bk = 5 + (t % 2)
            pv = self.psbf(bk, 8)
            for c3 in range(3):
                self.TR(pv[:, c3, :], ydiff[:, t, c3 * 128:(c3 + 1) * 128], self.identb[:], ["ydiff", "identb"], [self.psk(bk)])
            self.CP("act", yT[:, :, t * 128:(t + 1) * 128], pv[:, 0:3, :], [], [self.psk(bk), "yT"])
        for c3 in range(3):
            self.DMA("sp", self.ycatT[2 + c3], yT[:, c3, :], ["yT"], ["ycatT"], "st_ydiff")
        self.P.flush()


Builder.phase_attn = phase_attn


def phase_delta(self, l):
    S, NT = self.S, self.NT
    H = 6
    with ExitStack() as ph:
        tm = self.sb(ph, "tm", [128, NT, 1152], BF16)
        with ExitStack() as p1:
            wc = self.sb(p1, "wc", [128, 9, 3], F32)
            xp = [self.sb(p1, "xp%d" % i, [128, S + 2], F32) for i in range(2)]
            acc = [self.sb(p1, "acc%d" % i, [128, S], F32) for i in range(2)]
            sT = [self.sb(p1, "sT%d" % i, [128, S], BF16) for i in range(2)]
            self.DMA("sp", wc[:], self.dconvw[l].rearrange("(c p) j -> p c j", p=128), [], ["wc"], "ld_wc")
            for i in range(2):
                self.MEMSET("pool", xp[i][:, 0:1], 0.0, ["xp%d" % i])
                self.MEMSET("pool", xp[i][:, S + 1:S + 2], 0.0, ["xp%d" % i])
            cgen = self.conv_body(p1, l) if (self.want("C") and self.merge_conv) else iter(())
            for c in range(9):
                b = c % 2
                next(cgen, None)
                self.DMA("sp", xp[b][:, 1:S + 1], self.gqkvT[c], ["gqkvT"], ["xp%d" % b], "ld_xp%d" % b)
                self.TS("dve", acc[b][:], xp[b][:, 0:S], wc[:, c, 0:1], ALU.mult, ["xp%d" % b, "wc"], ["acc%d" % b])
                self.STT(acc[b][:], xp[b][:, 1:S + 1], wc[:, c, 1:2], acc[b][:], ALU.mult, ALU.add, ["xp%d" % b, "wc", "acc%d" % b], ["acc%d" % b])
                self.STT(acc[b][:], xp[b][:, 2:S + 2], wc[:, c, 2:3], acc[b][:], ALU.mult, ALU.add, ["xp%d" % b, "wc", "acc%d" % b], ["acc%d" % b])
                self.ACT(sT[b][:], acc[b][:], AF.Silu, ["acc%d" % b], ["sT%d" % b])
                for t0 in range(0, NT, 8):
                    n = min(8, NT - t0)
                    bk = 6 + ((c * 2 + t0 // 8) % 2)
                    pv = self.psbf(bk, 8)
                    for i in range(n):
                        t = t0 + i
                        self.TR(pv[:, i, :], sT[b][:, t * 128:(t + 1) * 128], self.identb[:], ["sT%d" % b, "identb"], [self.psk(bk)])
                    self.CP("act" if (t0 // 8) % 2 else "dve", tm[:, t0:t0 + n, c * 128:(c + 1) * 128], pv[:, 0:n, :], [], [self.psk(bk), "tm"])
            for _ in cgen:
                pass
            self.P.flush()
        onesb = self.sb(ph, "onesb1", [128, 128], BF16)
        cf = self.sb(ph, "cf", [128, 128], F32)
        Lmat = [self.sb(ph, "Lmat%d" % d, [128, 128], BF16) for d in range(2)]
        maskneg = [self.sb(ph, "maskneg%d" % d, [128, 128], F32) for d in range(2)]
        nstr = [self.sb(ph, "nstr%d" % d, [128, 128], F32) for d in range(2)]
        self.MEMSET("pool", onesb[:], 1.0, ["onesb1"])
        for d in range(2):
            pat, cm = ([[1, 128]], -1) if d == 0 else ([[-1, 128]], 1)
            self.MEMSET("pool", cf[:], 1.0, ["cf"])
            self.ASEL(cf[:], cf[:], pat, ALU.is_ge, 0.0, 0, cm, ["cf"], ["cf"])
            self.CP("pool", Lmat[d][:], cf[:], ["cf"], ["Lmat%d" % d])
            self.MEMSET("pool", maskneg[d][:], 0.0, ["maskneg%d" % d])
            self.ASEL(maskneg[d][:], maskneg[d][:], pat, ALU.is_ge, -1e30, 0, cm, ["maskneg%d" % d], ["maskneg%d" % d])
            self.MEMSET("pool", nstr[d][:], -1.0, ["nstr%d" % d])
            self.ASEL(nstr[d][:], nstr[d][:], pat, ALU.is_gt, 0.0, 0, cm, ["nstr%d" % d], ["nstr%d" % d])
        groups = [[(0, 0, 4, 0)], [(0, 4, 6, 0), (1, 0, 2, 2)], [(1, 2, 6, 0)]]
        bd = self.sb(ph, "bd", [128, 128], F32)
        self.DMA("sp", bd[:], self.bdmask, [], ["bd"], "ld_bd")
        nstrd = [self.sb(ph, "nstrd%d" % d, [128, 128], F32) for d in range(2)]
        nstro = [self.sb(ph, "nstro%d" % d, [128, 128], F32) for d in range(2)]
        for d in range(2):
            self.TT("pool", nstrd[d][:], nstr[d][:], bd[:], ALU.mult, ["nstr%d" % d, "bd"], ["nstrd%d" % d])
            self.TT("pool", nstro[d][:], nstr[d][:], nstrd[d][:], ALU.subtract, ["nstr%d" % d, "nstrd%d" % d], ["nstro%d" % d])
        nstrg_d, nstrg_o = [], []
        for gi, g in enumerate(groups):
            tld = self.sb(ph, "nstrgd%d" % gi, [128, 4, 128], BF16)
            tlo = self.sb(ph, "nstrgo%d" % gi, [128, 4, 128], BF16)
            for (d, h0, h1, s0) in g:
                n = h1 - h0
                self.CP("pool", tld[:, s0:s0 + n, :], nstrd[d][:].unsqueeze(1).broadcast_to([128, n, 128]), ["nstrd%d" % d], ["nstrgd%d" % gi])
                self.CP("pool", tlo[:, s0:s0 + n, :], nstro[d][:].unsqueeze(1).broadcast_to([128, n, 128]), ["nstro%d" % d], ["nstrgo%d" % gi])
            nstrg_d.append(tld)
            nstrg_o.append(tlo)
        qnT = self.sb(ph, "qnT", [128, 3, S], BF16)
        zst = self.sb(ph, "zst", [128, NT, 384], BF16)
        osum = self.sb(ph, "osum", [128, NT, 384], F32)
        sqs = self.sb(ph, "sqs", [128, 768], F32)
        ssq = self.sb(ph, "ssq", [128, NT, 12], F32)
        gin = self.sb(ph, "gin", [128, NT, 24], F32)
        dg = self.sb(ph, "dg", [128, 2, 12], F32)
        nw = self.sb(ph, "nw", [128, 64], F32)
        G = {}
        for nm in ("sbt", "gl", "gc", "egc", "gt", "egt", "edec", "r1"):
            G[nm] = self.sb(ph, "g_" + nm, [128, 2, NT, 6], F32)
        gls = [self.sb(ph, "gls%d" % i, [128, 2, NT, 6], BF16) for i in range(3)]
        gcs = [self.sb(ph, "gcs%d" % i, [128, 2, NT, 6], BF16) for i in range(3)]
        egtS = self.sb(ph, "egtS", [128, 2, NT, 3], F32)
        self.DMA("sp", zst[:], self.zs.rearrange("(t p) n -> p t n", p=128), ["zs"], ["zst"], "ld_zs")
        self.DMA("sp", gin[:], self.gates.rearrange("(t p) n -> p t n", p=128), ["gates"], ["gin"], "ld_gin")
        self.DMA("sp", dg[:].rearrange("p a b -> p (a b)"), self.dgate[l:l + 1].rearrange("o a b -> o (a b)").broadcast_to([128, 24]), [], ["dg"], "ld_dg")
        self.DMA("sp", nw[:], self.dnormw[l:l + 1, :].broadcast_to([128, 64]), [], ["nw"], "ld_nw")
        for t in range(NT):
            self.ACT(sqs[:], tm[:, t, 0:768], AF.Square, ["tm"], ["sqs"])
            self.REDUCE(ssq[:, t, :], sqs[:].rearrange("p (a b) -> p a b", b=64), ["sqs"], ["ssq"])
        self.TS("dve", ssq[:], ssq[:], 1e-6, ALU.add, ["ssq"], ["ssq"])
        self.ACT(ssq[:], ssq[:], AF.Sqrt, ["ssq"], ["ssq"])
        self.RECIP(ssq[:], ssq[:], ["ssq"], ["ssq"])
        self.TS("dve", ssq[:, :, 0:6], ssq[:, :, 0:6], 0.125, ALU.mult, ["ssq"], ["ssq"])
        for t in range(NT):
            self.TT("dve", tm[:, t, 0:384].rearrange("p (h e) -> p h e", h=6), tm[:, t, 0:384].rearrange("p (h e) -> p h e", h=6),
                    ssq[:, t, 0:6].unsqueeze(2).broadcast_to([128, 6, 64]), ALU.mult, ["tm", "ssq"], ["tm"])
            self.TT("pool", tm[:, t, 384:768].rearrange("p (h e) -> p h e", h=6), tm[:, t, 384:768].rearrange("p (h e) -> p h e", h=6),
                    ssq[:, t, 6:12].unsqueeze(2).broadcast_to([128, 6, 64]), ALU.mult, ["tm", "ssq"], ["tm"])
            bk = 6 + (t % 2)
            pv = self.psbf(bk, 8)
            for c3 in range(3):
                self.TR(pv[:, c3, :], tm[:, t, c3 * 128:(c3 + 1) * 128], self.identb[:], ["tm", "identb"], [self.psk(bk)])
            self.CP("act", qnT[:, :, t * 128:(t + 1) * 128], pv[:, 0:3, :], [], [self.psk(bk), "qnT"])
        self.ACT(dg[:, 0, :], dg[:, 0, :], AF.Exp, ["dg"], ["dg"])
        self.TS("dve", dg[:, 0, :], dg[:, 0, :], -1.0, ALU.mult, ["dg"], ["dg"])
        for d in range(2):
            bsl = gin[:, :, d * 6:(d + 1) * 6]
            asl = gin[:, :, 12 + d * 6:12 + (d + 1) * 6]
            self.ACT(G["sbt"][:, d, :, :], bsl, AF.Sigmoid, ["gin"], ["sbt"])
            self.ACT(G["sbt"][:, d, :, :], G["sbt"][:, d, :, :], AF.Sqrt, ["sbt"], ["sbt"])
            self.TT("dve", G["gl"][:, d, :, :], asl, dg[:, 1, d * 6:(d + 1) * 6].unsqueeze(1).broadcast_to([128, NT, 6]), ALU.add, ["gin", "dg"], ["gl"])
            self.ACT(G["gl"][:, d, :, :], G["gl"][:, d, :, :], AF.Exp, ["gl"], ["gl"])
            self.TS("dve", G["gl"][:, d, :, :], G["gl"][:, d, :, :], 1.0, ALU.add, ["gl"], ["gl"])
            self.ACT(G["gl"][:, d, :, :], G["gl"][:, d, :, :], AF.Ln, ["gl"], ["gl"])
            self.TT("dve", G["gl"][:, d, :, :], G["gl"][:, d, :, :], dg[:, 0, d * 6:(d + 1) * 6].unsqueeze(1).broadcast_to([128, NT, 6]), ALU.mult, ["gl", "dg"], ["gl"])

        def split3(src, dst, key_src, key_dst):
            r1 = G["r1"]
            self.CP("dve", dst[0][:], src[:], [key_src], [key_dst])
            self.TT("dve", r1[:], src[:], dst[0][:], ALU.subtract, [key_src, key_dst], ["r1"])
            self.CP("dve", dst[1][:], r1[:], ["r1"], [key_dst])
            self.TT("dve", r1[:], r1[:], dst[1][:], ALU.subtract, ["r1", key_dst], ["r1"])
            self.CP("dve", dst[2][:], r1[:], ["r1"], [key_dst])

        split3(G["gl"], gls, "gl", "gls")
        for d in range(2):
            bk = self.nb()
            for i in range(3):
                self.MM(self.ps[bk][:, 0:NT * 6], Lmat[d][:], gls[i][:, d, :, :].rearrange("p t h -> p (t h)"), i == 0, i == 2, ["Lmat%d" % d, "gls"], [self.psk(bk)])
            self.CP("dve", G["gc"][:, d, :, :].rearrange("p t h -> p (t h)"), self.ps[bk][:, 0:NT * 6], [], [self.psk(bk), "gc"])
            bk = self.nb()
            for i in range(3):
                self.MM(self.ps[bk][:, 0:NT * 6], onesb[:], gls[i][:, d, :, :].rearrange("p t h -> p (t h)"), i == 0, i == 2, ["onesb1", "gls"], [self.psk(bk)])
            self.CP("dve", G["gt"][:, d, :, :].rearrange("p t h -> p (t h)"), self.ps[bk][:, 0:NT * 6], [], [self.psk(bk), "gt"])
        self.ACT(G["egc"][:], G["gc"][:], AF.Exp, ["gc"], ["egc"])
        self.ACT(G["egt"][:], G["gt"][:], AF.Exp, ["gt"], ["egt"])
        self.TT("dve", G["edec"][:], G["gt"][:], G["gc"][:], ALU.subtract, ["gt", "gc"], ["edec"])
        self.ACT(G["edec"][:], G["edec"][:], AF.Exp, ["edec"], ["edec"])
        split3(G["gc"], gcs, "gc", "gcs")
        ev = G["egt"][:].rearrange("p d t (a two) -> p d t a two", two=2)
        self.CP("pool", egtS[0:64], ev[0:64, :, :, :, 0], ["egt"], ["egtS"])
        self.CP("pool", egtS[64:128], ev[64:128, :, :, :, 1], ["egt"], ["egtS"])
        p3 = ExitStack()

        def mk(name, shape, dt):
            return (self.sb(p3, name, shape, dt), name)

        PS = []
        for pb in range(2):
            d_ = {}
            for d in range(2):
                d_["r0", d] = mk("r0_%d%d" % (pb, d), [128, 6, 128], BF16)
                d_["kp", d] = mk("kp_%d%d" % (pb, d), [128, 6, 64], BF16)
                d_["kdec", d] = mk("kdec_%d%d" % (pb, d), [128, 6, 64], BF16)
                for hf in range(2):
                    d_["kpT", d, hf] = mk("kpT_%d%d%d" % (pb, d, hf), [128, 3, 128], BF16)
                    self.MEMSET("pool", d_["kpT", d, hf][0][:], 0.0, [d_["kpT", d, hf][1]])
                d_["AqkT", d] = mk("AqkT_%d%d" % (pb, d), [128, 6, 128], BF16)
                d_["ru", d] = mk("ru_%d%d" % (pb, d), [128, 6, 64], F32)
                d_["rwb", d] = mk("rwb_%d%d" % (pb, d), [128, 6, 64], BF16)
            PS.append(d_)
        RT = []
        for d in range(2):
            d_ = {}
            d_["wT"] = mk("wT%d" % d, [128, 3, 128], BF16)
            d_["Sst"] = mk("Sst%d" % d, [128, 3, 64], F32)
            d_["Sbf"] = [mk("Sbf%d_%d" % (d, hf), [128, 3, 64], BF16) for hf in range(2)]
            d_["tmpS"] = mk("tmpS%d" % d, [128, 3, 64], F32)
            d_["vpp"] = mk("vpp%d" % d, [128, 6, 64], BF16)
            d_["o1"] = mk("do1_%d" % d, [128, 6, 64], F32)
            self.MEMSET("pool", d_["Sst"][0][:], 0.0, [d_["Sst"][1]])
            for hf in range(2):
                self.MEMSET("pool", d_["Sbf"][hf][0][:], 0.0, [d_["Sbf"][hf][1]])
            RT.append(d_)
        SL = []
        for sl in range(3):
            d_ = {}
            d_["dgi"] = [mk("dgi%d_%d" % (sl, i), [128, 4, 128], BF16) for i in range(3)]
            d_["d0"] = mk("d0_%d" % sl, [128, 4, 128], F32)
            d_["tmpk"] = mk("tmpk%d" % sl, [128, 4, 128], F32)
            d_["PT"] = [mk("PT%d_%d" % (sl, i), [128, 4, 128], BF16) for i in range(5)]
            d_["Pm"] = [mk("Pm%d_%d" % (sl, i), [128, 4, 128], BF16) for i in range(2)]
            d_["Pd0"] = mk("Pd0_%d" % sl, [128, 4, 128], BF16)
            d_["PoT"] = mk("PoT%d" % sl, [128, 4, 128], BF16)
            d_["XTb"] = mk("XTb%d" % sl, [128, 4, 128], BF16)
            d_["banks"] = (2 * sl, 2 * sl + 1)
            SL.append(d_)
        nidentb = self.sb(p3, "nidentb", [128, 128], BF16)
        self.TS("pool", nidentb[:], self.identf[:], -1.0, ALU.mult, ["identf"], ["nidentb"])
        touched = set()
        B_T, B_REC = 6, 7

        def v4(bk):
            return self.ps[bk][:].rearrange("p (a b) -> p a b", b=128)

        def w6(bk):
            return self.ps[bk][:, 0:384].rearrange("p (h e) -> p h e", h=6)

        def tiles_of(step):
            return [step, NT - 1 - step]

        def prep(step):
            tt = tiles_of(step)
            P_ = PS[step % 2]
            for d in range(2):
                t = tt[d]
                kp, kpk = P_["kp", d]
                r0, r0k = P_["r0", d]
                kdec, kdk = P_["kdec", d]
                sb6 = G["sbt"][:, d, t, :].unsqueeze(2).broadcast_to([128, 6, 64])
                kn6 = tm[:, t, 384:768].rearrange("p (h e) -> p h e", h=6)
                v6 = tm[:, t, 768:1152].rearrange("p (h e) -> p h e", h=6)
                self.TT("pool", kp[:], kn6, sb6, ALU.mult, ["tm", "sbt"], [kpk])
                self.TT("pool", r0[:, :, 0:64], v6, sb6, ALU.mult, ["tm", "sbt"], [r0k])
                self.TT("pool", r0[:, :, 64:128], kp[:], G["egc"][:, d, t, :].unsqueeze(2).broadcast_to([128, 6, 64]), ALU.mult, [kpk, "egc"], [r0k])
                self.TT("pool", kdec[:], kp[:], G["edec"][:, d, t, :].unsqueeze(2).broadcast_to([128, 6, 64]), ALU.mult, [kpk, "edec"], [kdk])
                pv = self.psbf(B_T, 8)
                kpf = kp[:].rearrange("p h e -> p (h e)")
                for c3 in range(3):
                    self.TR(pv[:, c3, :], kpf[:, c3 * 128:(c3 + 1) * 128], self.identb[:], [kpk, "identb"], [self.psk(B_T)])
                self.CP("act", P_["kpT", d, 0][0][0:64], pv[0:64, 0:3, :], [], [self.psk(B_T), P_["kpT", d, 0][1]])
                self.CP("dve", P_["kpT", d, 1][0][64:128], pv[64:128, 0:3, :], [], [self.psk(B_T), P_["kpT", d, 1][1]])

        def ut_group(sl, step, gi):
            g = groups[gi]
            tt = tiles_of(step)
            P_ = PS[step % 2]
            T_ = SL[sl]
            BA, BB = T_["banks"]
            kA, kB = self.psk(BA), self.psk(BB)
            dgi = T_["dgi"]
            d0, d0k = T_["d0"]
            tmpk, tmpkk = T_["tmpk"]
            PT = T_["PT"]
            Pm = T_["Pm"]
            Pd0, Pd0k = T_["Pd0"]
            PoT, PoTk = T_["PoT"]
            XTb, XTbk = T_["XTb"]
            Xb, Xbk = Pm[0]
            Qb, Qbk = Pm[1]
            XT2b, XT2bk = PT[1]
            xjb, xjbk = PT[2]
            yjb, yjbk = PT[3]
            combos = []
            for (d, h0, h1, s0) in g:
                for h in range(h0, h1):
                    combos.append((d, h, s0 + h - h0))
            for i in range(3):
                for (d, h0, h1, s0) in g:
                    n = h1 - h0
                    self.TT("pool", dgi[i][0][:, s0:s0 + n, :], self.identb[:].unsqueeze(1).broadcast_to([128, n, 128]),
                            gcs[i][:, d, tt[d], h0:h1].unsqueeze(2).broadcast_to([128, n, 128]), ALU.mult, ["identb", "gcs"], [dgi[i][1]])
            first = True
            for (d, h, s) in combos:
                for i in range(3):
                    self.MM(v4(BA)[:, s, :], onesb[:], dgi[i][0][:, s, :], first, False, ["onesb1", dgi[i][1]], [kA], skip_group_check=True)
                    first = False
            for (d, h, s) in combos:
                p = h // 2
                kT_, kTk = P_["kpT", d, h % 2]
                self.MM(v4(BB)[:, s, :], kT_[:, p, :], kT_[:, p, :], True, True, [kTk], [kB])
            yield
            for (d, h, s) in combos:
                self.STT(d0[:, s, :], v4(BA)[:, s, :], G["gc"][:, d, tt[d], h:h + 1], maskneg[d][:], ALU.subtract, ALU.add,
                         ["gc", "maskneg%d" % d], [kA, d0k])
            self.ACT(d0[:], d0[:], AF.Exp, [d0k], [d0k])
            yield
            self.TT("dve", tmpk[:], v4(BB)[:], d0[:], ALU.mult, [d0k], [kB, tmpkk])
            for (d, h, s) in combos:
                p = h // 2
                t = tt[d]
                kT_, kTk = P_["kpT", d, h % 2]
                self.MM(v4(BB)[:, s, :], kT_[:, p, :], qnT[:, p, t * 128:(t + 1) * 128], True, True, [kTk, "qnT"], [kB])
            self.TT("pool", PT[0][0][:], tmpk[:], nstrg_d[gi][:], ALU.mult, [tmpkk, "nstrgd%d" % gi], [PT[0][1]])
            self.TT("pool", PoT[:], tmpk[:], nstrg_o[gi][:], ALU.mult, [tmpkk, "nstrgo%d" % gi], [PoTk])
            yield
            for (d, h0, h1, s0) in g:
                n = h1 - h0
                Aq, Aqk_ = P_["AqkT", d]
                self.TT("dve", Aq[:, h0:h1, :], v4(BB)[:, s0:s0 + n, :], d0[:, s0:s0 + n, :], ALU.mult, [d0k], [kB, Aqk_])
            yield
            pv = self.psbf(B_T, 8)
            for (d, h, s) in combos:
                self.TR(pv[:, s, :], PT[0][0][:, s, :], self.identb[:], [PT[0][1], "identb"], [self.psk(B_T)])
            self.CP("act", Pd0[:], pv[:, 0:4, :], [], [self.psk(B_T), Pd0k])
            yield
            first = True
            for (d, h, s) in combos:
                self.MM(v4(BA)[:, s, :], self.identb[:], self.identb[:], first, False, ["identb"], [kA], skip_group_check=True)
                first = False
                self.MM(v4(BA)[:, s, :], Pd0[:, s, :], self.identb[:], False, False, [Pd0k, "identb"], [kA], skip_group_check=True)
            for k in range(5):
                Pk, Pkk = (Pd0, Pd0k) if k == 0 else Pm[k % 2]
                if k > 0:
                    self.CP("act", XTb[:], v4(BA)[:], [], [kA, XTbk])
                    for (d, h, s) in combos:
                        self.MM(v4(BA)[:, s, :], Pk[:, s, :], XTb[:, s, :], False, k == 4, [Pkk, XTbk], [kA], skip_group_check=True)
                if k < 4:
                    nx = (k + 1) % 2
                    for (d, h, s) in combos:
                        self.MM(v4(BB)[:, s, :], PT[k][0][:, s, :], Pk[:, s, :], True, True, [PT[k][1], Pkk], [kB])
                    yield
                    self.CP("dve", Pm[nx][0][:], v4(BB)[:], [], [kB, Pm[nx][1]])
                    for (d, h, s) in combos:
                        self.MM(v4(BB)[:, s, :], Pk[:, s, :], PT[k][0][:, s, :], True, True, [PT[k][1], Pkk], [kB])
                    yield
                    self.CP("act", PT[k + 1][0][:], v4(BB)[:], [], [kB, PT[k + 1][1]])
                yield
            self.CP("act", XTb[:], v4(BA)[:], [], [kA, XTbk])
            yield
            pv = self.psbf(B_T, 8)
            for (d, h, s) in combos:
                self.TR(pv[:, s, :], XTb[:, s, :], self.identb[:], [XTbk, "identb"], [self.psk(B_T)])
            self.CP("dve", Xb[:], pv[:, 0:4, :], [], [self.psk(B_T), Xbk])
            first = True
            for (d, h, s) in combos:
                self.MM(v4(BB)[:, s, :], self.identb[:], self.identb[:], first, False, ["identb"], [kB], skip_group_check=True)
                first = False
                self.MM(v4(BB)[:, s, :], nidentb[:], XTb[:, s, :], False, False, ["nidentb", XTbk], [kB], skip_group_check=True)
                self.MM(v4(BB)[:, s, :], Pd0[:, s, :], XTb[:, s, :], False, True, [Pd0k, XTbk], [kB], skip_group_check=True)
            yield
            self.CP("act", Qb[:], v4(BB)[:], [], [kB, Qbk])
            yield
            first = True
            for (d, h, s) in combos:
                self.MM(v4(BB)[:, s, :], self.identb[:], XTb[:, s, :], first, False, ["identb", XTbk], [kB], skip_group_check=True)
                first = False
                self.MM(v4(BB)[:, s, :], Xb[:, s, :], Qb[:, s, :], False, True, [Xbk, Qbk], [kB], skip_group_check=True)
            yield
            self.CP("dve", XT2b[:], v4(BB)[:], [], [kB, XT2bk])
            yield
            for it in range(4):
                if it > 0:
                    first = True
                    for (d, h, s) in combos:
                        r0, r0k = P_["r0", d]
                        self.MM(v4(BB)[:, s, :], self.identb[:], r0[:, h, :], first, False, ["identb", r0k], [kB], skip_group_check=True)
                        first = False
                        self.MM(v4(BB)[:, s, :], PoT[:, s, :], xjb[:, s, :], False, True, [PoTk, xjbk], [kB], skip_group_check=True)
                    yield
                    self.CP("act", yjb[:], v4(BB)[:], [], [kB, yjbk])
                    yield
                for (d, h, s) in combos:
                    r0, r0k = P_["r0", d]
                    rhs = r0[:, h, :] if it == 0 else yjb[:, s, :]
                    rkeys = [r0k] if it == 0 else [yjbk]
                    self.MM(v4(BA)[:, s, :], XT2b[:, s, :], rhs, True, True, [XT2bk] + rkeys, [kA])
                yield
                if it < 3:
                    self.CP("dve", xjb[:], v4(BA)[:], [], [kA, xjbk])
                    yield
            for (d, h0, h1, s0) in g:
                n = h1 - h0
                ru, ruk = P_["ru", d]
                rwb, rwbk = P_["rwb", d]
                self.CP("dve", ru[:, h0:h1, :], v4(BA)[:, s0:s0 + n, 0:64], [], [kA, ruk])
                self.CP("act", rwb[:, h0:h1, :], v4(BA)[:, s0:s0 + n, 64:128], [], [kA, rwbk])

        def rec_dir(step, d):
            tt = tiles_of(step)
            t = tt[d]
            P_ = PS[step % 2]
            R_ = RT[d]
            wT, wTk = R_["wT"]
            Sst, Sstk = R_["Sst"]
            Sbf = R_["Sbf"]
            tmpS, tmpSk = R_["tmpS"]
            vpp, vppk = R_["vpp"]
            o1, o1k = R_["o1"]
            ru, ruk = P_["ru", d]
            rwb, rwbk = P_["rwb", d]
            Aq, Aqk_ = P_["AqkT", d]
            kdec, kdk = P_["kdec", d]
            kR = self.psk(B_REC)
            pv = self.psbf(B_T, 8)
            rwf = rwb[:].rearrange("p h e -> p (h e)")
            for c3 in range(3):
                self.TR(pv[:, c3, :], rwf[:, c3 * 128:(c3 + 1) * 128], self.identb[:], [rwbk, "identb"], [self.psk(B_T)])
            self.CP("act", wT[:], pv[:, 0:3, :], [], [self.psk(B_T), wTk])
            yield
            for h in range(6):
                p = h // 2
                self.MM(w6(B_REC)[:, h, :], wT[:, p, :], Sbf[h % 2][0][:, p, :], True, True, [wTk, Sbf[h % 2][1]], [kR])
            yield
            self.TT("dve", vpp[:], ru[:], w6(B_REC), ALU.subtract, [ruk], [kR, vppk])
            yield
            for h in range(6):
                p = h // 2
                self.MM(w6(B_REC)[:, h, :], qnT[:, p, t * 128:(t + 1) * 128], Sbf[h % 2][0][:, p, :], True, True, ["qnT", Sbf[h % 2][1]], [kR])
            yield
            self.TT("dve", o1[:], w6(B_REC), G["egc"][:, d, t, :].unsqueeze(2).broadcast_to([128, 6, 64]), ALU.mult, ["egc"], [kR, o1k])
            yield
            for h in range(6):
                self.MM(w6(B_REC)[:, h, :], Aq[:, h, :], vpp[:, h, :], True, True, [Aqk_, vppk], [kR])
            yield
            ot = osum[:, t, :].rearrange("p (h e) -> p h e", h=6)
            if t not in touched:
                touched.add(t)
                self.TT("dve", ot, o1[:], w6(B_REC), ALU.add, [o1k], [kR, ("osum", t)])
            else:
                self.TT("dve", o1[:], o1[:], w6(B_REC), ALU.add, [o1k], [kR, o1k])
                self.TT("pool", ot, ot, o1[:], ALU.add, [o1k, ("osum", t)], [("osum", t)])
            yield
            for h in range(6):
                p, base = h // 2, 64 * (h % 2)
                self.MM(self.ps[B_REC][base:base + 64, p * 64:(p + 1) * 64], kdec[:, h, :], vpp[:, h, :], True, True, [kdk, vppk], [kR])
            self.TT("pool", tmpS[:], Sst[:], egtS[:, d, t, :].unsqueeze(2).broadcast_to([128, 3, 64]), ALU.mult, [Sstk, "egtS"], [tmpSk])
            yield
            self.TT("dve", Sst[:], tmpS[:], self.ps[B_REC][:, 0:192].rearrange("p (a b) -> p a b", b=64), ALU.add, [tmpSk], [kR, Sstk])
            yield
            self.CP("act", Sbf[0][0][0:64], Sst[0:64], [Sstk], [Sbf[0][1]])
            self.CP("pool", Sbf[1][0][64:128], Sst[64:128], [Sstk], [Sbf[1][1]])

        ut_stream = [(st, gi) for st in range(NT) for gi in range(3)]
        ut_done = [0] * NT
        rec_done = [False] * NT
        rec_next = 0
        rec_active = 0
        active = []
        free_slots = [0, 1, 2]
        nxt = 0
        while True:
            while free_slots and nxt < len(ut_stream):
                st, gi = ut_stream[nxt]
                if gi == 0 and st >= 2 and not rec_done[st - 2]:
                    break
                nxt += 1
                if gi == 0:
                    prep(st)
                sl = free_slots.pop(0)
                active.append(["ut", ut_group(sl, st, gi), sl, st])
            if rec_active == 0 and rec_next < NT and ut_done[rec_next] == 3:
                def rec_step(st_):
                    yield from rec_dir(st_, 0)
                    yield
                    yield from rec_dir(st_, 1)
                active.append(["rec", rec_step(rec_next), None, rec_next])
                rec_active = 1
                rec_next += 1
            if not active:
                break
            for a_ in list(active):
                try:
                    next(a_[1])
                except StopIteration:
                    active.remove(a_)
                    if a_[0] == "ut":
                        free_slots.append(a_[2])
                        ut_done[a_[3]] += 1
                    else:
                        rec_active -= 1
                        rec_done[a_[3]] = True
        self.P.flush()
        p3.close()
        oss = self.sb(ph, "doss", [128, NT, 6], F32)
        y1 = self.sb(ph, "dy1", [128, 384], F32)
        nwz = self.sb(ph, "nwz", [128, 384], F32)
        ydel = [self.sb(ph, "ydel%d" % i, [128, 384], BF16) for i in range(2)]
        yT = self.sb(ph, "dyT", [128, 3, S], BF16)
        for t in range(NT):
            self.ACT(sqs[:, 0:384], osum[:, t, :], AF.Square, [("osum", t)], ["sqs"])
            self.REDUCE(oss[:, t, :], sqs[:, 0:384].rearrange("p (a b) -> p a b", b=64), ["sqs"], ["doss"])
        self.TS("dve", oss[:], oss[:], 1.0 / 64.0, ALU.mult, ["doss"], ["doss"], s2=1e-6, op1=ALU.add)
        self.ACT(oss[:], oss[:], AF.Sqrt, ["doss"], ["doss"])
        self.RECIP(oss[:], oss[:], ["doss"], ["doss"])
        for t in range(NT):
            b = t % 2
            self.TT("dve", y1[:].rearrange("p (h e) -> p h e", h=6), osum[:, t, :].rearrange("p (h e) -> p h e", h=6),
                    oss[:, t, :].unsqueeze(2).broadcast_to([128, 6, 64]), ALU.mult, [("osum", t), "doss"], ["dy1"])
            self.TT("pool", nwz[:].rearrange("p (h e) -> p h e", h=6), zst[:, t, :].rearrange("p (h e) -> p h e", h=6),
                    nw[:].unsqueeze(1).broadcast_to([128, 6, 64]), ALU.mult, ["zst", "nw"], ["nwz"])
            self.TT("dve", ydel[b][:], y1[:], nwz[:], ALU.mult, ["dy1", "nwz"], ["ydel%d" % b])
            bk = 4 + (t % 2)
            pv = self.psbf(bk, 8)
            for c3 in range(3):
                self.TR(pv[:, c3, :], ydel[b][:, c3 * 128:(c3 + 1) * 128], self.identb[:], ["ydel%d" % b, "identb"], [self.psk(bk)])
            self.CP("act", yT[:, :, t * 128:(t + 1) * 128], pv[:, 0:3, :], [], [self.psk(bk), "dyT"])
        for c3 in range(3):
            self.DMA("sp", self.ycatT[5 + c3], yT[:, c3, :], ["dyT"], ["ycatT"], "st_ydel")
        self.P.flush()


Builder.phase_delta = phase_delta
```

```python
import math
import os
from contextlib import ExitStack

import numpy as np
import concourse.bass as bass
import concourse.mybir as mybir
from concourse.bass_utils import run_bass_kernel_spmd

F32 = mybir.dt.float32
BF16 = mybir.dt.bfloat16
AF = mybir.ActivationFunctionType
ALU = mybir.AluOpType

D = 1024
IN_W = 3224
DFF = 4096
EPS = 1e-6
N_CORES = 8
SLOPES = [0.25, 0.0625, 0.015625, 0.00390625, 0.5, 0.125]
ALIBI_SKIP = 80.0


class Prog:
    QUEUES = ("pe", "act", "dve", "pool", "sp")

    def __init__(self, nc, es):
        self.nc = nc
        self.ops = []
        self.last_w = {}
        self.readers = {}
        self.sems = {}
        self.es = es
        self.cnt = {}
        self.waited = {q: {} for q in self.QUEUES}
        self.start = 0

    def sem(self, sig):
        if sig not in self.sems:
            nm = "s%d" % len(self.sems)
            self.sems[sig] = self.es.enter_context(self.nc.semaphore(nm))
        return self.sems[sig]

    def op(self, q, fn, reads=(), writes=(), dma=None):
        i = len(self.ops)
        sig = ("dma", dma) if dma is not None else q
        deps = {}
        for k in reads:
            w = self.last_w.get(k)
            if w is not None:
                deps[w] = True
        for k in writes:
            w = self.last_w.get(k)
            if w is not None:
                deps.setdefault(w, False)
            for r in self.readers.get(k, {}).values():
                deps.setdefault(r, False)
        need = []
        for j, raw in deps.items():
            oj = self.ops[j]
            if j < self.start:
                continue
            if oj["sig"] == q and dma is None:
                if q == "pe":
                    continue
            need.append(j)
            oj["needed"] = True
        self.ops.append(dict(q=q, fn=fn, sig=sig, deps=need, needed=(dma is not None), val=None))
        for k in writes:
            self.last_w[k] = i
            self.readers[k] = {}
        for k in reads:
            self.readers.setdefault(k, {})[sig] = i
        return i

    def wait_all_dma(self, q="sp"):
        need = []
        seen = set()
        for j in range(len(self.ops) - 1, -1, -1):
            oj = self.ops[j]
            if isinstance(oj["sig"], tuple) and oj["sig"] not in seen:
                seen.add(oj["sig"])
                need.append(j)
        self.ops.append(dict(q=q, fn=None, sig=q, deps=need, needed=False, val=None))

    def flush(self):
        self.wait_all_dma()
        nc = self.nc
        ops = self.ops[self.start:]
        self.start = len(self.ops)
        for o in ops:
            if o["needed"]:
                inc = 16 if isinstance(o["sig"], tuple) else 1
                self.cnt[o["sig"]] = self.cnt.get(o["sig"], 0) + inc
                o["val"] = self.cnt[o["sig"]]
                self.sem(o["sig"])
        allops = self.ops
        prog = self

        def run(qname, eng):
            waited = prog.waited[qname]
            for o in ops:
                if o["q"] != qname:
                    continue
                wl = {}
                for j in o["deps"]:
                    oj = allops[j]
                    wl[oj["sig"]] = max(wl.get(oj["sig"], 0), oj["val"])
                for sg, v in wl.items():
                    if waited.get(sg, 0) >= v:
                        continue
                    eng.wait_ge(prog.sems[sg], v)
                    waited[sg] = v
                if o["fn"] is not None:
                    inst = o["fn"](eng)
                    if o["needed"]:
                        inc = 16 if isinstance(o["sig"], tuple) else 1
                        inst.then_inc(prog.sems[o["sig"]], inc)

        with nc.Block() as block:
            @block.tensor
            def _(e):
                run("pe", e)

            @block.scalar
            def _(e):
                run("act", e)

            @block.vector
            def _(e):
                run("dve", e)

            @block.gpsimd
            def _(e):
                run("pool", e)

            @block.sync
            def _(e):
                run("sp", e)


class Builder:
    def __init__(self, S, depth, debug=False, phases=None):
        self.S = S
        self.depth = depth
        self.NT = S // 128
        self.TB = min(512, S)
        self.NTB = S // self.TB
        self.debug = debug
        self.phases = phases
        self.bank_rr = 0

    def ACT(self, out, in_, func, r, w, **kw):
        self.P.op("act", lambda e: e.activation(out=out, in_=in_, func=func, **kw), r, w)

    def TT(self, q, out, in0, in1, op, r, w):
        self.P.op(q, lambda e: e.tensor_tensor(out=out, in0=in0, in1=in1, op=op), r, w)

    def TS(self, q, out, in0, s1, op0, r, w, s2=None, op1=None):
        if op1 is None:
            self.P.op(q, lambda e: e.tensor_scalar(out=out, in0=in0, scalar1=s1, scalar2=None, op0=op0), r, w)
        else:
            self.P.op(q, lambda e: e.tensor_scalar(out=out, in0=in0, scalar1=s1, scalar2=s2, op0=op0, op1=op1), r, w)

    def STT(self, out, in0, scalar, in1, op0, op1, r, w):
        self.P.op("dve", lambda e: e.scalar_tensor_tensor(out=out, in0=in0, scalar=scalar, in1=in1, op0=op0, op1=op1), r, w)

    def CP(self, q, out, in_, r, w, scale=None):
        if q == "act":
            if scale is None:
                self.P.op("act", lambda e: e.activation(out=out, in_=in_, func=AF.Copy), r, w)
            else:
                self.P.op("act", lambda e: e.activation(out=out, in_=in_, func=AF.Copy, scale=scale), r, w)
        else:
            self.P.op(q, lambda e: e.tensor_copy(out=out, in_=in_), r, w)

    def MM(self, out, lhsT, rhs, start, stop, r, w, **kw):
        self.P.op("pe", lambda e: e.matmul(out, lhsT=lhsT, rhs=rhs, start=start, stop=stop, **kw), r, w)

    def TR(self, out, in_, ident, r, w):
        self.P.op("pe", lambda e: e.transpose(out=out, in_=in_, identity=ident), r, w)

    def DMA(self, q, out, in_, r, w, stream):
        self.P.op(q, lambda e: e.dma_start(out=out, in_=in_), r, w, dma=stream)

    def MEMSET(self, q, ap, val, w):
        self.P.op(q, lambda e: e.memset(ap, val), (), w)

    def RECIP(self, out, in_, r, w):
        self.P.op("dve", lambda e: e.reciprocal(out=out, in_=in_), r, w)

    def REDUCE(self, out, in_, r, w):
        self.P.op("dve", lambda e: e.tensor_reduce(out=out, in_=in_, axis=mybir.AxisListType.X, op=ALU.add), r, w)

    def ASEL(self, out, in_, pattern, cmp, fill, base, cm, r, w):
        self.P.op("pool", lambda e: e.affine_select(out=out, in_=in_, pattern=pattern, compare_op=cmp, fill=fill,
                                                    base=base, channel_multiplier=cm), r, w)

    def nb(self):
        b = self.bank_rr
        self.bank_rr = (self.bank_rr + 1) % 8
        return b

    def sb(self, ph, name, shape, dt):
        self.uid = getattr(self, "uid", 0) + 1
        return ph.enter_context(self.nc.sbuf_tensor("%s_u%d" % (name, self.uid), shape, dt))

    def psk(self, b):
        return "ps%d" % b

    def psbf(self, b, a):
        return self.ps[b][:].bitcast(BF16).rearrange("p (a b) -> p a b", a=a)

    def build(self):
        S, NT = self.S, self.NT
        L = self.depth
        nc = bass.Bass("TRN2", target_bir_lowering=False)
        self.nc = nc

        def din(name, shape, dt=F32):
            return nc.dram_tensor(name, shape, dt, kind="ExternalInput").ap()

        def dscr(name, shape, dt=F32):
            kind = "ExternalOutput" if self.debug else "Internal"
            return nc.dram_tensor(name, shape, dt, kind=kind).ap()

        self.x_in = din("x", [S, D])
        self.w_in = din("w_in", [L, D, IN_W])
        self.w_out = din("w_out", [L, D, D])
        self.w_ff1 = din("w_ff1", [L, D, DFF])
        self.w_ff2 = din("w_ff2", [L, DFF, D])
        self.normw = din("normw", [L, 4, D])
        self.convw = din("convw", [L, 256, 31])
        self.convp = din("convp", [L, 128, 2, 3])
        self.lam = din("lam", [L, 2, 2, 32])
        self.subw = din("subw", [L, 64])
        self.dconvw = din("dconvw", [L, 1152, 3])
        self.dgate = din("dgate", [L, 2, 12])
        self.dnormw = din("dnormw", [L, 64])
        self.dband = din("dband", [128, 2 * S - 128])
        self.bdmask = din("bdmask", [128, 128])
        self.out = nc.dram_tensor("out", [S, D], F32, kind="ExternalOutput").ap()
        self.xres = dscr("xres", [S, D])
        self.hg = dscr("hg", [2, 128, S], BF16)
        self.qT = dscr("qT", [4, 96, S], BF16)
        self.kT = dscr("kT", [4, 96, S], BF16)
        self.vtok = dscr("vtok", [S, 384], BF16)
        self.gqkvT = dscr("gqkvT", [9, 128, S])
        self.zs = dscr("zs", [S, 384], BF16)
        self.gates = dscr("gates", [S, 24])
        self.ycatT = dscr("ycatT", [8, 128, S], BF16)
        self.ypart = dscr("ypart", [S, D])

        with ExitStack() as es:
            self.P = Prog(nc, es)
            self.ps = [es.enter_context(nc.psum_tensor("psb%d" % i, [128, 512], F32)) for i in range(8)]
            self.identf = self.sb(es, "identf", [128, 128], F32)
            self.identb = self.sb(es, "identb", [128, 128], BF16)
            self.MEMSET("pool", self.identf[:], 1.0, ["identf"])
            self.ASEL(self.identf[:], self.identf[:], [[-1, 128]], ALU.is_equal, 0.0, 0, 1, ["identf"], ["identf"])
            self.CP("pool", self.identb[:], self.identf[:], ["identf"], ["identb"])
            self.P.flush()
            for l in range(L):
                self.layer(l)
            self.P.flush()
        return nc

    def want(self, name):
        return self.phases is None or name in self.phases

    def layer(self, l):
        S = self.S
        last = (l == self.depth - 1)
        src = self.x_in if l == 0 else self.xres
        if self.want("B"):
            self.phase_in_proj(l, src)
        self.merge_conv = self.want("C") and self.want("E")
        if self.want("C") and not self.merge_conv:
            self.phase_conv(l)
        if self.want("D"):
            self.phase_attn(l)
        if self.want("E"):
            self.phase_delta(l)
        if self.want("F") and self.want("G"):
            self.phase_out_mlp(l, src, self.out if last else self.xres)
        else:
            if self.want("F"):
                self.phase_out_proj(l, src)
            if self.want("G"):
                self.phase_mlp(l, self.out if last else self.xres)

    def norm_T(self, ph, src, wrow, hT):
        S, NT = self.S, self.NT
        xt = [self.sb(ph, "n_xt%d" % i, [128, D], F32) for i in range(2)]
        hb = [self.sb(ph, "n_hb%d" % i, [128, D], BF16) for i in range(2)]
        junk = self.sb(ph, "n_junk", [128, D], BF16)
        ssq = self.sb(ph, "n_ssq", [128, NT], F32)
        rstd = self.sb(ph, "n_rstd", [128, NT], F32)
        wbc = self.sb(ph, "n_wbc", [128, D], F32)
        self.DMA("sp", wbc[:], wrow.broadcast_to([128, D]), [], ["n_wbc"], "n_wbc")
        for t in range(NT):
            b = t % 2
            self.DMA("sp", xt[b][:], src[t * 128:(t + 1) * 128, :], [("x", t)], ["n_xt%d" % b], "n_xt%d" % b)
            self.ACT(junk[:], xt[b][:], AF.Square, ["n_xt%d" % b], ["n_junk", "n_ssq"], accum_out=ssq[:, t:t + 1])
        self.TS("dve", rstd[:], ssq[:], 1.0 / D, ALU.mult, ["n_ssq"], ["n_rstd"], s2=EPS, op1=ALU.add)
        self.ACT(rstd[:], rstd[:], AF.Sqrt, ["n_rstd"], ["n_rstd"])
        self.RECIP(rstd[:], rstd[:], ["n_rstd"], ["n_rstd"])
        for t in range(NT):
            b = t % 2
            self.DMA("sp", xt[b][:], src[t * 128:(t + 1) * 128, :], [("x", t)], ["n_xt%d" % b], "n_xt%d" % b)
            self.STT(hb[b][:], xt[b][:], rstd[:, t:t + 1], wbc[:], ALU.mult, ALU.mult,
                     ["n_xt%d" % b, "n_rstd", "n_wbc"], ["n_hb%d" % b])
            bk = self.nb()
            pv = self.psbf(bk, 8)
            for kc in range(8):
                self.TR(pv[:, kc, :], hb[b][:, kc * 128:(kc + 1) * 128], self.identb[:], ["n_hb%d" % b, "identb"], [self.psk(bk)])
            self.CP("act", hT[:, :, t * 128:(t + 1) * 128], pv[:, :, :], [], [self.psk(bk), "hT"])

    def phase_in_proj(self, l, src):
        S, NT, TB, NTB = self.S, self.NT, self.TB, self.NTB
        with ExitStack() as ph:
            hT = self.sb(ph, "hT", [128, 8, S], BF16)
            self.norm_T(ph, src, self.normw[l, 0:1, :], hT)
            groups = [("wA", 0, 512), ("wB", 512, 768), ("wC", 1280, 384), ("wD", 1664, 1152), ("wE", 2816, 408)]
            W = {}
            for nm, c0, n in groups:
                W[nm] = self.sb(ph, nm, [128, 8, n], BF16)
                self.DMA("pool", W[nm][:], self.w_in[l, :, c0:c0 + n].rearrange("(kc p) n -> p kc n", p=128), [], [nm], nm)
            hgs = self.sb(ph, "hgs", [128, 2, S], BF16)
            qs = self.sb(ph, "qs", [128, 4, S], BF16)
            ks = self.sb(ph, "ks", [128, 4, S], BF16)
            vs = self.sb(ph, "vs", [128, NT, 384], BF16)
            zst = self.sb(ph, "zst", [128, NT, 384], BF16)
            gst = self.sb(ph, "gst", [128, NT, 24], F32)
            sg = [self.sb(ph, "sg%d" % i, [128, TB], F32) for i in range(2)]
            dst = [self.sb(ph, "dst%d" % i, [128, TB], F32) for i in range(3)]
            i = 0
            for cc in range(2):
                for tb in range(NTB):
                    ts_ = slice(tb * TB, (tb + 1) * TB)
                    ba, bg = self.nb(), self.nb()
                    for kc in range(8):
                        self.MM(self.ps[bg][:, :TB], W["wA"][:, kc, 256 + cc * 128:256 + (cc + 1) * 128], hT[:, kc, ts_], kc == 0, kc == 7, ["wA", "hT"], [self.psk(bg)])
                    for kc in range(8):
                        self.MM(self.ps[ba][:, :TB], W["wA"][:, kc, cc * 128:(cc + 1) * 128], hT[:, kc, ts_], kc == 0, kc == 7, ["wA", "hT"], [self.psk(ba)])
                    s = i % 2
                    i += 1
                    self.ACT(sg[s][:], self.ps[bg][:, :TB], AF.Sigmoid, [], [self.psk(bg), "sg%d" % s])
                    self.TT("dve", hgs[:, cc, ts_], self.ps[ba][:, :TB], sg[s][:], ALU.mult, ["sg%d" % s], [self.psk(ba), "hgs"])
            for cc in range(2):
                self.DMA("sp", self.hg[cc], hgs[:, cc, :], ["hgs"], ["hg"], "st_hg")
            for which, stg, dst_d, scale in (("q", qs, self.qT, 32.0 ** -0.5), ("k", ks, self.kT, None)):
                base = 0 if which == "q" else 384
                for c in range(4):
                    for tb in range(NTB):
                        ts_ = slice(tb * TB, (tb + 1) * TB)
                        bk = self.nb()
                        for kc in range(8):
                            self.MM(self.ps[bk][0:96, :TB], W["wB"][:, kc, base + 96 * c:base + 96 * (c + 1)], hT[:, kc, ts_], kc == 0, kc == 7, ["wB", "hT"], [self.psk(bk)])
                        self.CP("act", stg[0:96, c, ts_], self.ps[bk][0:96, :TB], [], [self.psk(bk), which + "s"], scale=scale)
                for c in range(4):
                    self.DMA("sp", dst_d[c], stg[0:96, c, :], [which + "s"], [which + "T"], "st_" + which)
            for t in range(NT):
                bk = self.nb()
                for kc in range(8):
                    self.MM(self.ps[bk][:, :384], hT[:, kc, t * 128:(t + 1) * 128], W["wC"][:, kc, :], kc == 0, kc == 7, ["wC", "hT"], [self.psk(bk)])
                self.CP("dve", vs[:, t, :], self.ps[bk][:, :384], [], [self.psk(bk), "vs"])
            self.DMA("sp", self.vtok.rearrange("(t p) n -> p t n", p=128), vs[:], ["vs"], ["vtok"], "st_v")
            i = 0
            for c in range(9):
                for tb in range(NTB):
                    ts_ = slice(tb * TB, (tb + 1) * TB)
                    bk = self.nb()
                    for kc in range(8):
                        self.MM(self.ps[bk][:, :TB], W["wD"][:, kc, c * 128:(c + 1) * 128], hT[:, kc, ts_], kc == 0, kc == 7, ["wD", "hT"], [self.psk(bk)])
                    s = i % 3
                    i += 1
                    self.CP("dve" if i % 2 else "act", dst[s][:], self.ps[bk][:, :TB], [], [self.psk(bk), "dst%d" % s])
                    self.DMA("sp", self.gqkvT[c, :, ts_], dst[s][:], ["dst%d" % s], ["gqkvT"], "st_d%d" % s)
            for t in range(NT):
                bk = self.nb()
                for kc in range(8):
                    self.MM(self.ps[bk][:, :408], hT[:, kc, t * 128:(t + 1) * 128], W["wE"][:, kc, :], kc == 0, kc == 7, ["wE", "hT"], [self.psk(bk)])
                self.ACT(zst[:, t, :], self.ps[bk][:, :384], AF.Silu, [], [self.psk(bk), "zst"])
                self.CP("dve", gst[:, t, :], self.ps[bk][:, 384:408], [], [self.psk(bk), "gst"])
            self.DMA("sp", self.zs.rearrange("(t p) n -> p t n", p=128), zst[:], ["zst"], ["zs"], "st_z")
            self.DMA("sp", self.gates.rearrange("(t p) n -> p t n", p=128), gst[:], ["gst"], ["gates"], "st_g")
            self.P.flush()

    def post_norm_residual(self, ph_tiles, banks, t, src, dst, wbc, extra=None):
        xt, ytmp, small, junk = ph_tiles
        b = t % 2
        self.DMA("sp", xt[b][:], src[t * 128:(t + 1) * 128, :], [("x", t)], ["r_xt%d" % b], "r_xt%d" % b)
        ysrc = []
        for hf in range(2):
            bk = banks[hf]
            if extra is not None:
                etile, ekey = extra
                self.TT("dve", ytmp[b][:, hf * 512:(hf + 1) * 512], self.ps[bk][:, :], etile[:, hf * 512:(hf + 1) * 512], ALU.add,
                        [ekey], [self.psk(bk), "r_yt%d" % b])
            else:
                self.CP("dve", ytmp[b][:, hf * 512:(hf + 1) * 512], self.ps[bk][:, :], [], [self.psk(bk), "r_yt%d" % b])
        sm = small[b]
        self.ACT(junk[:], ytmp[b][:], AF.Square, ["r_yt%d" % b], ["r_junk", "r_sm%d" % b], accum_out=sm[:, 0:1])
        self.TS("dve", sm[:, 1:2], sm[:, 0:1], 1.0 / D, ALU.mult, ["r_sm%d" % b], ["r_sm%d" % b], s2=EPS, op1=ALU.add)
        self.ACT(sm[:, 2:3], sm[:, 1:2], AF.Sqrt, ["r_sm%d" % b], ["r_sm%d" % b])
        self.RECIP(sm[:, 3:4], sm[:, 2:3], ["r_sm%d" % b], ["r_sm%d" % b])
        self.STT(ytmp[b][:], ytmp[b][:], sm[:, 3:4], wbc[:], ALU.mult, ALU.mult, ["r_yt%d" % b, "r_sm%d" % b, "r_wbc"], ["r_yt%d" % b])
        self.TT("pool", xt[b][:], xt[b][:], ytmp[b][:], ALU.add, ["r_xt%d" % b, "r_yt%d" % b], ["r_xt%d" % b])
        self.DMA("sp", dst[t * 128:(t + 1) * 128, :], xt[b][:], ["r_xt%d" % b], [("x", t)], "r_st%d" % b)

    def res_tiles(self, ph):
        xt = [self.sb(ph, "r_xt%d" % i, [128, D], F32) for i in range(2)]
        ytmp = [self.sb(ph, "r_yt%d" % i, [128, D], F32) for i in range(2)]
        small = [self.sb(ph, "r_sm%d" % i, [128, 4], F32) for i in range(2)]
        junk = self.sb(ph, "r_junk", [128, D], BF16)
        return xt, ytmp, small, junk

    def phase_out_proj(self, l, src):
        S, NT = self.S, self.NT
        with ExitStack() as ph:
            yT = self.sb(ph, "ycat", [128, 8, S], BF16)
            wo = self.sb(ph, "wo", [128, 8, D], BF16)
            wbc = self.sb(ph, "r_wbc", [128, D], F32)
            tiles = self.res_tiles(ph)
            for c in range(8):
                self.DMA("sp", yT[:, c, :], self.ycatT[c], ["ycatT"], ["ycat"], "ld_ycat")
            self.DMA("pool", wo[:], self.w_out[l].rearrange("(kc p) n -> p kc n", p=128), [], ["wo"], "ld_wo")
            self.DMA("sp", wbc[:], self.normw[l, 1:2, :].broadcast_to([128, D]), [], ["r_wbc"], "ld_rwbc")
            for t in range(NT):
                banks = [self.nb(), self.nb()]
                for hf in range(2):
                    for kc in range(8):
                        self.MM(self.ps[banks[hf]][:, :], yT[:, kc, t * 128:(t + 1) * 128], wo[:, kc, hf * 512:(hf + 1) * 512], kc == 0, kc == 7,
                                ["ycat", "wo"], [self.psk(banks[hf])])
                self.post_norm_residual(tiles, banks, t, src, self.xres, wbc)
            self.P.flush()

    def phase_out_mlp(self, l, src, dst):
        S, NT = self.S, self.NT
        with ExitStack() as ph:
            hT = self.sb(ph, "hT", [128, 8, S], BF16)
            w1 = self.sb(ph, "w1", [128, 8, 2048], BF16)
            w2 = self.sb(ph, "w2", [128, 16, D], BF16)
            self.DMA("pool", w1[:], self.w_ff1[l, :, 0:2048].rearrange("(kc p) n -> p kc n", p=128), [], ["w1"], "ld_w1")
            self.DMA("pool", w2[:], self.w_ff2[l, 0:2048, :].rearrange("(j p) n -> p j n", p=128), [], ["w2"], "ld_w2")
            with ExitStack() as ph2:
                yT = self.sb(ph2, "ycat", [128, 8, S], BF16)
                wo = self.sb(ph2, "wo", [128, 8, D], BF16)
                wbc = self.sb(ph2, "r_wbc", [128, D], F32)
                tiles = self.res_tiles(ph2)
                for c in range(8):
                    self.DMA("sp", yT[:, c, :], self.ycatT[c], ["ycatT"], ["ycat"], "ld_ycat")
                self.DMA("pool", wo[:], self.w_out[l].rearrange("(kc p) n -> p kc n", p=128), [], ["wo"], "ld_wo")
                self.DMA("sp", wbc[:], self.normw[l, 1:2, :].broadcast_to([128, D]), [], ["r_wbc"], "ld_rwbc")
                for t in range(NT):
                    banks = [self.nb(), self.nb()]
                    for hf in range(2):
                        for kc in range(8):
                            self.MM(self.ps[banks[hf]][:, :], yT[:, kc, t * 128:(t + 1) * 128], wo[:, kc, hf * 512:(hf + 1) * 512], kc == 0, kc == 7,
                                    ["ycat", "wo"], [self.psk(banks[hf])])
                    self.post_norm_residual(tiles, banks, t, src, self.xres, wbc)
                self.norm_T(ph2, self.xres, self.normw[l, 2:3, :], hT)
                self.P.flush()
            self.mlp_main(ph, l, dst, hT, w1, w2, True)

    def phase_mlp(self, l, dst):
        S = self.S
        with ExitStack() as ph:
            hT = self.sb(ph, "hT", [128, 8, S], BF16)
            with ExitStack() as ph2:
                self.norm_T(ph2, self.xres, self.normw[l, 2:3, :], hT)
                self.P.flush()
            w1 = self.sb(ph, "w1", [128, 8, 2048], BF16)
            w2 = self.sb(ph, "w2", [128, 16, D], BF16)
            self.mlp_main(ph, l, dst, hT, w1, w2, False)

    def mlp_main(self, ph, l, dst, hT, w1, w2, first_loaded):
        S, NT, TB, NTB = self.S, self.NT, self.TB, self.NTB
        NQ = TB // 128
        if True:
            h1 = [self.sb(ph, "h1_%d" % i, [128, 16, TB], BF16) for i in range(2)]
            rl = [self.sb(ph, "rl%d" % i, [128, TB], BF16) for i in range(2)]
            yp = [self.sb(ph, "yp%d" % i, [128, D], F32) for i in range(2)]
            wbc = self.sb(ph, "r_wbc", [128, D], F32)
            tiles = self.res_tiles(ph)
            self.DMA("sp", wbc[:], self.normw[l, 3:4, :].broadcast_to([128, D]), [], ["r_wbc"], "ld_rwbc")
            for half in range(2):
                f0 = half * 2048
                if not (first_loaded and half == 0):
                    self.DMA("pool", w1[:], self.w_ff1[l, :, f0:f0 + 2048].rearrange("(kc p) n -> p kc n", p=128), [], ["w1"], "ld_w1")
                    self.DMA("pool", w2[:], self.w_ff2[l, f0:f0 + 2048, :].rearrange("(j p) n -> p j n", p=128), [], ["w2"], "ld_w2")
                for tb in range(NTB):
                    ts_ = slice(tb * TB, (tb + 1) * TB)
                    hb = h1[tb % 2]
                    hk = "h1_%d" % (tb % 2)
                    for j in range(16):
                        bk = self.nb()
                        for kc in range(8):
                            self.MM(self.ps[bk][:, :TB], w1[:, kc, j * 128:(j + 1) * 128], hT[:, kc, ts_], kc == 0, kc == 7, ["w1", "hT"], [self.psk(bk)])
                        r_ = j % 2
                        self.ACT(rl[r_][:], self.ps[bk][:, :TB], AF.Relu, [], [self.psk(bk), "rl%d" % r_])
                        self.TT("pool" if j % 2 else "dve", hb[:, j, :], rl[r_][:], rl[r_][:], ALU.mult, ["rl%d" % r_], [hk])
                    for ti in range(NQ):
                        t = tb * NQ + ti
                        banks = [self.nb(), self.nb()]
                        for hf in range(2):
                            for j in range(16):
                                self.MM(self.ps[banks[hf]][:, :], hb[:, j, ti * 128:(ti + 1) * 128], w2[:, j, hf * 512:(hf + 1) * 512], j == 0, j == 15,
                                        [hk, "w2"], [self.psk(banks[hf])])
                        if half == 0:
                            b = t % 2
                            for hf in range(2):
                                self.CP("dve" if hf else "act", yp[b][:, hf * 512:(hf + 1) * 512], self.ps[banks[hf]][:, :], [], [self.psk(banks[hf]), "yp%d" % b])
                            self.DMA("sp", self.ypart[t * 128:(t + 1) * 128, :], yp[b][:], ["yp%d" % b], [("ypart", t)], "st_yp%d" % b)
                        else:
                            b = t % 2
                            self.DMA("sp", yp[b][:], self.ypart[t * 128:(t + 1) * 128, :], [("ypart", t)], ["yp%d" % b], "ld_yp%d" % b)
                            self.post_norm_residual(tiles, banks, t, self.xres, dst, wbc, extra=(yp[b], "yp%d" % b))
            self.P.flush()


def prep_inputs(inp, S):
    f = lambda a: np.ascontiguousarray(np.asarray(a, dtype=np.float32))
    L = inp["w_in"].shape[0]
    shared = {
        "w_in": f(inp["w_in"]), "w_out": f(inp["w_out"]), "w_ff1": f(inp["w_ff1"]), "w_ff2": f(inp["w_ff2"]),
        "normw": f(np.stack([inp["pre_mix_w"], inp["post_mix_w"], inp["pre_mlp_w"], inp["post_mlp_w"]], axis=1)),
        "convw": f(np.transpose(np.asarray(inp["conv_dw_w"]), (0, 2, 1))),
        "convp": f(np.stack([np.asarray(inp["conv_dw_b"]).reshape(L, 2, 128), np.asarray(inp["conv_ln_w"]).reshape(L, 2, 128),
                             np.asarray(inp["conv_ln_b"]).reshape(L, 2, 128)], axis=-1).transpose(0, 2, 1, 3)),
        "lam": f(np.stack([np.stack([inp["diff_lambda_q1"], inp["diff_lambda_q2"]], axis=1),
                           np.stack([inp["diff_lambda_k1"], inp["diff_lambda_k2"]], axis=1)], axis=1)),
        "subw": f(inp["diff_subln_w"]),
        "dconvw": f(np.transpose(np.asarray(inp["delta_conv_w"]), (0, 2, 1))),
        "dgate": f(np.stack([np.asarray(inp["delta_A_log"]).reshape(L, 12), np.asarray(inp["delta_dt_bias"]).reshape(L, 12)], axis=1)),
        "dnormw": f(inp["delta_norm_w"]),
    }
    W = 2 * S - 128
    shared["dband"] = f(np.abs(np.arange(W)[None, :] - (S - 128) - np.arange(128)[:, None]))
    shared["bdmask"] = f((np.arange(128)[:, None] // 32) == (np.arange(128)[None, :] // 32))
    return shared


_NC_CACHE = {}


def kernel(**inputs):
    x = np.asarray(inputs["x"], dtype=np.float32)
    B, S, _ = x.shape
    L = inputs["w_in"].shape[0]
    shared = prep_inputs(inputs, S)
    key = (S, L)
    if key not in _NC_CACHE:
        _NC_CACHE[key] = Builder(S, L).build()
    nc = _NC_CACHE[key]
    in_maps = []
    for b in range(B):
        m = dict(shared)
        m["x"] = np.ascontiguousarray(x[b])
        in_maps.append(m)
    res = run_bass_kernel_spmd(nc, in_maps, core_ids=list(range(B)))
    return np.stack([np.asarray(r["out"], dtype=np.float32) for r in res.results], axis=0)


def phase_conv(self, l):
    with ExitStack() as ph:
        for _ in self.conv_body(ph, l):
            pass
        self.P.flush()


def conv_body(self, ph, l):
    S, NT, TB, NTB = self.S, self.NT, self.TB, self.NTB
    if True:
        hgp = self.sb(ph, "hgp", [128, 2, S + 30], BF16)
        wcol = self.sb(ph, "wcol", [128, 2, 31], F32)
        cpar = self.sb(ph, "cpar", [128, 2, 3], F32)
        diag = self.sb(ph, "diag", [128, 62, 128], BF16)
        onesb = self.sb(ph, "onesb", [128, 128], BF16)
        yb = [self.sb(ph, "yb%d" % i, [128, TB], F32) for i in range(2)]
        yh = [self.sb(ph, "yh%d" % i, [128, TB], BF16) for i in range(2)]
        zq = [self.sb(ph, "zq%d" % i, [128, TB], BF16) for i in range(2)]
        mean = self.sb(ph, "mean", [128, TB], F32)
        var = self.sb(ph, "var", [128, TB], F32)
        zt = [self.sb(ph, "zt%d" % i, [128, TB], F32) for i in range(2)]
        yout = self.sb(ph, "yout", [128, 2, S], BF16)
        self.MEMSET("pool", hgp[:, :, 0:15], 0.0, ["hgp"])
        self.MEMSET("pool", hgp[:, :, S + 15:S + 30], 0.0, ["hgp"])
        self.MEMSET("pool", onesb[:], 1.0 / 256.0, ["onesb"])
        for cc in range(2):
            self.DMA("sp", hgp[:, cc, 15:15 + S], self.hg[cc], ["hg"], ["hgp"], "ld_hgp")
        self.DMA("sp", wcol[:], self.convw[l].rearrange("(cc p) j -> p cc j", p=128), [], ["wcol"], "ld_wcol")
        self.DMA("sp", cpar[:], self.convp[l], [], ["cpar"], "ld_cpar")
        for cc in range(2):
            for j in range(31):
                if j % 3 == 1:
                    self.ACT(diag[:, cc * 31 + j, :], self.identf[:], AF.Copy, ["wcol", "identf"], ["diag"], scale=wcol[:, cc, j:j + 1])
                else:
                    self.TS("dve" if j % 3 == 0 else "pool", diag[:, cc * 31 + j, :], self.identf[:], wcol[:, cc, j:j + 1], ALU.mult, ["wcol", "identf"], ["diag"])
        yield
        for tb in range(NTB):
            ts_ = slice(tb * TB, (tb + 1) * TB)
            for cc in range(2):
                bk = self.nb()
                for j in range(31):
                    self.MM(self.ps[bk][:, :TB], diag[:, cc * 31 + j, :], hgp[:, cc, tb * TB + j:tb * TB + j + TB], j == 0, j == 30,
                            ["diag", "hgp"], [self.psk(bk)])
                self.ACT(yb[cc][:], self.ps[bk][:, :TB], AF.Identity, ["cpar"], [self.psk(bk), "yb%d" % cc], bias=cpar[:, cc, 0:1])
                self.CP("dve", yh[cc][:], yb[cc][:], ["yb%d" % cc], ["yh%d" % cc])
            bm = self.nb()
            for cc in range(2):
                self.MM(self.ps[bm][:, :TB], onesb[:], yh[cc][:], cc == 0, cc == 1, ["onesb", "yh%d" % cc], [self.psk(bm)])
            self.CP("dve", mean[:], self.ps[bm][:, :TB], [], [self.psk(bm), "mean"])
            for cc in range(2):
                self.TT("pool" if cc else "dve", zt[cc][:], yb[cc][:], mean[:], ALU.subtract, ["yb%d" % cc, "mean"], ["zt%d" % cc])
                self.ACT(zq[cc][:], zt[cc][:], AF.Square, ["zt%d" % cc], ["zq%d" % cc])
            be = self.nb()
            for cc in range(2):
                self.MM(self.ps[be][:, :TB], onesb[:], zq[cc][:], cc == 0, cc == 1, ["onesb", "zq%d" % cc], [self.psk(be)])
            self.TS("dve", var[:], self.ps[be][:, :TB], 1e-5, ALU.add, [], [self.psk(be), "var"])
            self.ACT(var[:], var[:], AF.Sqrt, ["var"], ["var"])
            self.RECIP(var[:], var[:], ["var"], ["var"])
            for cc in range(2):
                self.TT("pool" if cc else "dve", zt[cc][:], zt[cc][:], var[:], ALU.mult, ["zt%d" % cc, "var"], ["zt%d" % cc])
                self.ACT(yout[:, cc, ts_], zt[cc][:], AF.Silu, ["zt%d" % cc, "cpar"], ["yout"], scale=cpar[:, cc, 1:2], bias=cpar[:, cc, 2:3])
            yield
        for cc in range(2):
            self.DMA("sp", self.ycatT[cc], yout[:, cc, :], ["yout"], ["ycatT"], "st_yconv")


Builder.phase_conv = phase_conv
Builder.conv_body = conv_body


def phase_attn(self, l):
    S, NT = self.S, self.NT
    QB = min(512, S)
    NQB = S // QB
    NQ = QB // 128
    linit = 0.8 - 0.6 * math.exp(-0.3 * l)
    with ExitStack() as ph:
        QT = self.sb(ph, "QT", [128, 4, S], BF16)
        KT = self.sb(ph, "KT", [128, 4, S], BF16)
        V = self.sb(ph, "V", [128, NT, 6, 128], BF16)
        dband = self.sb(ph, "dband", [128, 2 * S - 128], F32)
        lamv = self.sb(ph, "lamv", [128, 2, 2, 32], F32)
        lprod = self.sb(ph, "lprod", [128, 2, 32], F32)
        lsm = self.sb(ph, "lsm", [128, 4], F32)
        wsub = self.sb(ph, "wsub", [128, 64], F32)
        ydiff = self.sb(ph, "ydiff", [128, NT, 384], BF16)
        yT = self.sb(ph, "yT", [128, 3, S], BF16)
        sc = [self.sb(ph, "sc%d" % i, [128, QB], F32) for i in range(5)]
        ET = [self.sb(ph, "ET%d" % i, [128, QB], BF16) for i in range(6)]
        rs = self.sb(ph, "rs", [128, NQ, 1], F32)
        oTb = self.sb(ph, "oTb", [65, QB], BF16)
        o1 = self.sb(ph, "o1", [128, NQ, 64], F32)
        o2 = self.sb(ph, "o2", [128, NQ, 64], F32)
        osq = self.sb(ph, "osq", [128, NQ, 64], F32)
        oss = self.sb(ph, "oss", [128, NQ], F32)
        for c in range(4):
            self.DMA("sp", QT[0:96, c, :], self.qT[c], ["qT"], ["QT"], "ld_QT")
            self.DMA("sp", KT[0:96, c, :], self.kT[c], ["kT"], ["KT"], "ld_KT")
        QTm = None
        if os.environ.get("K_QM", "0") == "1":
            QTm = self.sb(ph, "QTm", [128, 12, S], BF16)
            self.MEMSET("pool", QTm[0:96, :, :], 0.0, ["QTm"])
            for mi_ in range(12):
                c_, r_ = mi_ // 3, 32 * (mi_ % 3)
                eng_ = ("act", "dve", "pool")[mi_ % 3]
                self.CP(eng_, QTm[r_:r_ + 32, mi_, :], QT[r_:r_ + 32, c_, :], ["QT", "QTm"], ["QTm"])
        self.MEMSET("pool", V[:, :, :, 65:128], 0.0, ["V"])
        self.MEMSET("pool", V[:, :, :, 64:65], 1.0, ["V"])
        for t in range(NT):
            self.DMA("sp", V[:, t, :, 0:64], self.vtok[t * 128:(t + 1) * 128, :].rearrange("p (h e) -> p h e", h=6), ["vtok"], ["V"], "ld_V")
        self.DMA("sp", dband[:], self.dband, [], ["dband"], "ld_dband")
        self.DMA("sp", lamv[:].rearrange("p a b c -> p (a b c)"), self.lam[l:l + 1].rearrange("o a b c -> o (a b c)").broadcast_to([128, 128]), [], ["lamv"], "ld_lam")
        self.DMA("sp", wsub[:], self.subw[l:l + 1, :].broadcast_to([128, 64]), [], ["wsub"], "ld_subw")
        self.TT("dve", lprod[:], lamv[:, 0, :, :], lamv[:, 1, :, :], ALU.mult, ["lamv"], ["lprod"])
        self.REDUCE(lsm[:, 0:2], lprod[:], ["lprod"], ["lsm"])
        self.ACT(lsm[:, 0:2], lsm[:, 0:2], AF.Exp, ["lsm"], ["lsm"])
        self.TT("dve", lsm[:, 2:3], lsm[:, 1:2], lsm[:, 0:1], ALU.subtract, ["lsm"], ["lsm"])
        self.TS("dve", lsm[:, 3:4], lsm[:, 2:3], -linit, ALU.add, ["lsm"], ["lsm"])
        self.TS("dve", wsub[:], wsub[:], 1.0 - linit, ALU.mult, ["wsub"], ["wsub"])
        SB = [0, 1, 2, 5, 6]
        AB = [3, 4]
        LA = 4
        blocks = []
        ia = 0
        for h in range(6):
            for qb in range(NQB):
                for j in range(2):
                    ab = AB[ia % 2]
                    ia += 1
                    kts = []
                    for kt in range(NT):
                        dmin = max(0, kt * 128 - (qb * QB + QB - 1), qb * QB - (kt * 128 + 127))
                        if ALIBI_SKIP is None or dmin * SLOPES[h] <= ALIBI_SKIP:
                            kts.append(kt)
                    for kt in kts:
                        blocks.append((h, qb, j, kt, ab, kt == kts[0], kt == kts[-1]))
        nblk = len(blocks)

        def front(it):
            h, qb, j, kt, ab, kfirst, klast = blocks[it]
            m = SLOPES[h]
            mi = 2 * h + j
            c = mi // 3
            r0 = 32 * (mi % 3)
            sbk = SB[it % 5]
            si = it % 5
            ei = it % 6
            if QTm is not None:
                self.MM(self.ps[sbk][:, :QB], KT[0:96, c, kt * 128:(kt + 1) * 128], QTm[0:96, mi, qb * QB:(qb + 1) * QB], True, True,
                        ["KT", "QTm"], [self.psk(sbk)])
            else:
                self.MM(self.ps[sbk][:, :QB], KT[r0:r0 + 32, c, kt * 128:(kt + 1) * 128], QT[r0:r0 + 32, c, qb * QB:(qb + 1) * QB], True, True,
                        ["KT", "QT"], [self.psk(sbk)])
            for _ in range(int(os.environ.get("K_FILL", "0"))):
                self.MM(self.ps[7][:, :QB], KT[:, 0, 0:128], QT[:, 0, 0:QB], True, True, ["KT", "QT"], ["ps7"])
            off = qb * QB - kt * 128 + S - 128
            self.STT(sc[si][:], dband[:, off:off + QB], -m, self.ps[sbk][:, :QB], ALU.mult, ALU.add, ["dband"], [self.psk(sbk), "sc%d" % si])
            self.ACT(ET[ei][:], sc[si][:], AF.Exp, ["sc%d" % si], ["ET%d" % ei])

        pending = []

        def epilogue(h, qb, j, ab):
            accv = self.ps[ab][:, 0:NQ * 65].rearrange("p (a b) -> p a b", b=65)
            self.RECIP(rs[:], accv[:, :, 64:65], [], [self.psk(ab), "rs"])
            if j == 0:
                self.TT("dve", o1[:], accv[:, :, 0:64], rs[:].broadcast_to([128, NQ, 64]), ALU.mult, ["rs"], [self.psk(ab), "o1"])
                return
            self.TT("dve", o2[:], accv[:, :, 0:64], rs[:].broadcast_to([128, NQ, 64]), ALU.mult, ["rs"], [self.psk(ab), "o2"])
            self.STT(o2[:], o2[:], lsm[:, 3:4], o1[:], ALU.mult, ALU.add, ["o2", "o1", "lsm"], ["o2"])
            self.TT("pool", osq[:], o2[:], o2[:], ALU.mult, ["o2"], ["osq"])
            yield
            yield
            self.REDUCE(oss[:], osq[:], ["osq"], ["oss"])
            self.TS("dve", oss[:], oss[:], 1.0 / 64.0, ALU.mult, ["oss"], ["oss"], s2=1e-5, op1=ALU.add)
            self.ACT(oss[:], oss[:], AF.Sqrt, ["oss"], ["oss"])
            yield
            yield
            self.RECIP(oss[:], oss[:], ["oss"], ["oss"])
            self.TT("dve", o2[:], o2[:], oss[:].unsqueeze(2).broadcast_to([128, NQ, 64]), ALU.mult, ["o2", "oss"], ["o2"])
            self.TT("pool", ydiff[:, qb * NQ:(qb + 1) * NQ, h * 64:(h + 1) * 64], o2[:], wsub[:].unsqueeze(1).broadcast_to([128, NQ, 64]), ALU.mult,
                    ["o2", "wsub"], ["ydiff"])

        def back(it):
            h, qb, j, kt, ab, kfirst, klast = blocks[it]
            ei = it % 6
            accv = self.ps[ab][:, 0:NQ * 65].rearrange("p (a b) -> p a b", b=65)
            for qi in range(NQ):
                self.MM(accv[:, qi, :], ET[ei][:, qi * 128:(qi + 1) * 128], V[:, kt, h, 0:65], (kfirst and qi == 0), klast,
                        ["ET%d" % ei, "V"], [self.psk(ab)], skip_group_check=True)
            if klast:
                while pending:
                    pump()
                pending.append(epilogue(h, qb, j, ab))

        def pump():
            for g_ in list(pending):
                try:
                    next(g_)
                except StopIteration:
                    pending.remove(g_)

        for it in range(nblk + LA):
            if it < nblk:
                front(it)
            if it >= LA:
                back(it - LA)
            pump()
        while pending:
            pump()
        for t in range(NT):
            bk = 5 + (t % 2)
            pv = self.psbf(bk, 8)
            for c3 in range(3):
                self.TR(pv[:, c3, :], ydiff[:, t, c3 * 128:(c3 + 1) * 128], self.identb[:], ["ydiff", "identb"], [self.psk(bk)])
            self.CP("act", yT[:, :, t * 128:(t + 1) * 128], pv[:, 0:3, :], [], [self.psk(bk), "yT"])
        for c3 in range(3):
            self.DMA("sp", self.ycatT[2 + c3], yT[:, c3, :], ["yT"], ["ycatT"], "st_ydiff")
        self.P.flush()


Builder.phase_attn = phase_attn


def phase_delta(self, l):
    S, NT = self.S, self.NT
    H = 6
    with ExitStack() as ph:
        tm = self.sb(ph, "tm", [128, NT, 1152], BF16)
        with ExitStack() as p1:
            wc = self.sb(p1, "wc", [128, 9, 3], F32)
            xp = [self.sb(p1, "xp%d" % i, [128, S + 2], F32) for i in range(2)]
            acc = [self.sb(p1, "acc%d" % i, [128, S], F32) for i in range(2)]
            sT = [self.sb(p1, "sT%d" % i, [128, S], BF16) for i in range(2)]
            self.DMA("sp", wc[:], self.dconvw[l].rearrange("(c p) j -> p c j", p=128), [], ["wc"], "ld_wc")
            for i in range(2):
                self.MEMSET("pool", xp[i][:, 0:1], 0.0, ["xp%d" % i])
                self.MEMSET("pool", xp[i][:, S + 1:S + 2], 0.0, ["xp%d" % i])
            cgen = self.conv_body(p1, l) if (self.want("C") and self.merge_conv) else iter(())
            for c in range(9):
                b = c % 2
                next(cgen, None)
                self.DMA("sp", xp[b][:, 1:S + 1], self.gqkvT[c], ["gqkvT"], ["xp%d" % b], "ld_xp%d" % b)
                self.TS("dve", acc[b][:], xp[b][:, 0:S], wc[:, c, 0:1], ALU.mult, ["xp%d" % b, "wc"], ["acc%d" % b])
                self.STT(acc[b][:], xp[b][:, 1:S + 1], wc[:, c, 1:2], acc[b][:], ALU.mult, ALU.add, ["xp%d" % b, "wc", "acc%d" % b], ["acc%d" % b])
                self.STT(acc[b][:], xp[b][:, 2:S + 2], wc[:, c, 2:3], acc[b][:], ALU.mult, ALU.add, ["xp%d" % b, "wc", "acc%d" % b], ["acc%d" % b])
                self.ACT(sT[b][:], acc[b][:], AF.Silu, ["acc%d" % b], ["sT%d" % b])
                for t0 in range(0, NT, 8):
                    n = min(8, NT - t0)
                    bk = 6 + ((c * 2 + t0 // 8) % 2)
                    pv = self.psbf(bk, 8)
                    for i in range(n):
                        t = t0 + i
                        self.TR(pv[:, i, :], sT[b][:, t * 128:(t + 1) * 128], self.identb[:], ["sT%d" % b, "identb"], [self.psk(bk)])
                    self.CP("act" if (t0 // 8) % 2 else "dve", tm[:, t0:t0 + n, c * 128:(c + 1) * 128], pv[:, 0:n, :], [], [self.psk(bk), "tm"])
            for _ in cgen:
                pass
            self.P.flush()
        onesb = self.sb(ph, "onesb1", [128, 128], BF16)
        cf = self.sb(ph, "cf", [128, 128], F32)
        Lmat = [self.sb(ph, "Lmat%d" % d, [128, 128], BF16) for d in range(2)]
        maskneg = [self.sb(ph, "maskneg%d" % d, [128, 128], F32) for d in range(2)]
        nstr = [self.sb(ph, "nstr%d" % d, [128, 128], F32) for d in range(2)]
        self.MEMSET("pool", onesb[:], 1.0, ["onesb1"])
        for d in range(2):
            pat, cm = ([[1, 128]], -1) if d == 0 else ([[-1, 128]], 1)
            self.MEMSET("pool", cf[:], 1.0, ["cf"])
            self.ASEL(cf[:], cf[:], pat, ALU.is_ge, 0.0, 0, cm, ["cf"], ["cf"])
            self.CP("pool", Lmat[d][:], cf[:], ["cf"], ["Lmat%d" % d])
            self.MEMSET("pool", maskneg[d][:], 0.0, ["maskneg%d" % d])
            self.ASEL(maskneg[d][:], maskneg[d][:], pat, ALU.is_ge, -1e30, 0, cm, ["maskneg%d" % d], ["maskneg%d" % d])
            self.MEMSET("pool", nstr[d][:], -1.0, ["nstr%d" % d])
            self.ASEL(nstr[d][:], nstr[d][:], pat, ALU.is_gt, 0.0, 0, cm, ["nstr%d" % d], ["nstr%d" % d])
        groups = [[(0, 0, 4, 0)], [(0, 4, 6, 0), (1, 0, 2, 2)], [(1, 2, 6, 0)]]
        bd = self.sb(ph, "bd", [128, 128], F32)
        self.DMA("sp", bd[:], self.bdmask, [], ["bd"], "ld_bd")
        nstrd = [self.sb(ph, "nstrd%d" % d, [128, 128], F32) for d in range(2)]
        nstro = [self.sb(ph, "nstro%d" % d, [128, 128], F32) for d in range(2)]
        for d in range(2):
            self.TT("pool", nstrd[d][:], nstr[d][:], bd[:], ALU.mult, ["nstr%d" % d, "bd"], ["nstrd%d" % d])
            self.TT("pool", nstro[d][:], nstr[d][:], nstrd[d][:], ALU.subtract, ["nstr%d" % d, "nstrd%d" % d], ["nstro%d" % d])
        nstrg_d, nstrg_o = [], []
        for gi, g in enumerate(groups):
            tld = self.sb(ph, "nstrgd%d" % gi, [128, 4, 128], BF16)
            tlo = self.sb(ph, "nstrgo%d" % gi, [128, 4, 128], BF16)
            for (d, h0, h1, s0) in g:
                n = h1 - h0
                self.CP("pool", tld[:, s0:s0 + n, :], nstrd[d][:].unsqueeze(1).broadcast_to([128, n, 128]), ["nstrd%d" % d], ["nstrgd%d" % gi])
                self.CP("pool", tlo[:, s0:s0 + n, :], nstro[d][:].unsqueeze(1).broadcast_to([128, n, 128]), ["nstro%d" % d], ["nstrgo%d" % gi])
            nstrg_d.append(tld)
            nstrg_o.append(tlo)
        qnT = self.sb(ph, "qnT", [128, 3, S], BF16)
        zst = self.sb(ph, "zst", [128, NT, 384], BF16)
        osum = self.sb(ph, "osum", [128, NT, 384], F32)
        sqs = self.sb(ph, "sqs", [128, 768], F32)
        ssq = self.sb(ph, "ssq", [128, NT, 12], F32)
        gin = self.sb(ph, "gin", [128, NT, 24], F32)
        dg = self.sb(ph, "dg", [128, 2, 12], F32)
        nw = self.sb(ph, "nw", [128, 64], F32)
        G = {}
        for nm in ("sbt", "gl", "gc", "egc", "gt", "egt", "edec", "r1"):
            G[nm] = self.sb(ph, "g_" + nm, [128, 2, NT, 6], F32)
        gls = [self.sb(ph, "gls%d" % i, [128, 2, NT, 6], BF16) for i in range(3)]
        gcs = [self.sb(ph, "gcs%d" % i, [128, 2, NT, 6], BF16) for i in range(3)]
        egtS = self.sb(ph, "egtS", [128, 2, NT, 3], F32)
        self.DMA("sp", zst[:], self.zs.rearrange("(t p) n -> p t n", p=128), ["zs"], ["zst"], "ld_zs")
        self.DMA("sp", gin[:], self.gates.rearrange("(t p) n -> p t n", p=128), ["gates"], ["gin"], "ld_gin")
        self.DMA("sp", dg[:].rearrange("p a b -> p (a b)"), self.dgate[l:l + 1].rearrange("o a b -> o (a b)").broadcast_to([128, 24]), [], ["dg"], "ld_dg")
        self.DMA("sp", nw[:], self.dnormw[l:l + 1, :].broadcast_to([128, 64]), [], ["nw"], "ld_nw")
        for t in range(NT):
            self.ACT(sqs[:], tm[:, t, 0:768], AF.Square, ["tm"], ["sqs"])
            self.REDUCE(ssq[:, t, :], sqs[:].rearrange("p (a b) -> p a b", b=64), ["sqs"], ["ssq"])
        self.TS("dve", ssq[:], ssq[:], 1e-6, ALU.add, ["ssq"], ["ssq"])
        self.ACT(ssq[:], ssq[:], AF.Sqrt, ["ssq"], ["ssq"])
        self.RECIP(ssq[:], ssq[:], ["ssq"], ["ssq"])
        self.TS("dve", ssq[:, :, 0:6], ssq[:, :, 0:6], 0.125, ALU.mult, ["ssq"], ["ssq"])
        for t in range(NT):
            self.TT("dve", tm[:, t, 0:384].rearrange("p (h e) -> p h e", h=6), tm[:, t, 0:384].rearrange("p (h e) -> p h e", h=6),
                    ssq[:, t, 0:6].unsqueeze(2).broadcast_to([128, 6, 64]), ALU.mult, ["tm", "ssq"], ["tm"])
            self.TT("pool", tm[:, t, 384:768].rearrange("p (h e) -> p h e", h=6), tm[:, t, 384:768].rearrange("p (h e) -> p h e", h=6),
                    ssq[:, t, 6:12].unsqueeze(2).broadcast_to([128, 6, 64]), ALU.mult, ["tm", "ssq"], ["tm"])
            bk = 6 + (t % 2)
            pv = self.psbf(bk, 8)
            for c3 in range(3):
                self.TR(pv[:, c3, :], tm[:, t, c3 * 128:(c3 + 1) * 128], self.identb[:], ["tm", "identb"], [self.psk(bk)])
            self.CP("act", qnT[:, :, t * 128:(t + 1) * 128], pv[:, 0:3, :], [], [self.psk(bk), "qnT"])
        self.ACT(dg[:, 0, :], dg[:, 0, :], AF.Exp, ["dg"], ["dg"])
        self.TS("dve", dg[:, 0, :], dg[:, 0, :], -1.0, ALU.mult, ["dg"], ["dg"])
        for d in range(2):
            bsl = gin[:, :, d * 6:(d + 1) * 6]
            asl = gin[:, :, 12 + d * 6:12 + (d + 1) * 6]
            self.ACT(G["sbt"][:, d, :, :], bsl, AF.Sigmoid, ["gin"], ["sbt"])
            self.ACT(G["sbt"][:, d, :, :], G["sbt"][:, d, :, :], AF.Sqrt, ["sbt"], ["sbt"])
            self.TT("dve", G["gl"][:, d, :, :], asl, dg[:, 1, d * 6:(d + 1) * 6].unsqueeze(1).broadcast_to([128, NT, 6]), ALU.add, ["gin", "dg"], ["gl"])
            self.ACT(G["gl"][:, d, :, :], G["gl"][:, d, :, :], AF.Exp, ["gl"], ["gl"])
            self.TS("dve", G["gl"][:, d, :, :], G["gl"][:, d, :, :], 1.0, ALU.add, ["gl"], ["gl"])
            self.ACT(G["gl"][:, d, :, :], G["gl"][:, d, :, :], AF.Ln, ["gl"], ["gl"])
            self.TT("dve", G["gl"][:, d, :, :], G["gl"][:, d, :, :], dg[:, 0, d * 6:(d + 1) * 6].unsqueeze(1).broadcast_to([128, NT, 6]), ALU.mult, ["gl", "dg"], ["gl"])

        def split3(src, dst, key_src, key_dst):
            r1 = G["r1"]
            self.CP("dve", dst[0][:], src[:], [key_src], [key_dst])
            self.TT("dve", r1[:], src[:], dst[0][:], ALU.subtract, [key_src, key_dst], ["r1"])
            self.CP("dve", dst[1][:], r1[:], ["r1"], [key_dst])
            self.TT("dve", r1[:], r1[:], dst[1][:], ALU.subtract, ["r1", key_dst], ["r1"])
            self.CP("dve", dst[2][:], r1[:], ["r1"], [key_dst])

        split3(G["gl"], gls, "gl", "gls")
        for d in range(2):
            bk = self.nb()
            for i in range(3):
                self.MM(self.ps[bk][:, 0:NT * 6], Lmat[d][:], gls[i][:, d, :, :].rearrange("p t h -> p (t h)"), i == 0, i == 2, ["Lmat%d" % d, "gls"], [self.psk(bk)])
            self.CP("dve", G["gc"][:, d, :, :].rearrange("p t h -> p (t h)"), self.ps[bk][:, 0:NT * 6], [], [self.psk(bk), "gc"])
            bk = self.nb()
            for i in range(3):
                self.MM(self.ps[bk][:, 0:NT * 6], onesb[:], gls[i][:, d, :, :].rearrange("p t h -> p (t h)"), i == 0, i == 2, ["onesb1", "gls"], [self.psk(bk)])
            self.CP("dve", G["gt"][:, d, :, :].rearrange("p t h -> p (t h)"), self.ps[bk][:, 0:NT * 6], [], [self.psk(bk), "gt"])
        self.ACT(G["egc"][:], G["gc"][:], AF.Exp, ["gc"], ["egc"])
        self.ACT(G["egt"][:], G["gt"][:], AF.Exp, ["gt"], ["egt"])
        self.TT("dve", G["edec"][:], G["gt"][:], G["gc"][:], ALU.subtract, ["gt", "gc"], ["edec"])
        self.ACT(G["edec"][:], G["edec"][:], AF.Exp, ["edec"], ["edec"])
        split3(G["gc"], gcs, "gc", "gcs")
        ev = G["egt"][:].rearrange("p d t (a two) -> p d t a two", two=2)
        self.CP("pool", egtS[0:64], ev[0:64, :, :, :, 0], ["egt"], ["egtS"])
        self.CP("pool", egtS[64:128], ev[64:128, :, :, :, 1], ["egt"], ["egtS"])
        p3 = ExitStack()

        def mk(name, shape, dt):
            return (self.sb(p3, name, shape, dt), name)

        PS = []
        for pb in range(2):
            d_ = {}
            for d in range(2):
                d_["r0", d] = mk("r0_%d%d" % (pb, d), [128, 6, 128], BF16)
                d_["kp", d] = mk("kp_%d%d" % (pb, d), [128, 6, 64], BF16)
                d_["kdec", d] = mk("kdec_%d%d" % (pb, d), [128, 6, 64], BF16)
                for hf in range(2):
                    d_["kpT", d, hf] = mk("kpT_%d%d%d" % (pb, d, hf), [128, 3, 128], BF16)
                    self.MEMSET("pool", d_["kpT", d, hf][0][:], 0.0, [d_["kpT", d, hf][1]])
                d_["AqkT", d] = mk("AqkT_%d%d" % (pb, d), [128, 6, 128], BF16)
                d_["ru", d] = mk("ru_%d%d" % (pb, d), [128, 6, 64], F32)
                d_["rwb", d] = mk("rwb_%d%d" % (pb, d), [128, 6, 64], BF16)
            PS.append(d_)
        RT = []
        for d in range(2):
            d_ = {}
            d_["wT"] = mk("wT%d" % d, [128, 3, 128], BF16)
            d_["Sst"] = mk("Sst%d" % d, [128, 3, 64], F32)
            d_["Sbf"] = [mk("Sbf%d_%d" % (d, hf), [128, 3, 64], BF16) for hf in range(2)]
            d_["tmpS"] = mk("tmpS%d" % d, [128, 3, 64], F32)
            d_["vpp"] = mk("vpp%d" % d, [128, 6, 64], BF16)
            d_["o1"] = mk("do1_%d" % d, [128, 6, 64], F32)
            self.MEMSET("pool", d_["Sst"][0][:], 0.0, [d_["Sst"][1]])
            for hf in range(2):
                self.MEMSET("pool", d_["Sbf"][hf][0][:], 0.0, [d_["Sbf"][hf][1]])
            RT.append(d_)
        SL = []
        for sl in range(3):
            d_ = {}
            d_["dgi"] = [mk("dgi%d_%d" % (sl, i), [128, 4, 128], BF16) for i in range(3)]
            d_["d0"] = mk("d0_%d" % sl, [128, 4, 128], F32)
            d_["tmpk"] = mk("tmpk%d" % sl, [128, 4, 128], F32)
            d_["PT"] = [mk("PT%d_%d" % (sl, i), [128, 4, 128], BF16) for i in range(5)]
            d_["Pm"] = [mk("Pm%d_%d" % (sl, i), [128, 4, 128], BF16) for i in range(2)]
            d_["Pd0"] = mk("Pd0_%d" % sl, [128, 4, 128], BF16)
            d_["PoT"] = mk("PoT%d" % sl, [128, 4, 128], BF16)
            d_["XTb"] = mk("XTb%d" % sl, [128, 4, 128], BF16)
            d_["banks"] = (2 * sl, 2 * sl + 1)
            SL.append(d_)
        nidentb = self.sb(p3, "nidentb", [128, 128], BF16)
        self.TS("pool", nidentb[:], self.identf[:], -1.0, ALU.mult, ["identf"], ["nidentb"])
        touched = set()
        B_T, B_REC = 6, 7

        def v4(bk):
            return self.ps[bk][:].rearrange("p (a b) -> p a b", b=128)

        def w6(bk):
            return self.ps[bk][:, 0:384].rearrange("p (h e) -> p h e", h=6)

        def tiles_of(step):
            return [step, NT - 1 - step]

        def prep(step):
            tt = tiles_of(step)
            P_ = PS[step % 2]
            for d in range(2):
                t = tt[d]
                kp, kpk = P_["kp", d]
                r0, r0k = P_["r0", d]
                kdec, kdk = P_["kdec", d]
                sb6 = G["sbt"][:, d, t, :].unsqueeze(2).broadcast_to([128, 6, 64])
                kn6 = tm[:, t, 384:768].rearrange("p (h e) -> p h e", h=6)
                v6 = tm[:, t, 768:1152].rearrange("p (h e) -> p h e", h=6)
                self.TT("pool", kp[:], kn6, sb6, ALU.mult, ["tm", "sbt"], [kpk])
                self.TT("pool", r0[:, :, 0:64], v6, sb6, ALU.mult, ["tm", "sbt"], [r0k])
                self.TT("pool", r0[:, :, 64:128], kp[:], G["egc"][:, d, t, :].unsqueeze(2).broadcast_to([128, 6, 64]), ALU.mult, [kpk, "egc"], [r0k])
                self.TT("pool", kdec[:], kp[:], G["edec"][:, d, t, :].unsqueeze(2).broadcast_to([128, 6, 64]), ALU.mult, [kpk, "edec"], [kdk])
                pv = self.psbf(B_T, 8)
                kpf = kp[:].rearrange("p h e -> p (h e)")
                for c3 in range(3):
                    self.TR(pv[:, c3, :], kpf[:, c3 * 128:(c3 + 1) * 128], self.identb[:], [kpk, "identb"], [self.psk(B_T)])
                self.CP("act", P_["kpT", d, 0][0][0:64], pv[0:64, 0:3, :], [], [self.psk(B_T), P_["kpT", d, 0][1]])
                self.CP("dve", P_["kpT", d, 1][0][64:128], pv[64:128, 0:3, :], [], [self.psk(B_T), P_["kpT", d, 1][1]])

        def ut_group(sl, step, gi):
            g = groups[gi]
            tt = tiles_of(step)
            P_ = PS[step % 2]
            T_ = SL[sl]
            BA, BB = T_["banks"]
            kA, kB = self.psk(BA), self.psk(BB)
            dgi = T_["dgi"]
            d0, d0k = T_["d0"]
            tmpk, tmpkk = T_["tmpk"]
            PT = T_["PT"]
            Pm = T_["Pm"]
            Pd0, Pd0k = T_["Pd0"]
            PoT, PoTk = T_["PoT"]
            XTb, XTbk = T_["XTb"]
            Xb, Xbk = Pm[0]
            Qb, Qbk = Pm[1]
            XT2b, XT2bk = PT[1]
            xjb, xjbk = PT[2]
            yjb, yjbk = PT[3]
            combos = []
            for (d, h0, h1, s0) in g:
                for h in range(h0, h1):
                    combos.append((d, h, s0 + h - h0))
            for i in range(3):
                for (d, h0, h1, s0) in g:
                    n = h1 - h0
                    self.TT("pool", dgi[i][0][:, s0:s0 + n, :], self.identb[:].unsqueeze(1).broadcast_to([128, n, 128]),
                            gcs[i][:, d, tt[d], h0:h1].unsqueeze(2).broadcast_to([128, n, 128]), ALU.mult, ["identb", "gcs"], [dgi[i][1]])
            first = True
            for (d, h, s) in combos:
                for i in range(3):
                    self.MM(v4(BA)[:, s, :], onesb[:], dgi[i][0][:, s, :], first, False, ["onesb1", dgi[i][1]], [kA], skip_group_check=True)
                    first = False
            for (d, h, s) in combos:
                p = h // 2
                kT_, kTk = P_["kpT", d, h % 2]
                self.MM(v4(BB)[:, s, :], kT_[:, p, :], kT_[:, p, :], True, True, [kTk], [kB])
            yield
            for (d, h, s) in combos:
                self.STT(d0[:, s, :], v4(BA)[:, s, :], G["gc"][:, d, tt[d], h:h + 1], maskneg[d][:], ALU.subtract, ALU.add,
                         ["gc", "maskneg%d" % d], [kA, d0k])
            self.ACT(d0[:], d0[:], AF.Exp, [d0k], [d0k])
            yield
            self.TT("dve", tmpk[:], v4(BB)[:], d0[:], ALU.mult, [d0k], [kB, tmpkk])
            for (d, h, s) in combos:
                p = h // 2
                t = tt[d]
                kT_, kTk = P_["kpT", d, h % 2]
                self.MM(v4(BB)[:, s, :], kT_[:, p, :], qnT[:, p, t * 128:(t + 1) * 128], True, True, [kTk, "qnT"], [kB])
            self.TT("pool", PT[0][0][:], tmpk[:], nstrg_d[gi][:], ALU.mult, [tmpkk, "nstrgd%d" % gi], [PT[0][1]])
            self.TT("pool", PoT[:], tmpk[:], nstrg_o[gi][:], ALU.mult, [tmpkk, "nstrgo%d" % gi], [PoTk])
            yield
            for (d, h0, h1, s0) in g:
                n = h1 - h0
                Aq, Aqk_ = P_["AqkT", d]
                self.TT("dve", Aq[:, h0:h1, :], v4(BB)[:, s0:s0 + n, :], d0[:, s0:s0 + n, :], ALU.mult, [d0k], [kB, Aqk_])
            yield
            pv = self.psbf(B_T, 8)
            for (d, h, s) in combos:
                self.TR(pv[:, s, :], PT[0][0][:, s, :], self.identb[:], [PT[0][1], "identb"], [self.psk(B_T)])
            self.CP("act", Pd0[:], pv[:, 0:4, :], [], [self.psk(B_T), Pd0k])
            yield
            first = True
            for (d, h, s) in combos:
                self.MM(v4(BA)[:, s, :], self.identb[:], self.identb[:], first, False, ["identb"], [kA], skip_group_check=True)
                first = False
                self.MM(v4(BA)[:, s, :], Pd0[:, s, :], self.identb[:], False, False, [Pd0k, "identb"], [kA], skip_group_check=True)
            for k in range(5):
                Pk, Pkk = (Pd0, Pd0k) if k == 0 else Pm[k % 2]
                if k > 0:
                    self.CP("act", XTb[:], v4(BA)[:], [], [kA, XTbk])
                    for (d, h, s) in combos:
                        self.MM(v4(BA)[:, s, :], Pk[:, s, :], XTb[:, s, :], False, k == 4, [Pkk, XTbk], [kA], skip_group_check=True)
                if k < 4:
                    nx = (k + 1) % 2
                    for (d, h, s) in combos:
                        self.MM(v4(BB)[:, s, :], PT[k][0][:, s, :], Pk[:, s, :], True, True, [PT[k][1], Pkk], [kB])
                    yield
                    self.CP("dve", Pm[nx][0][:], v4(BB)[:], [], [kB, Pm[nx][1]])
                    for (d, h, s) in combos:
                        self.MM(v4(BB)[:, s, :], Pk[:, s, :], PT[k][0][:, s, :], True, True, [PT[k][1], Pkk], [kB])
                    yield
                    self.CP("act", PT[k + 1][0][:], v4(BB)[:], [], [kB, PT[k + 1][1]])
                yield
            self.CP("act", XTb[:], v4(BA)[:], [], [kA, XTbk])
            yield
            pv = self.psbf(B_T, 8)
            for (d, h, s) in combos:
                self.TR(pv[:, s, :], XTb[:, s, :], self.identb[:], [XTbk, "identb"], [self.psk(B_T)])
            self.CP("dve", Xb[:], pv[:, 0:4, :], [], [self.psk(B_T), Xbk])
            first = True
            for (d, h, s) in combos:
                self.MM(v4(BB)[:, s, :], self.identb[:], self.identb[:], first, False, ["identb"], [kB], skip_group_check=True)
                first = False
                self.MM(v4(BB)[:, s, :], nidentb[:], XTb[:, s, :], False, False, ["nidentb", XTbk], [kB], skip_group_check=True)
                self.MM(v4(BB)[:, s, :], Pd0[:, s, :], XTb[:, s, :], False, True, [Pd0k, XTbk], [kB], skip_group_check=True)
            yield
            self.CP("act", Qb[:], v4(BB)[:], [], [kB, Qbk])
            yield
            first = True
            for (d, h, s) in combos:
                self.MM(v4(BB)[:, s, :], self.identb[:], XTb[:, s, :], first, False, ["identb", XTbk], [kB], skip_group_check=True)
                first = False
                self.MM(v4(BB)[:, s, :], Xb[:, s, :], Qb[:, s, :], False, True, [Xbk, Qbk], [kB], skip_group_check=True)
            yield
            self.CP("dve", XT2b[:], v4(BB)[:], [], [kB, XT2bk])
            yield
            for it in range(4):
                if it > 0:
                    first = True
                    for (d, h, s) in combos:
                        r0, r0k = P_["r0", d]
                        self.MM(v4(BB)[:, s, :], self.identb[:], r0[:, h, :], first, False, ["identb", r0k], [kB], skip_group_check=True)
                        first = False
                        self.MM(v4(BB)[:, s, :], PoT[:, s, :], xjb[:, s, :], False, True, [PoTk, xjbk], [kB], skip_group_check=True)
                    yield
                    self.CP("act", yjb[:], v4(BB)[:], [], [kB, yjbk])
                    yield
                for (d, h, s) in combos:
                    r0, r0k = P_["r0", d]
                    rhs = r0[:, h, :] if it == 0 else yjb[:, s, :]
                    rkeys = [r0k] if it == 0 else [yjbk]
                    self.MM(v4(BA)[:, s, :], XT2b[:, s, :], rhs, True, True, [XT2bk] + rkeys, [kA])
                yield
                if it < 3:
                    self.CP("dve", xjb[:], v4(BA)[:], [], [kA, xjbk])
                    yield
            for (d, h0, h1, s0) in g:
                n = h1 - h0
                ru, ruk = P_["ru", d]
                rwb, rwbk = P_["rwb", d]
                self.CP("dve", ru[:, h0:h1, :], v4(BA)[:, s0:s0 + n, 0:64], [], [kA, ruk])
                self.CP("act", rwb[:, h0:h1, :], v4(BA)[:, s0:s0 + n, 64:128], [], [kA, rwbk])

        def rec_dir(step, d):
            tt = tiles_of(step)
            t = tt[d]
            P_ = PS[step % 2]
            R_ = RT[d]
            wT, wTk = R_["wT"]
            Sst, Sstk = R_["Sst"]
            Sbf = R_["Sbf"]
            tmpS, tmpSk = R_["tmpS"]
            vpp, vppk = R_["vpp"]
            o1, o1k = R_["o1"]
            ru, ruk = P_["ru", d]
            rwb, rwbk = P_["rwb", d]
            Aq, Aqk_ = P_["AqkT", d]
            kdec, kdk = P_["kdec", d]
            kR = self.psk(B_REC)
            pv = self.psbf(B_T, 8)
            rwf = rwb[:].rearrange("p h e -> p (h e)")
            for c3 in range(3):
                self.TR(pv[:, c3, :], rwf[:, c3 * 128:(c3 + 1) * 128], self.identb[:], [rwbk, "identb"], [self.psk(B_T)])
            self.CP("act", wT[:], pv[:, 0:3, :], [], [self.psk(B_T), wTk])
            yield
            for h in range(6):
                p = h // 2
                self.MM(w6(B_REC)[:, h, :], wT[:, p, :], Sbf[h % 2][0][:, p, :], True, True, [wTk, Sbf[h % 2][1]], [kR])
            yield
            self.TT("dve", vpp[:], ru[:], w6(B_REC), ALU.subtract, [ruk], [kR, vppk])
            yield
            for h in range(6):
                p = h // 2
                self.MM(w6(B_REC)[:, h, :], qnT[:, p, t * 128:(t + 1) * 128], Sbf[h % 2][0][:, p, :], True, True, ["qnT", Sbf[h % 2][1]], [kR])
            yield
            self.TT("dve", o1[:], w6(B_REC), G["egc"][:, d, t, :].unsqueeze(2).broadcast_to([128, 6, 64]), ALU.mult, ["egc"], [kR, o1k])
            yield
            for h in range(6):
                self.MM(w6(B_REC)[:, h, :], Aq[:, h, :], vpp[:, h, :], True, True, [Aqk_, vppk], [kR])
            yield
            ot = osum[:, t, :].rearrange("p (h e) -> p h e", h=6)
            if t not in touched:
                touched.add(t)
                self.TT("dve", ot, o1[:], w6(B_REC), ALU.add, [o1k], [kR, ("osum", t)])
            else:
                self.TT("dve", o1[:], o1[:], w6(B_REC), ALU.add, [o1k], [kR, o1k])
                self.TT("pool", ot, ot, o1[:], ALU.add, [o1k, ("osum", t)], [("osum", t)])
            yield
            for h in range(6):
                p, base = h // 2, 64 * (h % 2)
                self.MM(self.ps[B_REC][base:base + 64, p * 64:(p + 1) * 64], kdec[:, h, :], vpp[:, h, :], True, True, [kdk, vppk], [kR])
            self.TT("pool", tmpS[:], Sst[:], egtS[:, d, t, :].unsqueeze(2).broadcast_to([128, 3, 64]), ALU.mult, [Sstk, "egtS"], [tmpSk])
            yield
            self.TT("dve", Sst[:], tmpS[:], self.ps[B_REC][:, 0:192].rearrange("p (a b) -> p a b", b=64), ALU.add, [tmpSk], [kR, Sstk])
            yield
            self.CP("act", Sbf[0][0][0:64], Sst[0:64], [Sstk], [Sbf[0][1]])
            self.CP("pool", Sbf[1][0][64:128], Sst[64:128], [Sstk], [Sbf[1][1]])

        ut_stream = [(st, gi) for st in range(NT) for gi in range(3)]
        ut_done = [0] * NT
        rec_done = [False] * NT
        rec_next = 0
        rec_active = 0
        active = []
        free_slots = [0, 1, 2]
        nxt = 0
        while True:
            while free_slots and nxt < len(ut_stream):
                st, gi = ut_stream[nxt]
                if gi == 0 and st >= 2 and not rec_done[st - 2]:
                    break
                nxt += 1
                if gi == 0:
                    prep(st)
                sl = free_slots.pop(0)
                active.append(["ut", ut_group(sl, st, gi), sl, st])
            if rec_active == 0 and rec_next < NT and ut_done[rec_next] == 3:
                def rec_step(st_):
                    yield from rec_dir(st_, 0)
                    yield
                    yield from rec_dir(st_, 1)
                active.append(["rec", rec_step(rec_next), None, rec_next])
                rec_active = 1
                rec_next += 1
            if not active:
                break
            for a_ in list(active):
                try:
                    next(a_[1])
                except StopIteration:
                    active.remove(a_)
                    if a_[0] == "ut":
                        free_slots.append(a_[2])
                        ut_done[a_[3]] += 1
                    else:
                        rec_active -= 1
                        rec_done[a_[3]] = True
        self.P.flush()
        p3.close()
        oss = self.sb(ph, "doss", [128, NT, 6], F32)
        y1 = self.sb(ph, "dy1", [128, 384], F32)
        nwz = self.sb(ph, "nwz", [128, 384], F32)
        ydel = [self.sb(ph, "ydel%d" % i, [128, 384], BF16) for i in range(2)]
        yT = self.sb(ph, "dyT", [128, 3, S], BF16)
        for t in range(NT):
            self.ACT(sqs[:, 0:384], osum[:, t, :], AF.Square, [("osum", t)], ["sqs"])
            self.REDUCE(oss[:, t, :], sqs[:, 0:384].rearrange("p (a b) -> p a b", b=64), ["sqs"], ["doss"])
        self.TS("dve", oss[:], oss[:], 1.0 / 64.0, ALU.mult, ["doss"], ["doss"], s2=1e-6, op1=ALU.add)
        self.ACT(oss[:], oss[:], AF.Sqrt, ["doss"], ["doss"])
        self.RECIP(oss[:], oss[:], ["doss"], ["doss"])
        for t in range(NT):
            b = t % 2
            self.TT("dve", y1[:].rearrange("p (h e) -> p h e", h=6), osum[:, t, :].rearrange("p (h e) -> p h e", h=6),
                    oss[:, t, :].unsqueeze(2).broadcast_to([128, 6, 64]), ALU.mult, [("osum", t), "doss"], ["dy1"])
            self.TT("pool", nwz[:].rearrange("p (h e) -> p h e", h=6), zst[:, t, :].rearrange("p (h e) -> p h e", h=6),
                    nw[:].unsqueeze(1).broadcast_to([128, 6, 64]), ALU.mult, ["zst", "nw"], ["nwz"])
            self.TT("dve", ydel[b][:], y1[:], nwz[:], ALU.mult, ["dy1", "nwz"], ["ydel%d" % b])
            bk = 4 + (t % 2)
            pv = self.psbf(bk, 8)
            for c3 in range(3):
                self.TR(pv[:, c3, :], ydel[b][:, c3 * 128:(c3 + 1) * 128], self.identb[:], ["ydel%d" % b, "identb"], [self.psk(bk)])
            self.CP("act", yT[:, :, t * 128:(t + 1) * 128], pv[:, 0:3, :], [], [self.psk(bk), "dyT"])
        for c3 in range(3):
            self.DMA("sp", self.ycatT[5 + c3], yT[:, c3, :], ["dyT"], ["ycatT"], "st_ydel")
        self.P.flush()


Builder.phase_delta = phase_delta
```

```python
import math
import os
from contextlib import ExitStack

import numpy as np
import concourse.bass as bass
import concourse.mybir as mybir
from concourse.bass_utils import run_bass_kernel_spmd

F32 = mybir.dt.float32
BF16 = mybir.dt.bfloat16
AF = mybir.ActivationFunctionType
ALU = mybir.AluOpType

D = 1024
IN_W = 3224
DFF = 4096
EPS = 1e-6
N_CORES = 8
SLOPES = [0.25, 0.0625, 0.015625, 0.00390625, 0.5, 0.125]
ALIBI_SKIP = 80.0


class Prog:
    QUEUES = ("pe", "act", "dve", "pool", "sp")

    def __init__(self, nc, es):
        self.nc = nc
        self.ops = []
        self.last_w = {}
        self.readers = {}
        self.sems = {}
        self.es = es
        self.cnt = {}
        self.waited = {q: {} for q in self.QUEUES}
        self.start = 0

    def sem(self, sig):
        if sig not in self.sems:
            nm = "s%d" % len(self.sems)
            self.sems[sig] = self.es.enter_context(self.nc.semaphore(nm))
        return self.sems[sig]

    def op(self, q, fn, reads=(), writes=(), dma=None):
        i = len(self.ops)
        sig = ("dma", dma) if dma is not None else q
        deps = {}
        for k in reads:
            w = self.last_w.get(k)
            if w is not None:
                deps[w] = True
        for k in writes:
            w = self.last_w.get(k)
            if w is not None:
                deps.setdefault(w, False)
            for r in self.readers.get(k, {}).values():
                deps.setdefault(r, False)
        need = []
        for j, raw in deps.items():
            oj = self.ops[j]
            if j < self.start:
                continue
            if oj["sig"] == q and dma is None:
                if q == "pe":
                    continue
            need.append(j)
            oj["needed"] = True
        self.ops.append(dict(q=q, fn=fn, sig=sig, deps=need, needed=(dma is not None), val=None))
        for k in writes:
            self.last_w[k] = i
            self.readers[k] = {}
        for k in reads:
            self.readers.setdefault(k, {})[sig] = i
        return i

    def wait_all_dma(self, q="sp"):
        need = []
        seen = set()
        for j in range(len(self.ops) - 1, -1, -1):
            oj = self.ops[j]
            if isinstance(oj["sig"], tuple) and oj["sig"] not in seen:
                seen.add(oj["sig"])
                need.append(j)
        self.ops.append(dict(q=q, fn=None, sig=q, deps=need, needed=False, val=None))

    def flush(self):
        self.wait_all_dma()
        nc = self.nc
        ops = self.ops[self.start:]
        self.start = len(self.ops)
        for o in ops:
            if o["needed"]:
                inc = 16 if isinstance(o["sig"], tuple) else 1
                self.cnt[o["sig"]] = self.cnt.get(o["sig"], 0) + inc
                o["val"] = self.cnt[o["sig"]]
                self.sem(o["sig"])
        allops = self.ops
        prog = self

        def run(qname, eng):
            waited = prog.waited[qname]
            for o in ops:
                if o["q"] != qname:
                    continue
                wl = {}
                for j in o["deps"]:
                    oj = allops[j]
                    wl[oj["sig"]] = max(wl.get(oj["sig"], 0), oj["val"])
                for sg, v in wl.items():
                    if waited.get(sg, 0) >= v:
                        continue
                    eng.wait_ge(prog.sems[sg], v)
                    waited[sg] = v
                if o["fn"] is not None:
                    inst = o["fn"](eng)
                    if o["needed"]:
                        inc = 16 if isinstance(o["sig"], tuple) else 1
                        inst.then_inc(prog.sems[o["sig"]], inc)

        with nc.Block() as block:
            @block.tensor
            def _(e):
                run("pe", e)

            @block.scalar
            def _(e):
                run("act", e)

            @block.vector
            def _(e):
                run("dve", e)

            @block.gpsimd
            def _(e):
                run("pool", e)

            @block.sync
            def _(e):
                run("sp", e)


class Builder:
    def __init__(self, S, depth, debug=False, phases=None):
        self.S = S
        self.depth = depth
        self.NT = S // 128
        self.TB = min(512, S)
        self.NTB = S // self.TB
        self.debug = debug
        self.phases = phases
        self.bank_rr = 0

    def ACT(self, out, in_, func, r, w, **kw):
        self.P.op("act", lambda e: e.activation(out=out, in_=in_, func=func, **kw), r, w)

    def TT(self, q, out, in0, in1, op, r, w):
        self.P.op(q, lambda e: e.tensor_tensor(out=out, in0=in0, in1=in1, op=op), r, w)

    def TS(self, q, out, in0, s1, op0, r, w, s2=None, op1=None):
        if op1 is None:
            self.P.op(q, lambda e: e.tensor_scalar(out=out, in0=in0, scalar1=s1, scalar2=None, op0=op0), r, w)
        else:
            self.P.op(q, lambda e: e.tensor_scalar(out=out, in0=in0, scalar1=s1, scalar2=s2, op0=op0, op1=op1), r, w)

    def STT(self, out, in0, scalar, in1, op0, op1, r, w):
        self.P.op("dve", lambda e: e.scalar_tensor_tensor(out=out, in0=in0, scalar=scalar, in1=in1, op0=op0, op1=op1), r, w)

    def CP(self, q, out, in_, r, w, scale=None):
        if q == "act":
            if scale is None:
                self.P.op("act", lambda e: e.activation(out=out, in_=in_, func=AF.Copy), r, w)
            else:
                self.P.op("act", lambda e: e.activation(out=out, in_=in_, func=AF.Copy, scale=scale), r, w)
        else:
            self.P.op(q, lambda e: e.tensor_copy(out=out, in_=in_), r, w)

    def MM(self, out, lhsT, rhs, start, stop, r, w, **kw):
        self.P.op("pe", lambda e: e.matmul(out, lhsT=lhsT, rhs=rhs, start=start, stop=stop, **kw), r, w)

    def TR(self, out, in_, ident, r, w):
        self.P.op("pe", lambda e: e.transpose(out=out, in_=in_, identity=ident), r, w)

    def DMA(self, q, out, in_, r, w, stream):
        self.P.op(q, lambda e: e.dma_start(out=out, in_=in_), r, w, dma=stream)

    def MEMSET(self, q, ap, val, w):
        self.P.op(q, lambda e: e.memset(ap, val), (), w)

    def RECIP(self, out, in_, r, w):
        self.P.op("dve", lambda e: e.reciprocal(out=out, in_=in_), r, w)

    def REDUCE(self, out, in_, r, w):
        self.P.op("dve", lambda e: e.tensor_reduce(out=out, in_=in_, axis=mybir.AxisListType.X, op=ALU.add), r, w)

    def ASEL(self, out, in_, pattern, cmp, fill, base, cm, r, w):
        self.P.op("pool", lambda e: e.affine_select(out=out, in_=in_, pattern=pattern, compare_op=cmp, fill=fill,
                                                    base=base, channel_multiplier=cm), r, w)

    def nb(self):
        b = self.bank_rr
        self.bank_rr = (self.bank_rr + 1) % 8
        return b

    def sb(self, ph, name, shape, dt):
        self.uid = getattr(self, "uid", 0) + 1
        return ph.enter_context(self.nc.sbuf_tensor("%s_u%d" % (name, self.uid), shape, dt))

    def psk(self, b):
        return "ps%d" % b

    def psbf(self, b, a):
        return self.ps[b][:].bitcast(BF16).rearrange("p (a b) -> p a b", a=a)

    def build(self):
        S, NT = self.S, self.NT
        L = self.depth
        nc = bass.Bass("TRN2", target_bir_lowering=False)
        self.nc = nc

        def din(name, shape, dt=F32):
            return nc.dram_tensor(name, shape, dt, kind="ExternalInput").ap()

        def dscr(name, shape, dt=F32):
            kind = "ExternalOutput" if self.debug else "Internal"
            return nc.dram_tensor(name, shape, dt, kind=kind).ap()

        self.x_in = din("x", [S, D])
        self.w_in = din("w_in", [L, D, IN_W])
        self.w_out = din("w_out", [L, D, D])
        self.w_ff1 = din("w_ff1", [L, D, DFF])
        self.w_ff2 = din("w_ff2", [L, DFF, D])
        self.normw = din("normw", [L, 4, D])
        self.convw = din("convw", [L, 256, 31])
        self.convp = din("convp", [L, 128, 2, 3])
        self.lam = din("lam", [L, 2, 2, 32])
        self.subw = din("subw", [L, 64])
        self.dconvw = din("dconvw", [L, 1152, 3])
        self.dgate = din("dgate", [L, 2, 12])
        self.dnormw = din("dnormw", [L, 64])
        self.dband = din("dband", [128, 2 * S - 128])
        self.bdmask = din("bdmask", [128, 128])
        self.out = nc.dram_tensor("out", [S, D], F32, kind="ExternalOutput").ap()
        self.xres = dscr("xres", [S, D])
        self.hg = dscr("hg", [2, 128, S], BF16)
        self.qT = dscr("qT", [4, 96, S], BF16)
        self.kT = dscr("kT", [4, 96, S], BF16)
        self.vtok = dscr("vtok", [S, 384], BF16)
        self.gqkvT = dscr("gqkvT", [9, 128, S])
        self.zs = dscr("zs", [S, 384], BF16)
        self.gates = dscr("gates", [S, 24])
        self.ycatT = dscr("ycatT", [8, 128, S], BF16)
        self.ypart = dscr("ypart", [S, D])

        with ExitStack() as es:
            self.P = Prog(nc, es)
            self.ps = [es.enter_context(nc.psum_tensor("psb%d" % i, [128, 512], F32)) for i in range(8)]
            self.identf = self.sb(es, "identf", [128, 128], F32)
            self.identb = self.sb(es, "identb", [128, 128], BF16)
            self.MEMSET("pool", self.identf[:], 1.0, ["identf"])
            self.ASEL(self.identf[:], self.identf[:], [[-1, 128]], ALU.is_equal, 0.0, 0, 1, ["identf"], ["identf"])
            self.CP("pool", self.identb[:], self.identf[:], ["identf"], ["identb"])
            self.P.flush()
            for l in range(L):
                self.layer(l)
            self.P.flush()
        return nc

    def want(self, name):
        return self.phases is None or name in self.phases

    def layer(self, l):
        S = self.S
        last = (l == self.depth - 1)
        src = self.x_in if l == 0 else self.xres
        if self.want("B"):
            self.phase_in_proj(l, src)
        self.merge_conv = self.want("C") and self.want("E")
        if self.want("C") and not self.merge_conv:
            self.phase_conv(l)
        if self.want("D"):
            self.phase_attn(l)
        if self.want("E"):
            self.phase_delta(l)
        if self.want("F") and self.want("G"):
            self.phase_out_mlp(l, src, self.out if last else self.xres)
        else:
            if self.want("F"):
                self.phase_out_proj(l, src)
            if self.want("G"):
                self.phase_mlp(l, self.out if last else self.xres)

    def norm_T(self, ph, src, wrow, hT):
        S, NT = self.S, self.NT
        xt = [self.sb(ph, "n_xt%d" % i, [128, D], F32) for i in range(2)]
        hb = [self.sb(ph, "n_hb%d" % i, [128, D], BF16) for i in range(2)]
        junk = self.sb(ph, "n_junk", [128, D], BF16)
        ssq = self.sb(ph, "n_ssq", [128, NT], F32)
        rstd = self.sb(ph, "n_rstd", [128, NT], F32)
        wbc = self.sb(ph, "n_wbc", [128, D], F32)
        self.DMA("sp", wbc[:], wrow.broadcast_to([128, D]), [], ["n_wbc"], "n_wbc")
        for t in range(NT):
            b = t % 2
            self.DMA("sp", xt[b][:], src[t * 128:(t + 1) * 128, :], [("x", t)], ["n_xt%d" % b], "n_xt%d" % b)
            self.ACT(junk[:], xt[b][:], AF.Square, ["n_xt%d" % b], ["n_junk", "n_ssq"], accum_out=ssq[:, t:t + 1])
        self.TS("dve", rstd[:], ssq[:], 1.0 / D, ALU.mult, ["n_ssq"], ["n_rstd"], s2=EPS, op1=ALU.add)
        self.ACT(rstd[:], rstd[:], AF.Sqrt, ["n_rstd"], ["n_rstd"])
        self.RECIP(rstd[:], rstd[:], ["n_rstd"], ["n_rstd"])
        for t in range(NT):
            b = t % 2
            self.DMA("sp", xt[b][:], src[t * 128:(t + 1) * 128, :], [("x", t)], ["n_xt%d" % b], "n_xt%d" % b)
            self.STT(hb[b][:], xt[b][:], rstd[:, t:t + 1], wbc[:], ALU.mult, ALU.mult,
                     ["n_xt%d" % b, "n_rstd", "n_wbc"], ["n_hb%d" % b])
            bk = self.nb()
            pv = self.psbf(bk, 8)
            for kc in range(8):
                self.TR(pv[:, kc, :], hb[b][:, kc * 128:(kc + 1) * 128], self.identb[:], ["n_hb%d" % b, "identb"], [self.psk(bk)])
            self.CP("act", hT[:, :, t * 128:(t + 1) * 128], pv[:, :, :], [], [self.psk(bk), "hT"])

    def phase_in_proj(self, l, src):
        S, NT, TB, NTB = self.S, self.NT, self.TB, self.NTB
        with ExitStack() as ph:
            hT = self.sb(ph, "hT", [128, 8, S], BF16)
            self.norm_T(ph, src, self.normw[l, 0:1, :], hT)
            groups = [("wA", 0, 512), ("wB", 512, 768), ("wC", 1280, 384), ("wD", 1664, 1152), ("wE", 2816, 408)]
            W = {}
            for nm, c0, n in groups:
                W[nm] = self.sb(ph, nm, [128, 8, n], BF16)
                self.DMA("pool", W[nm][:], self.w_in[l, :, c0:c0 + n].rearrange("(kc p) n -> p kc n", p=128), [], [nm], nm)
            hgs = self.sb(ph, "hgs", [128, 2, S], BF16)
            qs = self.sb(ph, "qs", [128, 4, S], BF16)
            ks = self.sb(ph, "ks", [128, 4, S], BF16)
            vs = self.sb(ph, "vs", [128, NT, 384], BF16)
            zst = self.sb(ph, "zst", [128, NT, 384], BF16)
            gst = self.sb(ph, "gst", [128, NT, 24], F32)
            sg = [self.sb(ph, "sg%d" % i, [128, TB], F32) for i in range(2)]
            dst = [self.sb(ph, "dst%d" % i, [128, TB], F32) for i in range(3)]
            i = 0
            for cc in range(2):
                for tb in range(NTB):
                    ts_ = slice(tb * TB, (tb + 1) * TB)
                    ba, bg = self.nb(), self.nb()
                    for kc in range(8):
                        self.MM(self.ps[bg][:, :TB], W["wA"][:, kc, 256 + cc * 128:256 + (cc + 1) * 128], hT[:, kc, ts_], kc == 0, kc == 7, ["wA", "hT"], [self.psk(bg)])
                    for kc in range(8):
                        self.MM(self.ps[ba][:, :TB], W["wA"][:, kc, cc * 128:(cc + 1) * 128], hT[:, kc, ts_], kc == 0, kc == 7, ["wA", "hT"], [self.psk(ba)])
                    s = i % 2
                    i += 1
                    self.ACT(sg[s][:], self.ps[bg][:, :TB], AF.Sigmoid, [], [self.psk(bg), "sg%d" % s])
                    self.TT("dve", hgs[:, cc, ts_], self.ps[ba][:, :TB], sg[s][:], ALU.mult, ["sg%d" % s], [self.psk(ba), "hgs"])
            for cc in range(2):
                self.DMA("sp", self.hg[cc], hgs[:, cc, :], ["hgs"], ["hg"], "st_hg")
            for which, stg, dst_d, scale in (("q", qs, self.qT, 32.0 ** -0.5), ("k", ks, self.kT, None)):
                base = 0 if which == "q" else 384
                for c in range(4):
                    for tb in range(NTB):
                        ts_ = slice(tb * TB, (tb + 1) * TB)
                        bk = self.nb()
                        for kc in range(8):
                            self.MM(self.ps[bk][0:96, :TB], W["wB"][:, kc, base + 96 * c:base + 96 * (c + 1)], hT[:, kc, ts_], kc == 0, kc == 7, ["wB", "hT"], [self.psk(bk)])
                        self.CP("act", stg[0:96, c, ts_], self.ps[bk][0:96, :TB], [], [self.psk(bk), which + "s"], scale=scale)
                for c in range(4):
                    self.DMA("sp", dst_d[c], stg[0:96, c, :], [which + "s"], [which + "T"], "st_" + which)
            for t in range(NT):
                bk = self.nb()
                for kc in range(8):
                    self.MM(self.ps[bk][:, :384], hT[:, kc, t * 128:(t + 1) * 128], W["wC"][:, kc, :], kc == 0, kc == 7, ["wC", "hT"], [self.psk(bk)])
                self.CP("dve", vs[:, t, :], self.ps[bk][:, :384], [], [self.psk(bk), "vs"])
            self.DMA("sp", self.vtok.rearrange("(t p) n -> p t n", p=128), vs[:], ["vs"], ["vtok"], "st_v")
            i = 0
            for c in range(9):
                for tb in range(NTB):
                    ts_ = slice(tb * TB, (tb + 1) * TB)
                    bk = self.nb()
                    for kc in range(8):
                        self.MM(self.ps[bk][:, :TB], W["wD"][:, kc, c * 128:(c + 1) * 128], hT[:, kc, ts_], kc == 0, kc == 7, ["wD", "hT"], [self.psk(bk)])
                    s = i % 3
                    i += 1
                    self.CP("dve" if i % 2 else "act", dst[s][:], self.ps[bk][:, :TB], [], [self.psk(bk), "dst%d" % s])
                    self.DMA("sp", self.gqkvT[c, :, ts_], dst[s][:], ["dst%d" % s], ["gqkvT"], "st_d%d" % s)
            for t in range(NT):
                bk = self.nb()
                for kc in range(8):
                    self.MM(self.ps[bk][:, :408], hT[:, kc, t * 128:(t + 1) * 128], W["wE"][:, kc, :], kc == 0, kc == 7, ["wE", "hT"], [self.psk(bk)])
                self.ACT(zst[:, t, :], self.ps[bk][:, :384], AF.Silu, [], [self.psk(bk), "zst"])
                self.CP("dve", gst[:, t, :], self.ps[bk][:, 384:408], [], [self.psk(bk), "gst"])
            self.DMA("sp", self.zs.rearrange("(t p) n -> p t n", p=128), zst[:], ["zst"], ["zs"], "st_z")
            self.DMA("sp", self.gates.rearrange("(t p) n -> p t n", p=128), gst[:], ["gst"], ["gates"], "st_g")
            self.P.flush()

    def post_norm_residual(self, ph_tiles, banks, t, src, dst, wbc, extra=None):
        xt, ytmp, small, junk = ph_tiles
        b = t % 2
        self.DMA("sp", xt[b][:], src[t * 128:(t + 1) * 128, :], [("x", t)], ["r_xt%d" % b], "r_xt%d" % b)
        ysrc = []
        for hf in range(2):
            bk = banks[hf]
            if extra is not None:
                etile, ekey = extra
                self.TT("dve", ytmp[b][:, hf * 512:(hf + 1) * 512], self.ps[bk][:, :], etile[:, hf * 512:(hf + 1) * 512], ALU.add,
                        [ekey], [self.psk(bk), "r_yt%d" % b])
            else:
                self.CP("dve", ytmp[b][:, hf * 512:(hf + 1) * 512], self.ps[bk][:, :], [], [self.psk(bk), "r_yt%d" % b])
        sm = small[b]
        self.ACT(junk[:], ytmp[b][:], AF.Square, ["r_yt%d" % b], ["r_junk", "r_sm%d" % b], accum_out=sm[:, 0:1])
        self.TS("dve", sm[:, 1:2], sm[:, 0:1], 1.0 / D, ALU.mult, ["r_sm%d" % b], ["r_sm%d" % b], s2=EPS, op1=ALU.add)
        self.ACT(sm[:, 2:3], sm[:, 1:2], AF.Sqrt, ["r_sm%d" % b], ["r_sm%d" % b])
        self.RECIP(sm[:, 3:4], sm[:, 2:3], ["r_sm%d" % b], ["r_sm%d" % b])
        self.STT(ytmp[b][:], ytmp[b][:], sm[:, 3:4], wbc[:], ALU.mult, ALU.mult, ["r_yt%d" % b, "r_sm%d" % b, "r_wbc"], ["r_yt%d" % b])
        self.TT("pool", xt[b][:], xt[b][:], ytmp[b][:], ALU.add, ["r_xt%d" % b, "r_yt%d" % b], ["r_xt%d" % b])
        self.DMA("sp", dst[t * 128:(t + 1) * 128, :], xt[b][:], ["r_xt%d" % b], [("x", t)], "r_st%d" % b)

    def res_tiles(self, ph):
        xt = [self.sb(ph, "r_xt%d" % i, [128, D], F32) for i in range(2)]
        ytmp = [self.sb(ph, "r_yt%d" % i, [128, D], F32) for i in range(2)]
        small = [self.sb(ph, "r_sm%d" % i, [128, 4], F32) for i in range(2)]
        junk = self.sb(ph, "r_junk", [128, D], BF16)
        return xt, ytmp, small, junk

    def phase_out_proj(self, l, src):
        S, NT = self.S, self.NT
        with ExitStack() as ph:
            yT = self.sb(ph, "ycat", [128, 8, S], BF16)
            wo = self.sb(ph, "wo", [128, 8, D], BF16)
            wbc = self.sb(ph, "r_wbc", [128, D], F32)
            tiles = self.res_tiles(ph)
            for c in range(8):
                self.DMA("sp", yT[:, c, :], self.ycatT[c], ["ycatT"], ["ycat"], "ld_ycat")
            self.DMA("pool", wo[:], self.w_out[l].rearrange("(kc p) n -> p kc n", p=128), [], ["wo"], "ld_wo")
            self.DMA("sp", wbc[:], self.normw[l, 1:2, :].broadcast_to([128, D]), [], ["r_wbc"], "ld_rwbc")
            for t in range(NT):
                banks = [self.nb(), self.nb()]
                for hf in range(2):
                    for kc in range(8):
                        self.MM(self.ps[banks[hf]][:, :], yT[:, kc, t * 128:(t + 1) * 128], wo[:, kc, hf * 512:(hf + 1) * 512], kc == 0, kc == 7,
                                ["ycat", "wo"], [self.psk(banks[hf])])
                self.post_norm_residual(tiles, banks, t, src, self.xres, wbc)
            self.P.flush()

    def phase_out_mlp(self, l, src, dst):
        S, NT = self.S, self.NT
        with ExitStack() as ph:
            hT = self.sb(ph, "hT", [128, 8, S], BF16)
            w1 = self.sb(ph, "w1", [128, 8, 2048], BF16)
            w2 = self.sb(ph, "w2", [128, 16, D], BF16)
            self.DMA("pool", w1[:], self.w_ff1[l, :, 0:2048].rearrange("(kc p) n -> p kc n", p=128), [], ["w1"], "ld_w1")
            self.DMA("pool", w2[:], self.w_ff2[l, 0:2048, :].rearrange("(j p) n -> p j n", p=128), [], ["w2"], "ld_w2")
            with ExitStack() as ph2:
                yT = self.sb(ph2, "ycat", [128, 8, S], BF16)
                wo = self.sb(ph2, "wo", [128, 8, D], BF16)
                wbc = self.sb(ph2, "r_wbc", [128, D], F32)
                tiles = self.res_tiles(ph2)
                for c in range(8):
                    self.DMA("sp", yT[:, c, :], self.ycatT[c], ["ycatT"], ["ycat"], "ld_ycat")
                self.DMA("pool", wo[:], self.w_out[l].rearrange("(kc p) n -> p kc n", p=128), [], ["wo"], "ld_wo")
                self.DMA("sp", wbc[:], self.normw[l, 1:2, :].broadcast_to([128, D]), [], ["r_wbc"], "ld_rwbc")
                for t in range(NT):
                    banks = [self.nb(), self.nb()]
                    for hf in range(2):
                        for kc in range(8):
                            self.MM(self.ps[banks[hf]][:, :], yT[:, kc, t * 128:(t + 1) * 128], wo[:, kc, hf * 512:(hf + 1) * 512], kc == 0, kc == 7,
                                    ["ycat", "wo"], [self.psk(banks[hf])])
                    self.post_norm_residual(tiles, banks, t, src, self.xres, wbc)
                self.norm_T(ph2, self.xres, self.normw[l, 2:3, :], hT)
                self.P.flush()
            self.mlp_main(ph, l, dst, hT, w1, w2, True)

    def phase_mlp(self, l, dst):
        S = self.S
        with ExitStack() as ph:
            hT = self.sb(ph, "hT", [128, 8, S], BF16)
            with ExitStack() as ph2:
                self.norm_T(ph2, self.xres, self.normw[l, 2:3, :], hT)
                self.P.flush()
            w1 = self.sb(ph, "w1", [128, 8, 2048], BF16)
            w2 = self.sb(ph, "w2", [128, 16, D], BF16)
            self.mlp_main(ph, l, dst, hT, w1, w2, False)

    def mlp_main(self, ph, l, dst, hT, w1, w2, first_loaded):
        S, NT, TB, NTB = self.S, self.NT, self.TB, self.NTB
        NQ = TB // 128
        if True:
            h1 = [self.sb(ph, "h1_%d" % i, [128, 16, TB], BF16) for i in range(2)]
            rl = [self.sb(ph, "rl%d" % i, [128, TB], BF16) for i in range(2)]
            yp = [self.sb(ph, "yp%d" % i, [128, D], F32) for i in range(2)]
            wbc = self.sb(ph, "r_wbc", [128, D], F32)
            tiles = self.res_tiles(ph)
            self.DMA("sp", wbc[:], self.normw[l, 3:4, :].broadcast_to([128, D]), [], ["r_wbc"], "ld_rwbc")
            for half in range(2):
                f0 = half * 2048
                if not (first_loaded and half == 0):
                    self.DMA("pool", w1[:], self.w_ff1[l, :, f0:f0 + 2048].rearrange("(kc p) n -> p kc n", p=128), [], ["w1"], "ld_w1")
                    self.DMA("pool", w2[:], self.w_ff2[l, f0:f0 + 2048, :].rearrange("(j p) n -> p j n", p=128), [], ["w2"], "ld_w2")
                for tb in range(NTB):
                    ts_ = slice(tb * TB, (tb + 1) * TB)
                    hb = h1[tb % 2]
                    hk = "h1_%d" % (tb % 2)
                    for j in range(16):
                        bk = self.nb()
                        for kc in range(8):
                            self.MM(self.ps[bk][:, :TB], w1[:, kc, j * 128:(j + 1) * 128], hT[:, kc, ts_], kc == 0, kc == 7, ["w1", "hT"], [self.psk(bk)])
                        r_ = j % 2
                        self.ACT(rl[r_][:], self.ps[bk][:, :TB], AF.Relu, [], [self.psk(bk), "rl%d" % r_])
                        self.TT("pool" if j % 2 else "dve", hb[:, j, :], rl[r_][:], rl[r_][:], ALU.mult, ["rl%d" % r_], [hk])
                    for ti in range(NQ):
                        t = tb * NQ + ti
                        banks = [self.nb(), self.nb()]
                        for hf in range(2):
                            for j in range(16):
                                self.MM(self.ps[banks[hf]][:, :], hb[:, j, ti * 128:(ti + 1) * 128], w2[:, j, hf * 512:(hf + 1) * 512], j == 0, j == 15,
                                        [hk, "w2"], [self.psk(banks[hf])])
                        if half == 0:
                            b = t % 2
                            for hf in range(2):
                                self.CP("dve" if hf else "act", yp[b][:, hf * 512:(hf + 1) * 512], self.ps[banks[hf]][:, :], [], [self.psk(banks[hf]), "yp%d" % b])
                            self.DMA("sp", self.ypart[t * 128:(t + 1) * 128, :], yp[b][:], ["yp%d" % b], [("ypart", t)], "st_yp%d" % b)
                        else:
                            b = t % 2
                            self.DMA("sp", yp[b][:], self.ypart[t * 128:(t + 1) * 128, :], [("ypart", t)], ["yp%d" % b], "ld_yp%d" % b)
                            self.post_norm_residual(tiles, banks, t, self.xres, dst, wbc, extra=(yp[b], "yp%d" % b))
            self.P.flush()


def prep_inputs(inp, S):
    f = lambda a: np.ascontiguousarray(np.asarray(a, dtype=np.float32))
    L = inp["w_in"].shape[0]
    shared = {
        "w_in": f(inp["w_in"]), "w_out": f(inp["w_out"]), "w_ff1": f(inp["w_ff1"]), "w_ff2": f(inp["w_ff2"]),
        "normw": f(np.stack([inp["pre_mix_w"], inp["post_mix_w"], inp["pre_mlp_w"], inp["post_mlp_w"]], axis=1)),
        "convw": f(np.transpose(np.asarray(inp["conv_dw_w"]), (0, 2, 1))),
        "convp": f(np.stack([np.asarray(inp["conv_dw_b"]).reshape(L, 2, 128), np.asarray(inp["conv_ln_w"]).reshape(L, 2, 128),
                             np.asarray(inp["conv_ln_b"]).reshape(L, 2, 128)], axis=-1).transpose(0, 2, 1, 3)),
        "lam": f(np.stack([np.stack([inp["diff_lambda_q1"], inp["diff_lambda_q2"]], axis=1),
                           np.stack([inp["diff_lambda_k1"], inp["diff_lambda_k2"]], axis=1)], axis=1)),
        "subw": f(inp["diff_subln_w"]),
        "dconvw": f(np.transpose(np.asarray(inp["delta_conv_w"]), (0, 2, 1))),
        "dgate": f(np.stack([np.asarray(inp["delta_A_log"]).reshape(L, 12), np.asarray(inp["delta_dt_bias"]).reshape(L, 12)], axis=1)),
        "dnormw": f(inp["delta_norm_w"]),
    }
    W = 2 * S - 128
    shared["dband"] = f(np.abs(np.arange(W)[None, :] - (S - 128) - np.arange(128)[:, None]))
    shared["bdmask"] = f((np.arange(128)[:, None] // 32) == (np.arange(128)[None, :] // 32))
    return shared


_NC_CACHE = {}


def kernel(**inputs):
    x = np.asarray(inputs["x"], dtype=np.float32)
    B, S, _ = x.shape
    L = inputs["w_in"].shape[0]
    shared = prep_inputs(inputs, S)
    key = (S, L)
    if key not in _NC_CACHE:
        _NC_CACHE[key] = Builder(S, L).build()
    nc = _NC_CACHE[key]
    in_maps = []
    for b in range(B):
        m = dict(shared)
        m["x"] = np.ascontiguousarray(x[b])
        in_maps.append(m)
    res = run_bass_kernel_spmd(nc, in_maps, core_ids=list(range(B)))
    return np.stack([np.asarray(r["out"], dtype=np.float32) for r in res.results], axis=0)


def phase_conv(self, l):
    with ExitStack() as ph:
        for _ in self.conv_body(ph, l):
            pass
        self.P.flush()


def conv_body(self, ph, l):
    S, NT, TB, NTB = self.S, self.NT, self.TB, self.NTB
    if True:
        hgp = self.sb(ph, "hgp", [128, 2, S + 30], BF16)
        wcol = self.sb(ph, "wcol", [128, 2, 31], F32)
        cpar = self.sb(ph, "cpar", [128, 2, 3], F32)
        diag = self.sb(ph, "diag", [128, 62, 128], BF16)
        onesb = self.sb(ph, "onesb", [128, 128], BF16)
        yb = [self.sb(ph, "yb%d" % i, [128, TB], F32) for i in range(2)]
        yh = [self.sb(ph, "yh%d" % i, [128, TB], BF16) for i in range(2)]
        zq = [self.sb(ph, "zq%d" % i, [128, TB], BF16) for i in range(2)]
        mean = self.sb(ph, "mean", [128, TB], F32)
        var = self.sb(ph, "var", [128, TB], F32)
        zt = [self.sb(ph, "zt%d" % i, [128, TB], F32) for i in range(2)]
        yout = self.sb(ph, "yout", [128, 2, S], BF16)
        self.MEMSET("pool", hgp[:, :, 0:15], 0.0, ["hgp"])
        self.MEMSET("pool", hgp[:, :, S + 15:S + 30], 0.0, ["hgp"])
        self.MEMSET("pool", onesb[:], 1.0 / 256.0, ["onesb"])
        for cc in range(2):
            self.DMA("sp", hgp[:, cc, 15:15 + S], self.hg[cc], ["hg"], ["hgp"], "ld_hgp")
        self.DMA("sp", wcol[:], self.convw[l].rearrange("(cc p) j -> p cc j", p=128), [], ["wcol"], "ld_wcol")
        self.DMA("sp", cpar[:], self.convp[l], [], ["cpar"], "ld_cpar")
        for cc in range(2):
            for j in range(31):
                if j % 3 == 1:
                    self.ACT(diag[:, cc * 31 + j, :], self.identf[:], AF.Copy, ["wcol", "identf"], ["diag"], scale=wcol[:, cc, j:j + 1])
                else:
                    self.TS("dve" if j % 3 == 0 else "pool", diag[:, cc * 31 + j, :], self.identf[:], wcol[:, cc, j:j + 1], ALU.mult, ["wcol", "identf"], ["diag"])
        yield
        for tb in range(NTB):
            ts_ = slice(tb * TB, (tb + 1) * TB)
            for cc in range(2):
                bk = self.nb()
                for j in range(31):
                    self.MM(self.ps[bk][:, :TB], diag[:, cc * 31 + j, :], hgp[:, cc, tb * TB + j:tb * TB + j + TB], j == 0, j == 30,
                            ["diag", "hgp"], [self.psk(bk)])
                self.ACT(yb[cc][:], self.ps[bk][:, :TB], AF.Identity, ["cpar"], [self.psk(bk), "yb%d" % cc], bias=cpar[:, cc, 0:1])
                self.CP("dve", yh[cc][:], yb[cc][:], ["yb%d" % cc], ["yh%d" % cc])
            bm = self.nb()
            for cc in range(2):
                self.MM(self.ps[bm][:, :TB], onesb[:], yh[cc][:], cc == 0, cc == 1, ["onesb", "yh%d" % cc], [self.psk(bm)])
            self.CP("dve", mean[:], self.ps[bm][:, :TB], [], [self.psk(bm), "mean"])
            for cc in range(2):
                self.TT("pool" if cc else "dve", zt[cc][:], yb[cc][:], mean[:], ALU.subtract, ["yb%d" % cc, "mean"], ["zt%d" % cc])
                self.ACT(zq[cc][:], zt[cc][:], AF.Square, ["zt%d" % cc], ["zq%d" % cc])
            be = self.nb()
            for cc in range(2):
                self.MM(self.ps[be][:, :TB], onesb[:], zq[cc][:], cc == 0, cc == 1, ["onesb", "zq%d" % cc], [self.psk(be)])
            self.TS("dve", var[:], self.ps[be][:, :TB], 1e-5, ALU.add, [], [self.psk(be), "var"])
            self.ACT(var[:], var[:], AF.Sqrt, ["var"], ["var"])
            self.RECIP(var[:], var[:], ["var"], ["var"])
            for cc in range(2):
                self.TT("pool" if cc else "dve", zt[cc][:], zt[cc][:], var[:], ALU.mult, ["zt%d" % cc, "var"], ["zt%d" % cc])
                self.ACT(yout[:, cc, ts_], zt[cc][:], AF.Silu, ["zt%d" % cc, "cpar"], ["yout"], scale=cpar[:, cc, 1:2], bias=cpar[:, cc, 2:3])
            yield
        for cc in range(2):
            self.DMA("sp", self.ycatT[cc], yout[:, cc, :], ["yout"], ["ycatT"], "st_yconv")


Builder.phase_conv = phase_conv
Builder.conv_body = conv_body


def phase_attn(self, l):
    S, NT = self.S, self.NT
    QB = min(512, S)
    NQB = S // QB
    NQ = QB // 128
    linit = 0.8 - 0.6 * math.exp(-0.3 * l)
    with ExitStack() as ph:
        QT = self.sb(ph, "QT", [128, 4, S], BF16)
        KT = self.sb(ph, "KT", [128, 4, S], BF16)
        V = self.sb(ph, "V", [128, NT, 6, 128], BF16)
        dband = self.sb(ph, "dband", [128, 2 * S - 128], F32)
        lamv = self.sb(ph, "lamv", [128, 2, 2, 32], F32)
        lprod = self.sb(ph, "lprod", [128, 2, 32], F32)
        lsm = self.sb(ph, "lsm", [128, 4], F32)
        wsub = self.sb(ph, "wsub", [128, 64], F32)
        ydiff = self.sb(ph, "ydiff", [128, NT, 384], BF16)
        yT = self.sb(ph, "yT", [128, 3, S], BF16)
        sc = [self.sb(ph, "sc%d" % i, [128, QB], F32) for i in range(6)]
        ET = [self.sb(ph, "ET%d" % i, [128, QB], BF16) for i in range(7)]
        rs = self.sb(ph, "rs", [128, NQ, 1], F32)
        oTb = self.sb(ph, "oTb", [65, QB], BF16)
        o1 = self.sb(ph, "o1", [128, NQ, 64], F32)
        o2 = self.sb(ph, "o2", [128, NQ, 64], F32)
        osq = self.sb(ph, "osq", [128, NQ, 64], F32)
        oss = self.sb(ph, "oss", [128, NQ], F32)
        for c in range(4):
            self.DMA("sp", QT[0:96, c, :], self.qT[c], ["qT"], ["QT"], "ld_QT")
            self.DMA("sp", KT[0:96, c, :], self.kT[c], ["kT"], ["KT"], "ld_KT")
        QTm = None
        if os.environ.get("K_QM", "0") == "1":
            QTm = self.sb(ph, "QTm", [128, 12, S], BF16)
            self.MEMSET("pool", QTm[0:96, :, :], 0.0, ["QTm"])
            for mi_ in range(12):
                c_, r_ = mi_ // 3, 32 * (mi_ % 3)
                eng_ = ("act", "dve", "pool")[mi_ % 3]
                self.CP(eng_, QTm[r_:r_ + 32, mi_, :], QT[r_:r_ + 32, c_, :], ["QT", "QTm"], ["QTm"])
        self.MEMSET("pool", V[:, :, :, 65:128], 0.0, ["V"])
        self.MEMSET("pool", V[:, :, :, 64:65], 1.0, ["V"])
        for t in range(NT):
            self.DMA("sp", V[:, t, :, 0:64], self.vtok[t * 128:(t + 1) * 128, :].rearrange("p (h e) -> p h e", h=6), ["vtok"], ["V"], "ld_V")
        self.DMA("sp", dband[:], self.dband, [], ["dband"], "ld_dband")
        self.DMA("sp", lamv[:].rearrange("p a b c -> p (a b c)"), self.lam[l:l + 1].rearrange("o a b c -> o (a b c)").broadcast_to([128, 128]), [], ["lamv"], "ld_lam")
        self.DMA("sp", wsub[:], self.subw[l:l + 1, :].broadcast_to([128, 64]), [], ["wsub"], "ld_subw")
        self.TT("dve", lprod[:], lamv[:, 0, :, :], lamv[:, 1, :, :], ALU.mult, ["lamv"], ["lprod"])
        self.REDUCE(lsm[:, 0:2], lprod[:], ["lprod"], ["lsm"])
        self.ACT(lsm[:, 0:2], lsm[:, 0:2], AF.Exp, ["lsm"], ["lsm"])
        self.TT("dve", lsm[:, 2:3], lsm[:, 1:2], lsm[:, 0:1], ALU.subtract, ["lsm"], ["lsm"])
        self.TS("dve", lsm[:, 3:4], lsm[:, 2:3], -linit, ALU.add, ["lsm"], ["lsm"])
        self.TS("dve", wsub[:], wsub[:], 1.0 - linit, ALU.mult, ["wsub"], ["wsub"])
        SB = [0, 1, 2, 5, 6, 7]
        AB = [3, 4]
        LA = 5
        blocks = []
        ia = 0
        for h in range(6):
            for qb in range(NQB):
                for j in range(2):
                    ab = AB[ia % 2]
                    ia += 1
                    kts = []
                    for kt in range(NT):
                        dmin = max(0, kt * 128 - (qb * QB + QB - 1), qb * QB - (kt * 128 + 127))
                        if ALIBI_SKIP is None or dmin * SLOPES[h] <= ALIBI_SKIP:
                            kts.append(kt)
                    for kt in kts:
                        blocks.append((h, qb, j, kt, ab, kt == kts[0], kt == kts[-1]))
        nblk = len(blocks)

        def front(it):
            h, qb, j, kt, ab, kfirst, klast = blocks[it]
            m = SLOPES[h]
            mi = 2 * h + j
            c = mi // 3
            r0 = 32 * (mi % 3)
            sbk = SB[it % 6]
            si = it % 6
            ei = it % 7
            if QTm is not None:
                self.MM(self.ps[sbk][:, :QB], KT[0:96, c, kt * 128:(kt + 1) * 128], QTm[0:96, mi, qb * QB:(qb + 1) * QB], True, True,
                        ["KT", "QTm"], [self.psk(sbk)])
            else:
                self.MM(self.ps[sbk][:, :QB], KT[r0:r0 + 32, c, kt * 128:(kt + 1) * 128], QT[r0:r0 + 32, c, qb * QB:(qb + 1) * QB], True, True,
                        ["KT", "QT"], [self.psk(sbk)])
            for _ in range(int(os.environ.get("K_FILL", "0"))):
                self.MM(self.ps[7][:, :QB], KT[:, 0, 0:128], QT[:, 0, 0:QB], True, True, ["KT", "QT"], ["ps7"])
            off = qb * QB - kt * 128 + S - 128
            self.STT(sc[si][:], dband[:, off:off + QB], -m, self.ps[sbk][:, :QB], ALU.mult, ALU.add, ["dband"], [self.psk(sbk), "sc%d" % si])
            self.ACT(ET[ei][:], sc[si][:], AF.Exp, ["sc%d" % si], ["ET%d" % ei])

        pending = []

        def epilogue(h, qb, j, ab):
            accv = self.ps[ab][:, 0:NQ * 65].rearrange("p (a b) -> p a b", b=65)
            self.RECIP(rs[:], accv[:, :, 64:65], [], [self.psk(ab), "rs"])
            if j == 0:
                self.TT("dve", o1[:], accv[:, :, 0:64], rs[:].broadcast_to([128, NQ, 64]), ALU.mult, ["rs"], [self.psk(ab), "o1"])
                return
            self.TT("dve", o2[:], accv[:, :, 0:64], rs[:].broadcast_to([128, NQ, 64]), ALU.mult, ["rs"], [self.psk(ab), "o2"])
            self.STT(o2[:], o2[:], lsm[:, 3:4], o1[:], ALU.mult, ALU.add, ["o2", "o1", "lsm"], ["o2"])
            self.TT("pool", osq[:], o2[:], o2[:], ALU.mult, ["o2"], ["osq"])
            yield
            yield
            self.REDUCE(oss[:], osq[:], ["osq"], ["oss"])
            self.TS("dve", oss[:], oss[:], 1.0 / 64.0, ALU.mult, ["oss"], ["oss"], s2=1e-5, op1=ALU.add)
            self.ACT(oss[:], oss[:], AF.Sqrt, ["oss"], ["oss"])
            yield
            yield
            self.RECIP(oss[:], oss[:], ["oss"], ["oss"])
            self.TT("dve", o2[:], o2[:], oss[:].unsqueeze(2).broadcast_to([128, NQ, 64]), ALU.mult, ["o2", "oss"], ["o2"])
            self.TT("pool", ydiff[:, qb * NQ:(qb + 1) * NQ, h * 64:(h + 1) * 64], o2[:], wsub[:].unsqueeze(1).broadcast_to([128, NQ, 64]), ALU.mult,
                    ["o2", "wsub"], ["ydiff"])

        def back(it):
            h, qb, j, kt, ab, kfirst, klast = blocks[it]
            ei = it % 7
            accv = self.ps[ab][:, 0:NQ * 65].rearrange("p (a b) -> p a b", b=65)
            for qi in range(NQ):
                self.MM(accv[:, qi, :], ET[ei][:, qi * 128:(qi + 1) * 128], V[:, kt, h, 0:65], (kfirst and qi == 0), klast,
                        ["ET%d" % ei, "V"], [self.psk(ab)], skip_group_check=True)
            if klast:
                while pending:
                    pump()
                pending.append(epilogue(h, qb, j, ab))

        def pump():
            for g_ in list(pending):
                try:
                    next(g_)
                except StopIteration:
                    pending.remove(g_)

        for it in range(nblk + LA):
            if it < nblk:
                front(it)
            if it >= LA:
                back(it - LA)
            pump()
        while pending:
            pump()
        for t in range(NT):
            bk = 5 + (t % 2)
            pv = self.psbf(bk, 8)
            for c3 in range(3):
                self.TR(pv[:, c3, :], ydiff[:, t, c3 * 128:(c3 + 1) * 128], self.identb[:], ["ydiff", "identb"], [self.psk(bk)])
            self.CP("act", yT[:, :, t * 128:(t + 1) * 128], pv[:, 0:3, :], [], [self.psk(bk), "yT"])
        for c3 in range(3):
            self.DMA("sp", self.ycatT[2 + c3], yT[:, c3, :], ["yT"], ["ycatT"], "st_ydiff")
        self.P.flush()


Builder.phase_attn = phase_attn


def phase_delta(self, l):
    S, NT = self.S, self.NT
    H = 6
    with ExitStack() as ph:
        tm = self.sb(ph, "tm", [128, NT, 1152], BF16)
        with ExitStack() as p1:
            wc = self.sb(p1, "wc", [128, 9, 3], F32)
            xp = [self.sb(p1, "xp%d" % i, [128, S + 2], F32) for i in range(2)]
            acc = [self.sb(p1, "acc%d" % i, [128, S], F32) for i in range(2)]
            sT = [self.sb(p1, "sT%d" % i, [128, S], BF16) for i in range(2)]
            self.DMA("sp", wc[:], self.dconvw[l].rearrange("(c p) j -> p c j", p=128), [], ["wc"], "ld_wc")
            for i in range(2):
                self.MEMSET("pool", xp[i][:, 0:1], 0.0, ["xp%d" % i])
                self.MEMSET("pool", xp[i][:, S + 1:S + 2], 0.0, ["xp%d" % i])
            cgen = self.conv_body(p1, l) if (self.want("C") and self.merge_conv) else iter(())
            for c in range(9):
                b = c % 2
                next(cgen, None)
                self.DMA("sp", xp[b][:, 1:S + 1], self.gqkvT[c], ["gqkvT"], ["xp%d" % b], "ld_xp%d" % b)
                self.TS("dve", acc[b][:], xp[b][:, 0:S], wc[:, c, 0:1], ALU.mult, ["xp%d" % b, "wc"], ["acc%d" % b])
                self.STT(acc[b][:], xp[b][:, 1:S + 1], wc[:, c, 1:2], acc[b][:], ALU.mult, ALU.add, ["xp%d" % b, "wc", "acc%d" % b], ["acc%d" % b])
                self.STT(acc[b][:], xp[b][:, 2:S + 2], wc[:, c, 2:3], acc[b][:], ALU.mult, ALU.add, ["xp%d" % b, "wc", "acc%d" % b], ["acc%d" % b])
                self.ACT(sT[b][:], acc[b][:], AF.Silu, ["acc%d" % b], ["sT%d" % b])
                for t0 in range(0, NT, 8):
                    n = min(8, NT - t0)
                    bk = 6 + ((c * 2 + t0 // 8) % 2)
                    pv = self.psbf(bk, 8)
                    for i in range(n):
                        t = t0 + i
                        self.TR(pv[:, i, :], sT[b][:, t * 128:(t + 1) * 128], self.identb[:], ["sT%d" % b, "identb"], [self.psk(bk)])
                    self.CP("act" if (t0 // 8) % 2 else "dve", tm[:, t0:t0 + n, c * 128:(c + 1) * 128], pv[:, 0:n, :], [], [self.psk(bk), "tm"])
            for _ in cgen:
                pass
            self.P.flush()
        onesb = self.sb(ph, "onesb1", [128, 128], BF16)
        cf = self.sb(ph, "cf", [128, 128], F32)
        Lmat = [self.sb(ph, "Lmat%d" % d, [128, 128], BF16) for d in range(2)]
        maskneg = [self.sb(ph, "maskneg%d" % d, [128, 128], F32) for d in range(2)]
        nstr = [self.sb(ph, "nstr%d" % d, [128, 128], F32) for d in range(2)]
        self.MEMSET("pool", onesb[:], 1.0, ["onesb1"])
        for d in range(2):
            pat, cm = ([[1, 128]], -1) if d == 0 else ([[-1, 128]], 1)
            self.MEMSET("pool", cf[:], 1.0, ["cf"])
            self.ASEL(cf[:], cf[:], pat, ALU.is_ge, 0.0, 0, cm, ["cf"], ["cf"])
            self.CP("pool", Lmat[d][:], cf[:], ["cf"], ["Lmat%d" % d])
            self.MEMSET("pool", maskneg[d][:], 0.0, ["maskneg%d" % d])
            self.ASEL(maskneg[d][:], maskneg[d][:], pat, ALU.is_ge, -1e30, 0, cm, ["maskneg%d" % d], ["maskneg%d" % d])
            self.MEMSET("pool", nstr[d][:], -1.0, ["nstr%d" % d])
            self.ASEL(nstr[d][:], nstr[d][:], pat, ALU.is_gt, 0.0, 0, cm, ["nstr%d" % d], ["nstr%d" % d])
        groups = [[(0, 0, 4, 0)], [(0, 4, 6, 0), (1, 0, 2, 2)], [(1, 2, 6, 0)]]
        bd = self.sb(ph, "bd", [128, 128], F32)
        self.DMA("sp", bd[:], self.bdmask, [], ["bd"], "ld_bd")
        nstrd = [self.sb(ph, "nstrd%d" % d, [128, 128], F32) for d in range(2)]
        nstro = [self.sb(ph, "nstro%d" % d, [128, 128], F32) for d in range(2)]
        for d in range(2):
            self.TT("pool", nstrd[d][:], nstr[d][:], bd[:], ALU.mult, ["nstr%d" % d, "bd"], ["nstrd%d" % d])
            self.TT("pool", nstro[d][:], nstr[d][:], nstrd[d][:], ALU.subtract, ["nstr%d" % d, "nstrd%d" % d], ["nstro%d" % d])
        nstrg_d, nstrg_o = [], []
        for gi, g in enumerate(groups):
            tld = self.sb(ph, "nstrgd%d" % gi, [128, 4, 128], BF16)
            tlo = self.sb(ph, "nstrgo%d" % gi, [128, 4, 128], BF16)
            for (d, h0, h1, s0) in g:
                n = h1 - h0
                self.CP("pool", tld[:, s0:s0 + n, :], nstrd[d][:].unsqueeze(1).broadcast_to([128, n, 128]), ["nstrd%d" % d], ["nstrgd%d" % gi])
                self.CP("pool", tlo[:, s0:s0 + n, :], nstro[d][:].unsqueeze(1).broadcast_to([128, n, 128]), ["nstro%d" % d], ["nstrgo%d" % gi])
            nstrg_d.append(tld)
            nstrg_o.append(tlo)
        qnT = self.sb(ph, "qnT", [128, 3, S], BF16)
        zst = self.sb(ph, "zst", [128, NT, 384], BF16)
        osum = self.sb(ph, "osum", [128, NT, 384], F32)
        sqs = self.sb(ph, "sqs", [128, 768], F32)
        ssq = self.sb(ph, "ssq", [128, NT, 12], F32)
        gin = self.sb(ph, "gin", [128, NT, 24], F32)
        dg = self.sb(ph, "dg", [128, 2, 12], F32)
        nw = self.sb(ph, "nw", [128, 64], F32)
        G = {}
        for nm in ("sbt", "gl", "gc", "egc", "gt", "egt", "edec", "r1"):
            G[nm] = self.sb(ph, "g_" + nm, [128, 2, NT, 6], F32)
        gls = [self.sb(ph, "gls%d" % i, [128, 2, NT, 6], BF16) for i in range(3)]
        gcs = [self.sb(ph, "gcs%d" % i, [128, 2, NT, 6], BF16) for i in range(3)]
        egtS = self.sb(ph, "egtS", [128, 2, NT, 3], F32)
        self.DMA("sp", zst[:], self.zs.rearrange("(t p) n -> p t n", p=128), ["zs"], ["zst"], "ld_zs")
        self.DMA("sp", gin[:], self.gates.rearrange("(t p) n -> p t n", p=128), ["gates"], ["gin"], "ld_gin")
        self.DMA("sp", dg[:].rearrange("p a b -> p (a b)"), self.dgate[l:l + 1].rearrange("o a b -> o (a b)").broadcast_to([128, 24]), [], ["dg"], "ld_dg")
        self.DMA("sp", nw[:], self.dnormw[l:l + 1, :].broadcast_to([128, 64]), [], ["nw"], "ld_nw")
        for t in range(NT):
            self.ACT(sqs[:], tm[:, t, 0:768], AF.Square, ["tm"], ["sqs"])
            self.REDUCE(ssq[:, t, :], sqs[:].rearrange("p (a b) -> p a b", b=64), ["sqs"], ["ssq"])
        self.TS("dve", ssq[:], ssq[:], 1e-6, ALU.add, ["ssq"], ["ssq"])
        self.ACT(ssq[:], ssq[:], AF.Sqrt, ["ssq"], ["ssq"])
        self.RECIP(ssq[:], ssq[:], ["ssq"], ["ssq"])
        self.TS("dve", ssq[:, :, 0:6], ssq[:, :, 0:6], 0.125, ALU.mult, ["ssq"], ["ssq"])
        for t in range(NT):
            self.TT("dve", tm[:, t, 0:384].rearrange("p (h e) -> p h e", h=6), tm[:, t, 0:384].rearrange("p (h e) -> p h e", h=6),
                    ssq[:, t, 0:6].unsqueeze(2).broadcast_to([128, 6, 64]), ALU.mult, ["tm", "ssq"], ["tm"])
            self.TT("pool", tm[:, t, 384:768].rearrange("p (h e) -> p h e", h=6), tm[:, t, 384:768].rearrange("p (h e) -> p h e", h=6),
                    ssq[:, t, 6:12].unsqueeze(2).broadcast_to([128, 6, 64]), ALU.mult, ["tm", "ssq"], ["tm"])
            bk = 6 + (t % 2)
            pv = self.psbf(bk, 8)
            for c3 in range(3):
                self.TR(pv[:, c3, :], tm[:, t, c3 * 128:(c3 + 1) * 128], self.identb[:], ["tm", "identb"], [self.psk(bk)])
            self.CP("act", qnT[:, :, t * 128:(t + 1) * 128], pv[:, 0:3, :], [], [self.psk(bk), "qnT"])
        self.ACT(dg[:, 0, :], dg[:, 0, :], AF.Exp, ["dg"], ["dg"])
        self.TS("dve", dg[:, 0, :], dg[:, 0, :], -1.0, ALU.mult, ["dg"], ["dg"])
        for d in range(2):
            bsl = gin[:, :, d * 6:(d + 1) * 6]
            asl = gin[:, :, 12 + d * 6:12 + (d + 1) * 6]
            self.ACT(G["sbt"][:, d, :, :], bsl, AF.Sigmoid, ["gin"], ["sbt"])
            self.ACT(G["sbt"][:, d, :, :], G["sbt"][:, d, :, :], AF.Sqrt, ["sbt"], ["sbt"])
            self.TT("dve", G["gl"][:, d, :, :], asl, dg[:, 1, d * 6:(d + 1) * 6].unsqueeze(1).broadcast_to([128, NT, 6]), ALU.add, ["gin", "dg"], ["gl"])
            self.ACT(G["gl"][:, d, :, :], G["gl"][:, d, :, :], AF.Exp, ["gl"], ["gl"])
            self.TS("dve", G["gl"][:, d, :, :], G["gl"][:, d, :, :], 1.0, ALU.add, ["gl"], ["gl"])
            self.ACT(G["gl"][:, d, :, :], G["gl"][:, d, :, :], AF.Ln, ["gl"], ["gl"])
            self.TT("dve", G["gl"][:, d, :, :], G["gl"][:, d, :, :], dg[:, 0, d * 6:(d + 1) * 6].unsqueeze(1).broadcast_to([128, NT, 6]), ALU.mult, ["gl", "dg"], ["gl"])

        def split3(src, dst, key_src, key_dst):
            r1 = G["r1"]
            self.CP("dve", dst[0][:], src[:], [key_src], [key_dst])
            self.TT("dve", r1[:], src[:], dst[0][:], ALU.subtract, [key_src, key_dst], ["r1"])
            self.CP("dve", dst[1][:], r1[:], ["r1"], [key_dst])
            self.TT("dve", r1[:], r1[:], dst[1][:], ALU.subtract, ["r1", key_dst], ["r1"])
            self.CP("dve", dst[2][:], r1[:], ["r1"], [key_dst])

        split3(G["gl"], gls, "gl", "gls")
        for d in range(2):
            bk = self.nb()
            for i in range(3):
                self.MM(self.ps[bk][:, 0:NT * 6], Lmat[d][:], gls[i][:, d, :, :].rearrange("p t h -> p (t h)"), i == 0, i == 2, ["Lmat%d" % d, "gls"], [self.psk(bk)])
            self.CP("dve", G["gc"][:, d, :, :].rearrange("p t h -> p (t h)"), self.ps[bk][:, 0:NT * 6], [], [self.psk(bk), "gc"])
            bk = self.nb()
            for i in range(3):
                self.MM(self.ps[bk][:, 0:NT * 6], onesb[:], gls[i][:, d, :, :].rearrange("p t h -> p (t h)"), i == 0, i == 2, ["onesb1", "gls"], [self.psk(bk)])
            self.CP("dve", G["gt"][:, d, :, :].rearrange("p t h -> p (t h)"), self.ps[bk][:, 0:NT * 6], [], [self.psk(bk), "gt"])
        self.ACT(G["egc"][:], G["gc"][:], AF.Exp, ["gc"], ["egc"])
        self.ACT(G["egt"][:], G["gt"][:], AF.Exp, ["gt"], ["egt"])
        self.TT("dve", G["edec"][:], G["gt"][:], G["gc"][:], ALU.subtract, ["gt", "gc"], ["edec"])
        self.ACT(G["edec"][:], G["edec"][:], AF.Exp, ["edec"], ["edec"])
        split3(G["gc"], gcs, "gc", "gcs")
        ev = G["egt"][:].rearrange("p d t (a two) -> p d t a two", two=2)
        self.CP("pool", egtS[0:64], ev[0:64, :, :, :, 0], ["egt"], ["egtS"])
        self.CP("pool", egtS[64:128], ev[64:128, :, :, :, 1], ["egt"], ["egtS"])
        p3 = ExitStack()

        def mk(name, shape, dt):
            return (self.sb(p3, name, shape, dt), name)

        PS = []
        for pb in range(2):
            d_ = {}
            for d in range(2):
                d_["r0", d] = mk("r0_%d%d" % (pb, d), [128, 6, 128], BF16)
                d_["kp", d] = mk("kp_%d%d" % (pb, d), [128, 6, 64], BF16)
                d_["kdec", d] = mk("kdec_%d%d" % (pb, d), [128, 6, 64], BF16)
                for hf in range(2):
                    d_["kpT", d, hf] = mk("kpT_%d%d%d" % (pb, d, hf), [128, 3, 128], BF16)
                    self.MEMSET("pool", d_["kpT", d, hf][0][:], 0.0, [d_["kpT", d, hf][1]])
                d_["AqkT", d] = mk("AqkT_%d%d" % (pb, d), [128, 6, 128], BF16)
                d_["ru", d] = mk("ru_%d%d" % (pb, d), [128, 6, 64], F32)
                d_["rwb", d] = mk("rwb_%d%d" % (pb, d), [128, 6, 64], BF16)
            PS.append(d_)
        RT = []
        for d in range(2):
            d_ = {}
            d_["wT"] = mk("wT%d" % d, [128, 3, 128], BF16)
            d_["Sst"] = mk("Sst%d" % d, [128, 3, 64], F32)
            d_["Sbf"] = [mk("Sbf%d_%d" % (d, hf), [128, 3, 64], BF16) for hf in range(2)]
            d_["tmpS"] = mk("tmpS%d" % d, [128, 3, 64], F32)
            d_["vpp"] = mk("vpp%d" % d, [128, 6, 64], BF16)
            d_["o1"] = mk("do1_%d" % d, [128, 6, 64], F32)
            self.MEMSET("pool", d_["Sst"][0][:], 0.0, [d_["Sst"][1]])
            for hf in range(2):
                self.MEMSET("pool", d_["Sbf"][hf][0][:], 0.0, [d_["Sbf"][hf][1]])
            RT.append(d_)
        SL = []
        for sl in range(3):
            d_ = {}
            d_["dgi"] = [mk("dgi%d_%d" % (sl, i), [128, 4, 128], BF16) for i in range(3)]
            d_["d0"] = mk("d0_%d" % sl, [128, 4, 128], F32)
            d_["tmpk"] = mk("tmpk%d" % sl, [128, 4, 128], F32)
            d_["PT"] = [mk("PT%d_%d" % (sl, i), [128, 4, 128], BF16) for i in range(5)]
            d_["Pm"] = [mk("Pm%d_%d" % (sl, i), [128, 4, 128], BF16) for i in range(2)]
            d_["Pd0"] = mk("Pd0_%d" % sl, [128, 4, 128], BF16)
            d_["PoT"] = mk("PoT%d" % sl, [128, 4, 128], BF16)
            d_["XTb"] = mk("XTb%d" % sl, [128, 4, 128], BF16)
            d_["banks"] = (2 * sl, 2 * sl + 1)
            SL.append(d_)
        nidentb = self.sb(p3, "nidentb", [128, 128], BF16)
        self.TS("pool", nidentb[:], self.identf[:], -1.0, ALU.mult, ["identf"], ["nidentb"])
        touched = set()
        B_T, B_REC = 6, 7

        def v4(bk):
            return self.ps[bk][:].rearrange("p (a b) -> p a b", b=128)

        def w6(bk):
            return self.ps[bk][:, 0:384].rearrange("p (h e) -> p h e", h=6)

        def tiles_of(step):
            return [step, NT - 1 - step]

        def prep(step):
            tt = tiles_of(step)
            P_ = PS[step % 2]
            for d in range(2):
                t = tt[d]
                kp, kpk = P_["kp", d]
                r0, r0k = P_["r0", d]
                kdec, kdk = P_["kdec", d]
                sb6 = G["sbt"][:, d, t, :].unsqueeze(2).broadcast_to([128, 6, 64])
                kn6 = tm[:, t, 384:768].rearrange("p (h e) -> p h e", h=6)
                v6 = tm[:, t, 768:1152].rearrange("p (h e) -> p h e", h=6)
                self.TT("pool", kp[:], kn6, sb6, ALU.mult, ["tm", "sbt"], [kpk])
                self.TT("pool", r0[:, :, 0:64], v6, sb6, ALU.mult, ["tm", "sbt"], [r0k])
                self.TT("pool", r0[:, :, 64:128], kp[:], G["egc"][:, d, t, :].unsqueeze(2).broadcast_to([128, 6, 64]), ALU.mult, [kpk, "egc"], [r0k])
                self.TT("pool", kdec[:], kp[:], G["edec"][:, d, t, :].unsqueeze(2).broadcast_to([128, 6, 64]), ALU.mult, [kpk, "edec"], [kdk])
                pv = self.psbf(B_T, 8)
                kpf = kp[:].rearrange("p h e -> p (h e)")
                for c3 in range(3):
                    self.TR(pv[:, c3, :], kpf[:, c3 * 128:(c3 + 1) * 128], self.identb[:], [kpk, "identb"], [self.psk(B_T)])
                self.CP("act", P_["kpT", d, 0][0][0:64], pv[0:64, 0:3, :], [], [self.psk(B_T), P_["kpT", d, 0][1]])
                self.CP("dve", P_["kpT", d, 1][0][64:128], pv[64:128, 0:3, :], [], [self.psk(B_T), P_["kpT", d, 1][1]])

        def ut_group(sl, step, gi):
            g = groups[gi]
            tt = tiles_of(step)
            P_ = PS[step % 2]
            T_ = SL[sl]
            BA, BB = T_["banks"]
            kA, kB = self.psk(BA), self.psk(BB)
            dgi = T_["dgi"]
            d0, d0k = T_["d0"]
            tmpk, tmpkk = T_["tmpk"]
            PT = T_["PT"]
            Pm = T_["Pm"]
            Pd0, Pd0k = T_["Pd0"]
            PoT, PoTk = T_["PoT"]
            XTb, XTbk = T_["XTb"]
            Xb, Xbk = Pm[0]
            Qb, Qbk = Pm[1]
            XT2b, XT2bk = PT[1]
            xjb, xjbk = PT[2]
            yjb, yjbk = PT[3]
            combos = []
            for (d, h0, h1, s0) in g:
                for h in range(h0, h1):
                    combos.append((d, h, s0 + h - h0))
            for i in range(3):
                for (d, h0, h1, s0) in g:
                    n = h1 - h0
                    self.TT("pool", dgi[i][0][:, s0:s0 + n, :], self.identb[:].unsqueeze(1).broadcast_to([128, n, 128]),
                            gcs[i][:, d, tt[d], h0:h1].unsqueeze(2).broadcast_to([128, n, 128]), ALU.mult, ["identb", "gcs"], [dgi[i][1]])
            first = True
            for (d, h, s) in combos:
                for i in range(3):
                    self.MM(v4(BA)[:, s, :], onesb[:], dgi[i][0][:, s, :], first, False, ["onesb1", dgi[i][1]], [kA], skip_group_check=True)
                    first = False
            for (d, h, s) in combos:
                p = h // 2
                kT_, kTk = P_["kpT", d, h % 2]
                self.MM(v4(BB)[:, s, :], kT_[:, p, :], kT_[:, p, :], True, True, [kTk], [kB])
            yield
            for (d, h, s) in combos:
                self.STT(d0[:, s, :], v4(BA)[:, s, :], G["gc"][:, d, tt[d], h:h + 1], maskneg[d][:], ALU.subtract, ALU.add,
                         ["gc", "maskneg%d" % d], [kA, d0k])
            self.ACT(d0[:], d0[:], AF.Exp, [d0k], [d0k])
            yield
            self.TT("dve", tmpk[:], v4(BB)[:], d0[:], ALU.mult, [d0k], [kB, tmpkk])
            for (d, h, s) in combos:
                p = h // 2
                t = tt[d]
                kT_, kTk = P_["kpT", d, h % 2]
                self.MM(v4(BB)[:, s, :], kT_[:, p, :], qnT[:, p, t * 128:(t + 1) * 128], True, True, [kTk, "qnT"], [kB])
            self.TT("pool", PT[0][0][:], tmpk[:], nstrg_d[gi][:], ALU.mult, [tmpkk, "nstrgd%d" % gi], [PT[0][1]])
            self.TT("pool", PoT[:], tmpk[:], nstrg_o[gi][:], ALU.mult, [tmpkk, "nstrgo%d" % gi], [PoTk])
            yield
            for (d, h0, h1, s0) in g:
                n = h1 - h0
                Aq, Aqk_ = P_["AqkT", d]
                self.TT("dve", Aq[:, h0:h1, :], v4(BB)[:, s0:s0 + n, :], d0[:, s0:s0 + n, :], ALU.mult, [d0k], [kB, Aqk_])
            yield
            pv = self.psbf(B_T, 8)
            for (d, h, s) in combos:
                self.TR(pv[:, s, :], PT[0][0][:, s, :], self.identb[:], [PT[0][1], "identb"], [self.psk(B_T)])
            self.CP("act", Pd0[:], pv[:, 0:4, :], [], [self.psk(B_T), Pd0k])
            yield
            first = True
            for (d, h, s) in combos:
                self.MM(v4(BA)[:, s, :], self.identb[:], self.identb[:], first, False, ["identb"], [kA], skip_group_check=True)
                first = False
                self.MM(v4(BA)[:, s, :], Pd0[:, s, :], self.identb[:], False, False, [Pd0k, "identb"], [kA], skip_group_check=True)
            for k in range(5):
                Pk, Pkk = (Pd0, Pd0k) if k == 0 else Pm[k % 2]
                if k > 0:
                    self.CP("act", XTb[:], v4(BA)[:], [], [kA, XTbk])
                    for (d, h, s) in combos:
                        self.MM(v4(BA)[:, s, :], Pk[:, s, :], XTb[:, s, :], False, k == 4, [Pkk, XTbk], [kA], skip_group_check=True)
                if k < 4:
                    nx = (k + 1) % 2
                    for (d, h, s) in combos:
                        self.MM(v4(BB)[:, s, :], PT[k][0][:, s, :], Pk[:, s, :], True, True, [PT[k][1], Pkk], [kB])
                    yield
                    self.CP("dve", Pm[nx][0][:], v4(BB)[:], [], [kB, Pm[nx][1]])
                    for (d, h, s) in combos:
                        self.MM(v4(BB)[:, s, :], Pk[:, s, :], PT[k][0][:, s, :], True, True, [PT[k][1], Pkk], [kB])
                    yield
                    self.CP("act", PT[k + 1][0][:], v4(BB)[:], [], [kB, PT[k + 1][1]])
                yield
            self.CP("act", XTb[:], v4(BA)[:], [], [kA, XTbk])
            yield
            pv = self.psbf(B_T, 8)
            for (d, h, s) in combos:
                self.TR(pv[:, s, :], XTb[:, s, :], self.identb[:], [XTbk, "identb"], [self.psk(B_T)])
            self.CP("dve", Xb[:], pv[:, 0:4, :], [], [self.psk(B_T), Xbk])
            first = True
            for (d, h, s) in combos:
                self.MM(v4(BB)[:, s, :], self.identb[:], self.identb[:], first, False, ["identb"], [kB], skip_group_check=True)
                first = False
                self.MM(v4(BB)[:, s, :], nidentb[:], XTb[:, s, :], False, False, ["nidentb", XTbk], [kB], skip_group_check=True)
                self.MM(v4(BB)[:, s, :], Pd0[:, s, :], XTb[:, s, :], False, True, [Pd0k, XTbk], [kB], skip_group_check=True)
            yield
            self.CP("act", Qb[:], v4(BB)[:], [], [kB, Qbk])
            yield
            first = True
            for (d, h, s) in combos:
                self.MM(v4(BB)[:, s, :], self.identb[:], XTb[:, s, :], first, False, ["identb", XTbk], [kB], skip_group_check=True)
                first = False
                self.MM(v4(BB)[:, s, :], Xb[:, s, :], Qb[:, s, :], False, True, [Xbk, Qbk], [kB], skip_group_check=True)
            yield
            self.CP("dve", XT2b[:], v4(BB)[:], [], [kB, XT2bk])
            yield
            for it in range(4):
                if it > 0:
                    first = True
                    for (d, h, s) in combos:
                        r0, r0k = P_["r0", d]
                        self.MM(v4(BB)[:, s, :], self.identb[:], r0[:, h, :], first, False, ["identb", r0k], [kB], skip_group_check=True)
                        first = False
                        self.MM(v4(BB)[:, s, :], PoT[:, s, :], xjb[:, s, :], False, True, [PoTk, xjbk], [kB], skip_group_check=True)
                    yield
                    self.CP("act", yjb[:], v4(BB)[:], [], [kB, yjbk])
                    yield
                for (d, h, s) in combos:
                    r0, r0k = P_["r0", d]
                    rhs = r0[:, h, :] if it == 0 else yjb[:, s, :]
                    rkeys = [r0k] if it == 0 else [yjbk]
                    self.MM(v4(BA)[:, s, :], XT2b[:, s, :], rhs, True, True, [XT2bk] + rkeys, [kA])
                yield
                if it < 3:
                    self.CP("dve", xjb[:], v4(BA)[:], [], [kA, xjbk])
                    yield
            for (d, h0, h1, s0) in g:
                n = h1 - h0
                ru, ruk = P_["ru", d]
                rwb, rwbk = P_["rwb", d]
                self.CP("dve", ru[:, h0:h1, :], v4(BA)[:, s0:s0 + n, 0:64], [], [kA, ruk])
                self.CP("act", rwb[:, h0:h1, :], v4(BA)[:, s0:s0 + n, 64:128], [], [kA, rwbk])

        def rec_dir(step, d):
            tt = tiles_of(step)
            t = tt[d]
            P_ = PS[step % 2]
            R_ = RT[d]
            wT, wTk = R_["wT"]
            Sst, Sstk = R_["Sst"]
            Sbf = R_["Sbf"]
            tmpS, tmpSk = R_["tmpS"]
            vpp, vppk = R_["vpp"]
            o1, o1k = R_["o1"]
            ru, ruk = P_["ru", d]
            rwb, rwbk = P_["rwb", d]
            Aq, Aqk_ = P_["AqkT", d]
            kdec, kdk = P_["kdec", d]
            kR = self.psk(B_REC)
            pv = self.psbf(B_T, 8)
            rwf = rwb[:].rearrange("p h e -> p (h e)")
            for c3 in range(3):
                self.TR(pv[:, c3, :], rwf[:, c3 * 128:(c3 + 1) * 128], self.identb[:], [rwbk, "identb"], [self.psk(B_T)])
            self.CP("act", wT[:], pv[:, 0:3, :], [], [self.psk(B_T), wTk])
            yield
            for h in range(6):
                p = h // 2
                self.MM(w6(B_REC)[:, h, :], wT[:, p, :], Sbf[h % 2][0][:, p, :], True, True, [wTk, Sbf[h % 2][1]], [kR])
            yield
            self.TT("dve", vpp[:], ru[:], w6(B_REC), ALU.subtract, [ruk], [kR, vppk])
            yield
            for h in range(6):
                p = h // 2
                self.MM(w6(B_REC)[:, h, :], qnT[:, p, t * 128:(t + 1) * 128], Sbf[h % 2][0][:, p, :], True, True, ["qnT", Sbf[h % 2][1]], [kR])
            yield
            self.TT("dve", o1[:], w6(B_REC), G["egc"][:, d, t, :].unsqueeze(2).broadcast_to([128, 6, 64]), ALU.mult, ["egc"], [kR, o1k])
            yield
            for h in range(6):
                self.MM(w6(B_REC)[:, h, :], Aq[:, h, :], vpp[:, h, :], True, True, [Aqk_, vppk], [kR])
            yield
            ot = osum[:, t, :].rearrange("p (h e) -> p h e", h=6)
            if t not in touched:
                touched.add(t)
                self.TT("dve", ot, o1[:], w6(B_REC), ALU.add, [o1k], [kR, ("osum", t)])
            else:
                self.TT("dve", o1[:], o1[:], w6(B_REC), ALU.add, [o1k], [kR, o1k])
                self.TT("pool", ot, ot, o1[:], ALU.add, [o1k, ("osum", t)], [("osum", t)])
            yield
            for h in range(6):
                p, base = h // 2, 64 * (h % 2)
                self.MM(self.ps[B_REC][base:base + 64, p * 64:(p + 1) * 64], kdec[:, h, :], vpp[:, h, :], True, True, [kdk, vppk], [kR])
            self.TT("pool", tmpS[:], Sst[:], egtS[:, d, t, :].unsqueeze(2).broadcast_to([128, 3, 64]), ALU.mult, [Sstk, "egtS"], [tmpSk])
            yield
            self.TT("dve", Sst[:], tmpS[:], self.ps[B_REC][:, 0:192].rearrange("p (a b) -> p a b", b=64), ALU.add, [tmpSk], [kR, Sstk])
            yield
            self.CP("act", Sbf[0][0][0:64], Sst[0:64], [Sstk], [Sbf[0][1]])
            self.CP("pool", Sbf[1][0][64:128], Sst[64:128], [Sstk], [Sbf[1][1]])

        ut_stream = [(st, gi) for st in range(NT) for gi in range(3)]
        ut_done = [0] * NT
        rec_done = [False] * NT
        rec_next = 0
        rec_active = 0
        active = []
        free_slots = [0, 1, 2]
        nxt = 0
        while True:
            while free_slots and nxt < len(ut_stream):
                st, gi = ut_stream[nxt]
                if gi == 0 and st >= 2 and not rec_done[st - 2]:
                    break
                nxt += 1
                if gi == 0:
                    prep(st)
                sl = free_slots.pop(0)
                active.append(["ut", ut_group(sl, st, gi), sl, st])
            if rec_active == 0 and rec_next < NT and ut_done[rec_next] == 3:
                def rec_step(st_):
                    yield from rec_dir(st_, 0)
                    yield
                    yield from rec_dir(st_, 1)
                active.append(["rec", rec_step(rec_next), None, rec_next])
                rec_active = 1
                rec_next += 1
            if not active:
                break
            for a_ in list(active):
                try:
                    next(a_[1])
                except StopIteration:
                    active.remove(a_)
                    if a_[0] == "ut":
                        free_slots.append(a_[2])
                        ut_done[a_[3]] += 1
                    else:
                        rec_active -= 1
                        rec_done[a_[3]] = True
        self.P.flush()
        p3.close()
        oss = self.sb(ph, "doss", [128, NT, 6], F32)
        y1 = self.sb(ph, "dy1", [128, 384], F32)
        nwz = self.sb(ph, "nwz", [128, 384], F32)
        ydel = [self.sb(ph, "ydel%d" % i, [128, 384], BF16) for i in range(2)]
        yT = self.sb(ph, "dyT", [128, 3, S], BF16)
        for t in range(NT):
            self.ACT(sqs[:, 0:384], osum[:, t, :], AF.Square, [("osum", t)], ["sqs"])
            self.REDUCE(oss[:, t, :], sqs[:, 0:384].rearrange("p (a b) -> p a b", b=64), ["sqs"], ["doss"])
        self.TS("dve", oss[:], oss[:], 1.0 / 64.0, ALU.mult, ["doss"], ["doss"], s2=1e-6, op1=ALU.add)
        self.ACT(oss[:], oss[:], AF.Sqrt, ["doss"], ["doss"])
        self.RECIP(oss[:], oss[:], ["doss"], ["doss"])
        for t in range(NT):
            b = t % 2
            self.TT("dve", y1[:].rearrange("p (h e) -> p h e", h=6), osum[:, t, :].rearrange("p (h e) -> p h e", h=6),
                    oss[:, t, :].unsqueeze(2).broadcast_to([128, 6, 64]), ALU.mult, [("osum", t), "doss"], ["dy1"])
            self.TT("pool", nwz[:].rearrange("p (h e) -> p h e", h=6), zst[:, t, :].rearrange("p (h e) -> p h e", h=6),
                    nw[:].unsqueeze(1).broadcast_to([128, 6, 64]), ALU.mult, ["zst", "nw"], ["nwz"])
            self.TT("dve", ydel[b][:], y1[:], nwz[:], ALU.mult, ["dy1", "nwz"], ["ydel%d" % b])
            bk = 4 + (t % 2)
            pv = self.psbf(bk, 8)
            for c3 in range(3):
                self.TR(pv[:, c3, :], ydel[b][:, c3 * 128:(c3 + 1) * 128], self.identb[:], ["ydel%d" % b, "identb"], [self.psk(bk)])
            self.CP("act", yT[:, :, t * 128:(t + 1) * 128], pv[:, 0:3, :], [], [self.psk(bk), "dyT"])
        for c3 in range(3):
            self.DMA("sp", self.ycatT[5 + c3], yT[:, c3, :], ["dyT"], ["ycatT"], "st_ydel")
        self.P.flush()


Builder.phase_delta = phase_delta
```
